# Optimizing a Trainium2 kernel written in Bass

```python
import math
import jax, jax.numpy as jnp
from jax import lax
import numpy as np

D_MODEL = 1024
BATCH = 32
SEQ = 256
DEPTH = 2
DEC_BATCH = 8
DEC_SEQ = 4096
PAST_LEN = 256

GRID_W = 64
HEAD_DIM = 64
MIX_A = D_MODEL // 4
MIX_B = D_MODEL // 4
MIX_C = D_MODEL // 4
MIX_D = D_MODEL // 4
MIX_WIDTH = MIX_A + MIX_B + MIX_C + MIX_D
RW_HEADS = MIX_A // HEAD_DIM
RW_DECAY_LORA = 64
RW_A_LORA = 64
RW_GATE_LORA = 128
RW_COLS = 3 * MIX_A + RW_DECAY_LORA + RW_A_LORA + RW_GATE_LORA
RW_EPS = 64e-5
S5_CH = 16
S5_GROUPS = MIX_B // S5_CH
S5_STATE = 64
HG_HEADS = MIX_C // HEAD_DIM
HG_CHUNK = 64
NA_HEADS = MIX_D // HEAD_DIM
NA_ROWS = 8
NA_COLS = 16
Q_BLOCK = 128
D_FF = 128 * ((8 * D_MODEL // 3 + 127) // 128)
N_MOD = 9
IN_COLS = RW_COLS + MIX_B + 5 * MIX_C + 3 * MIX_D
ALPHA = (2 * DEPTH) ** 0.25
BETA = (8 * DEPTH) ** -0.25
LN_EPS = 1e-5
F_TINY = 1e-30

kernel_name = 'hybrid_flow_trunk_step'


def _split(p, sizes):
    idx = np.cumsum(sizes)[:-1].tolist()
    return jnp.split(p, idx, axis=-1)


def _norm_last(y, eps):
    yf = y.astype(jnp.float32)
    mu = jnp.mean(yf, -1, keepdims=True)
    var = jnp.mean(jnp.square(yf - mu), -1, keepdims=True)
    return (yf - mu) * lax.rsqrt(var + eps)


def layer_norm(x, g, b):
    return (_norm_last(x, LN_EPS) * g + b).astype(x.dtype)


def modulate(x, shift, scale):
    return x * (1 + scale) + shift


def swiglu(h, w_in, w_out):
    gate, up = jnp.split(h @ w_in, 2, axis=-1)
    return (jax.nn.silu(gate) * up) @ w_out


def token_shift(p, mu_prev, mu_next):
    prev = jnp.pad(p[:, :-1], ((0, 0), (1, 0), (0, 0)))
    nxt = jnp.pad(p[:, 1:], ((0, 0), (0, 1), (0, 0)))
    return p + mu_prev * (prev - p) + mu_next * (nxt - p)


def rwkv7_scan(r, w, k, v, kk, kka, s0, reverse):
    xs = tuple(jnp.moveaxis(t, 1, 0) for t in (r, w, k, v, kk, kka))

    def step(S, inp):
        r_t, w_t, k_t, v_t, kk_t, b_t = inp
        sa = -jnp.einsum('bhvk,bhk->bhv', S, kk_t)
        S = S * w_t[:, :, None, :] + sa[..., None] * b_t[:, :, None, :] + v_t[..., None] * k_t[:, :, None, :]
        return S, jnp.einsum('bhvk,bhk->bhv', S, r_t)

    S, ys = lax.scan(step, s0, xs, reverse=reverse)
    return S, jnp.moveaxis(ys, 0, 1)


def rwkv7_mix(pA, s0, lp):
    B, T, _ = pA.shape
    pA = token_shift(pA, lp['rw_mu'][0], lp['rw_mu'][1])
    r, k, v, wd, ad, gd = _split(pA, (MIX_A, MIX_A, MIX_A, RW_DECAY_LORA, RW_A_LORA, RW_GATE_LORA))
    a = jax.nn.sigmoid(lp['rw_a0'] + ad @ lp['rw_a_up'])
    g = jax.nn.sigmoid(gd) @ lp['rw_g_up']
    heads = lambda t: t.reshape(B, T, RW_HEADS, HEAD_DIM)
    kk = heads(k * lp['rw_k_k'])
    kk = kk / jnp.maximum(jnp.linalg.norm(kk, axis=-1, keepdims=True), 1e-12)
    k_eff = heads(k * (1 + (a - 1) * lp['rw_k_a']))
    r_h, v_h = heads(r), heads(v)
    kka = kk * heads(a)
    tw = jnp.tanh(wd)
    y = jnp.zeros_like(r_h)
    finals = []
    for d in range(2):
        w_log = -jax.nn.softplus(-(lp['rw_w0'][d] + tw @ lp['rw_w_up'][d])) - 0.5
        decay = heads(jnp.exp(-jnp.exp(w_log)))
        S, y_d = rwkv7_scan(r_h, decay, k_eff, v_h, kk, kka, s0[:, d], reverse=(d == 1))
        y = y + y_d
        finals.append(S)
    y = _norm_last(y, RW_EPS).reshape(B, T, MIX_A) * lp['rw_lnx_g'] + lp['rw_lnx_b']
    bonus = (jnp.sum(r_h * k_eff * lp['rw_r_k'], -1, keepdims=True) * v_h).reshape(B, T, MIX_A)
    return ((y + bonus) * g).astype(pA.dtype), jnp.stack(finals, 1)


def s5_scan(ab_re, ab_im, bu_re, bu_im, s0_re, s0_im, reverse):
    idx = -1 if reverse else 0
    bu_re = bu_re.at[:, idx].add(ab_re * s0_re - ab_im * s0_im)
    bu_im = bu_im.at[:, idx].add(ab_re * s0_im + ab_im * s0_re)
    T = bu_re.shape[1]
    a_re = jnp.broadcast_to(ab_re, (1, T) + ab_re.shape)
    a_im = jnp.broadcast_to(ab_im, (1, T) + ab_im.shape)

    def combine(e1, e2):
        ar1, ai1, br1, bi1 = e1
        ar2, ai2, br2, bi2 = e2
        return (ar1 * ar2 - ai1 * ai2, ar1 * ai2 + ai1 * ar2,
                ar2 * br1 - ai2 * bi1 + br2, ar2 * bi1 + ai2 * br1 + bi2)

    _, _, s_re, s_im = lax.associative_scan(combine, (a_re, a_im, bu_re, bu_im), reverse=reverse, axis=1)
    return s_re, s_im


def s5_mix(u, s0, lp):
    B, T, _ = u.shape
    ug = u.reshape(B, T, S5_GROUPS, S5_CH)
    y = jnp.zeros_like(ug)
    finals = []
    for d in range(2):
        dt = jnp.exp(lp['s5_log_dt'][d])[:, None]
        lam_re, lam_im = lp['s5_a_re'][d], lp['s5_a_im'][d]
        mag = jnp.exp(lam_re * dt)
        ab_re, ab_im = mag * jnp.cos(lam_im * dt), mag * jnp.sin(lam_im * dt)
        den = lam_re * lam_re + lam_im * lam_im
        z_re = ((ab_re - 1) * lam_re + ab_im * lam_im) / den
        z_im = (ab_im * lam_re - (ab_re - 1) * lam_im) / den
        bb_re = z_re[..., None] * lp['s5_b_re'] - z_im[..., None] * lp['s5_b_im']
        bb_im = z_re[..., None] * lp['s5_b_im'] + z_im[..., None] * lp['s5_b_re']
        bu_re = jnp.einsum('btgc,gnc->btgn', ug, bb_re)
        bu_im = jnp.einsum('btgc,gnc->btgn', ug, bb_im)
        s_re, s_im = s5_scan(ab_re, ab_im, bu_re, bu_im, s0[:, d, ..., 0], s0[:, d, ..., 1], reverse=(d == 1))
        y = y + jnp.einsum('btgn,gcn->btgc', s_re, lp['s5_c_re']) - jnp.einsum('btgn,gcn->btgc', s_im, lp['s5_c_im'])
        idx = 0 if d == 1 else -1
        finals.append(jnp.stack([s_re[:, idx], s_im[:, idx]], -1))
    y = y.reshape(B, T, MIX_B) + lp['s5_d'] * u
    gl = jax.nn.gelu(y)
    return gl * jax.nn.sigmoid(gl @ lp['s5_glu_w'] + lp['s5_glu_b']), jnp.stack(finals, 1)


def gla_chunk_scan(q, k, v, log_f, s0):
    B, T, H, dk = q.shape
    dv = v.shape[-1]
    n = T // HG_CHUNK
    to_chunks = lambda t: t.reshape(B, n, HG_CHUNK, H, t.shape[-1]).transpose(1, 0, 3, 2, 4)
    mask = jnp.tril(jnp.ones((HG_CHUNK, HG_CHUNK), bool))[:, :, None]

    def step(S, inp):
        qc, kc, vc, gc = inp
        b = jnp.cumsum(gc, axis=2)
        diff = b[:, :, :, None, :] - b[:, :, None, :, :]
        decay = jnp.where(mask, jnp.exp(jnp.where(mask, diff, 0.0)), 0.0)
        att = jnp.einsum('bhtd,bhsd,bhtsd->bhts', qc, kc, decay)
        o = att @ vc + jnp.einsum('bhtd,bhdv->bhtv', qc * jnp.exp(b), S)
        b_last = b[:, :, -1:, :]
        S = jnp.exp(b_last[:, :, 0, :, None]) * S + jnp.einsum('bhsd,bhsv->bhdv', kc * jnp.exp(b_last - b), vc)
        return S, o

    S, o = lax.scan(step, s0, tuple(to_chunks(t) for t in (q, k, v, log_f)))
    return S, o.transpose(1, 0, 3, 2, 4).reshape(B, T, H, dv)


def hgrn2_mix(pC, s0, lb, lp):
    B, T, _ = pC.shape
    q, f_fw, f_bw, i, g = _split(pC, (MIX_C,) * 5)
    heads = lambda t: t.reshape(B, T, HG_HEADS, HEAD_DIM)
    q, v = heads(jax.nn.silu(q)), heads(i)
    o = jnp.zeros_like(v)
    finals = []
    for d, fz in enumerate((f_fw, f_bw)):
        f = lb + (1 - lb) * jax.nn.sigmoid(fz.astype(jnp.float32))
        log_f = heads(jnp.log(jnp.maximum(f, F_TINY)).astype(pC.dtype))
        k = heads((1 - lb) * jax.nn.sigmoid(-fz))
        if d == 0:
            S, o_d = gla_chunk_scan(q, k, v, log_f, s0[:, d])
        else:
            S, o_d = gla_chunk_scan(*(jnp.flip(t, 1) for t in (q, k, v, log_f)), s0[:, d])
            o_d = jnp.flip(o_d, 1)
        o = o + o_d
        finals.append(S)
    of = o.astype(jnp.float32)
    of = of * lax.rsqrt(jnp.mean(jnp.square(of), -1, keepdims=True) + LN_EPS)
    out = of.reshape(B, T, MIX_C) * lp['hg_norm_g'] * jax.nn.silu(g)
    return out.astype(pC.dtype), jnp.stack(finals, 1)


def context_attention(q, k, v):
    B, S, H, d = q.shape
    nb = S // Q_BLOCK
    qb = q.reshape(B, nb, Q_BLOCK, H, d).transpose(1, 0, 2, 3, 4)
    scale = d ** -0.5

    def block(qi):
        s = jnp.einsum('bqhd,bkhd->bhqk', qi, k) * scale
        pr = jax.nn.softmax(s.astype(jnp.float32), axis=-1).astype(v.dtype)
        return jnp.einsum('bhqk,bkhd->bqhd', pr, v)

    out = lax.map(block, qb)
    return out.transpose(1, 0, 2, 3, 4).reshape(B, S, H * d)


def neighbourhood_attention(q, k, v, kc, vc, rpb):
    B, T, H, d = q.shape
    rows = T // GRID_W
    kr = min(NA_ROWS, rows)
    grid = lambda t: t.reshape(B, rows, GRID_W, H, d).transpose(0, 3, 1, 2, 4)
    qg, kg, vg = grid(q), grid(k), grid(v)
    cs = jnp.clip(jnp.arange(GRID_W) - NA_COLS // 2, 0, GRID_W - NA_COLS)
    col_idx = cs[:, None] + jnp.arange(NA_COLS)[None, :]
    dc = col_idx - jnp.arange(GRID_W)[:, None] + NA_COLS - 1
    scale = d ** -0.5
    n_loc = kr * NA_COLS

    def row_fn(r):
        rs = jnp.clip(r - kr // 2, 0, rows - kr)
        q_r = lax.dynamic_index_in_dim(qg, r, axis=2, keepdims=False)
        k_win = lax.dynamic_slice_in_dim(kg, rs, kr, axis=2)[:, :, :, col_idx]
        v_win = lax.dynamic_slice_in_dim(vg, rs, kr, axis=2)[:, :, :, col_idx]
        dr = rs + jnp.arange(kr) - r + NA_ROWS - 1
        bias = rpb[:, dr[None, :, None], dc[:, None, :]]
        s_loc = jnp.einsum('bhwd,bhiwjd->bhwij', q_r, k_win) * scale + bias
        s_ctx = jnp.einsum('bhwd,bphd->bhwp', q_r, kc) * scale
        s = jnp.concatenate([s_loc.reshape(B, H, GRID_W, n_loc), s_ctx], -1)
        pr = jax.nn.softmax(s.astype(jnp.float32), axis=-1).astype(v.dtype)
        p_loc = pr[..., :n_loc].reshape(B, H, GRID_W, kr, NA_COLS)
        return (jnp.einsum('bhwij,bhiwjd->bhwd', p_loc, v_win)
                + jnp.einsum('bhwp,bphd->bhwd', pr[..., n_loc:], vc))

    out = lax.map(row_fn, jnp.arange(rows))
    return out.transpose(1, 0, 3, 2, 4).reshape(B, T, H * d)


def trunk_layer(x, mod, lp, lb, ctx):
    sh1, sc1, g1, sh2, sc2, g2, sh3, sc3, g3 = jnp.split(mod, N_MOD, axis=-1)
    x = layer_norm(ALPHA * x + 0.5 * g1 * swiglu(modulate(x, sh1, sc1), lp['ffn_w_in'][0], lp['ffn_w_out'][0]),
                   lp['ln_g'][0], lp['ln_b'][0])
    B, T, _ = x.shape
    p = modulate(x, sh2, sc2) @ lp['w_in']
    pA, pB, pC, pD = _split(p, (RW_COLS, MIX_B, 5 * MIX_C, 3 * MIX_D))
    if ctx is None:
        s_rw = jnp.zeros((B, 2, RW_HEADS, HEAD_DIM, HEAD_DIM), x.dtype)
        s_s5 = jnp.zeros((B, 2, S5_GROUPS, S5_STATE, 2), x.dtype)
        s_hg = jnp.zeros((B, 2, HG_HEADS, HEAD_DIM, HEAD_DIM), x.dtype)
    else:
        k_ctx, v_ctx, s_rw, s_s5, s_hg = ctx
    oA, fA = rwkv7_mix(pA, s_rw, lp)
    oB, fB = s5_mix(pB, s_s5, lp)
    oC, fC = hgrn2_mix(pC, s_hg, lb, lp)
    q, k, v = (t.reshape(B, T, NA_HEADS, HEAD_DIM) for t in _split(pD, (MIX_D,) * 3))
    if ctx is None:
        oD = context_attention(q, k, v)
    else:
        oD = neighbourhood_attention(q, k, v, k_ctx, v_ctx, lp['na_rpb'])
    mix = jnp.concatenate([oA, oB, oC, oD], -1) @ lp['w_out']
    x = layer_norm(ALPHA * x + g2 * mix, lp['ln_g'][1], lp['ln_b'][1])
    x = layer_norm(ALPHA * x + 0.5 * g3 * swiglu(modulate(x, sh3, sc3), lp['ffn_w_in'][1], lp['ffn_w_out'][1]),
                   lp['ln_g'][2], lp['ln_b'][2])
    if ctx is None:
        return x, (k, v, fA, fB, fC)
    return x, None


def setup_inputs(seed: int = 0) -> dict:
    key = jax.random.key(seed)
    keys = jax.random.split(key, 48)
    kit = iter(range(48))
    nrm = lambda shape, s=1.0: s * jax.random.normal(keys[next(kit)], shape, jnp.float32)
    L = DEPTH
    n_idx = jnp.arange(S5_STATE, dtype=jnp.float32)
    return {
        'x_prompt': nrm((BATCH, SEQ, D_MODEL)),
        'x_sample': nrm((DEC_BATCH, DEC_SEQ, D_MODEL)),
        'cache_na_k': nrm((DEC_BATCH, DEPTH, PAST_LEN, NA_HEADS, HEAD_DIM)),
        'cache_na_v': nrm((DEC_BATCH, DEPTH, PAST_LEN, NA_HEADS, HEAD_DIM)),
        'state_rwkv': nrm((DEC_BATCH, DEPTH, 2, RW_HEADS, HEAD_DIM, HEAD_DIM), 0.3),
        'state_s5': nrm((DEC_BATCH, DEPTH, 2, S5_GROUPS, S5_STATE, 2), 0.1),
        'state_hgrn': nrm((DEC_BATCH, DEPTH, 2, HG_HEADS, HEAD_DIM, HEAD_DIM), 0.3),
        'c': nrm((DEC_BATCH, D_MODEL)),
        'c_ctx': nrm((D_MODEL,)),
        'w_mod': nrm((L, D_MODEL, N_MOD * D_MODEL), 0.5 * D_MODEL ** -0.5),
        'b_mod': nrm((L, N_MOD * D_MODEL), 0.02),
        'ln_g': 1.0 + nrm((L, 3, D_MODEL), 0.02),
        'ln_b': nrm((L, 3, D_MODEL), 0.02),
        'ffn_w_in': nrm((L, 2, D_MODEL, 2 * D_FF), D_MODEL ** -0.5),
        'ffn_w_out': nrm((L, 2, D_FF, D_MODEL), BETA * D_FF ** -0.5),
        'w_in': nrm((L, D_MODEL, IN_COLS), D_MODEL ** -0.5),
        'w_out': nrm((L, MIX_WIDTH, D_MODEL), BETA * MIX_WIDTH ** -0.5),
        'rw_mu': 0.25 + nrm((L, 2, RW_COLS), 0.05),
        'rw_w0': jnp.linspace(-6.0, -1.0, MIX_A) + 0.5 + nrm((L, 2, MIX_A), 0.1),
        'rw_w_up': nrm((L, 2, RW_DECAY_LORA, MIX_A), 0.5 * RW_DECAY_LORA ** -0.5),
        'rw_a0': nrm((L, MIX_A), 0.1),
        'rw_a_up': nrm((L, RW_A_LORA, MIX_A), 0.5 * RW_A_LORA ** -0.5),
        'rw_g_up': nrm((L, RW_GATE_LORA, MIX_A), RW_GATE_LORA ** -0.5),
        'rw_k_k': 0.85 + nrm((L, MIX_A), 0.02),
        'rw_k_a': 1.0 + nrm((L, MIX_A), 0.02),
        'rw_r_k': nrm((L, RW_HEADS, HEAD_DIM), 0.1),
        'rw_lnx_g': 1.0 + nrm((L, MIX_A), 0.02),
        'rw_lnx_b': nrm((L, MIX_A), 0.02),
        's5_a_re': -0.5 + nrm((L, 2, S5_GROUPS, S5_STATE), 0.01),
        's5_a_im': math.pi * n_idx + nrm((L, 2, S5_GROUPS, S5_STATE), 0.01),
        's5_log_dt': jax.random.uniform(keys[next(kit)], (L, 2, S5_GROUPS), jnp.float32,
                                        minval=math.log(1e-3), maxval=math.log(1e-1)),
        's5_b_re': nrm((L, S5_GROUPS, S5_STATE, S5_CH), (2 * S5_CH) ** -0.5),
        's5_b_im': nrm((L, S5_GROUPS, S5_STATE, S5_CH), (2 * S5_CH) ** -0.5),
        's5_c_re': nrm((L, S5_GROUPS, S5_CH, S5_STATE), S5_STATE ** -0.5),
        's5_c_im': nrm((L, S5_GROUPS, S5_CH, S5_STATE), S5_STATE ** -0.5),
        's5_d': nrm((L, MIX_B)),
        's5_glu_w': nrm((L, MIX_B, MIX_B), MIX_B ** -0.5),
        's5_glu_b': nrm((L, MIX_B), 0.02),
        'hg_lb': nrm((L, MIX_C)),
        'hg_norm_g': 1.0 + nrm((L, MIX_C), 0.02),
        'na_rpb': nrm((L, NA_HEADS, 2 * NA_ROWS - 1, 2 * NA_COLS - 1), 0.1),
    }


def reference(x_prompt, x_sample, cache_na_k, cache_na_v, state_rwkv, state_s5, state_hgrn, c, c_ctx,
              w_mod, b_mod, ln_g, ln_b, ffn_w_in, ffn_w_out, w_in, w_out,
              rw_mu, rw_w0, rw_w_up, rw_a0, rw_a_up, rw_g_up, rw_k_k, rw_k_a, rw_r_k, rw_lnx_g, rw_lnx_b,
              s5_a_re, s5_a_im, s5_log_dt, s5_b_re, s5_b_im, s5_c_re, s5_c_im, s5_d, s5_glu_w, s5_glu_b,
              hg_lb, hg_norm_g, na_rpb):
    lbs = jax.nn.softmax(hg_lb.astype(jnp.float32), axis=0)
    lb_all = (jnp.cumsum(lbs, axis=0) - lbs[0]).astype(hg_lb.dtype)
    y_p, y_s = x_prompt, x_sample
    nk, nv, nrw, ns5, nhg = [], [], [], [], []
    for l in range(DEPTH):
        lp = {
            'ffn_w_in': ffn_w_in[l], 'ffn_w_out': ffn_w_out[l], 'ln_g': ln_g[l], 'ln_b': ln_b[l],
            'w_in': w_in[l], 'w_out': w_out[l],
            'rw_mu': rw_mu[l], 'rw_w0': rw_w0[l], 'rw_w_up': rw_w_up[l], 'rw_a0': rw_a0[l],
            'rw_a_up': rw_a_up[l], 'rw_g_up': rw_g_up[l], 'rw_k_k': rw_k_k[l], 'rw_k_a': rw_k_a[l],
            'rw_r_k': rw_r_k[l], 'rw_lnx_g': rw_lnx_g[l], 'rw_lnx_b': rw_lnx_b[l],
            's5_a_re': s5_a_re[l], 's5_a_im': s5_a_im[l], 's5_log_dt': s5_log_dt[l],
            's5_b_re': s5_b_re[l], 's5_b_im': s5_b_im[l], 's5_c_re': s5_c_re[l], 's5_c_im': s5_c_im[l],
            's5_d': s5_d[l], 's5_glu_w': s5_glu_w[l], 's5_glu_b': s5_glu_b[l],
            'hg_norm_g': hg_norm_g[l], 'na_rpb': na_rpb[l],
        }
        mod_ctx = (jax.nn.silu(c_ctx) @ w_mod[l] + b_mod[l])[None, None, :]
        mod_lat = (jax.nn.silu(c) @ w_mod[l] + b_mod[l])[:, None, :]
        y_p, (k_l, v_l, s_rw, s_s5, s_hg) = trunk_layer(y_p, mod_ctx, lp, lb_all[l], None)
        y_s, _ = trunk_layer(y_s, mod_lat, lp, lb_all[l],
                             (cache_na_k[:, l], cache_na_v[:, l], state_rwkv[:, l], state_s5[:, l], state_hgrn[:, l]))
        nk.append(k_l)
        nv.append(v_l)
        nrw.append(s_rw)
        ns5.append(s_s5)
        nhg.append(s_hg)
    return (y_p, y_s, jnp.stack(nk, 1), jnp.stack(nv, 1), jnp.stack(nrw, 1), jnp.stack(ns5, 1), jnp.stack(nhg, 1))
```

```python
import bisect
import math
from contextlib import ExitStack
import numpy as np
import concourse.bass as bass
import concourse.mybir as mybir
from concourse.bass_utils import run_bass_kernel_spmd

F32 = mybir.dt.float32
BF16 = mybir.dt.bfloat16
AF = mybir.ActivationFunctionType
ALU = mybir.AluOpType
AX = mybir.AxisListType
EPOCH = 20000
NDMASEM = 12

D = 1024
L = 2
DFF = 2816
NF = DFF // 128
INC = 3328
ALPHA = (2 * L) ** 0.25
LN_EPS = 1e-5
RW_EPS = 64e-5
NEG = -30000.0


class Eng:
    def __init__(self, K, name, eng, is_pe=False):
        self.K = K
        self.name = name
        self.eng = eng
        self.is_pe = is_pe
        self.insts = []
        self.marks = []
        self.sems = []
        self.seen = {}
        self.dseen = {}

    def sem_for(self, m):
        e = (m - 1) // EPOCH
        while len(self.sems) <= e:
            self.sems.append(self.K.new_sem(f"{self.name}_e{len(self.sems)}"))
        return self.sems[e], (m - 1) % EPOCH + 1


class DmaQ:
    def __init__(self, K, name, issuer):
        self.K = K
        self.name = name
        self.issuer = issuer
        self.n = 0
        self.sems = [K.new_sem(f"{name}_d{i}") for i in range(NDMASEM)]

    def semval(self, i):
        return self.sems[i % NDMASEM], 16 * (i // NDMASEM + 1)


class Res:
    __slots__ = ("w", "r")

    def __init__(self):
        self.w = None
        self.r = {}


class K:
    def __init__(self, nc):
        self.nc = nc
        self.es = ExitStack()
        self.nsem = 0
        self.pe = Eng(self, "pe", nc.tensor, is_pe=True)
        self.dve = Eng(self, "dve", nc.vector)
        self.act = Eng(self, "act", nc.scalar)
        self.pool = Eng(self, "pool", nc.gpsimd)
        self.sp = Eng(self, "sp", nc.sync)
        self.engs = {e.name: e for e in (self.pe, self.dve, self.act, self.pool, self.sp)}
        self.ld = DmaQ(self, "ld", self.sp)
        self.st = DmaQ(self, "st", self.pool)
        self.dq = {"ld": self.ld, "st": self.st}
        self.res = {}
        self.ninst = 0
        self.nwait = 0
        self.rr = 0

    def new_sem(self, name):
        self.nsem += 1
        return self.es.enter_context(self.nc.semaphore(name))

    def sb(self, name, shape, dt=F32, stack=None):
        self.uid = getattr(self, "uid", 0) + 1
        return (stack or self.es).enter_context(self.nc.sbuf_tensor(f"{name}_u{self.uid}", list(shape), dt))

    def ps(self, name, shape, dt=F32, stack=None):
        return (stack or self.es).enter_context(self.nc.psum_tensor(name, list(shape), dt))

    def _wait_inst(self, E, xname, idx):
        X = self.engs[xname]
        if E is X and E.is_pe:
            return
        p = bisect.bisect_left(X.marks, idx)
        if p < len(X.marks):
            m = p + 1
        else:
            m = len(X.marks) + 1
            sem, val = X.sem_for(m)
            X.insts[idx].then_inc(sem, 1)
            X.marks.append(idx)
        if E.seen.get(xname, 0) >= m:
            return
        sem, val = X.sem_for(m)
        E.eng.wait_ge(sem, val)
        self.nwait += 1
        E.seen[xname] = m

    def _wait_dma(self, E, qname, i):
        Q = self.dq[qname]
        key = (qname, i % NDMASEM)
        if E.dseen.get(key, -1) >= i:
            return
        sem, val = Q.semval(i)
        E.eng.wait_ge(sem, val)
        self.nwait += 1
        E.dseen[key] = i

    def _wait(self, E, dep):
        if dep[0] == "dma":
            self._wait_dma(E, dep[1], dep[2])
        else:
            self._wait_inst(E, dep[1], dep[2])

    @staticmethod
    def _rkey(me):
        if me[0] == "i":
            return ("i", me[1])
        return ("dma", me[1], me[2] % NDMASEM)

    def _deps(self, reads, writes):
        deps = {}

        def add(d):
            k = self._rkey(d)
            if k not in deps or deps[k][2] < d[2]:
                deps[k] = d
        for r in reads:
            rs = self.res.get(r)
            if rs is not None and rs.w is not None:
                add(rs.w)
        for w in writes:
            rs = self.res.get(w)
            if rs is not None:
                if rs.w is not None:
                    add(rs.w)
                for d in rs.r.values():
                    add(d)
        return list(deps.values())

    def _update(self, me, reads, writes):
        k = self._rkey(me)
        for r in reads:
            rs = self.res.get(r)
            if rs is None:
                rs = self.res[r] = Res()
            rs.r[k] = me
        for w in writes:
            rs = self.res.get(w)
            if rs is None:
                rs = self.res[w] = Res()
            rs.w = me
            rs.r = {}

    def op(self, E, fn, reads=(), writes=()):
        bk = [r for r in reads if isinstance(r, tuple) and r and r[0] == "bank"]
        if bk:
            reads = [r for r in reads if r not in bk]
            writes = list(writes) + bk
        for d in self._deps(reads, writes):
            self._wait(E, d)
        h = fn()
        idx = len(E.insts)
        E.insts.append(h)
        self._update(("i", E.name, idx), reads, writes)
        self.ninst += 1
        return h

    def dma(self, Q, out, in_, reads=(), writes=(), **kw):
        E = Q.issuer
        for d in self._deps(reads, writes):
            self._wait(E, d)
        i = Q.n
        if i >= NDMASEM:
            self._wait_dma(E, Q.name, i - NDMASEM)
        sem, val = Q.semval(i)
        h = E.eng.dma_start(out=out, in_=in_, **kw)
        h.then_inc(sem, 16)
        Q.n += 1
        self._update(("dma", Q.name, i), reads, writes)
        self.ninst += 1
        return h

    def load(self, out, in_, reads=(), writes=(), **kw):
        return self.dma(self.ld, out, in_, reads, writes, **kw)

    def store(self, out, in_, reads=(), writes=(), **kw):
        return self.dma(self.st, out, in_, reads, writes, **kw)

    def barrier(self):
        for E in self.engs.values():
            for X in self.engs.values():
                if X.insts:
                    self._wait_inst(E, X.name, len(X.insts) - 1)
            for Q in self.dq.values():
                for i in range(max(0, Q.n - NDMASEM), Q.n):
                    self._wait_dma(E, Q.name, i)
        self.res = {}

    def E(self, e):
        return self.engs[e]

    def any2(self):
        self.rr += 1
        return "dve" if self.rr % 2 else "act"

    def mm(self, out, lhsT, rhs, start, stop, r, w):
        nc = self.nc
        return self.op(self.pe, lambda: nc.tensor.matmul(out, lhsT=lhsT, rhs=rhs, start=start, stop=stop), r, w)

    def tr(self, out, in_, ident, r, w):
        nc = self.nc
        return self.op(self.pe, lambda: nc.tensor.transpose(out, in_, ident), r, w)

    def actf(self, out, in_, func, r, w, scale=1.0, bias=None):
        nc = self.nc
        if bias is None:
            return self.op(self.act, lambda: nc.scalar.activation(out=out, in_=in_, func=func, scale=scale), r, w)
        return self.op(self.act, lambda: nc.scalar.activation(out=out, in_=in_, func=func, scale=scale, bias=bias), r, w)

    def cp(self, e, out, in_, r, w):
        nc = self.nc
        if e == "act":
            return self.op(self.act, lambda: nc.scalar.copy(out, in_), r, w)
        eng = nc.vector if e == "dve" else nc.gpsimd
        return self.op(self.E(e), lambda: eng.tensor_copy(out, in_), r, w)

    def tt(self, e, out, in0, in1, op, r, w):
        eng = self.nc.vector if e == "dve" else self.nc.gpsimd
        return self.op(self.E(e), lambda: eng.tensor_tensor(out=out, in0=in0, in1=in1, op=op), r, w)

    def ts(self, e, out, in0, s1, s2, op0, op1, r, w):
        eng = self.nc.vector if e == "dve" else self.nc.gpsimd
        if s2 is None:
            return self.op(self.E(e), lambda: eng.tensor_scalar(out=out, in0=in0, scalar1=s1, scalar2=None, op0=op0), r, w)
        return self.op(self.E(e), lambda: eng.tensor_scalar(out=out, in0=in0, scalar1=s1, scalar2=s2, op0=op0, op1=op1), r, w)

    def stt(self, out, in0, scalar, in1, op0, op1, r, w):
        nc = self.nc
        return self.op(self.dve, lambda: nc.vector.scalar_tensor_tensor(out=out, in0=in0, scalar=scalar, in1=in1, op0=op0, op1=op1), r, w)

    def memset(self, e, ap, val, w):
        eng = self.nc.vector if e == "dve" else self.nc.gpsimd
        return self.op(self.E(e), lambda: eng.memset(ap, val), (), w)

    def recip(self, out, in_, r, w):
        nc = self.nc
        return self.op(self.dve, lambda: nc.vector.reciprocal(out=out, in_=in_), r, w)

    def scan(self, out, d0, d1, init, r, w):
        nc = self.nc
        return self.op(self.dve, lambda: nc.vector.tensor_tensor_scan(out=out, data0=d0, data1=d1, initial=init, op0=ALU.mult, op1=ALU.add), r, w)


class Cfg:
    def __init__(self, TS=4096, NP=4, TP=256, debug=False, stages=None):
        self.TS, self.NP, self.TP = TS, NP, TP
        self.NTOK = TS + NP * TP
        self.debug = debug
        self.stages = stages
        tiles = []
        t = 0
        while t < TS:
            n = min(512, TS - t)
            tiles.append((t, n, 1))
            t += n
        while t < self.NTOK:
            n = min(512, self.NTOK - t)
            tiles.append((t, n, 0))
            t += n
        self.tiles = tiles
        self.seqs = [(0, TS, 1, 0)] + [(TS + p * TP, TP, 0, p) for p in range(NP)]


class Ctx:
    pass


def build(cfg):
    nc = bass.Bass("TRN2", target_bir_lowering=False)
    k = K(nc)
    c = Ctx()
    c.nc, c.k, c.cfg = nc, k, cfg
    NTOK, TS, NP, TP = cfg.NTOK, cfg.TS, cfg.NP, cfg.TP
    skind = "ExternalOutput" if cfg.debug else "Internal"

    def din(name, shape, dt=F32):
        return nc.dram_tensor(name, list(shape), dt, kind="ExternalInput").ap()

    def dout(name, shape):
        return nc.dram_tensor(name, list(shape), F32, kind="ExternalOutput").ap()

    def dscr(name, shape, dt=F32):
        return nc.dram_tensor(name, list(shape), dt, kind=skind).ap()

    I = c.I = {}
    I["xin"] = din("xin", [NTOK, D])
    I["ident"] = din("ident", [128, 128])
    I["ccol"] = din("ccol", [128, 8, 2])
    I["w_mod"] = din("w_mod", [L, D, 9 * D])
    I["bmod"] = din("bmod", [L, 128, 72])
    I["lng"] = din("lng", [L, 3, 128, 8])
    I["lnb"] = din("lnb", [L, 3, 128, 8])
    I["ffn_w_in"] = din("ffn_w_in", [L, 2, D, 2 * DFF])
    I["ffn_w_out"] = din("ffn_w_out", [L, 2, DFF, D])
    I["w_in"] = din("w_in", [L, D, INC])
    I["w_out"] = din("w_out", [L, D, D])
    I["cache_k"] = din("cache_k", [L, 256, 256])
    I["cache_v"] = din("cache_v", [L, 256, 256])
    I["rpbT"] = din("rpbT", [L, 64, 4, 15, 64])
    I["namask"] = din("namask", [64, 64])
    I["tmask"] = din("tmask", [6, 64, 64])
    I["bones"] = din("bones", [128, 128])
    I["hgcol"] = din("hgcol", [128, 2, 3])
    I["hgng"] = din("hgng", [L, 128, 2])
    I["st_hg"] = din("st_hg", [L, 2, 4, 64, 64])
    I["rwmu"] = din("rwmu", [L, 2, 128, 8])
    I["rwmu_ad"] = din("rwmu_ad", [L, 64, 2])
    I["rww0"] = din("rww0", [L, 2, 128, 2])
    I["rw_w_up"] = din("rw_w_up", [L, 2, 64, 256])
    I["rw_a_up"] = din("rw_a_up", [L, 64, 256])
    I["rw_g_up"] = din("rw_g_up", [L, 128, 256])
    I["rwcol"] = din("rwcol", [L, 128, 2, 6])
    I["st_rw"] = din("st_rw", [L, 2, 4, 64, 64])
    I["s5lam"] = din("s5lam", [L, 2, 128, 8, 3])
    I["s5BT"] = din("s5BT", [L, 2, 8, 128, 128])
    I["s5CT"] = din("s5CT", [L, 2, 8, 128, 128])
    I["s5col"] = din("s5col", [L, 128, 2, 2])
    I["s5glu"] = din("s5glu", [L, 256, 256])
    I["st_s5"] = din("st_s5", [L, 2, 128, 8, 2])
    O = c.O = {}
    O["y_s"] = dout("y_s", [TS, D])
    O["y_p"] = dout("y_p", [NP * TP, D])
    O["nk"] = dout("nk", [NP, L, TP, 256])
    O["nv"] = dout("nv", [NP, L, TP, 256])
    O["nhg"] = dout("nhg", [NP, L, 2, 4, 64, 64])
    O["nrw"] = dout("nrw", [NP, L, 2, 4, 64, 64])
    O["ns5"] = dout("ns5", [NP, L, 2, 16, 64, 2])
    S = c.S = {}
    S["XT"] = dscr("XT", [D, NTOK])
    S["X1T"] = dscr("X1T", [D, NTOK])
    S["PT"] = dscr("PT", [INC, NTOK])
    S["OT"] = dscr("OT", [D, NTOK])
    S["VTOK"] = dscr("VTOK", [NTOK, 256])
    for sfx in ("h", "r"):
        S["LA" + sfx] = dscr("LA" + sfx, [8, 256, NTOK])
        S["LX" + sfx] = dscr("LX" + sfx, [2, 256, NTOK])
    for sfx in ("h", "r", "5"):
        S["YS" + sfx] = dscr("YS" + sfx, [2, 256, NTOK])
    S["W1s"] = dscr("W1s", [L, 2, 44, 128, 1024], BF16)
    S["W2s"] = dscr("W2s", [L, 2, 8, 128, NF * 128], BF16)
    S["Wis"] = dscr("Wis", [L, 26, 128, 1024], BF16)
    S["Wkv"] = dscr("Wkv", [L, 128, 8, 512], BF16)
    S["Wos"] = dscr("Wos", [L, 8, 128, 1024], BF16)

    c.bank = [k.ps(f"bank{i}", [128, 512]) for i in range(8)]
    c.bi = 0
    c.freeb = list(range(8))
    c.ident = k.sb("ident_sb", [128, 128])
    k.load(c.ident[:], I["ident"], writes=["ident"])
    c.onesD = k.sb("onesD", [128, 128])
    k.memset("dve", c.onesD[:], 1.0 / D, ["onesD"])
    c.epsln = k.sb("epsln", [128, 1])
    k.memset("dve", c.epsln[:], LN_EPS / (ALPHA * ALPHA), ["epsln"])
    c.modc = k.sb("modc", [128, 72, 2])
    c.osc = k.sb("osc", [128, 3, 8, 2])
    c.gco = k.sb("gco", [128, 3, 8, 2])
    c.lng = k.sb("lng_sb", [128, L, 3, 8])
    c.lnb = k.sb("lnb_sb", [128, L, 3, 8])
    k.load(c.lng[:], I["lng"].rearrange("l i p c -> p l i c"), writes=["lng"])
    k.load(c.lnb[:], I["lnb"].rearrange("l i p c -> p l i c"), writes=["lnb"])
    c.tmask = k.sb("tmask_sb", [64, 6, 64])
    k.load(c.tmask[:], I["tmask"].rearrange("m a b -> a m b"), writes=["tmask"])
    c.bones = k.sb("bones_sb", [128, 128])
    k.load(c.bones[:], I["bones"], writes=["bones"])
    c.bones64 = k.sb("bones64_sb", [128, 128])
    k.ts("dve", c.bones64[:], c.bones[:], 1.0 / 64, None, ALU.mult, None, ["bones"], ["bones64"])

    st = cfg.stages
    if st is None or "cast" in st:
        phase_cast(c)
    if st is None or "t0" in st:
        phase_transpose_in(c)
    for l in range(L):
        if st is None or "mod" in st:
            phase_mod(c, l)
        if st is None or "A" in st:
            phase_A(c, l)
        if st is None or "mix" in st:
            phase_hg(c, l, "prep")
            phase_rw(c, l, "prep")
            es1 = ExitStack()
            g1 = phase_rw(c, l, "scan", es1, NL=3)
            g2 = phase_attn(c, l, "scan", es1)
            run_concurrent([g1, g2])
            k.barrier()
            es1.close()
            es2 = ExitStack()
            g3 = phase_hg(c, l, "scan", es2, NL=3)
            g4 = phase_s5(c, l, "scan", es2)
            run_concurrent([g3, g4])
            k.barrier()
            es2.close()
            phase_rw(c, l, "out")
            phase_hg(c, l, "out")
            phase_s5(c, l, "out")
        if st is not None and "attn" in st:
            phase_attn(c, l)
        if st is not None and "hg" in st:
            phase_hg(c, l)
        if st is not None and "rw" in st:
            phase_rw(c, l)
        if st is not None and "s5" in st:
            phase_s5(c, l)
        if st is not None and "A_only" in st:
            break
        if st is None or "C" in st:
            phase_C(c, l)
        if st is not None and "L0_only" in st:
            break
    assert sorted(c.freeb) == list(range(8)), c.freeb
    if st is None or "tout" in st:
        phase_transpose_out(c)
    k.barrier()
    c.k.es_keep = k.es
    return nc, c


def nextbank(c):
    i = c.freeb.pop(0)
    c.freeb.append(i)
    return c.bank[i], ("bank", i)


def acquire(c):
    while not c.freeb:
        yield
    i = c.freeb.pop(0)
    return c.bank[i], ("bank", i)


def release(c, pk):
    c.freeb.append(pk[1])


def phase_cast(c):
    k, nc, I, S = c.k, c.nc, c.I, c.S
    es = ExitStack()
    NB = 2
    f32t = [k.sb(f"cast_f{i}", [128, 4 * 1024], F32, es) for i in range(NB)]
    b16t = [k.sb(f"cast_b{i}", [128, 4 * 1024], BF16, es) for i in range(NB)]
    cnt = [0]
    engs = ["dve", "act", "pool"]

    def job(pairs, per):
        i = cnt[0] % NB
        e = engs[cnt[0] % 3]
        cnt[0] += 1
        n = per * len(pairs)
        for j, (src_ap, dst_ap) in enumerate(pairs):
            a_, b_ = src_ap.shape[1], src_ap.shape[2]
            k.load(f32t[i][:, j * per:(j + 1) * per].rearrange("p (a b) -> p a b", a=a_), src_ap, writes=[("cf", i, j)])
        k.cp(e, b16t[i][:, 0:n], f32t[i][:, 0:n], [("cf", i, j) for j in range(len(pairs))], [("cb", i)])
        for j, (src_ap, dst_ap) in enumerate(pairs):
            k.store(dst_ap, b16t[i][:, j * per:(j + 1) * per], reads=[("cb", i)])

    for l in range(L):
        for f in range(2):
            src = I["ffn_w_in"][l, f].rearrange("(kc p) n -> p kc n", p=128)
            for g0 in range(0, 44, 4):
                job([(src[:, :, g * 128:(g + 1) * 128], S["W1s"][l, f, g]) for g in range(g0, g0 + 4)], 1024)
            src = I["ffn_w_out"][l, f].rearrange("(fc p) n -> p fc n", p=128)
            for g in range(8):
                job([(src[:, :, g * 128:(g + 1) * 128], S["W2s"][l, f, g])], NF * 128)
        src = I["w_in"][l].rearrange("(kc p) n -> p kc n", p=128)
        for g0 in range(0, 26, 2):
            job([(src[:, :, g * 128:(g + 1) * 128], S["Wis"][l, g]) for g in range(g0, g0 + 2)], 1024)
        job([(src[:, :, 2816:3328], S["Wkv"][l].rearrange("p kc n -> p (kc n)"))], 4096)
        src = I["w_out"][l].rearrange("(kc p) n -> p kc n", p=128)
        for g0 in range(0, 8, 4):
            job([(src[:, :, g * 128:(g + 1) * 128], S["Wos"][l, g]) for g in range(g0, g0 + 4)], 1024)
    k.barrier()
    es.close()


def phase_transpose_in(c):
    k, nc, cfg = c.k, c.nc, c.cfg
    es = ExitStack()
    XTv = c.S["XT"].rearrange("(c p) t -> p c t", p=128)
    NB = 2
    xin_t = [k.sb(f"ti_x{i}", [128, 4, D], F32, es) for i in range(NB)]
    xT_t = [k.sb(f"ti_xT{i}", [128, 8, 512], F32, es) for i in range(NB)]
    for ti, (t0, n, var) in enumerate(cfg.tiles):
        b = ti % NB
        ns = n // 128
        k.load(xin_t[b][:, 0:ns, :], c.I["xin"][t0:t0 + n, :].rearrange("(s p) d -> p s d", p=128), writes=[("tix", b)])
        for ch in range(8):
            p, pk = nextbank(c)
            for s in range(ns):
                k.tr(p[:, s * 128:(s + 1) * 128], xin_t[b][:, s, ch * 128:(ch + 1) * 128], c.ident[:], [("tix", b), "ident"], [pk])
            k.cp(k.any2(), xT_t[b][:, ch, 0:n], p[:, 0:n], [pk], [("tixT", b, ch)])
        k.store(XTv[:, :, t0:t0 + n], xT_t[b][:, :, 0:n], reads=[("tixT", b, ch) for ch in range(8)], writes=[("XT", ti)])
    k.barrier()
    es.close()


def phase_transpose_out(c):
    k, nc, cfg = c.k, c.nc, c.cfg
    es = ExitStack()
    XTv = c.S["XT"].rearrange("(c p) t -> p c t", p=128)
    NB = 2
    xT_t = [k.sb(f"to_xT{i}", [128, 8, 512], F32, es) for i in range(NB)]
    yt = [k.sb(f"to_y{i}", [128, 4, D], F32, es) for i in range(NB)]
    for ti, (t0, n, var) in enumerate(cfg.tiles):
        b = ti % NB
        ns = n // 128
        k.load(xT_t[b][:, :, 0:n], XTv[:, :, t0:t0 + n], reads=[("XT", ti)], writes=[("toxT", b)])
        for s in range(ns):
            for h in range(2):
                p, pk = nextbank(c)
                for cc in range(4):
                    ch = h * 4 + cc
                    k.tr(p[:, cc * 128:(cc + 1) * 128], xT_t[b][:, ch, s * 128:(s + 1) * 128], c.ident[:], [("toxT", b), "ident"], [pk])
                k.cp(k.any2(), yt[b][:, s, h * 512:(h + 1) * 512], p[:], [pk], [("toy", b, s, h)])
        if var == 1:
            dst = c.O["y_s"][t0:t0 + n, :]
        else:
            dst = c.O["y_p"][t0 - cfg.TS:t0 - cfg.TS + n, :]
        k.store(dst.rearrange("(s p) d -> p s d", p=128), yt[b][:, 0:ns, :],
                reads=[("toy", b, s, h) for s in range(ns) for h in range(2)])
    k.barrier()
    es.close()


def phase_mod(c, l):
    k, nc, I = c.k, c.nc, c.I
    es = ExitStack()
    ccol = k.sb("mod_c", [128, 8, 2], F32, es)
    csil = k.sb("mod_cs", [128, 8, 2], F32, es)
    bm = k.sb("mod_bm", [128, 72], F32, es)
    k.load(ccol[:], I["ccol"], writes=["ccol"])
    k.load(bm[:], I["bmod"][l], writes=["bm"])
    k.actf(csil[:], ccol[:], AF.Silu, ["ccol"], ["csil"])
    FB = 1152
    NB = 2
    wt = [k.sb(f"mod_w{i}", [128, 8, FB], F32, es) for i in range(NB)]
    p, pk = nextbank(c)
    wv = I["w_mod"][l].rearrange("(kc p) f -> p kc f", p=128)
    for bi in range(9 * D // FB):
        b = bi % NB
        k.load(wt[b][:], wv[:, :, bi * FB:(bi + 1) * FB], writes=[("modw", b)])
        for fj in range(FB // 128):
            f = bi * (FB // 128) + fj
            for kc in range(8):
                k.mm(p[:, 2 * f:2 * f + 2], wt[b][:, kc, fj * 128:(fj + 1) * 128], csil[:, kc, :], kc == 0, kc == 7,
                     [("modw", b), "csil"], [pk])
    k.tt("dve", c.modc[:], p[:, 0:144].rearrange("p (f n) -> p f n", n=2), bm[:].unsqueeze(2).to_broadcast([128, 72, 2]), ALU.add,
         [pk, "bm"], ["modc"])
    for i in range(3):
        k.ts("dve", c.osc[:, i], c.modc[:, (3 * i + 1) * 8:(3 * i + 2) * 8, :], 1.0, None, ALU.add, None, ["modc"], ["osc"])
        coef = (0.5 if i != 1 else 1.0) / ALPHA
        k.ts("dve", c.gco[:, i], c.modc[:, (3 * i + 2) * 8:(3 * i + 3) * 8, :], coef, None, ALU.mult, None, ["modc"], ["gco"])
    k.barrier()
    es.close()


def layernorm(c, z, zk, n, gcol, bcol, xout, xoutk, tmp, li):
    k, nc = c.k, c.nc
    sq, msq, rstd = tmp["sq"], tmp["msq"], tmp["rstd"]
    pm, pmk = nextbank(c)
    pe2, pe2k = nextbank(c)
    for ch in range(8):
        k.actf(sq[:, ch, 0:n], z[:, ch, 0:n], AF.Square, [(zk, ch)], [("lnsq", ch)])
    for ch in range(8):
        k.mm(pm[:, 0:n], c.onesD[:], z[:, ch, 0:n], ch == 0, ch == 7, [(zk, ch), "onesD"], [pmk])
    for ch in range(8):
        k.mm(pe2[:, 0:n], c.onesD[:], sq[:, ch, 0:n], ch == 0, ch == 7, [("lnsq", ch), "onesD"], [pe2k])
    k.actf(msq[:, 0:n], pm[:, 0:n], AF.Square, [pmk], ["lnmsq"])
    k.tt("dve", msq[:, 0:n], pe2[:, 0:n], msq[:, 0:n], ALU.subtract, [pe2k, "lnmsq"], ["lnmsq"])
    k.actf(rstd[:, 0:n], msq[:, 0:n], AF.Sqrt, ["lnmsq", "epsln"], ["lnrstd"], bias=c.epsln[:, 0:1])
    k.recip(rstd[:, 0:n], rstd[:, 0:n], ["lnrstd"], ["lnrstd"])
    for ch in range(8):
        k.tt("dve", sq[:, ch, 0:n], z[:, ch, 0:n], pm[:, 0:n], ALU.subtract, [(zk, ch), pmk, ("lnsq", ch)], [("lnsq", ch)])
        e = "pool" if ch % 2 else "dve"
        k.tt(e, sq[:, ch, 0:n], sq[:, ch, 0:n], rstd[:, 0:n], ALU.mult, [("lnsq", ch), "lnrstd"], [("lnsq", ch)])
        k.actf(xout[:, ch, 0:n], sq[:, ch, 0:n], AF.Identity, [("lnsq", ch), "lng", "lnb"], [(xoutk, ch)],
               scale=gcol[:, ch:ch + 1], bias=bcol[:, ch:ch + 1])


def modulate(c, x, xk, n, i, var, xm, xmk):
    k = c.k
    for ch in range(8):
        k.actf(xm[:, ch, 0:n], x[:, ch, 0:n], AF.Identity, [(xk, ch), "osc", "modc"], [(xmk, ch)],
               scale=c.osc[:, i, ch, var:var + 1], bias=c.modc[:, (3 * i) * 8 + ch, var:var + 1])


def ffn(c, l, f, xm, xmk, n, h, zres, zresk, i, var, wb):
    k, nc, S = c.k, c.nc, c.S
    w1, w2, sg = wb["w1"], wb["w2"], wb["sg"]
    NW1 = len(w1)
    order = []
    for fc in range(NF):
        order.append(fc)
        order.append(NF + fc)

    def ldw1(j):
        b = j % NW1
        k.load(w1[b][:], S["W1s"][l, f, order[j]], writes=[("w1", b)])
    PF = NW1 - 1
    for j in range(min(PF, len(order))):
        ldw1(j)
    for fc in range(NF):
        banks = []
        for half in range(2):
            j = 2 * fc + half
            if j + PF < len(order):
                ldw1(j + PF)
            b = j % NW1
            p, pk = nextbank(c)
            banks.append((p, pk))
            for kc in range(8):
                k.mm(p[:, 0:n], w1[b][:, kc * 128:(kc + 1) * 128], xm[:, kc, 0:n], kc == 0, kc == 7, [("w1", b), (xmk, kc)], [pk])
        (pg, pgk), (pu, puk) = banks
        sb_ = fc % 2
        k.actf(sg[sb_][:, 0:n], pg[:, 0:n], AF.Silu, [pgk], [("sg", sb_)])
        k.tt("dve", h[:, fc, 0:n], sg[sb_][:, 0:n], pu[:, 0:n], ALU.mult, [("sg", sb_), puk], [("h", fc)])
    NW2 = len(w2)
    for dc in range(min(NW2 - 1, 8)):
        k.load(w2[dc % NW2][:], S["W2s"][l, f, dc], writes=[("w2", dc % NW2)])
    for dc in range(8):
        if dc + NW2 - 1 < 8:
            d2 = dc + NW2 - 1
            k.load(w2[d2 % NW2][:], S["W2s"][l, f, d2], writes=[("w2", d2 % NW2)])
        b = dc % NW2
        p, pk = nextbank(c)
        for fc in range(NF):
            k.mm(p[:, 0:n], w2[b][:, fc * 128:(fc + 1) * 128], h[:, fc, 0:n], fc == 0, fc == NF - 1, [("w2", b), ("h", fc)], [pk])
        k.stt(zres[:, dc, 0:n], p[:, 0:n], c.gco[:, i, dc, var:var + 1], zres[:, dc, 0:n], ALU.mult, ALU.add,
              [pk, "gco", (zresk, dc)], [(zresk, dc)])


def alloc_AC(c, es):
    k = c.k
    t = {}
    t["x"] = [k.sb(f"ac_x{i}", [128, 8, 512], F32, es) for i in range(2)]
    t["x1"] = k.sb("ac_x1", [128, 8, 512], F32, es)
    t["xm"] = k.sb("ac_xm", [128, 8, 512], BF16, es)
    t["h"] = k.sb("ac_h", [128, NF, 512], BF16, es)
    t["ln"] = {"sq": k.sb("ac_sq", [128, 8, 512], F32, es), "msq": k.sb("ac_msq", [128, 512], F32, es),
               "rstd": k.sb("ac_rstd", [128, 512], F32, es)}
    t["wb"] = {"w1": [k.sb(f"ac_w1_{i}", [128, 1024], BF16, es) for i in range(4)],
               "w2": [k.sb(f"ac_w2_{i}", [128, NF * 128], BF16, es) for i in range(2)],
               "sg": [k.sb(f"ac_sg{i}", [128, 512], F32, es) for i in range(2)]}
    t["wi"] = [k.sb(f"ac_wi{i}", [128, 1024], BF16, es) for i in range(4)]
    t["wkv"] = k.sb("ac_wkv", [128, 8, 512], BF16, es)
    t["pb"] = [k.sb(f"ac_pb{i}", [128, 2, 512], F32, es) for i in range(2)]
    t["tok"] = [k.sb(f"ac_tok{i}", [128, 512], F32, es) for i in range(2)]
    return t


def phase_A(c, l):
    k, nc, cfg, S, O = c.k, c.nc, c.cfg, c.S, c.O
    es = ExitStack()
    t = alloc_AC(c, es)
    XTv = S["XT"].rearrange("(c p) t -> p c t", p=128)
    X1Tv = S["X1T"].rearrange("(c p) t -> p c t", p=128)
    PTv = S["PT"].rearrange("(c p) t -> p c t", p=128)
    k.load(t["wkv"][:], S["Wkv"][l], writes=["wkv"])
    tiles = cfg.tiles

    def ldx(ti):
        t0, n, var = tiles[ti]
        b = ti % 2
        k.load(t["x"][b][:, :, 0:n], XTv[:, :, t0:t0 + n], reads=[("XT", ti)], writes=[(("x", b), ch) for ch in range(8)])
    ldx(0)
    for ti, (t0, n, var) in enumerate(tiles):
        b = ti % 2
        x, xk = t["x"][b], ("x", b)
        if ti + 1 < len(tiles):
            ldx(ti + 1)
        modulate(c, x, xk, n, 0, var, t["xm"], "xm")
        ffn(c, l, 0, t["xm"], "xm", n, t["h"], x, xk, 0, var, t["wb"])
        layernorm(c, x, xk, n, c.lng[:, l, 0, :], c.lnb[:, l, 0, :], t["x1"], "x1", t["ln"], 0)
        k.store(X1Tv[:, :, t0:t0 + n], t["x1"][:, :, 0:n], reads=[("x1", ch) for ch in range(8)], writes=[("X1T", ti)])
        modulate(c, t["x1"], "x1", n, 1, var, t["xm"], "xm")
        wi = t["wi"]
        NWI = len(wi)
        for j in range(NWI - 1):
            k.load(wi[j][:], S["Wis"][l, j], writes=[("wi", j)])
        for cc in range(26):
            if cc + NWI - 1 < 26:
                j = cc + NWI - 1
                k.load(wi[j % NWI][:], S["Wis"][l, j], writes=[("wi", j % NWI)])
            b2 = cc % NWI
            p, pk = nextbank(c)
            for kc in range(8):
                k.mm(p[:, 0:n], wi[b2][:, kc * 128:(kc + 1) * 128], t["xm"][:, kc, 0:n], kc == 0, kc == 7, [("wi", b2), ("xm", kc)], [pk])
            pbi = (cc // 2) % 2
            k.cp(k.any2(), t["pb"][pbi][:, cc % 2, 0:n], p[:, 0:n], [pk], [("pb", pbi, cc % 2)])
            if cc % 2 == 1:
                k.store(PTv[:, cc - 1:cc + 1, t0:t0 + n], t["pb"][pbi][:, :, 0:n], reads=[("pb", pbi, 0), ("pb", pbi, 1)], writes=[("PT", ti)])
        for s in range(n // 128):
            p, pk = nextbank(c)
            for kc in range(8):
                k.mm(p[:, :], t["xm"][:, kc, s * 128:(s + 1) * 128], t["wkv"][:, kc, :], kc == 0, kc == 7, [("xm", kc), "wkv"], [pk])
            tb = s % 2
            k.cp(k.any2(), t["tok"][tb][:], p[:], [pk], [("tok", tb)])
            ta = t0 + s * 128
            k.store(S["VTOK"][ta:ta + 128, :], t["tok"][tb][:, 256:512], reads=[("tok", tb)], writes=[("VTOK", ti)])
            if var == 0:
                q = ta - cfg.TS
                pi_, tt_ = q // cfg.TP, q % cfg.TP
                k.store(O["nk"][pi_, l, tt_:tt_ + 128, :], t["tok"][tb][:, 0:256], reads=[("tok", tb)])
                k.store(O["nv"][pi_, l, tt_:tt_ + 128, :], t["tok"][tb][:, 256:512], reads=[("tok", tb)])
    k.barrier()
    es.close()


def phase_C(c, l):
    k, nc, cfg, S = c.k, c.nc, c.cfg, c.S
    es = ExitStack()
    t = alloc_AC(c, es)
    XTv = S["XT"].rearrange("(c p) t -> p c t", p=128)
    X1Tv = S["X1T"].rearrange("(c p) t -> p c t", p=128)
    OTv = S["OT"].rearrange("(c p) t -> p c t", p=128)
    tiles = cfg.tiles
    ot = t["ln"]["sq"]
    wo = t["wi"]
    for ti, (t0, n, var) in enumerate(tiles):
        x1 = t["x1"]
        k.load(x1[:, :, 0:n], X1Tv[:, :, t0:t0 + n], reads=[("X1T", ti)], writes=[("x1", ch) for ch in range(8)])
        k.load(ot[:, :, 0:n], OTv[:, :, t0:t0 + n], reads=[("OT", ti)], writes=[("lnsq", ch) for ch in range(8)])
        for ch in range(8):
            k.cp(k.any2(), t["xm"][:, ch, 0:n], ot[:, ch, 0:n], [("lnsq", ch)], [("xm", ch)])
        NWO = len(wo)
        for j in range(NWO - 1):
            k.load(wo[j][:], S["Wos"][l, j], writes=[("wi", j)])
        for dc in range(8):
            if dc + NWO - 1 < 8:
                j = dc + NWO - 1
                k.load(wo[j % NWO][:], S["Wos"][l, j], writes=[("wi", j % NWO)])
            b2 = dc % NWO
            p, pk = nextbank(c)
            for kc in range(8):
                k.mm(p[:, 0:n], wo[b2][:, kc * 128:(kc + 1) * 128], t["xm"][:, kc, 0:n], kc == 0, kc == 7, [("wi", b2), ("xm", kc)], [pk])
            k.stt(x1[:, dc, 0:n], p[:, 0:n], c.gco[:, 1, dc, var:var + 1], x1[:, dc, 0:n], ALU.mult, ALU.add,
                  [pk, "gco", ("x1", dc)], [("x1", dc)])
        x2 = t["x"][0]
        layernorm(c, x1, "x1", n, c.lng[:, l, 1, :], c.lnb[:, l, 1, :], x2, ("x", 0), t["ln"], 1)
        modulate(c, x2, ("x", 0), n, 2, var, t["xm"], "xm")
        ffn(c, l, 1, t["xm"], "xm", n, t["h"], x2, ("x", 0), 2, var, t["wb"])
        x3 = t["x"][1]
        layernorm(c, x2, ("x", 0), n, c.lng[:, l, 2, :], c.lnb[:, l, 2, :], x3, ("x", 1), t["ln"], 2)
        k.store(XTv[:, :, t0:t0 + n], x3[:, :, 0:n], reads=[(("x", 1), ch) for ch in range(8)], writes=[("XT", ti)])
    k.barrier()
    es.close()


def attn_core_gen(c, q_ap, nq, A, bias_ap, B, out_ap, rkeys, okey, T, slot):
    k = c.k
    ev = []
    na, nb = len(A), len(B)
    if A:
        pa, pak = yield from acquire(c)
        for i, (kt, v) in enumerate(A):
            k.mm(pa[0:64, i * nq:(i + 1) * nq], kt, q_ap, True, True, rkeys, [pak])
    if B:
        pb, pbk = yield from acquire(c)
        for i, (kt, v) in enumerate(B):
            k.mm(pb[0:64, i * nq:(i + 1) * nq], kt, q_ap, True, True, rkeys, [pbk])
    yield
    if A:
        ea, eak = T["EA"][slot], ("EA", slot)
        k.stt(ea[:, 0:na, 0:nq], pa[0:64, 0:na * nq].rearrange("p (a q) -> p a q", q=nq), 0.125, bias_ap, ALU.mult, ALU.add,
              [pak, "at_B"], [eak])
        k.actf(ea[:, 0:na, 0:nq], ea[:, 0:na, 0:nq], AF.Exp, [eak], [eak])
        for i, (kt, v) in enumerate(A):
            ev.append((ea[:, i, 0:nq], v, eak))
        release(c, pak)
    if B:
        eb, ebk = T["EB"][slot], ("EB", slot)
        k.actf(eb[:, 0:nb * nq], pb[0:64, 0:nb * nq], AF.Exp, [pbk], [ebk], scale=0.125)
        for i, (kt, v) in enumerate(B):
            ev.append((eb[:, i * nq:(i + 1) * nq], v, ebk))
        release(c, pbk)
    yield
    pn, pnk = yield from acquire(c)
    n = len(ev)
    for i, (e, v, ek) in enumerate(ev):
        k.mm(pn[0:64, 0:nq], v, e, i == 0, i == n - 1, rkeys + [ek], [pnk])
    for i, (e, v, ek) in enumerate(ev):
        k.mm(pn[0:64, nq:2 * nq], T["ones"][:], e, i == 0, i == n - 1, ["at_ones", ek], [pnk])
    yield
    rd = T["rden"][slot]
    k.recip(rd[:, 0:nq], pn[0:64, nq:2 * nq], [pnk], [("rden", slot)])
    k.tt("dve", out_ap, pn[0:64, 0:nq], rd[:, 0:nq], ALU.mult, [pnk, ("rden", slot)], [okey])
    release(c, pnk)
    yield


def phase_attn(c, l, part=None, es_ext=None):
    k, nc, cfg, S, I = c.k, c.nc, c.cfg, c.S, c.I
    es = ExitStack() if es_ext is None else es_ext
    TS, TP = cfg.TS, cfg.TP
    R = TS // 64
    Tmax = max(TS, TP)
    NS = 3
    qT = k.sb("at_q", [64, Tmax], F32, es)
    kT = k.sb("at_k", [64, Tmax], F32, es)
    vt = k.sb("at_v", [64, Tmax // 64, 64], F32, es)
    ot = k.sb("at_o", [64, Tmax], F32, es)
    T = {}
    T["ones"] = k.sb("at_ones", [64, 64], F32, es)
    k.memset("dve", T["ones"][:], 1.0, ["at_ones"])
    T["EA"] = [k.sb(f"at_ea{i}", [64, 8, 64], F32, es) for i in range(NS)]
    T["EB"] = [k.sb(f"at_eb{i}", [64, 512], F32, es) for i in range(NS)]
    T["rden"] = [k.sb(f"at_rd{i}", [64, 128], F32, es) for i in range(NS)]
    Bt = k.sb("at_B", [64, 15, 64], F32, es)
    mask = k.sb("at_mask", [64, 64], F32, es)
    kctok = k.sb("at_kctok", [64, 4, 64], F32, es)
    kcT = k.sb("at_kcT", [64, 4, 64], F32, es)
    vc = k.sb("at_vc", [64, 4, 64], F32, es)
    k.load(mask[:], I["namask"], writes=["at_mask"])
    rk = ["at_q", "at_k", "at_v", "at_kcT", "at_vc"]
    slots = list(range(NS))

    def gen():
        for h in range(4):
            hs = slice(64 * h, 64 * h + 64)
            k.load(qT[:, 0:TS], S["PT"][2560 + 64 * h:2560 + 64 * h + 64, 0:TS], writes=["at_q"])
            k.load(kT[:, 0:TS], S["PT"][2816 + 64 * h:2816 + 64 * h + 64, 0:TS], writes=["at_k"])
            k.load(vt[:, 0:R, :], S["VTOK"][0:TS, hs].rearrange("(r c) d -> c r d", c=64), writes=["at_v"])
            k.load(Bt[:], I["rpbT"][l, :, h], writes=["at_B"])
            k.tt("dve", Bt[:], Bt[:], mask[:].unsqueeze(1).to_broadcast([64, 15, 64]), ALU.add, ["at_B", "at_mask"], ["at_B"])
            k.load(kctok[:], I["cache_k"][l][:, hs].rearrange("(ch c) d -> c ch d", c=64), writes=["at_kctok"])
            k.load(vc[:], I["cache_v"][l][:, hs].rearrange("(ch c) d -> c ch d", c=64), writes=["at_vc"])
            p, pk = yield from acquire(c)
            for ch in range(4):
                k.tr(p[0:64, ch * 64:(ch + 1) * 64], kctok[:, ch, :], c.ident[0:64, 0:64], ["at_kctok", "ident"], [pk])
            k.cp("dve", kcT[:].rearrange("p a b -> p (a b)"), p[0:64, 0:256], [pk], ["at_kcT"])
            release(c, pk)
            yield
            jobs = []
            kr = min(8, R)
            for r in range(R):
                rs = min(max(r - kr // 2, 0), R - kr)
                dr0 = rs - r + 7
                A = [(kT[:, (rs + i) * 64:(rs + i + 1) * 64], vt[:, rs + i, :]) for i in range(kr)]
                B = [(kcT[:, j, :], vc[:, j, :]) for j in range(4)]
                jobs.append(lambda slot, r=r, A=A, B=B, dr0=dr0: attn_core_gen(
                    c, qT[:, r * 64:(r + 1) * 64], 64, A, Bt[:, dr0:dr0 + kr, :], B, ot[:, r * 64:(r + 1) * 64], rk, ("at_o", r), T, slot))
            yield from drive_gen(jobs, slots)
            k.store(S["OT"][768 + 64 * h:768 + 64 * h + 64, 0:TS], ot[:, 0:TS], reads=[("at_o", r) for r in range(R)])
            yield
            for (s0, Tn, kind, pi_) in cfg.seqs[1:]:
                k.load(qT[:, 0:Tn], S["PT"][2560 + 64 * h:2560 + 64 * h + 64, s0:s0 + Tn], writes=["at_q"])
                k.load(kT[:, 0:Tn], S["PT"][2816 + 64 * h:2816 + 64 * h + 64, s0:s0 + Tn], writes=["at_k"])
                k.load(vt[:, 0:Tn // 64, :], S["VTOK"][s0:s0 + Tn, hs].rearrange("(r c) d -> c r d", c=64), writes=["at_v"])
                yield
                jobs = []
                B = [(kT[:, j * 64:(j + 1) * 64], vt[:, j, :]) for j in range(Tn // 64)]
                for qb in range(Tn // 128):
                    jobs.append(lambda slot, qb=qb, B=B: attn_core_gen(
                        c, qT[:, qb * 128:(qb + 1) * 128], 128, [], None, B, ot[:, qb * 128:(qb + 1) * 128], rk, ("at_o", qb), T, slot))
                yield from drive_gen(jobs, slots)
                k.store(S["OT"][768 + 64 * h:768 + 64 * h + 64, s0:s0 + Tn], ot[:, 0:Tn], reads=[("at_o", qb) for qb in range(Tn // 128)])
                yield

    g = gen()
    if part == "scan":
        return g
    run_concurrent([g])
    k.barrier()
    es.close()


def la_alloc(c, es, delta, CH, nch, tag):
    k = c.k
    TT = nch * CH
    t = {"TT": TT, "nch": nch, "C": CH, "tag": tag}

    def fm(name):
        return k.sb(f"la{tag}_{name}", [64, TT], F32, es)

    def tk(name):
        return k.sb(f"la{tag}_{name}", [64, nch, 64], F32, es)
    t["d"] = []
    for d in range(2):
        u = {}
        for nm in ["K", "LW", "cum", "cumc", "Eabs", "Erel", "Em", "Qabs", "Qrel", "Kd", "Ke", "ybuf", "R", "V"]:
            u[nm] = fm(f"{nm}{d}")
        u["Gm"] = k.sb(f"la{tag}_Gm{d}", [64, nch], F32, es)
        u["KeT"], u["RKT"], u["Vm"] = tk(f"KeT{d}"), tk(f"RKT{d}"), tk(f"Vm{d}")
        u["S"] = [k.sb(f"la{tag}_S{d}_{i}", [64, 64], F32, es) for i in range(2)]
        u["si"] = 0
        if delta:
            for nm in ["KK", "BK", "cp", "E0", "KKabs", "KKrel", "Bd", "Be"]:
                u[nm] = fm(f"{nm}{d}")
            for nm in ["BeT", "RBT", "AkT", "M", "P", "IP", "Q", "M2", "P2"]:
                u[nm] = tk(f"{nm}{d}")
            u["rhs0"] = k.sb(f"la{tag}_rhs0{d}", [64, 64], F32, es)
            u["U"] = k.sb(f"la{tag}_U{d}", [64, 64], F32, es)
        t["d"].append(u)
    t["stmp"] = k.sb(f"la{tag}_stmp", [64, 64], F32, es)
    return t


def la_consts(c, es, CH, nch):
    k = c.k
    TT = CH * nch
    cst = {}
    cst["rmask"] = k.sb("la_rmask", [64, TT], F32, es)
    cst["rmaskb"] = k.sb("la_rmaskb", [64, TT], F32, es)
    k.memset("dve", cst["rmask"][:], 1.0, ["rmask"])
    k.memset("dve", cst["rmask"][:].rearrange("p (a b) -> p a b", b=CH)[:, :, 0:1], 0.0, ["rmask"])
    k.memset("dve", cst["rmaskb"][:], 1.0, ["rmask"])
    k.memset("dve", cst["rmaskb"][:].rearrange("p (a b) -> p a b", b=CH)[:, :, CH - 1:CH], 0.0, ["rmask"])
    return cst


def la_lane(c, t, cst, delta, arrs, YS, seq, st_in, st_out, transpose_state):
    k = c.k
    s0, T = seq
    CH = t["C"]
    MID = CH // 2
    TT = t["TT"]
    nch = t["nch"]
    assert T % TT == 0
    ntile = T // TT
    tm = c.tmask
    I64 = c.ident[0:64, 0:64]
    tag = t["tag"]
    U_ = t["d"]
    DK = [(tag, 0), (tag, 1)]

    def v3(ap):
        return ap[:, 0:TT].rearrange("p (a b) -> p a b", b=CH)

    def bc(ap2):
        return ap2[:, 0:nch].unsqueeze(2).to_broadcast([64, nch, CH])

    def mbc(mi):
        return tm[0:CH, mi, 0:CH].unsqueeze(1).to_broadcast([CH, nch, CH])

    def rvf(d):
        return (lambda ap: ap[:, 0:TT]) if d == 0 else (lambda ap: ap[:, 0:TT][:, ::-1])
    ibc = c.ident[0:64, 0:64].unsqueeze(1).to_broadcast([64, nch, 64])
    stk = ("stmp", tag)
    for d in range(2):
        u, dk = U_[d], DK[d]
        u["si"] = 0
        S0 = u["S"][0]
        if st_in is None:
            k.memset("dve", S0[:], 0.0, [("S", dk, 0)])
        elif transpose_state:
            k.load(t["stmp"][:], st_in[d], writes=[stk])
            p, pk = yield from acquire(c)
            k.tr(p[0:64, 0:64], t["stmp"][:], I64, [stk, "ident"], [pk])
            k.cp("dve", S0[:], p[0:64, 0:64], [pk], [("S", dk, 0)])
            release(c, pk)
        else:
            k.load(S0[:], st_in[d], writes=[("S", dk, 0)])
    yield
    for it in range(ntile):
        tis = [it, ntile - 1 - it]
        for d in range(2):
            u, dk = U_[d], DK[d]
            a0 = s0 + tis[d] * TT
            sl = slice(a0, a0 + TT)
            k.load(u["R"][:, 0:TT], arrs["R"][:, sl], writes=[("R", dk)])
            k.load(u["V"][:, 0:TT], arrs["V"][:, sl], writes=[("V", dk)])
            k.load(u["K"][:, 0:TT], arrs[f"K{d}"][:, sl], writes=[("K", dk)])
            k.load(u["LW"][:, 0:TT], arrs[f"LW{d}"][:, sl], writes=[("LW", dk)])
            if delta:
                k.load(u["KK"][:, 0:TT], arrs["KK"][:, sl], writes=[("KK", dk)])
                k.load(u["BK"][:, 0:TT], arrs["BK"][:, sl], writes=[("BK", dk)])
        yield
        pv = []
        for d in range(2):
            u, dk = U_[d], DK[d]
            p, pk = yield from acquire(c)
            pv.append((p, pk))
            for ch in range(nch):
                k.tr(p[0:CH, ch * 64:(ch + 1) * 64], u["V"][:, ch * CH:(ch + 1) * CH], I64, [("V", dk), "ident"], [pk])
        yield
        for d in range(2):
            u, dk = U_[d], DK[d]
            rv = rvf(d)
            p, pk = pv[d]
            k.cp("act", u["Vm"][0:CH, 0:nch, :].rearrange("p a b -> p (a b)"), p[0:CH, 0:nch * 64], [pk], [("Vm", dk)])
            release(c, pk)
            k.scan(rv(u["cum"]), rv(cst["rmask"] if d == 0 else cst["rmaskb"]), rv(u["LW"]), 0.0, [("LW", dk), "rmask"], [("cum", dk)])
            k.actf(u["Eabs"][:, 0:TT], u["cum"][:, 0:TT], AF.Exp, [("cum", dk)], [("Eabs", dk)])
            k.tt("dve", v3(u["cumc"]), v3(u["cum"]), v3(u["cum"])[:, :, MID:MID + 1].to_broadcast([64, nch, CH]), ALU.subtract,
                 [("cum", dk)], [("cumc", dk)])
            k.actf(u["Erel"][:, 0:TT], u["cumc"][:, 0:TT], AF.Exp, [("cumc", dk)], [("Erel", dk)])
            k.actf(u["Em"][:, 0:TT], u["cumc"][:, 0:TT], AF.Exp, [("cumc", dk)], [("Em", dk)], scale=-1.0)
            last = CH - 1 if d == 0 else 0
            k.actf(u["Gm"][:, 0:nch], v3(u["cumc"])[:, :, last], AF.Exp, [("cumc", dk)], [("Gm", dk)])
        yield
        for d in range(2):
            u, dk = U_[d], DK[d]
            k.tt("pool", u["Qabs"][:, 0:TT], u["R"][:, 0:TT], u["Eabs"][:, 0:TT], ALU.mult, [("R", dk), ("Eabs", dk)], [("Qabs", dk)])
            k.tt("pool", u["Qrel"][:, 0:TT], u["R"][:, 0:TT], u["Erel"][:, 0:TT], ALU.mult, [("R", dk), ("Erel", dk)], [("Qrel", dk)])
            k.tt("dve", u["Kd"][:, 0:TT], u["K"][:, 0:TT], u["Em"][:, 0:TT], ALU.mult, [("K", dk), ("Em", dk)], [("Kd", dk)])
            k.tt("dve", v3(u["Ke"]), v3(u["Kd"]), bc(u["Gm"]), ALU.mult, [("Kd", dk), ("Gm", dk)], [("Ke", dk)])
            if delta:
                k.tt("dve", u["cp"][:, 0:TT], u["cum"][:, 0:TT], u["LW"][:, 0:TT], ALU.subtract, [("cum", dk), ("LW", dk)], [("cp", dk)])
                k.actf(u["E0"][:, 0:TT], u["cp"][:, 0:TT], AF.Exp, [("cp", dk)], [("E0", dk)])
                k.tt("pool", u["KKabs"][:, 0:TT], u["KK"][:, 0:TT], u["E0"][:, 0:TT], ALU.mult, [("KK", dk), ("E0", dk)], [("KKabs", dk)])
                k.tt("dve", v3(u["cp"]), v3(u["cp"]), v3(u["cum"])[:, :, MID:MID + 1].to_broadcast([64, nch, CH]), ALU.subtract,
                     [("cp", dk), ("cum", dk)], [("cp", dk)])
                k.actf(u["E0"][:, 0:TT], u["cp"][:, 0:TT], AF.Exp, [("cp", dk)], [("E0", dk)])
                k.tt("pool", u["KKrel"][:, 0:TT], u["KK"][:, 0:TT], u["E0"][:, 0:TT], ALU.mult, [("KK", dk), ("E0", dk)], [("KKrel", dk)])
                k.tt("dve", u["Bd"][:, 0:TT], u["BK"][:, 0:TT], u["Em"][:, 0:TT], ALU.mult, [("BK", dk), ("Em", dk)], [("Bd", dk)])
                k.tt("dve", v3(u["Be"]), v3(u["Bd"]), bc(u["Gm"]), ALU.mult, [("Bd", dk), ("Gm", dk)], [("Be", dk)])
        yield
        pv = []
        for d in range(2):
            u, dk = U_[d], DK[d]
            p, pk = yield from acquire(c)
            pv.append((p, pk))
            for ch in range(nch):
                k.tr(p[0:CH, ch * 64:(ch + 1) * 64], u["Ke"][:, ch * CH:(ch + 1) * CH], I64, [("Ke", dk), "ident"], [pk])
            if delta:
                for ch in range(nch):
                    k.tr(p[0:CH, (nch + ch) * 64:(nch + ch + 1) * 64], u["Be"][:, ch * CH:(ch + 1) * CH], I64, [("Be", dk), "ident"], [pk])
        yield
        for d in range(2):
            u, dk = U_[d], DK[d]
            p, pk = pv[d]
            k.cp("act", u["KeT"][0:CH, 0:nch, :].rearrange("p a b -> p (a b)"), p[0:CH, 0:nch * 64], [pk], [("KeT", dk)])
            if delta:
                k.cp("dve", u["BeT"][0:CH, 0:nch, :].rearrange("p a b -> p (a b)"), p[0:CH, nch * 64:2 * nch * 64], [pk], [("BeT", dk)])
            release(c, pk)
        yield
        W4 = nch * CH
        specs = [("RKT", "Kd", "Qrel", "MI")]
        if delta:
            specs += [("RBT", "Bd", "Qrel", "MI"), ("AkT", "Kd", "KKrel", "MS"), ("M", "Bd", "KKrel", "MSneg"), ("P", "KKrel", "Bd", "MSntneg")]
        per = max(1, 512 // W4)
        for g0 in range(0, len(specs), per):
            grp = specs[g0:g0 + per]
            pv = []
            for d in range(2):
                u, dk = U_[d], DK[d]
                p, pk = yield from acquire(c)
                pv.append((p, pk))
                for si, (dst, lh, rh, mk_) in enumerate(grp):
                    for ch in range(nch):
                        cs_ = slice(ch * CH, (ch + 1) * CH)
                        k.mm(p[0:CH, si * W4 + ch * CH:si * W4 + (ch + 1) * CH], u[lh][:, cs_], u[rh][:, cs_], True, True,
                             [(lh, dk), (rh, dk)], [pk])
            yield
            for d in range(2):
                u, dk = U_[d], DK[d]
                p, pk = pv[d]
                mids = {"MI": 0, "MS": 1, "MSntneg": 5, "MSneg": 4} if d == 0 else {"MI": 2, "MS": 3, "MSntneg": 4, "MSneg": 5}
                for si, (dst, lh, rh, mk_) in enumerate(grp):
                    k.tt("dve", u[dst][0:CH, 0:nch, 0:CH],
                         p[0:CH, si * W4:(si + 1) * W4].rearrange("p (a b) -> p a b", b=CH) if False else
                         p[0:CH, si * W4:(si + 1) * W4].rearrange("p (a b) -> p a b", b=CH), mbc(mids[mk_]), ALU.mult,
                         [pk, "tmask"], [(dst, dk)])
                release(c, pk)
            yield
        if delta:
            cur = [["M", "P", "M2", "P2"], ["M", "P", "M2", "P2"]]
            for d in range(2):
                u, dk = U_[d], DK[d]
                k.tt("dve", u["Q"][:, 0:nch, :], u["M"][:, 0:nch, :], ibc, ALU.add, [("M", dk), "ident"], [("Q", dk)])
            W2 = nch * 64
            for lev in range(1, 6):
                pv = []
                for d in range(2):
                    u, dk = U_[d], DK[d]
                    Mc, Pc, Mn, Pn = cur[d]
                    p, pk = yield from acquire(c)
                    pv.append((p, pk))
                    for ch in range(nch):
                        k.mm(p[0:64, ch * 64:(ch + 1) * 64], u[Mc][:, ch, :], u[Pc][:, ch, :], True, True, [(Mc, dk), (Pc, dk)], [pk])
                    if lev < 5:
                        for ch in range(nch):
                            k.mm(p[0:64, W2 + ch * 64:W2 + (ch + 1) * 64], u[Pc][:, ch, :], u[Mc][:, ch, :], True, True, [(Mc, dk), (Pc, dk)], [pk])
                yield
                for d in range(2):
                    u, dk = U_[d], DK[d]
                    Mc, Pc, Mn, Pn = cur[d]
                    p, pk = pv[d]
                    k.cp("act", u[Pn][:, 0:nch, :].rearrange("p a b -> p (a b)"), p[0:64, 0:W2], [pk], [(Pn, dk)])
                    k.tt("dve", u["IP"][:, 0:nch, :], p[0:64, 0:W2].rearrange("p (a b) -> p a b", b=64), ibc, ALU.add, [pk, "ident"], [("IP", dk)])
                    if lev < 5:
                        k.cp("act", u[Mn][:, 0:nch, :].rearrange("p a b -> p (a b)"), p[0:64, W2:2 * W2], [pk], [(Mn, dk)])
                    release(c, pk)
                yield
                pv = []
                for d in range(2):
                    u, dk = U_[d], DK[d]
                    p, pk = yield from acquire(c)
                    pv.append((p, pk))
                    for ch in range(nch):
                        k.mm(p[0:64, ch * 64:(ch + 1) * 64], u["IP"][:, ch, :], u["Q"][:, ch, :], True, True, [("IP", dk), ("Q", dk)], [pk])
                yield
                for d in range(2):
                    u, dk = U_[d], DK[d]
                    p, pk = pv[d]
                    k.cp("dve", u["Q"][:, 0:nch, :].rearrange("p a b -> p (a b)"), p[0:64, 0:W2], [pk], [("Q", dk)])
                    release(c, pk)
                    Mc, Pc, Mn, Pn = cur[d]
                    cur[d] = [Mn, Pn, Mc, Pc]
                yield
        for ci in range(nch):
            chs = [ci, nch - 1 - ci]
            if delta:
                pv = []
                for d in range(2):
                    u, dk = U_[d], DK[d]
                    ch = chs[d]
                    cs_ = slice(ch * CH, (ch + 1) * CH)
                    Sc, Sck = u["S"][u["si"]], ("S", dk, u["si"])
                    p, pk = yield from acquire(c)
                    pv.append((p, pk))
                    k.mm(p[0:64, 0:64], u["KKabs"][:, cs_], Sc[:], True, False, [("KKabs", dk), Sck], [pk])
                    k.mm(p[0:64, 0:64], u["AkT"][:, ch, :], u["Vm"][:, ch, :], False, True, [("AkT", dk), ("Vm", dk)], [pk])
                yield
                for d in range(2):
                    u, dk = U_[d], DK[d]
                    p, pk = pv[d]
                    k.cp("dve" if d == 0 else "act", u["rhs0"][:], p[0:64, 0:64], [pk], [("rhs0", dk)])
                    release(c, pk)
                yield
                pv = []
                for d in range(2):
                    u, dk = U_[d], DK[d]
                    ch = chs[d]
                    p, pk = yield from acquire(c)
                    pv.append((p, pk))
                    k.mm(p[0:64, 0:64], u["Q"][:, ch, :], u["rhs0"][:], True, True, [("Q", dk), ("rhs0", dk)], [pk])
                yield
                for d in range(2):
                    u, dk = U_[d], DK[d]
                    p, pk = pv[d]
                    if d == 0:
                        k.ts("dve", u["U"][:], p[0:64, 0:64], -1.0, None, ALU.mult, None, [pk], [("U", dk)])
                    else:
                        k.actf(u["U"][:], p[0:64, 0:64], AF.Copy, [pk], [("U", dk)], scale=-1.0)
                    release(c, pk)
                yield
            pv = []
            for d in range(2):
                u, dk = U_[d], DK[d]
                ch = chs[d]
                cs_ = slice(ch * CH, (ch + 1) * CH)
                Sc, Sck = u["S"][u["si"]], ("S", dk, u["si"])
                p, pk = yield from acquire(c)
                pv.append((p, pk))
                k.mm(p[0:64, 0:CH], Sc[:], u["Qabs"][:, cs_], True, False, [Sck, ("Qabs", dk)], [pk])
                k.mm(p[0:64, 0:CH], u["Vm"][0:CH, ch, :], u["RKT"][0:CH, ch, 0:CH], False, not delta, [("Vm", dk), ("RKT", dk)], [pk])
                if delta:
                    k.mm(p[0:64, 0:CH], u["U"][0:CH, :], u["RBT"][0:CH, ch, 0:CH], False, True, [("U", dk), ("RBT", dk)], [pk])
                k.mm(p[0:64, 64:128], u["KeT"][0:CH, ch, :], u["Vm"][0:CH, ch, :], True, not delta, [("KeT", dk), ("Vm", dk)], [pk])
                if delta:
                    k.mm(p[0:64, 64:128], u["BeT"][0:CH, ch, :], u["U"][:], False, True, [("BeT", dk), ("U", dk)], [pk])
            yield
            for d in range(2):
                u, dk = U_[d], DK[d]
                ch = chs[d]
                cs_ = slice(ch * CH, (ch + 1) * CH)
                p, pk = pv[d]
                Sc, Sck = u["S"][u["si"]], ("S", dk, u["si"])
                Sn, Snk = u["S"][1 - u["si"]], ("S", dk, 1 - u["si"])
                k.cp("act", u["ybuf"][:, cs_], p[0:64, 0:CH], [pk], [("ybuf", dk)])
                gi = ch * CH + (CH - 1 if d == 0 else 0)
                k.stt(Sn[:], Sc[:], u["Eabs"][:, gi:gi + 1], p[0:64, 64:128], ALU.mult, ALU.add, [Sck, ("Eabs", dk), pk], [Snk])
                release(c, pk)
                u["si"] = 1 - u["si"]
            yield
        for d in range(2):
            u, dk = U_[d], DK[d]
            a0 = s0 + tis[d] * TT
            k.store(YS[d][:, a0:a0 + TT], u["ybuf"][:, 0:TT], reads=[("ybuf", dk)])
        yield
    if st_out is not None:
        for d in range(2):
            u, dk = U_[d], DK[d]
            Sc, Sck = u["S"][u["si"]], ("S", dk, u["si"])
            if transpose_state:
                p, pk = yield from acquire(c)
                k.tr(p[0:64, 0:64], Sc[:], I64, [Sck, "ident"], [pk])
                k.cp("dve", t["stmp"][:], p[0:64, 0:64], [pk], [stk])
                release(c, pk)
                k.store(st_out[d], t["stmp"][:], reads=[stk])
            else:
                k.store(st_out[d], Sc[:], reads=[Sck])
    yield


def drive_gen(jobs, tilesets):
    pending = list(jobs)
    free = list(tilesets)
    active = []
    while pending or active:
        while pending and free:
            ts = free.pop(0)
            active.append((pending.pop(0)(ts), ts))
        for item in list(active):
            g, ts = item
            try:
                next(g)
            except StopIteration:
                active.remove(item)
                free.append(ts)
        yield


def run_concurrent(gens):
    gens = list(gens)
    while gens:
        for g in list(gens):
            try:
                next(g)
            except StopIteration:
                gens.remove(g)


def drive(jobs, tilesets):
    run_concurrent([drive_gen(jobs, tilesets)])


def phase_hg(c, l, part=None, es_ext=None, NL=4):
    k, nc, cfg, S, I, O = c.k, c.nc, c.cfg, c.S, c.I, c.O
    NTOK = cfg.NTOK
    PTv = S["PT"].rearrange("(c p) t -> p c t", p=128)
    LAv = S["LAh"].rearrange("a (c p) t -> a p c t", p=128)
    LXv = S["LXh"].rearrange("a (c p) t -> a p c t", p=128)
    if part in (None, "prep"):
        es = ExitStack()
        col = k.sb("hg_col", [128, 2, 3], F32, es)
        ng = k.sb("hg_ng", [128, 2], F32, es)
        lb = k.sb("hg_lb", [128, 2], F32, es)
        oml = k.sb("hg_oml", [128, 2], F32, es)
        noml = k.sb("hg_noml", [128, 2], F32, es)
        k.load(col[:], I["hgcol"], writes=["hgcol"])
        k.load(ng[:], I["hgng"][l], writes=["hgng"])
        if l == 0:
            k.memset("dve", lb[:], 0.0, ["hglb"])
        else:
            k.tt("dve", lb[:], col[:, :, 1], col[:, :, 0], ALU.subtract, ["hgcol"], ["hglb"])
            k.actf(lb[:], lb[:], AF.Sigmoid, ["hglb"], ["hglb"])
        k.ts("dve", oml[:], lb[:], -1.0, 1.0, ALU.mult, ALU.add, ["hglb"], ["hgoml"])
        k.ts("dve", noml[:], oml[:], -1.0, None, ALU.mult, None, ["hgoml"], ["hgnoml"])
        pin = [k.sb(f"hg_pin{i}", [128, 10, 512], F32, es) for i in range(1)]
        wk = {nm: k.sb("hg_" + nm, [128, 2, 512], F32, es) for nm in ["R", "sig", "f", "K", "go"]}
        LAv = S["LAh"].rearrange("a (c p) t -> a p c t", p=128)
        LXv = S["LXh"].rearrange("a (c p) t -> a p c t", p=128)
        for ti, (t0, n, var) in enumerate(cfg.tiles):
            pi_ = pin[0]
            k.load(pi_[:, :, 0:n], PTv[:, 10:20, t0:t0 + n], reads=["PT"], writes=["hgpin"])
            k.actf(wk["R"][:, :, 0:n], pi_[:, 0:2, 0:n], AF.Silu, ["hgpin"], ["hgR"])
            k.store(LAv[0][:, :, t0:t0 + n], wk["R"][:, :, 0:n], reads=["hgR"], writes=["LA"])
            k.actf(wk["go"][:, :, 0:n], pi_[:, 8:10, 0:n], AF.Silu, ["hgpin"], ["hggo"])
            for hc in range(2):
                k.ts("dve", wk["go"][:, hc, 0:n], wk["go"][:, hc, 0:n], ng[:, hc:hc + 1], None, ALU.mult, None, ["hggo", "hgng"], ["hggo"])
            k.store(LXv[0][:, :, t0:t0 + n], wk["go"][:, :, 0:n], reads=["hggo"], writes=["LX"])
            for d in range(2):
                k.actf(wk["sig"][:, :, 0:n], pi_[:, 2 + 2 * d:4 + 2 * d, 0:n], AF.Sigmoid, ["hgpin"], ["hgsig"])
                for hc in range(2):
                    k.ts("dve", wk["f"][:, hc, 0:n], wk["sig"][:, hc, 0:n], oml[:, hc:hc + 1], lb[:, hc:hc + 1], ALU.mult, ALU.add,
                         ["hgsig", "hgoml", "hglb"], ["hgf"])
                    k.ts("pool", wk["K"][:, hc, 0:n], wk["sig"][:, hc, 0:n], noml[:, hc:hc + 1], oml[:, hc:hc + 1], ALU.mult, ALU.add,
                         ["hgsig", "hgoml", "hgnoml"], ["hgK"])
                k.ts("dve", wk["f"][:, :, 0:n], wk["f"][:, :, 0:n], 1e-30, None, ALU.max, None, ["hgf"], ["hgf"])
                k.actf(wk["f"][:, :, 0:n], wk["f"][:, :, 0:n], AF.Ln, ["hgf"], ["hgf"])
                k.store(LAv[4 + d][:, :, t0:t0 + n], wk["f"][:, :, 0:n], reads=["hgf"], writes=["LA"])
                k.store(LAv[2 + d][:, :, t0:t0 + n], wk["K"][:, :, 0:n], reads=["hgK"], writes=["LA"])
        k.barrier()
        es.close()

    if part in (None, "scan"):
      es = ExitStack() if es_ext is None else es_ext
      tsets = [la_alloc(c, es, False, 32, 4, f"h{i}") for i in range(NL)]
      cst = la_consts(c, es, 32, 4)
      jobs = []
      for (s0, T, kind, pi_) in cfg.seqs:
          for h in range(4):
              rows = slice(64 * h, 64 * h + 64)
              arrs = {"R": S["LAh"][0][rows], "V": S["PT"][2048 + 64 * h:2048 + 64 * h + 64], "K0": S["LAh"][2][rows], "K1": S["LAh"][3][rows],
                      "LW0": S["LAh"][4][rows], "LW1": S["LAh"][5][rows]}
              YS = [S["YSh"][0][rows], S["YSh"][1][rows]]
              st_in = [I["st_hg"][l, d, h] for d in range(2)] if kind == 1 else None
              st_out = [O["nhg"][pi_, l, d, h] for d in range(2)] if kind == 0 else None
              jobs.append(lambda ts, arrs=arrs, YS=YS, s0=s0, T=T, st_in=st_in, st_out=st_out:
                          la_lane(c, ts, cst, False, arrs, YS, (s0, T), st_in, st_out, False))
      g = drive_gen(jobs, tsets)
      if part == "scan":
          return g
      run_concurrent([g])
      k.barrier()
      es.close()
    if (cfg.stages is not None and "hg_noout" in cfg.stages) or part not in (None, "out"):
        return
    es = ExitStack()
    YSv = S["YSh"].rearrange("a (c p) t -> a p c t", p=128)
    OTv = S["OT"].rearrange("(c p) t -> p c t", p=128)
    y0 = k.sb("hgo_y0", [128, 2, 512], F32, es)
    y1 = k.sb("hgo_y1", [128, 2, 512], F32, es)
    go = k.sb("hgo_go", [128, 2, 512], F32, es)
    sq = k.sb("hgo_sq", [128, 2, 512], F32, es)
    epsc = k.sb("hgo_eps", [128, 1], F32, es)
    k.memset("dve", epsc[:], LN_EPS, ["hgeps"])
    for ti, (t0, n, var) in enumerate(cfg.tiles):
        k.load(y0[:, :, 0:n], YSv[0][:, :, t0:t0 + n], reads=["YS"], writes=["y0"])
        k.load(y1[:, :, 0:n], YSv[1][:, :, t0:t0 + n], reads=["YS"], writes=["y1"])
        k.load(go[:, :, 0:n], LXv[0][:, :, t0:t0 + n], reads=["LX"], writes=["go"])
        k.tt("dve", y0[:, :, 0:n], y0[:, :, 0:n], y1[:, :, 0:n], ALU.add, ["y0", "y1"], ["y0"])
        k.actf(sq[:, :, 0:n], y0[:, :, 0:n], AF.Square, ["y0"], ["sq"])
        for hc in range(2):
            p, pk = nextbank(c)
            k.mm(p[:, 0:n], c.bones64[:], sq[:, hc, 0:n], True, True, ["bones64", "sq"], [pk])
            k.actf(y1[:, hc, 0:n], p[:, 0:n], AF.Sqrt, [pk, "hgeps", "y1"], ["y1"], bias=epsc[:, 0:1])
        k.recip(y1[:, :, 0:n], y1[:, :, 0:n], ["y1"], ["y1"])
        k.tt("dve", y0[:, :, 0:n], y0[:, :, 0:n], y1[:, :, 0:n], ALU.mult, ["y0", "y1"], ["y0"])
        k.tt("dve", y0[:, :, 0:n], y0[:, :, 0:n], go[:, :, 0:n], ALU.mult, ["y0", "go"], ["y0"])
        k.store(OTv[:, 4:6, t0:t0 + n], y0[:, :, 0:n], reads=["y0"], writes=["OT"])
    k.barrier()
    es.close()


def phase_rw(c, l, part=None, es_ext=None, NL=4):
    k, nc, cfg, S, I, O = c.k, c.nc, c.cfg, c.S, c.I, c.O
    PTv = S["PT"].rearrange("(c p) t -> p c t", p=128)
    LAv = S["LAr"].rearrange("a (c p) t -> a p c t", p=128)
    LXv = S["LXr"].rearrange("a (c p) t -> a p c t", p=128)
    if part in (None, "prep"):
        es = ExitStack()
        mu = k.sb("rw_mu", [128, 2, 8], F32, es)
        cm = k.sb("rw_cm", [128, 8], F32, es)
        muad = k.sb("rw_muad", [64, 2], F32, es)
        cmad = k.sb("rw_cmad", [64, 1], F32, es)
        w0 = k.sb("rw_w0", [128, 2, 2], F32, es)
        col = k.sb("rw_col", [128, 2, 6], F32, es)
        omka = k.sb("rw_omka", [128, 2], F32, es)
        wup = k.sb("rw_wup", [64, 2, 256], F32, es)
        aup = k.sb("rw_aup", [64, 256], F32, es)
        gup = k.sb("rw_gup", [128, 256], F32, es)
        k.load(mu[:], I["rwmu"][l].rearrange("i p c -> p i c"), writes=["rwmu"])
        k.load(muad[:], I["rwmu_ad"][l], writes=["rwmuad"])
        k.load(w0[:], I["rww0"][l].rearrange("i p c -> p i c"), writes=["rww0"])
        k.load(col[:], I["rwcol"][l], writes=["rwcol"])
        k.load(wup[:], I["rw_w_up"][l].rearrange("i p c -> p i c"), writes=["rwwup"])
        k.load(aup[:], I["rw_a_up"][l], writes=["rwaup"])
        k.load(gup[:], I["rw_g_up"][l], writes=["rwgup"])
        k.tt("dve", cm[:], mu[:, 0, :], mu[:, 1, :], ALU.add, ["rwmu"], ["rwcm"])
        k.ts("dve", cm[:], cm[:], -1.0, 1.0, ALU.mult, ALU.add, ["rwcm"], ["rwcm"])
        k.tt("dve", cmad[:], muad[:, 0:1], muad[:, 1:2], ALU.add, ["rwmuad"], ["rwcmad"])
        k.ts("dve", cmad[:], cmad[:], -1.0, 1.0, ALU.mult, ALU.add, ["rwcmad"], ["rwcmad"])
        k.ts("dve", omka[:], col[:, :, 2], -1.0, 1.0, ALU.mult, ALU.add, ["rwcol"], ["rwomka"])
        pa = k.sb("rw_pa", [128, 8, 514], F32, es)
        pad = k.sb("rw_pad", [64, 514], F32, es)
        sh = k.sb("rw_sh", [128, 8, 512], F32, es)
        t2 = k.sb("rw_t2", [128, 8, 512], F32, es)
        adsh = k.sb("rw_adsh", [64, 512], F32, es)
        tw = k.sb("rw_tw", [64, 512], F32, es)
        sgd = k.sb("rw_sgd", [128, 512], F32, es)
        W = {nm: k.sb("rw_" + nm, [128, 2, 512], F32, es) for nm in ["a", "g", "lw0", "lw1", "kk", "kka", "keff", "tmp", "bon"]}
        EC = math.exp(-0.5)
        for (s0, T, kind, pi_) in cfg.seqs:
            for t0 in range(0, T, 512):
                n = min(512, T - t0)
                a0 = s0 + t0
                lo = max(s0, a0 - 1)
                hi = min(s0 + T, a0 + n + 1)
                if t0 == 0:
                    k.memset("dve", pa[:, :, 0:1], 0.0, ["rwpa"])
                    k.memset("dve", pad[:, 0:1], 0.0, ["rwpad"])
                if t0 + n == T:
                    k.memset("dve", pa[:, :, n + 1:n + 2], 0.0, ["rwpa"])
                    k.memset("dve", pad[:, n + 1:n + 2], 0.0, ["rwpad"])
                o0 = lo - (a0 - 1)
                k.load(pa[:, :, o0:o0 + hi - lo], PTv[:, 0:8, lo:hi], reads=["PT"], writes=["rwpa"])
                k.load(pad[:, o0:o0 + hi - lo], S["PT"][832:896, lo:hi], reads=["PT"], writes=["rwpad"])
                k.tt("dve", sh[:, :, 0:n], pa[:, :, 1:n + 1], cm[:].unsqueeze(2).to_broadcast([128, 8, n]), ALU.mult, ["rwpa", "rwcm"], ["rwsh"])
                k.tt("pool", t2[:, :, 0:n], pa[:, :, 0:n], mu[:, 0, :].unsqueeze(2).to_broadcast([128, 8, n]), ALU.mult, ["rwpa", "rwmu"], ["rwt2"])
                k.tt("dve", sh[:, :, 0:n], sh[:, :, 0:n], t2[:, :, 0:n], ALU.add, ["rwsh", "rwt2"], ["rwsh"])
                k.tt("pool", t2[:, :, 0:n], pa[:, :, 2:n + 2], mu[:, 1, :].unsqueeze(2).to_broadcast([128, 8, n]), ALU.mult, ["rwpa", "rwmu"], ["rwt2"])
                k.tt("dve", sh[:, :, 0:n], sh[:, :, 0:n], t2[:, :, 0:n], ALU.add, ["rwsh", "rwt2"], ["rwsh"])
                k.ts("dve", adsh[:, 0:n], pad[:, 1:n + 1], cmad[:, 0:1], None, ALU.mult, None, ["rwpad", "rwcmad"], ["rwadsh"])
                k.stt(adsh[:, 0:n], pad[:, 0:n], muad[:, 0:1], adsh[:, 0:n], ALU.mult, ALU.add, ["rwpad", "rwmuad", "rwadsh"], ["rwadsh"])
                k.stt(adsh[:, 0:n], pad[:, 2:n + 2], muad[:, 1:2], adsh[:, 0:n], ALU.mult, ALU.add, ["rwpad", "rwmuad", "rwadsh"], ["rwadsh"])
                k.actf(tw[:, 0:n], sh[0:64, 6, 0:n], AF.Tanh, ["rwsh"], ["rwtw"])
                k.actf(sgd[:, 0:n], sh[:, 7, 0:n], AF.Sigmoid, ["rwsh"], ["rwsgd"])
                for hc in range(2):
                    hsl = slice(hc * 128, (hc + 1) * 128)
                    r_, k_, v_ = sh[:, hc, 0:n], sh[:, 2 + hc, 0:n], sh[:, 4 + hc, 0:n]
                    p, pk = nextbank(c)
                    k.mm(p[:, 0:n], aup[:, hsl], adsh[:, 0:n], True, True, ["rwaup", "rwadsh"], [pk])
                    k.actf(W["a"][:, hc, 0:n], p[:, 0:n], AF.Sigmoid, [pk, "rwcol"], [("rwa", hc)], bias=col[:, hc, 0:1])
                    p, pk = nextbank(c)
                    k.mm(p[:, 0:n], gup[:, hsl], sgd[:, 0:n], True, True, ["rwgup", "rwsgd"], [pk])
                    k.cp("act", W["g"][:, hc, 0:n], p[:, 0:n], [pk], [("rwg", hc)])
                    for d in range(2):
                        p, pk = nextbank(c)
                        k.mm(p[:, 0:n], wup[:, d, hsl], tw[:, 0:n], True, True, ["rwwup", "rwtw"], [pk])
                        lw = W[f"lw{d}"]
                        k.actf(lw[:, hc, 0:n], p[:, 0:n], AF.Sigmoid, [pk, "rww0"], [("rwlw", d, hc)], bias=w0[:, d, hc:hc + 1])
                        k.ts("dve", lw[:, hc, 0:n], lw[:, hc, 0:n], -EC, None, ALU.mult, None, [("rwlw", d, hc)], [("rwlw", d, hc)])
                    kk = W["kk"]
                    k.ts("dve", kk[:, hc, 0:n], k_, col[:, hc, 1:2], None, ALU.mult, None, ["rwsh", "rwcol"], [("rwkk", hc)])
                    k.actf(W["tmp"][:, hc, 0:n], kk[:, hc, 0:n], AF.Square, [("rwkk", hc)], [("rwtmp", hc)])
                    p, pk = nextbank(c)
                    k.mm(p[:, 0:n], c.bones[:], W["tmp"][:, hc, 0:n], True, True, ["bones", ("rwtmp", hc)], [pk])
                    k.actf(W["tmp"][:, hc, 0:n], p[:, 0:n], AF.Sqrt, [pk], [("rwtmp", hc)])
                    k.ts("dve", W["tmp"][:, hc, 0:n], W["tmp"][:, hc, 0:n], 1e-12, None, ALU.max, None, [("rwtmp", hc)], [("rwtmp", hc)])
                    k.recip(W["tmp"][:, hc, 0:n], W["tmp"][:, hc, 0:n], [("rwtmp", hc)], [("rwtmp", hc)])
                    k.tt("dve", kk[:, hc, 0:n], kk[:, hc, 0:n], W["tmp"][:, hc, 0:n], ALU.mult, [("rwkk", hc), ("rwtmp", hc)], [("rwkk", hc)])
                    k.tt("pool", W["kka"][:, hc, 0:n], kk[:, hc, 0:n], W["a"][:, hc, 0:n], ALU.mult, [("rwkk", hc), ("rwa", hc)], [("rwkka", hc)])
                    k.ts("dve", W["tmp"][:, hc, 0:n], W["a"][:, hc, 0:n], col[:, hc, 2:3], omka[:, hc:hc + 1], ALU.mult, ALU.add,
                         [("rwa", hc), "rwcol", "rwomka", ("rwtmp", hc)], [("rwtmp", hc)])
                    k.tt("dve", W["keff"][:, hc, 0:n], k_, W["tmp"][:, hc, 0:n], ALU.mult, ["rwsh", ("rwtmp", hc)], [("rwkeff", hc)])
                    k.tt("dve", W["tmp"][:, hc, 0:n], r_, W["keff"][:, hc, 0:n], ALU.mult, ["rwsh", ("rwkeff", hc), ("rwtmp", hc)], [("rwtmp", hc)])
                    k.ts("dve", W["tmp"][:, hc, 0:n], W["tmp"][:, hc, 0:n], col[:, hc, 3:4], None, ALU.mult, None, [("rwtmp", hc), "rwcol"], [("rwtmp", hc)])
                    p, pk = nextbank(c)
                    k.mm(p[:, 0:n], c.bones[:], W["tmp"][:, hc, 0:n], True, True, ["bones", ("rwtmp", hc)], [pk])
                    k.tt("dve", W["bon"][:, hc, 0:n], p[:, 0:n], v_, ALU.mult, [pk, "rwsh"], [("rwbon", hc)])
                sl = slice(a0, a0 + n)
                k.store(LAv[0][:, :, sl], sh[:, 0:2, 0:n], reads=["rwsh"], writes=["LA"])
                k.store(LAv[1][:, :, sl], sh[:, 4:6, 0:n], reads=["rwsh"], writes=["LA"])
                k.store(LAv[2][:, :, sl], W["keff"][:, :, 0:n], reads=[("rwkeff", 0), ("rwkeff", 1)], writes=["LA"])
                k.store(LAv[4][:, :, sl], W["lw0"][:, :, 0:n], reads=[("rwlw", 0, 0), ("rwlw", 0, 1)], writes=["LA"])
                k.store(LAv[5][:, :, sl], W["lw1"][:, :, 0:n], reads=[("rwlw", 1, 0), ("rwlw", 1, 1)], writes=["LA"])
                k.store(LAv[6][:, :, sl], W["kk"][:, :, 0:n], reads=[("rwkk", 0), ("rwkk", 1)], writes=["LA"])
                k.store(LAv[7][:, :, sl], W["kka"][:, :, 0:n], reads=[("rwkka", 0), ("rwkka", 1)], writes=["LA"])
                k.store(LXv[0][:, :, sl], W["g"][:, :, 0:n], reads=[("rwg", 0), ("rwg", 1)], writes=["LX"])
                k.store(LXv[1][:, :, sl], W["bon"][:, :, 0:n], reads=[("rwbon", 0), ("rwbon", 1)], writes=["LX"])
        k.barrier()
        es.close()

    if (cfg.stages is not None and "rw_prep_only" in cfg.stages) or part == "prep":
        return
    if part in (None, "scan"):
        es = ExitStack() if es_ext is None else es_ext
        tsets = [la_alloc(c, es, True, 64, 2, f"r{i}") for i in range(NL)]
        cst = la_consts(c, es, 64, 2)
        jobs = []
        for (s0, T, kind, pi_) in cfg.seqs:
            for h in range(4):
                rows = slice(64 * h, 64 * h + 64)
                arrs = {"R": S["LAr"][0][rows], "V": S["LAr"][1][rows], "K0": S["LAr"][2][rows], "K1": S["LAr"][2][rows],
                        "LW0": S["LAr"][4][rows], "LW1": S["LAr"][5][rows], "KK": S["LAr"][6][rows], "BK": S["LAr"][7][rows]}
                YS = [S["YSr"][0][rows], S["YSr"][1][rows]]
                st_in = [I["st_rw"][l, d, h] for d in range(2)] if kind == 1 else None
                st_out = [O["nrw"][pi_, l, d, h] for d in range(2)] if kind == 0 else None
                jobs.append(lambda ts, arrs=arrs, YS=YS, s0=s0, T=T, st_in=st_in, st_out=st_out:
                            la_lane(c, ts, cst, True, arrs, YS, (s0, T), st_in, st_out, True))
        g = drive_gen(jobs, tsets)
        if part == "scan":
            return g
        run_concurrent([g])
        k.barrier()
        es.close()

    if (cfg.stages is not None and "rw_noout" in cfg.stages) or part not in (None, "out"):
        return
    es = ExitStack()
    YSv = S["YSr"].rearrange("a (c p) t -> a p c t", p=128)
    OTv = S["OT"].rearrange("(c p) t -> p c t", p=128)
    col = k.sb("rwo_col", [128, 2, 6], F32, es)
    k.load(col[:], I["rwcol"][l], writes=["rwcol"])
    y0 = k.sb("rwo_y0", [128, 2, 512], F32, es)
    y1 = k.sb("rwo_y1", [128, 2, 512], F32, es)
    g = k.sb("rwo_g", [128, 2, 512], F32, es)
    bon = k.sb("rwo_bon", [128, 2, 512], F32, es)
    sq = k.sb("rwo_sq", [128, 2, 512], F32, es)
    epsc = k.sb("rwo_eps", [128, 1], F32, es)
    k.memset("dve", epsc[:], RW_EPS, ["rweps"])
    for ti, (t0, n, var) in enumerate(cfg.tiles):
        k.load(y0[:, :, 0:n], YSv[0][:, :, t0:t0 + n], reads=["YS"], writes=["y0"])
        k.load(y1[:, :, 0:n], YSv[1][:, :, t0:t0 + n], reads=["YS"], writes=["y1"])
        k.load(g[:, :, 0:n], LXv[0][:, :, t0:t0 + n], reads=["LX"], writes=["g"])
        k.load(bon[:, :, 0:n], LXv[1][:, :, t0:t0 + n], reads=["LX"], writes=["bon"])
        k.tt("dve", y0[:, :, 0:n], y0[:, :, 0:n], y1[:, :, 0:n], ALU.add, ["y0", "y1"], ["y0"])
        for hc in range(2):
            p, pk = nextbank(c)
            k.mm(p[:, 0:n], c.bones64[:], y0[:, hc, 0:n], True, True, ["bones64", "y0"], [pk])
            k.tt("dve", y1[:, hc, 0:n], y0[:, hc, 0:n], p[:, 0:n], ALU.subtract, ["y0", pk, "y1"], [("yc", hc)])
            k.actf(sq[:, hc, 0:n], y1[:, hc, 0:n], AF.Square, [("yc", hc)], [("sq", hc)])
            p, pk = nextbank(c)
            k.mm(p[:, 0:n], c.bones64[:], sq[:, hc, 0:n], True, True, ["bones64", ("sq", hc)], [pk])
            k.actf(sq[:, hc, 0:n], p[:, 0:n], AF.Sqrt, [pk, "rweps"], [("sq", hc)], bias=epsc[:, 0:1])
            k.recip(sq[:, hc, 0:n], sq[:, hc, 0:n], [("sq", hc)], [("sq", hc)])
            k.tt("dve", y1[:, hc, 0:n], y1[:, hc, 0:n], sq[:, hc, 0:n], ALU.mult, [("yc", hc), ("sq", hc)], [("yc", hc)])
            k.ts("dve", y1[:, hc, 0:n], y1[:, hc, 0:n], col[:, hc, 4:5], col[:, hc, 5:6], ALU.mult, ALU.add, [("yc", hc), "rwcol"], [("yc", hc)])
            k.tt("dve", y1[:, hc, 0:n], y1[:, hc, 0:n], bon[:, hc, 0:n], ALU.add, [("yc", hc), "bon"], [("yc", hc)])
            k.tt("dve", y1[:, hc, 0:n], y1[:, hc, 0:n], g[:, hc, 0:n], ALU.mult, [("yc", hc), "g"], [("yc", hc)])
        k.store(OTv[:, 0:2, t0:t0 + n], y1[:, :, 0:n], reads=[("yc", 0), ("yc", 1)], writes=["OT", "y1"])
    k.barrier()
    es.close()


def phase_s5(c, l, part=None, es_ext=None):
    k, nc, cfg, S, I, O = c.k, c.nc, c.cfg, c.S, c.I, c.O
    PI = math.pi
    TT = 128
    PTv = S["PT"].rearrange("(c p) t -> p c t", p=128)
    YSv = S["YS5"].rearrange("a (c p) t -> a p c t", p=128)
    if part in (None, "scan"):
        es = ExitStack() if es_ext is None else es_ext
        lam = k.sb("s5_lam", [128, 2, 8, 3], F32, es)
        k.load(lam[:], I["s5lam"][l].rearrange("d p j r -> p d j r"), writes=["s5lam"])
        BT = k.sb("s5_BT", [128, 2, 8, 128], F32, es)
        CT = k.sb("s5_CT", [128, 2, 8, 128], F32, es)
        for r in range(2):
            k.load(BT[:, r], I["s5BT"][l, r].rearrange("j p n -> p j n"), writes=["s5BT"])
            k.load(CT[:, r], I["s5CT"][l, r].rearrange("j p n -> p j n"), writes=["s5CT"])
        sm = {nm: k.sb("s5_" + nm, [128, 2, 8], F32, es) for nm in
              ["dt", "mag", "th", "th2", "msk", "cos", "sin", "abre", "abim", "den", "zre", "zim", "t1", "t2", "cw", "sw", "cw2"]}
        RT = {nm: k.sb("s5_" + nm, [128, 2, 8, TT], F32, es) for nm in ["RTre", "RTim", "DZre", "DZim"]}
        tA = k.sb("s5_tA", [128, 8, TT], F32, es)
        tB = k.sb("s5_tB", [128, 8, TT], F32, es)
        magz = k.sb("s5_magz", [128, 2, 8, TT], F32, es)
        A_ = lambda nm: sm[nm][:]
        are, aim, ldt = lam[:, :, :, 0], lam[:, :, :, 1], lam[:, :, :, 2]

        def tts(e, o, a, b, op, r, w):
            k.tt(e, sm[o][:], a, b, op, r, w)

        k.actf(A_("dt"), ldt, AF.Exp, ["s5lam"], ["dt"])
        tts("dve", "t1", are, A_("dt"), ALU.mult, ["s5lam", "dt"], ["t1"])
        k.actf(A_("mag"), A_("t1"), AF.Exp, ["t1"], ["mag"])
        tts("dve", "th", aim, A_("dt"), ALU.mult, ["s5lam", "dt"], ["th"])

        def reduce_pi(nm, iters):
            for _ in range(iters):
                k.ts("dve", A_("msk"), A_(nm), PI, -2.0 * PI, ALU.is_gt, ALU.mult, [nm], ["msk"])
                tts("dve", nm, A_(nm), A_("msk"), ALU.add, [nm, "msk"], [nm])
                k.ts("dve", A_("msk"), A_(nm), -PI, 2.0 * PI, ALU.is_lt, ALU.mult, [nm], ["msk"])
                tts("dve", nm, A_(nm), A_("msk"), ALU.add, [nm, "msk"], [nm])
        reduce_pi("th", 5)
        k.ts("dve", A_("th2"), A_("th"), PI / 2, None, ALU.add, None, ["th"], ["th2"])
        reduce_pi("th2", 1)
        k.actf(A_("sin"), A_("th"), AF.Sin, ["th"], ["sin"])
        k.actf(A_("cos"), A_("th2"), AF.Sin, ["th2"], ["cos"])
        tts("dve", "abre", A_("mag"), A_("cos"), ALU.mult, ["mag", "cos"], ["abre"])
        tts("dve", "abim", A_("mag"), A_("sin"), ALU.mult, ["mag", "sin"], ["abim"])
        tts("dve", "t1", are, are, ALU.mult, ["s5lam"], ["t1"])
        tts("dve", "t2", aim, aim, ALU.mult, ["s5lam"], ["t2"])
        tts("dve", "den", A_("t1"), A_("t2"), ALU.add, ["t1", "t2"], ["den"])
        k.recip(A_("den"), A_("den"), ["den"], ["den"])
        k.ts("dve", A_("t1"), A_("abre"), -1.0, None, ALU.add, None, ["abre"], ["t1"])
        tts("dve", "zre", A_("t1"), are, ALU.mult, ["t1", "s5lam"], ["zre"])
        tts("dve", "t2", A_("abim"), aim, ALU.mult, ["abim", "s5lam"], ["t2"])
        tts("dve", "zre", A_("zre"), A_("t2"), ALU.add, ["zre", "t2"], ["zre"])
        tts("dve", "zre", A_("zre"), A_("den"), ALU.mult, ["zre", "den"], ["zre"])
        tts("dve", "zim", A_("abim"), are, ALU.mult, ["abim", "s5lam"], ["zim"])
        tts("dve", "t2", A_("t1"), aim, ALU.mult, ["t1", "s5lam"], ["t2"])
        tts("dve", "zim", A_("zim"), A_("t2"), ALU.subtract, ["zim", "t2"], ["zim"])
        tts("dve", "zim", A_("zim"), A_("den"), ALU.mult, ["zim", "den"], ["zim"])
        for d in range(2):
            Rre, Rim = RT["RTre"][:, d], RT["RTim"][:, d]
            k.cp("dve", Rre[:, :, 0:1], sm["cos"][:, d, :].unsqueeze(2), ["cos"], [("RT", d)])
            k.cp("dve", Rim[:, :, 0:1], sm["sin"][:, d, :].unsqueeze(2), ["sin"], [("RT", d)])
            k.cp("dve", sm["cw"][:, d, :], sm["cos"][:, d, :], ["cos"], [("cw", d)])
            k.cp("dve", sm["sw"][:, d, :], sm["sin"][:, d, :], ["sin"], [("sw", d)])
            w = 1
            while w < TT:
                cwb = sm["cw"][:, d, :].unsqueeze(2).to_broadcast([128, 8, w])
                swb = sm["sw"][:, d, :].unsqueeze(2).to_broadcast([128, 8, w])
                k.tt("dve", tA[:, :, 0:w], Rre[:, :, 0:w], cwb, ALU.mult, [("RT", d), ("cw", d)], ["tA"])
                k.tt("pool", tB[:, :, 0:w], Rim[:, :, 0:w], swb, ALU.mult, [("RT", d), ("sw", d)], ["tB"])
                k.tt("dve", Rre[:, :, w:2 * w], tA[:, :, 0:w], tB[:, :, 0:w], ALU.subtract, ["tA", "tB", ("RT", d)], [("RT", d)])
                k.tt("dve", tA[:, :, 0:w], Rre[:, :, 0:w], swb, ALU.mult, [("RT", d), ("sw", d)], ["tA"])
                k.tt("pool", tB[:, :, 0:w], Rim[:, :, 0:w], cwb, ALU.mult, [("RT", d), ("cw", d)], ["tB"])
                k.tt("dve", Rim[:, :, w:2 * w], tA[:, :, 0:w], tB[:, :, 0:w], ALU.add, ["tA", "tB", ("RT", d)], [("RT", d)])
                k.tt("dve", sm["t1"][:, d, :], sm["cw"][:, d, :], sm["cw"][:, d, :], ALU.mult, [("cw", d), "t1"], ["t1"])
                k.tt("dve", sm["t2"][:, d, :], sm["sw"][:, d, :], sm["sw"][:, d, :], ALU.mult, [("sw", d), "t2"], ["t2"])
                k.tt("dve", sm["cw2"][:, d, :], sm["cw"][:, d, :], sm["sw"][:, d, :], ALU.mult, [("cw", d), ("sw", d)], ["cw2"])
                k.tt("dve", sm["cw"][:, d, :], sm["t1"][:, d, :], sm["t2"][:, d, :], ALU.subtract, ["t1", "t2"], [("cw", d)])
                k.ts("dve", sm["sw"][:, d, :], sm["cw2"][:, d, :], 2.0, None, ALU.mult, None, ["cw2"], [("sw", d)])
                w *= 2
            zre_b = sm["zre"][:, d, :].unsqueeze(2).to_broadcast([128, 8, TT])
            zim_b = sm["zim"][:, d, :].unsqueeze(2).to_broadcast([128, 8, TT])
            k.tt("dve", tA[:], Rre, zre_b, ALU.mult, [("RT", d), "zre"], ["tA"])
            k.tt("pool", tB[:], Rim, zim_b, ALU.mult, [("RT", d), "zim"], ["tB"])
            k.tt("dve", RT["DZre"][:, d], tA[:], tB[:], ALU.add, ["tA", "tB"], [("DZ", d)])
            k.tt("dve", tA[:], Rre, zim_b, ALU.mult, [("RT", d), "zim"], ["tA"])
            k.tt("pool", tB[:], Rim, zre_b, ALU.mult, [("RT", d), "zre"], ["tB"])
            k.tt("dve", RT["DZim"][:, d], tA[:], tB[:], ALU.subtract, ["tA", "tB", ("DZ", d)], [("DZ", d)])
        for d in range(2):
            k.cp("act", magz[:, d], sm["mag"][:, d, :].unsqueeze(2).to_broadcast([128, 8, TT]), ["mag"], [("magz", d)])
            fi = 0 if d == 0 else TT - 1
            k.memset("dve", magz[:, d, :, fi:fi + 1], 0.0, [("magz", d)])
        wsets = []
        for i in range(2):
            w = {nm: k.sb(f"s5w{i}_{nm}", [128, 8, TT], F32, es) for nm in ["A", "B", "C", "D"]}
            w["Braw"] = k.sb(f"s5w{i}_Braw", [128, 8, 2, TT], F32, es)
            w["u"] = k.sb(f"s5w{i}_u", [128, 2, TT], F32, es)
            w["y"] = k.sb(f"s5w{i}_y", [128, 2, TT], F32, es)
            w["init"] = k.sb(f"s5w{i}_init", [128, 8, 2], F32, es)
            w["cin"] = k.sb(f"s5w{i}_cin", [128, 8, 2], F32, es)
            w["tag"] = i
            wsets.append(w)

        def sweep(w, s0, T, kind, pi_, d):
            tg = w["tag"]
            K_ = lambda nm: ("s5", tg, nm)
            nt = T // TT
            init, cin = w["init"], w["cin"]
            if kind == 1:
                k.load(init[:], I["st_s5"][l, d], writes=[K_("init")])
            else:
                k.memset("dve", init[:], 0.0, [K_("init")])
            first = 0 if d == 0 else TT - 1
            last = TT - 1 if d == 0 else 0
            rvt = (lambda ap: ap) if d == 0 else (lambda ap: ap[:, :, ::-1])
            flat = lambda ap: ap.rearrange("p a b -> p (a b)")
            rvf = (lambda ap: flat(ap)) if d == 0 else (lambda ap: flat(ap)[:, ::-1])
            DZr, DZi = rvt(RT["DZre"][:, d]), rvt(RT["DZim"][:, d])
            Rr, Ri = rvt(RT["RTre"][:, d]), rvt(RT["RTim"][:, d])
            A, B, C, D_ = w["A"], w["B"], w["C"], w["D"]
            yield
            for it in range(nt):
                ti = it if d == 0 else nt - 1 - it
                a0 = s0 + ti * TT
                k.load(w["u"][:], PTv[:, 8:10, a0:a0 + TT], writes=[K_("u")])
                yield
                for j2 in range(4):
                    p, pk = yield from acquire(c)
                    for jj in range(2):
                        j = 2 * j2 + jj
                        for r in range(2):
                            k.mm(p[:, (2 * jj + r) * TT:(2 * jj + r + 1) * TT], BT[:, r, j, :], w["u"][:, j // 4, :], True, True, ["s5BT", K_("u")], [pk])
                    k.cp("act", w["Braw"][:, 2 * j2:2 * j2 + 2].rearrange("p a r t -> p (a r t)"), p[:, 0:4 * TT], [pk], [K_("Braw")])
                    release(c, pk)
                yield
                Br, Bi = w["Braw"][:, :, 0, :], w["Braw"][:, :, 1, :]
                k.tt("dve", A[:], Br, DZr, ALU.mult, [K_("Braw"), ("DZ", d)], [K_("A")])
                k.tt("pool", B[:], Bi, DZi, ALU.mult, [K_("Braw"), ("DZ", d)], [K_("B")])
                yield
                k.tt("dve", A[:], A[:], B[:], ALU.subtract, [K_("A"), K_("B")], [K_("A")])
                k.tt("pool", C[:], Br, DZi, ALU.mult, [K_("Braw"), ("DZ", d)], [K_("C")])
                yield
                k.tt("pool", B[:], Bi, DZr, ALU.mult, [K_("Braw"), ("DZ", d), K_("A")], [K_("B")])
                k.tt("dve", cin[:], init[:], sm["mag"][:, d, :].unsqueeze(2).to_broadcast([128, 8, 2]), ALU.mult, [K_("init"), "mag"], [K_("cin")])
                k.tt("dve", A[:, :, first:first + 1], A[:, :, first:first + 1], cin[:, :, 0:1], ALU.add, [K_("A"), K_("cin")], [K_("A")])
                yield
                k.tt("pool", B[:], B[:], C[:], ALU.add, [K_("B"), K_("C")], [K_("B")])
                k.scan(rvf(C[:]), rvf(magz[:, d]), rvf(A[:]), 0.0, [K_("A"), ("magz", d), K_("B")], [K_("C")])
                yield
                k.tt("dve", B[:, :, first:first + 1], B[:, :, first:first + 1], cin[:, :, 1:2], ALU.add, [K_("B"), K_("cin")], [K_("B")])
                k.scan(rvf(D_[:]), rvf(magz[:, d]), rvf(B[:]), 0.0, [K_("B"), ("magz", d)], [K_("D")])
                yield
                k.tt("dve", A[:], C[:], Rr, ALU.mult, [K_("C"), ("RT", d)], [K_("A")])
                k.tt("pool", B[:], D_[:], Ri, ALU.mult, [K_("D"), ("RT", d)], [K_("B")])
                yield
                k.tt("dve", A[:], A[:], B[:], ALU.subtract, [K_("A"), K_("B")], [K_("A")])
                k.tt("pool", B[:], C[:], Ri, ALU.mult, [K_("C"), ("RT", d), K_("A")], [K_("B")])
                yield
                k.tt("dve", C[:], D_[:], Rr, ALU.mult, [K_("D"), ("RT", d), K_("B")], [K_("C")])
                yield
                k.stt(B[:], B[:], -1.0, C[:], ALU.mult, ALU.subtract, [K_("B"), K_("C")], [K_("B")])
                yield
                k.cp("act", init[:, :, 0:1], A[:, :, last:last + 1], [K_("A")], [K_("init")])
                k.actf(init[:, :, 1:2], B[:, :, last:last + 1], AF.Copy, [K_("B")], [K_("init")], scale=-1.0)
                for kc in range(2):
                    p, pk = yield from acquire(c)
                    for jj in range(4):
                        j = 4 * kc + jj
                        k.mm(p[:, 0:TT], CT[:, 0, j, :], A[:, j, :], jj == 0, False, ["s5CT", K_("A")], [pk])
                        k.mm(p[:, 0:TT], CT[:, 1, j, :], B[:, j, :], False, jj == 3, ["s5CT", K_("B")], [pk])
                    k.cp("act", w["y"][:, kc, :], p[:, 0:TT], [pk], [K_("y")])
                    release(c, pk)
                yield
                k.store(YSv[d][:, :, a0:a0 + TT], w["y"][:], reads=[K_("y")])
                yield
            if kind == 0:
                k.store(O["ns5"][pi_, l, d].rearrange("g n r -> (g n) r").rearrange("(j p) r -> p j r", p=128), init[:], reads=[K_("init")])
            yield

        jobs = []
        for (s0, T, kind, pi_) in cfg.seqs:
            for d in range(2):
                jobs.append(lambda w, s0=s0, T=T, kind=kind, pi_=pi_, d=d: sweep(w, s0, T, kind, pi_, d))
        g = drive_gen(jobs, wsets)
        if part == "scan":
            return g
        run_concurrent([g])
        k.barrier()
        es.close()

    if (cfg.stages is not None and "s5_noout" in cfg.stages) or part not in (None, "out"):
        return
    es = ExitStack()
    OTv = S["OT"].rearrange("(c p) t -> p c t", p=128)
    col = k.sb("s5o_col", [128, 2, 2], F32, es)
    glu = k.sb("s5o_glu", [128, 2, 256], F32, es)
    k.load(col[:], I["s5col"][l], writes=["s5col"])
    k.load(glu[:], I["s5glu"][l].rearrange("(kc p) n -> p kc n", p=128), writes=["s5glu"])
    y0 = k.sb("s5o_y0", [128, 2, 512], F32, es)
    y1 = k.sb("s5o_y1", [128, 2, 512], F32, es)
    u = k.sb("s5o_u", [128, 2, 512], F32, es)
    t_ = k.sb("s5o_t", [128, 2, 512], F32, es)
    for ti, (t0, n, var) in enumerate(cfg.tiles):
        k.load(y0[:, :, 0:n], YSv[0][:, :, t0:t0 + n], reads=["YS"], writes=["y0"])
        k.load(y1[:, :, 0:n], YSv[1][:, :, t0:t0 + n], reads=["YS"], writes=["y1"])
        k.load(u[:, :, 0:n], PTv[:, 8:10, t0:t0 + n], reads=["PT"], writes=["u"])
        k.tt("dve", y0[:, :, 0:n], y0[:, :, 0:n], y1[:, :, 0:n], ALU.add, ["y0", "y1"], ["y0"])
        for hc in range(2):
            k.stt(y0[:, hc, 0:n], u[:, hc, 0:n], col[:, hc, 0:1], y0[:, hc, 0:n], ALU.mult, ALU.add, ["u", "s5col", "y0"], ["y0"])
        k.actf(t_[:, :, 0:n], y0[:, :, 0:n], AF.Square, ["y0"], ["t"])
        k.ts("dve", t_[:, :, 0:n], t_[:, :, 0:n], 0.044715, 1.0, ALU.mult, ALU.add, ["t"], ["t"])
        k.tt("dve", t_[:, :, 0:n], t_[:, :, 0:n], y0[:, :, 0:n], ALU.mult, ["t", "y0"], ["t"])
        k.actf(t_[:, :, 0:n], t_[:, :, 0:n], AF.Sigmoid, ["t"], ["t"], scale=1.5957691216057308)
        k.tt("dve", y0[:, :, 0:n], y0[:, :, 0:n], t_[:, :, 0:n], ALU.mult, ["t", "y0"], ["y0"])
        for hc in range(2):
            p, pk = nextbank(c)
            for kc in range(2):
                k.mm(p[:, 0:n], glu[:, kc, hc * 128:(hc + 1) * 128], y0[:, kc, 0:n], kc == 0, kc == 1, ["s5glu", "y0"], [pk])
            k.actf(y1[:, hc, 0:n], p[:, 0:n], AF.Sigmoid, [pk, "s5col", "y1"], [("s5sg", hc)], bias=col[:, hc, 1:2])
            k.tt("dve", y1[:, hc, 0:n], y1[:, hc, 0:n], y0[:, hc, 0:n], ALU.mult, [("s5sg", hc), "y0"], [("s5sg", hc)])
        k.store(OTv[:, 2:4, t0:t0 + n], y1[:, :, 0:n], reads=[("s5sg", 0), ("s5sg", 1)], writes=["OT", "y1"])
    k.barrier()
    es.close()

def cols(v):
    v = np.asarray(v)
    n = v.shape[-1] // 128
    return np.ascontiguousarray(np.swapaxes(v.reshape(v.shape[:-1] + (n, 128)), -1, -2))


def prep_core(inp, b, cfg):
    NP, TP = cfg.NP, cfg.TP
    m = {}
    xs = inp["x_sample"][b]
    xp = inp["x_prompt"][b * NP:(b + 1) * NP].reshape(NP * TP, D)
    m["xin"] = np.ascontiguousarray(np.concatenate([xs, xp], axis=0))
    m["ident"] = np.eye(128, dtype=np.float32)
    cc = np.stack([cols(inp["c_ctx"]), cols(inp["c"][b])], axis=-1)
    m["ccol"] = np.ascontiguousarray(cc.astype(np.float32))
    m["w_mod"] = inp["w_mod"]
    m["bmod"] = cols(inp["b_mod"])
    m["lng"] = cols(inp["ln_g"])
    m["lnb"] = cols(inp["ln_b"])
    for nm in ("ffn_w_in", "ffn_w_out", "w_in", "w_out"):
        m[nm] = inp[nm]
    m["cache_k"] = np.ascontiguousarray(inp["cache_na_k"][b].reshape(L, 256, 256))
    m["cache_v"] = np.ascontiguousarray(inp["cache_na_v"][b].reshape(L, 256, 256))
    cc_, ww_ = np.meshgrid(np.arange(64), np.arange(64), indexing="ij")
    idx = np.clip(cc_ - ww_ + 15, 0, 30)
    rp = inp["na_rpb"][:, :, :, idx]
    m["rpbT"] = np.ascontiguousarray(np.transpose(rp, (0, 3, 1, 2, 4)))
    cs = np.clip(ww_ - 8, 0, 48)
    m["namask"] = np.where((cc_ >= cs) & (cc_ < cs + 16), 0.0, NEG).astype(np.float32)
    a_, b_ = np.meshgrid(np.arange(64), np.arange(64), indexing="ij")
    UI, US, LI, LS = (a_ <= b_), (a_ < b_), (a_ >= b_), (a_ > b_)
    m["tmask"] = np.stack([UI, US, LI, LS, -1.0 * US, -1.0 * LS]).astype(np.float32)
    bo = np.zeros((128, 128), np.float32)
    bo[0:64, 0:64] = 1.0
    bo[64:128, 64:128] = 1.0
    m["bones"] = bo
    hl = cols(inp["hg_lb"])
    m["hgcol"] = np.ascontiguousarray(np.stack([hl[0], hl[1], hl[1]], axis=-1).astype(np.float32))
    m["hgng"] = cols(inp["hg_norm_g"])
    m["st_hg"] = np.ascontiguousarray(inp["state_hgrn"][b])
    m["rwmu"] = cols(inp["rw_mu"])
    m["rwmu_ad"] = np.ascontiguousarray(np.transpose(inp["rw_mu"][:, :, 832:896], (0, 2, 1)))
    m["rww0"] = cols(inp["rw_w0"])
    m["rw_w_up"] = inp["rw_w_up"]
    m["rw_a_up"] = inp["rw_a_up"]
    m["rw_g_up"] = inp["rw_g_up"]
    rk = inp["rw_r_k"].reshape(L, 256)
    m["rwcol"] = np.ascontiguousarray(np.stack([cols(inp["rw_a0"]), cols(inp["rw_k_k"]), cols(inp["rw_k_a"]), cols(rk),
                                                 cols(inp["rw_lnx_g"]), cols(inp["rw_lnx_b"])], axis=-1).astype(np.float32))
    m["st_rw"] = np.ascontiguousarray(inp["state_rwkv"][b])
    def scol(v):
        return cols(v.reshape(v.shape[:-2] + (1024,)))
    ldt = np.repeat(inp["s5_log_dt"][..., None], 64, axis=-1)
    m["s5lam"] = np.ascontiguousarray(np.stack([scol(inp["s5_a_re"]), scol(inp["s5_a_im"]), scol(ldt)], axis=-1).astype(np.float32))
    BT = np.zeros((L, 2, 8, 128, 128), np.float32)
    CT = np.zeros((L, 2, 8, 128, 128), np.float32)
    for r, (bsrc, csrc) in enumerate(((inp["s5_b_re"], inp["s5_c_re"]), (inp["s5_b_im"], inp["s5_c_im"]))):
        for j in range(8):
            for gl in range(2):
                g = 2 * j + gl
                r0 = 32 * (j % 4) + 16 * gl
                BT[:, r, j, r0:r0 + 16, 64 * gl:64 * gl + 64] = np.transpose(bsrc[:, g], (0, 2, 1))
                CT[:, r, j, 64 * gl:64 * gl + 64, r0:r0 + 16] = np.transpose(csrc[:, g], (0, 2, 1))
    m["s5BT"], m["s5CT"] = BT, CT
    m["s5col"] = np.ascontiguousarray(np.stack([cols(inp["s5_d"]), cols(inp["s5_glu_b"])], axis=-1).astype(np.float32))
    m["s5glu"] = inp["s5_glu_w"]
    st = inp["state_s5"][b].reshape(L, 2, 8, 128, 2)
    m["st_s5"] = np.ascontiguousarray(np.transpose(st, (0, 1, 3, 2, 4)))
    return m


_CACHE = {}


def kernel(**inputs):
    cfg = Cfg()
    inp = {k_: np.asarray(v) for k_, v in inputs.items()}
    if "nc" not in _CACHE:
        _CACHE["nc"] = build(cfg)
    nc, c = _CACHE["nc"]
    in_maps = [prep_core(inp, b, cfg) for b in range(8)]
    res = run_bass_kernel_spmd(nc, in_maps, core_ids=list(range(8)))
    R = res.results
    NP, TP = cfg.NP, cfg.TP
    y_p = np.concatenate([r["y_p"].reshape(NP, TP, D) for r in R], axis=0)
    y_s = np.stack([r["y_s"] for r in R], axis=0)
    nk = np.concatenate([r["nk"].reshape(NP, L, TP, 4, 64) for r in R], axis=0)
    nv = np.concatenate([r["nv"].reshape(NP, L, TP, 4, 64) for r in R], axis=0)
    nrw = np.concatenate([r["nrw"] for r in R], axis=0)
    ns5 = np.concatenate([r["ns5"] for r in R], axis=0)
    nhg = np.concatenate([r["nhg"] for r in R], axis=0)
    return (y_p.astype(np.float32), y_s.astype(np.float32), nk.astype(np.float32), nv.astype(np.float32),
            nrw.astype(np.float32), ns5.astype(np.float32), nhg.astype(np.float32))
```

```python
import bisect
import math
from contextlib import ExitStack
import numpy as np
import concourse.bass as bass
import concourse.mybir as mybir
from concourse.bass_utils import run_bass_kernel_spmd

F32 = mybir.dt.float32
BF16 = mybir.dt.bfloat16
AF = mybir.ActivationFunctionType
ALU = mybir.AluOpType
AX = mybir.AxisListType
EPOCH = 20000
NDMASEM = 12

D = 1024
L = 2
DFF = 2816
NF = DFF // 128
INC = 3328
ALPHA = (2 * L) ** 0.25
LN_EPS = 1e-5
RW_EPS = 64e-5
NEG = -30000.0


class Eng:
    def __init__(self, K, name, eng, is_pe=False):
        self.K = K
        self.name = name
        self.eng = eng
        self.is_pe = is_pe
        self.insts = []
        self.marks = []
        self.sems = []
        self.seen = {}
        self.dseen = {}

    def sem_for(self, m):
        e = (m - 1) // EPOCH
        while len(self.sems) <= e:
            self.sems.append(self.K.new_sem(f"{self.name}_e{len(self.sems)}"))
        return self.sems[e], (m - 1) % EPOCH + 1


class DmaQ:
    def __init__(self, K, name, issuer):
        self.K = K
        self.name = name
        self.issuer = issuer
        self.n = 0
        self.sems = [K.new_sem(f"{name}_d{i}") for i in range(NDMASEM)]

    def semval(self, i):
        return self.sems[i % NDMASEM], 16 * (i // NDMASEM + 1)


class Res:
    __slots__ = ("w", "r")

    def __init__(self):
        self.w = None
        self.r = {}


class K:
    def __init__(self, nc):
        self.nc = nc
        self.es = ExitStack()
        self.nsem = 0
        self.pe = Eng(self, "pe", nc.tensor, is_pe=True)
        self.dve = Eng(self, "dve", nc.vector)
        self.act = Eng(self, "act", nc.scalar)
        self.pool = Eng(self, "pool", nc.gpsimd)
        self.sp = Eng(self, "sp", nc.sync)
        self.engs = {e.name: e for e in (self.pe, self.dve, self.act, self.pool, self.sp)}
        self.ld = DmaQ(self, "ld", self.sp)
        self.st = DmaQ(self, "st", self.pool)
        self.dq = {"ld": self.ld, "st": self.st}
        self.res = {}
        self.ninst = 0
        self.nwait = 0
        self.rr = 0

    def new_sem(self, name):
        self.nsem += 1
        return self.es.enter_context(self.nc.semaphore(name))

    def sb(self, name, shape, dt=F32, stack=None):
        self.uid = getattr(self, "uid", 0) + 1
        return (stack or self.es).enter_context(self.nc.sbuf_tensor(f"{name}_u{self.uid}", list(shape), dt))

    def ps(self, name, shape, dt=F32, stack=None):
        return (stack or self.es).enter_context(self.nc.psum_tensor(name, list(shape), dt))

    def _wait_inst(self, E, xname, idx):
        X = self.engs[xname]
        if E is X and E.is_pe:
            return
        p = bisect.bisect_left(X.marks, idx)
        if p < len(X.marks):
            m = p + 1
        else:
            m = len(X.marks) + 1
            sem, val = X.sem_for(m)
            X.insts[idx].then_inc(sem, 1)
            X.marks.append(idx)
        if E.seen.get(xname, 0) >= m:
            return
        sem, val = X.sem_for(m)
        E.eng.wait_ge(sem, val)
        self.nwait += 1
        E.seen[xname] = m

    def _wait_dma(self, E, qname, i):
        Q = self.dq[qname]
        key = (qname, i % NDMASEM)
        if E.dseen.get(key, -1) >= i:
            return
        sem, val = Q.semval(i)
        E.eng.wait_ge(sem, val)
        self.nwait += 1
        E.dseen[key] = i

    def _wait(self, E, dep):
        if dep[0] == "dma":
            self._wait_dma(E, dep[1], dep[2])
        else:
            self._wait_inst(E, dep[1], dep[2])

    @staticmethod
    def _rkey(me):
        if me[0] == "i":
            return ("i", me[1])
        return ("dma", me[1], me[2] % NDMASEM)

    def _deps(self, reads, writes):
        deps = {}

        def add(d):
            k = self._rkey(d)
            if k not in deps or deps[k][2] < d[2]:
                deps[k] = d
        for r in reads:
            rs = self.res.get(r)
            if rs is not None and rs.w is not None:
                add(rs.w)
        for w in writes:
            rs = self.res.get(w)
            if rs is not None:
                if rs.w is not None:
                    add(rs.w)
                for d in rs.r.values():
                    add(d)
        return list(deps.values())

    def _update(self, me, reads, writes):
        k = self._rkey(me)
        for r in reads:
            rs = self.res.get(r)
            if rs is None:
                rs = self.res[r] = Res()
            rs.r[k] = me
        for w in writes:
            rs = self.res.get(w)
            if rs is None:
                rs = self.res[w] = Res()
            rs.w = me
            rs.r = {}

    def op(self, E, fn, reads=(), writes=()):
        bk = [r for r in reads if isinstance(r, tuple) and r and r[0] == "bank"]
        if bk:
            reads = [r for r in reads if r not in bk]
            writes = list(writes) + bk
        for d in self._deps(reads, writes):
            self._wait(E, d)
        h = fn()
        idx = len(E.insts)
        E.insts.append(h)
        self._update(("i", E.name, idx), reads, writes)
        self.ninst += 1
        return h

    def dma(self, Q, out, in_, reads=(), writes=(), **kw):
        E = Q.issuer
        for d in self._deps(reads, writes):
            self._wait(E, d)
        i = Q.n
        if i >= NDMASEM:
            self._wait_dma(E, Q.name, i - NDMASEM)
        sem, val = Q.semval(i)
        h = E.eng.dma_start(out=out, in_=in_, **kw)
        h.then_inc(sem, 16)
        Q.n += 1
        self._update(("dma", Q.name, i), reads, writes)
        self.ninst += 1
        return h

    def load(self, out, in_, reads=(), writes=(), **kw):
        return self.dma(self.ld, out, in_, reads, writes, **kw)

    def store(self, out, in_, reads=(), writes=(), **kw):
        return self.dma(self.st, out, in_, reads, writes, **kw)

    def barrier(self):
        for E in self.engs.values():
            for X in self.engs.values():
                if X.insts:
                    self._wait_inst(E, X.name, len(X.insts) - 1)
            for Q in self.dq.values():
                for i in range(max(0, Q.n - NDMASEM), Q.n):
                    self._wait_dma(E, Q.name, i)
        self.res = {}

    def E(self, e):
        return self.engs[e]

    def any2(self):
        self.rr += 1
        return "dve" if self.rr % 2 else "act"

    def mm(self, out, lhsT, rhs, start, stop, r, w):
        nc = self.nc
        return self.op(self.pe, lambda: nc.tensor.matmul(out, lhsT=lhsT, rhs=rhs, start=start, stop=stop), r, w)

    def tr(self, out, in_, ident, r, w):
        nc = self.nc
        return self.op(self.pe, lambda: nc.tensor.transpose(out, in_, ident), r, w)

    def actf(self, out, in_, func, r, w, scale=1.0, bias=None):
        nc = self.nc
        if bias is None:
            return self.op(self.act, lambda: nc.scalar.activation(out=out, in_=in_, func=func, scale=scale), r, w)
        return self.op(self.act, lambda: nc.scalar.activation(out=out, in_=in_, func=func, scale=scale, bias=bias), r, w)

    def cp(self, e, out, in_, r, w):
        nc = self.nc
        if e == "act":
            return self.op(self.act, lambda: nc.scalar.copy(out, in_), r, w)
        eng = nc.vector if e == "dve" else nc.gpsimd
        return self.op(self.E(e), lambda: eng.tensor_copy(out, in_), r, w)

    def tt(self, e, out, in0, in1, op, r, w):
        eng = self.nc.vector if e == "dve" else self.nc.gpsimd
        return self.op(self.E(e), lambda: eng.tensor_tensor(out=out, in0=in0, in1=in1, op=op), r, w)

    def ts(self, e, out, in0, s1, s2, op0, op1, r, w):
        eng = self.nc.vector if e == "dve" else self.nc.gpsimd
        if s2 is None:
            return self.op(self.E(e), lambda: eng.tensor_scalar(out=out, in0=in0, scalar1=s1, scalar2=None, op0=op0), r, w)
        return self.op(self.E(e), lambda: eng.tensor_scalar(out=out, in0=in0, scalar1=s1, scalar2=s2, op0=op0, op1=op1), r, w)

    def stt(self, out, in0, scalar, in1, op0, op1, r, w):
        nc = self.nc
        return self.op(self.dve, lambda: nc.vector.scalar_tensor_tensor(out=out, in0=in0, scalar=scalar, in1=in1, op0=op0, op1=op1), r, w)

    def memset(self, e, ap, val, w):
        eng = self.nc.vector if e == "dve" else self.nc.gpsimd
        return self.op(self.E(e), lambda: eng.memset(ap, val), (), w)

    def recip(self, out, in_, r, w):
        nc = self.nc
        return self.op(self.dve, lambda: nc.vector.reciprocal(out=out, in_=in_), r, w)

    def scan(self, out, d0, d1, init, r, w):
        nc = self.nc
        return self.op(self.dve, lambda: nc.vector.tensor_tensor_scan(out=out, data0=d0, data1=d1, initial=init, op0=ALU.mult, op1=ALU.add), r, w)


class Cfg:
    def __init__(self, TS=4096, NP=4, TP=256, debug=False, stages=None):
        self.TS, self.NP, self.TP = TS, NP, TP
        self.NTOK = TS + NP * TP
        self.debug = debug
        self.stages = stages
        tiles = []
        t = 0
        while t < TS:
            n = min(512, TS - t)
            tiles.append((t, n, 1))
            t += n
        while t < self.NTOK:
            n = min(512, self.NTOK - t)
            tiles.append((t, n, 0))
            t += n
        self.tiles = tiles
        self.seqs = [(0, TS, 1, 0)] + [(TS + p * TP, TP, 0, p) for p in range(NP)]


class Ctx:
    pass


def build(cfg):
    nc = bass.Bass("TRN2", target_bir_lowering=False)
    k = K(nc)
    c = Ctx()
    c.nc, c.k, c.cfg = nc, k, cfg
    NTOK, TS, NP, TP = cfg.NTOK, cfg.TS, cfg.NP, cfg.TP
    skind = "ExternalOutput" if cfg.debug else "Internal"

    def din(name, shape, dt=F32):
        return nc.dram_tensor(name, list(shape), dt, kind="ExternalInput").ap()

    def dout(name, shape):
        return nc.dram_tensor(name, list(shape), F32, kind="ExternalOutput").ap()

    def dscr(name, shape, dt=F32):
        return nc.dram_tensor(name, list(shape), dt, kind=skind).ap()

    I = c.I = {}
    I["xin"] = din("xin", [NTOK, D])
    I["ident"] = din("ident", [128, 128])
    I["ccol"] = din("ccol", [128, 8, 2])
    I["w_mod"] = din("w_mod", [L, D, 9 * D])
    I["bmod"] = din("bmod", [L, 128, 72])
    I["lng"] = din("lng", [L, 3, 128, 8])
    I["lnb"] = din("lnb", [L, 3, 128, 8])
    I["ffn_w_in"] = din("ffn_w_in", [L, 2, D, 2 * DFF])
    I["ffn_w_out"] = din("ffn_w_out", [L, 2, DFF, D])
    I["w_in"] = din("w_in", [L, D, INC])
    I["w_out"] = din("w_out", [L, D, D])
    I["cache_k"] = din("cache_k", [L, 256, 256])
    I["cache_v"] = din("cache_v", [L, 256, 256])
    I["rpbT"] = din("rpbT", [L, 64, 4, 15, 64])
    I["namask"] = din("namask", [64, 64])
    I["tmask"] = din("tmask", [6, 64, 64])
    I["bones"] = din("bones", [128, 128])
    I["hgcol"] = din("hgcol", [128, 2, 3])
    I["hgng"] = din("hgng", [L, 128, 2])
    I["st_hg"] = din("st_hg", [L, 2, 4, 64, 64])
    I["rwmu"] = din("rwmu", [L, 2, 128, 8])
    I["rwmu_ad"] = din("rwmu_ad", [L, 64, 2])
    I["rww0"] = din("rww0", [L, 2, 128, 2])
    I["rw_w_up"] = din("rw_w_up", [L, 2, 64, 256])
    I["rw_a_up"] = din("rw_a_up", [L, 64, 256])
    I["rw_g_up"] = din("rw_g_up", [L, 128, 256])
    I["rwcol"] = din("rwcol", [L, 128, 2, 6])
    I["st_rw"] = din("st_rw", [L, 2, 4, 64, 64])
    I["s5lam"] = din("s5lam", [L, 2, 128, 8, 3])
    I["s5BT"] = din("s5BT", [L, 2, 8, 128, 128])
    I["s5CT"] = din("s5CT", [L, 2, 8, 128, 128])
    I["s5col"] = din("s5col", [L, 128, 2, 2])
    I["s5glu"] = din("s5glu", [L, 256, 256])
    I["st_s5"] = din("st_s5", [L, 2, 128, 8, 2])
    O = c.O = {}
    O["y_s"] = dout("y_s", [TS, D])
    O["y_p"] = dout("y_p", [NP * TP, D])
    O["nk"] = dout("nk", [NP, L, TP, 256])
    O["nv"] = dout("nv", [NP, L, TP, 256])
    O["nhg"] = dout("nhg", [NP, L, 2, 4, 64, 64])
    O["nrw"] = dout("nrw", [NP, L, 2, 4, 64, 64])
    O["ns5"] = dout("ns5", [NP, L, 2, 16, 64, 2])
    S = c.S = {}
    S["XT"] = dscr("XT", [D, NTOK])
    S["X1T"] = dscr("X1T", [D, NTOK])
    S["PT"] = dscr("PT", [INC, NTOK])
    S["OT"] = dscr("OT", [D, NTOK])
    S["VTOK"] = dscr("VTOK", [NTOK, 256])
    for sfx in ("h", "r"):
        S["LA" + sfx] = dscr("LA" + sfx, [8, 256, NTOK])
        S["LX" + sfx] = dscr("LX" + sfx, [2, 256, NTOK])
    for sfx in ("h", "r", "5"):
        S["YS" + sfx] = dscr("YS" + sfx, [2, 256, NTOK])
    S["W1s"] = dscr("W1s", [L, 2, 44, 128, 1024], BF16)
    S["W2s"] = dscr("W2s", [L, 2, 8, 128, NF * 128], BF16)
    S["Wis"] = dscr("Wis", [L, 26, 128, 1024], BF16)
    S["Wkv"] = dscr("Wkv", [L, 128, 8, 512], BF16)
    S["Wos"] = dscr("Wos", [L, 8, 128, 1024], BF16)

    c.bank = [k.ps(f"bank{i}", [128, 512]) for i in range(8)]
    c.bi = 0
    c.freeb = list(range(8))
    c.ident = k.sb("ident_sb", [128, 128])
    k.load(c.ident[:], I["ident"], writes=["ident"])
    c.onesD = k.sb("onesD", [128, 128])
    k.memset("dve", c.onesD[:], 1.0 / D, ["onesD"])
    c.epsln = k.sb("epsln", [128, 1])
    k.memset("dve", c.epsln[:], LN_EPS / (ALPHA * ALPHA), ["epsln"])
    c.modc = k.sb("modc", [128, 72, 2])
    c.osc = k.sb("osc", [128, 3, 8, 2])
    c.gco = k.sb("gco", [128, 3, 8, 2])
    c.lng = k.sb("lng_sb", [128, L, 3, 8])
    c.lnb = k.sb("lnb_sb", [128, L, 3, 8])
    k.load(c.lng[:], I["lng"].rearrange("l i p c -> p l i c"), writes=["lng"])
    k.load(c.lnb[:], I["lnb"].rearrange("l i p c -> p l i c"), writes=["lnb"])
    c.tmask = k.sb("tmask_sb", [64, 6, 64])
    k.load(c.tmask[:], I["tmask"].rearrange("m a b -> a m b"), writes=["tmask"])
    c.bones = k.sb("bones_sb", [128, 128])
    k.load(c.bones[:], I["bones"], writes=["bones"])
    c.bones64 = k.sb("bones64_sb", [128, 128])
    k.ts("dve", c.bones64[:], c.bones[:], 1.0 / 64, None, ALU.mult, None, ["bones"], ["bones64"])

    st = cfg.stages
    if st is None or "cast" in st:
        phase_cast(c)
    if st is None or "t0" in st:
        phase_transpose_in(c)
    for l in range(L):
        if st is None or "mod" in st:
            phase_mod(c, l)
        if st is None or "A" in st:
            phase_A(c, l)
        if st is None or "mix" in st:
            phase_hg(c, l, "prep")
            phase_rw(c, l, "prep")
            es1 = ExitStack()
            g1 = phase_rw(c, l, "scan", es1, NL=2)
            g2 = phase_s5(c, l, "scan", es1, NW=1)
            run_concurrent([g1, g2])
            k.barrier()
            es1.close()
            es2 = ExitStack()
            g3 = phase_hg(c, l, "scan", es2, NL=4)
            g4 = phase_attn(c, l, "scan", es2)
            run_concurrent([g3, g4])
            k.barrier()
            es2.close()
            phase_rw(c, l, "out")
            phase_hg(c, l, "out")
            phase_s5(c, l, "out")
        if st is not None and "attn" in st:
            phase_attn(c, l)
        if st is not None and "hg" in st:
            phase_hg(c, l)
        if st is not None and "rw" in st:
            phase_rw(c, l)
        if st is not None and "s5" in st:
            phase_s5(c, l)
        if st is not None and "A_only" in st:
            break
        if st is None or "C" in st:
            phase_C(c, l)
        if st is not None and "L0_only" in st:
            break
    assert sorted(c.freeb) == list(range(8)), c.freeb
    if st is None or "tout" in st:
        phase_transpose_out(c)
    k.barrier()
    c.k.es_keep = k.es
    return nc, c


def nextbank(c):
    i = c.freeb.pop(0)
    c.freeb.append(i)
    return c.bank[i], ("bank", i)


def acquire(c):
    while not c.freeb:
        yield
    i = c.freeb.pop(0)
    return c.bank[i], ("bank", i)


def release(c, pk):
    c.freeb.append(pk[1])


def phase_cast(c):
    k, nc, I, S = c.k, c.nc, c.I, c.S
    es = ExitStack()
    NB = 2
    f32t = [k.sb(f"cast_f{i}", [128, 4 * 1024], F32, es) for i in range(NB)]
    b16t = [k.sb(f"cast_b{i}", [128, 4 * 1024], BF16, es) for i in range(NB)]
    cnt = [0]
    engs = ["dve", "act", "pool"]

    def job(pairs, per):
        i = cnt[0] % NB
        e = engs[cnt[0] % 3]
        cnt[0] += 1
        n = per * len(pairs)
        for j, (src_ap, dst_ap) in enumerate(pairs):
            a_, b_ = src_ap.shape[1], src_ap.shape[2]
            k.load(f32t[i][:, j * per:(j + 1) * per].rearrange("p (a b) -> p a b", a=a_), src_ap, writes=[("cf", i, j)])
        k.cp(e, b16t[i][:, 0:n], f32t[i][:, 0:n], [("cf", i, j) for j in range(len(pairs))], [("cb", i)])
        for j, (src_ap, dst_ap) in enumerate(pairs):
            k.store(dst_ap, b16t[i][:, j * per:(j + 1) * per], reads=[("cb", i)])

    for l in range(L):
        for f in range(2):
            src = I["ffn_w_in"][l, f].rearrange("(kc p) n -> p kc n", p=128)
            for g0 in range(0, 44, 4):
                job([(src[:, :, g * 128:(g + 1) * 128], S["W1s"][l, f, g]) for g in range(g0, g0 + 4)], 1024)
            src = I["ffn_w_out"][l, f].rearrange("(fc p) n -> p fc n", p=128)
            for g in range(8):
                job([(src[:, :, g * 128:(g + 1) * 128], S["W2s"][l, f, g])], NF * 128)
        src = I["w_in"][l].rearrange("(kc p) n -> p kc n", p=128)
        for g0 in range(0, 26, 2):
            job([(src[:, :, g * 128:(g + 1) * 128], S["Wis"][l, g]) for g in range(g0, g0 + 2)], 1024)
        job([(src[:, :, 2816:3328], S["Wkv"][l].rearrange("p kc n -> p (kc n)"))], 4096)
        src = I["w_out"][l].rearrange("(kc p) n -> p kc n", p=128)
        for g0 in range(0, 8, 4):
            job([(src[:, :, g * 128:(g + 1) * 128], S["Wos"][l, g]) for g in range(g0, g0 + 4)], 1024)
    k.barrier()
    es.close()


def phase_transpose_in(c):
    k, nc, cfg = c.k, c.nc, c.cfg
    es = ExitStack()
    XTv = c.S["XT"].rearrange("(c p) t -> p c t", p=128)
    NB = 2
    xin_t = [k.sb(f"ti_x{i}", [128, 4, D], F32, es) for i in range(NB)]
    xT_t = [k.sb(f"ti_xT{i}", [128, 8, 512], F32, es) for i in range(NB)]
    for ti, (t0, n, var) in enumerate(cfg.tiles):
        b = ti % NB
        ns = n // 128
        k.load(xin_t[b][:, 0:ns, :], c.I["xin"][t0:t0 + n, :].rearrange("(s p) d -> p s d", p=128), writes=[("tix", b)])
        for ch in range(8):
            p, pk = nextbank(c)
            for s in range(ns):
                k.tr(p[:, s * 128:(s + 1) * 128], xin_t[b][:, s, ch * 128:(ch + 1) * 128], c.ident[:], [("tix", b), "ident"], [pk])
            k.cp(k.any2(), xT_t[b][:, ch, 0:n], p[:, 0:n], [pk], [("tixT", b, ch)])
        k.store(XTv[:, :, t0:t0 + n], xT_t[b][:, :, 0:n], reads=[("tixT", b, ch) for ch in range(8)], writes=[("XT", ti)])
    k.barrier()
    es.close()


def phase_transpose_out(c):
    k, nc, cfg = c.k, c.nc, c.cfg
    es = ExitStack()
    XTv = c.S["XT"].rearrange("(c p) t -> p c t", p=128)
    NB = 2
    xT_t = [k.sb(f"to_xT{i}", [128, 8, 512], F32, es) for i in range(NB)]
    yt = [k.sb(f"to_y{i}", [128, 4, D], F32, es) for i in range(NB)]
    for ti, (t0, n, var) in enumerate(cfg.tiles):
        b = ti % NB
        ns = n // 128
        k.load(xT_t[b][:, :, 0:n], XTv[:, :, t0:t0 + n], reads=[("XT", ti)], writes=[("toxT", b)])
        for s in range(ns):
            for h in range(2):
                p, pk = nextbank(c)
                for cc in range(4):
                    ch = h * 4 + cc
                    k.tr(p[:, cc * 128:(cc + 1) * 128], xT_t[b][:, ch, s * 128:(s + 1) * 128], c.ident[:], [("toxT", b), "ident"], [pk])
                k.cp(k.any2(), yt[b][:, s, h * 512:(h + 1) * 512], p[:], [pk], [("toy", b, s, h)])
        if var == 1:
            dst = c.O["y_s"][t0:t0 + n, :]
        else:
            dst = c.O["y_p"][t0 - cfg.TS:t0 - cfg.TS + n, :]
        k.store(dst.rearrange("(s p) d -> p s d", p=128), yt[b][:, 0:ns, :],
                reads=[("toy", b, s, h) for s in range(ns) for h in range(2)])
    k.barrier()
    es.close()


def phase_mod(c, l):
    k, nc, I = c.k, c.nc, c.I
    es = ExitStack()
    ccol = k.sb("mod_c", [128, 8, 2], F32, es)
    csil = k.sb("mod_cs", [128, 8, 2], F32, es)
    bm = k.sb("mod_bm", [128, 72], F32, es)
    k.load(ccol[:], I["ccol"], writes=["ccol"])
    k.load(bm[:], I["bmod"][l], writes=["bm"])
    k.actf(csil[:], ccol[:], AF.Silu, ["ccol"], ["csil"])
    FB = 1152
    NB = 2
    wt = [k.sb(f"mod_w{i}", [128, 8, FB], F32, es) for i in range(NB)]
    p, pk = nextbank(c)
    wv = I["w_mod"][l].rearrange("(kc p) f -> p kc f", p=128)
    for bi in range(9 * D // FB):
        b = bi % NB
        k.load(wt[b][:], wv[:, :, bi * FB:(bi + 1) * FB], writes=[("modw", b)])
        for fj in range(FB // 128):
            f = bi * (FB // 128) + fj
            for kc in range(8):
                k.mm(p[:, 2 * f:2 * f + 2], wt[b][:, kc, fj * 128:(fj + 1) * 128], csil[:, kc, :], kc == 0, kc == 7,
                     [("modw", b), "csil"], [pk])
    k.tt("dve", c.modc[:], p[:, 0:144].rearrange("p (f n) -> p f n", n=2), bm[:].unsqueeze(2).to_broadcast([128, 72, 2]), ALU.add,
         [pk, "bm"], ["modc"])
    for i in range(3):
        k.ts("dve", c.osc[:, i], c.modc[:, (3 * i + 1) * 8:(3 * i + 2) * 8, :], 1.0, None, ALU.add, None, ["modc"], ["osc"])
        coef = (0.5 if i != 1 else 1.0) / ALPHA
        k.ts("dve", c.gco[:, i], c.modc[:, (3 * i + 2) * 8:(3 * i + 3) * 8, :], coef, None, ALU.mult, None, ["modc"], ["gco"])
    k.barrier()
    es.close()


def layernorm(c, z, zk, n, gcol, bcol, xout, xoutk, tmp, li):
    k, nc = c.k, c.nc
    sq, msq, rstd = tmp["sq"], tmp["msq"], tmp["rstd"]
    pm, pmk = nextbank(c)
    pe2, pe2k = nextbank(c)
    for ch in range(8):
        k.actf(sq[:, ch, 0:n], z[:, ch, 0:n], AF.Square, [(zk, ch)], [("lnsq", ch)])
    for ch in range(8):
        k.mm(pm[:, 0:n], c.onesD[:], z[:, ch, 0:n], ch == 0, ch == 7, [(zk, ch), "onesD"], [pmk])
    for ch in range(8):
        k.mm(pe2[:, 0:n], c.onesD[:], sq[:, ch, 0:n], ch == 0, ch == 7, [("lnsq", ch), "onesD"], [pe2k])
    k.actf(msq[:, 0:n], pm[:, 0:n], AF.Square, [pmk], ["lnmsq"])
    k.tt("dve", msq[:, 0:n], pe2[:, 0:n], msq[:, 0:n], ALU.subtract, [pe2k, "lnmsq"], ["lnmsq"])
    k.actf(rstd[:, 0:n], msq[:, 0:n], AF.Sqrt, ["lnmsq", "epsln"], ["lnrstd"], bias=c.epsln[:, 0:1])
    k.recip(rstd[:, 0:n], rstd[:, 0:n], ["lnrstd"], ["lnrstd"])
    for ch in range(8):
        k.tt("dve", sq[:, ch, 0:n], z[:, ch, 0:n], pm[:, 0:n], ALU.subtract, [(zk, ch), pmk, ("lnsq", ch)], [("lnsq", ch)])
        e = "pool" if ch % 2 else "dve"
        k.tt(e, sq[:, ch, 0:n], sq[:, ch, 0:n], rstd[:, 0:n], ALU.mult, [("lnsq", ch), "lnrstd"], [("lnsq", ch)])
        k.actf(xout[:, ch, 0:n], sq[:, ch, 0:n], AF.Identity, [("lnsq", ch), "lng", "lnb"], [(xoutk, ch)],
               scale=gcol[:, ch:ch + 1], bias=bcol[:, ch:ch + 1])


def modulate(c, x, xk, n, i, var, xm, xmk):
    k = c.k
    for ch in range(8):
        k.actf(xm[:, ch, 0:n], x[:, ch, 0:n], AF.Identity, [(xk, ch), "osc", "modc"], [(xmk, ch)],
               scale=c.osc[:, i, ch, var:var + 1], bias=c.modc[:, (3 * i) * 8 + ch, var:var + 1])


def ffn(c, l, f, xm, xmk, n, h, zres, zresk, i, var, wb):
    k, nc, S = c.k, c.nc, c.S
    w1, w2, sg = wb["w1"], wb["w2"], wb["sg"]
    NW1 = len(w1)
    order = []
    for fc in range(NF):
        order.append(fc)
        order.append(NF + fc)

    def ldw1(j):
        b = j % NW1
        k.load(w1[b][:], S["W1s"][l, f, order[j]], writes=[("w1", b)])
    PF = NW1 - 1
    for j in range(min(PF, len(order))):
        ldw1(j)
    for fc in range(NF):
        banks = []
        for half in range(2):
            j = 2 * fc + half
            if j + PF < len(order):
                ldw1(j + PF)
            b = j % NW1
            p, pk = nextbank(c)
            banks.append((p, pk))
            for kc in range(8):
                k.mm(p[:, 0:n], w1[b][:, kc * 128:(kc + 1) * 128], xm[:, kc, 0:n], kc == 0, kc == 7, [("w1", b), (xmk, kc)], [pk])
        (pg, pgk), (pu, puk) = banks
        sb_ = fc % 2
        k.actf(sg[sb_][:, 0:n], pg[:, 0:n], AF.Silu, [pgk], [("sg", sb_)])
        k.tt("dve", h[:, fc, 0:n], sg[sb_][:, 0:n], pu[:, 0:n], ALU.mult, [("sg", sb_), puk], [("h", fc)])
    NW2 = len(w2)
    for dc in range(min(NW2 - 1, 8)):
        k.load(w2[dc % NW2][:], S["W2s"][l, f, dc], writes=[("w2", dc % NW2)])
    for dc in range(8):
        if dc + NW2 - 1 < 8:
            d2 = dc + NW2 - 1
            k.load(w2[d2 % NW2][:], S["W2s"][l, f, d2], writes=[("w2", d2 % NW2)])
        b = dc % NW2
        p, pk = nextbank(c)
        for fc in range(NF):
            k.mm(p[:, 0:n], w2[b][:, fc * 128:(fc + 1) * 128], h[:, fc, 0:n], fc == 0, fc == NF - 1, [("w2", b), ("h", fc)], [pk])
        k.stt(zres[:, dc, 0:n], p[:, 0:n], c.gco[:, i, dc, var:var + 1], zres[:, dc, 0:n], ALU.mult, ALU.add,
              [pk, "gco", (zresk, dc)], [(zresk, dc)])


def alloc_AC(c, es):
    k = c.k
    t = {}
    t["x"] = [k.sb(f"ac_x{i}", [128, 8, 512], F32, es) for i in range(2)]
    t["x1"] = k.sb("ac_x1", [128, 8, 512], F32, es)
    t["xm"] = k.sb("ac_xm", [128, 8, 512], BF16, es)
    t["h"] = k.sb("ac_h", [128, NF, 512], BF16, es)
    t["ln"] = {"sq": k.sb("ac_sq", [128, 8, 512], F32, es), "msq": k.sb("ac_msq", [128, 512], F32, es),
               "rstd": k.sb("ac_rstd", [128, 512], F32, es)}
    t["wb"] = {"w1": [k.sb(f"ac_w1_{i}", [128, 1024], BF16, es) for i in range(4)],
               "w2": [k.sb(f"ac_w2_{i}", [128, NF * 128], BF16, es) for i in range(2)],
               "sg": [k.sb(f"ac_sg{i}", [128, 512], F32, es) for i in range(2)]}
    t["wi"] = [k.sb(f"ac_wi{i}", [128, 1024], BF16, es) for i in range(4)]
    t["wkv"] = k.sb("ac_wkv", [128, 8, 512], BF16, es)
    t["pb"] = [k.sb(f"ac_pb{i}", [128, 2, 512], F32, es) for i in range(2)]
    t["tok"] = [k.sb(f"ac_tok{i}", [128, 512], F32, es) for i in range(2)]
    return t


def phase_A(c, l):
    k, nc, cfg, S, O = c.k, c.nc, c.cfg, c.S, c.O
    es = ExitStack()
    t = alloc_AC(c, es)
    XTv = S["XT"].rearrange("(c p) t -> p c t", p=128)
    X1Tv = S["X1T"].rearrange("(c p) t -> p c t", p=128)
    PTv = S["PT"].rearrange("(c p) t -> p c t", p=128)
    k.load(t["wkv"][:], S["Wkv"][l], writes=["wkv"])
    tiles = cfg.tiles

    def ldx(ti):
        t0, n, var = tiles[ti]
        b = ti % 2
        k.load(t["x"][b][:, :, 0:n], XTv[:, :, t0:t0 + n], reads=[("XT", ti)], writes=[(("x", b), ch) for ch in range(8)])
    ldx(0)
    for ti, (t0, n, var) in enumerate(tiles):
        b = ti % 2
        x, xk = t["x"][b], ("x", b)
        if ti + 1 < len(tiles):
            ldx(ti + 1)
        modulate(c, x, xk, n, 0, var, t["xm"], "xm")
        ffn(c, l, 0, t["xm"], "xm", n, t["h"], x, xk, 0, var, t["wb"])
        layernorm(c, x, xk, n, c.lng[:, l, 0, :], c.lnb[:, l, 0, :], t["x1"], "x1", t["ln"], 0)
        k.store(X1Tv[:, :, t0:t0 + n], t["x1"][:, :, 0:n], reads=[("x1", ch) for ch in range(8)], writes=[("X1T", ti)])
        modulate(c, t["x1"], "x1", n, 1, var, t["xm"], "xm")
        wi = t["wi"]
        NWI = len(wi)
        for j in range(NWI - 1):
            k.load(wi[j][:], S["Wis"][l, j], writes=[("wi", j)])
        for cc in range(26):
            if cc + NWI - 1 < 26:
                j = cc + NWI - 1
                k.load(wi[j % NWI][:], S["Wis"][l, j], writes=[("wi", j % NWI)])
            b2 = cc % NWI
            p, pk = nextbank(c)
            for kc in range(8):
                k.mm(p[:, 0:n], wi[b2][:, kc * 128:(kc + 1) * 128], t["xm"][:, kc, 0:n], kc == 0, kc == 7, [("wi", b2), ("xm", kc)], [pk])
            pbi = (cc // 2) % 2
            k.cp(k.any2(), t["pb"][pbi][:, cc % 2, 0:n], p[:, 0:n], [pk], [("pb", pbi, cc % 2)])
            if cc % 2 == 1:
                k.store(PTv[:, cc - 1:cc + 1, t0:t0 + n], t["pb"][pbi][:, :, 0:n], reads=[("pb", pbi, 0), ("pb", pbi, 1)], writes=[("PT", ti)])
        for s in range(n // 128):
            p, pk = nextbank(c)
            for kc in range(8):
                k.mm(p[:, :], t["xm"][:, kc, s * 128:(s + 1) * 128], t["wkv"][:, kc, :], kc == 0, kc == 7, [("xm", kc), "wkv"], [pk])
            tb = s % 2
            k.cp(k.any2(), t["tok"][tb][:], p[:], [pk], [("tok", tb)])
            ta = t0 + s * 128
            k.store(S["VTOK"][ta:ta + 128, :], t["tok"][tb][:, 256:512], reads=[("tok", tb)], writes=[("VTOK", ti)])
            if var == 0:
                q = ta - cfg.TS
                pi_, tt_ = q // cfg.TP, q % cfg.TP
                k.store(O["nk"][pi_, l, tt_:tt_ + 128, :], t["tok"][tb][:, 0:256], reads=[("tok", tb)])
                k.store(O["nv"][pi_, l, tt_:tt_ + 128, :], t["tok"][tb][:, 256:512], reads=[("tok", tb)])
    k.barrier()
    es.close()


def phase_C(c, l):
    k, nc, cfg, S = c.k, c.nc, c.cfg, c.S
    es = ExitStack()
    t = alloc_AC(c, es)
    XTv = S["XT"].rearrange("(c p) t -> p c t", p=128)
    X1Tv = S["X1T"].rearrange("(c p) t -> p c t", p=128)
    OTv = S["OT"].rearrange("(c p) t -> p c t", p=128)
    tiles = cfg.tiles
    ot = t["ln"]["sq"]
    wo = t["wi"]
    for ti, (t0, n, var) in enumerate(tiles):
        x1 = t["x1"]
        k.load(x1[:, :, 0:n], X1Tv[:, :, t0:t0 + n], reads=[("X1T", ti)], writes=[("x1", ch) for ch in range(8)])
        k.load(ot[:, :, 0:n], OTv[:, :, t0:t0 + n], reads=[("OT", ti)], writes=[("lnsq", ch) for ch in range(8)])
        for ch in range(8):
            k.cp(k.any2(), t["xm"][:, ch, 0:n], ot[:, ch, 0:n], [("lnsq", ch)], [("xm", ch)])
        NWO = len(wo)
        for j in range(NWO - 1):
            k.load(wo[j][:], S["Wos"][l, j], writes=[("wi", j)])
        for dc in range(8):
            if dc + NWO - 1 < 8:
                j = dc + NWO - 1
                k.load(wo[j % NWO][:], S["Wos"][l, j], writes=[("wi", j % NWO)])
            b2 = dc % NWO
            p, pk = nextbank(c)
            for kc in range(8):
                k.mm(p[:, 0:n], wo[b2][:, kc * 128:(kc + 1) * 128], t["xm"][:, kc, 0:n], kc == 0, kc == 7, [("wi", b2), ("xm", kc)], [pk])
            k.stt(x1[:, dc, 0:n], p[:, 0:n], c.gco[:, 1, dc, var:var + 1], x1[:, dc, 0:n], ALU.mult, ALU.add,
                  [pk, "gco", ("x1", dc)], [("x1", dc)])
        x2 = t["x"][0]
        layernorm(c, x1, "x1", n, c.lng[:, l, 1, :], c.lnb[:, l, 1, :], x2, ("x", 0), t["ln"], 1)
        modulate(c, x2, ("x", 0), n, 2, var, t["xm"], "xm")
        ffn(c, l, 1, t["xm"], "xm", n, t["h"], x2, ("x", 0), 2, var, t["wb"])
        x3 = t["x"][1]
        layernorm(c, x2, ("x", 0), n, c.lng[:, l, 2, :], c.lnb[:, l, 2, :], x3, ("x", 1), t["ln"], 2)
        k.store(XTv[:, :, t0:t0 + n], x3[:, :, 0:n], reads=[(("x", 1), ch) for ch in range(8)], writes=[("XT", ti)])
    k.barrier()
    es.close()


def attn_core_gen(c, q_ap, nq, A, bias_ap, B, out_ap, rkeys, okey, T, slot):
    k = c.k
    ev = []
    na, nb = len(A), len(B)
    if A:
        pa, pak = yield from acquire(c)
        for i, (kt, v) in enumerate(A):
            k.mm(pa[0:64, i * nq:(i + 1) * nq], kt, q_ap, True, True, rkeys, [pak])
    if B:
        pb, pbk = yield from acquire(c)
        for i, (kt, v) in enumerate(B):
            k.mm(pb[0:64, i * nq:(i + 1) * nq], kt, q_ap, True, True, rkeys, [pbk])
    yield
    if A:
        ea, eak = T["EA"][slot], ("EA", slot)
        eab, eabk = T["EAb"][slot], ("EAb", slot)
        k.stt(ea[:, 0:na, 0:nq], pa[0:64, 0:na * nq].rearrange("p (a q) -> p a q", q=nq), 0.125, bias_ap, ALU.mult, ALU.add,
              [pak, "at_B"], [eak])
        release(c, pak)
        k.actf(eab[:, 0:na, 0:nq], ea[:, 0:na, 0:nq], AF.Exp, [eak], [eabk])
        for i, (kt, v) in enumerate(A):
            ev.append((eab[:, i, 0:nq], v, eabk))
    if B:
        eb, ebk = T["EB"][slot], ("EB", slot)
        k.actf(eb[:, 0:nb * nq], pb[0:64, 0:nb * nq], AF.Exp, [pbk], [ebk], scale=0.125)
        release(c, pbk)
        for i, (kt, v) in enumerate(B):
            ev.append((eb[:, i * nq:(i + 1) * nq], v, ebk))
    yield
    pn, pnk = yield from acquire(c)
    n = len(ev)
    for i, (e, v, ek) in enumerate(ev):
        k.mm(pn[0:nq, 0:65], e, v, i == 0, i == n - 1, rkeys + [ek], [pnk])
    yield
    rd, rdk = T["rden"][slot], ("rden", slot)
    otk, otkk = T["otok"][slot], ("otok", slot)
    k.recip(rd[0:nq, 0:1], pn[0:nq, 64:65], [pnk], [rdk])
    k.ts("dve", otk[0:nq, :], pn[0:nq, 0:64], rd[0:nq, 0:1], None, ALU.mult, None, [pnk, rdk], [otkk])
    yield
    k.tr(pn[0:64, 128:128 + nq], otk[0:nq, :], c.ident[0:nq, 0:nq], [otkk, "ident"], [pnk])
    yield
    k.cp("act", out_ap, pn[0:64, 128:128 + nq], [pnk], [okey])
    release(c, pnk)
    yield


def phase_attn(c, l, part=None, es_ext=None):
    k, nc, cfg, S, I = c.k, c.nc, c.cfg, c.S, c.I
    es = ExitStack() if es_ext is None else es_ext
    TS, TP = cfg.TS, cfg.TP
    R = TS // 64
    Tmax = max(TS, TP)
    NS = 3
    stg = k.sb("at_stg", [64, Tmax], F32, es)
    qT = k.sb("at_q", [64, Tmax], BF16, es)
    kT = k.sb("at_k", [64, Tmax], BF16, es)
    vt = k.sb("at_v", [64, Tmax // 64, 65], BF16, es)
    ot = k.sb("at_o", [64, Tmax], F32, es)
    T = {}
    T["EA"] = [k.sb(f"at_ea{i}", [64, 8, 64], F32, es) for i in range(NS)]
    T["EAb"] = [k.sb(f"at_eab{i}", [64, 8, 64], BF16, es) for i in range(NS)]
    T["EB"] = [k.sb(f"at_eb{i}", [64, 512], BF16, es) for i in range(NS)]
    T["rden"] = [k.sb(f"at_rd{i}", [128, 1], F32, es) for i in range(NS)]
    T["otok"] = [k.sb(f"at_otok{i}", [128, 64], F32, es) for i in range(NS)]
    Bt = k.sb("at_B", [64, 15, 64], F32, es)
    mask = k.sb("at_mask", [64, 64], F32, es)
    kctok = k.sb("at_kctok", [64, 4, 64], F32, es)
    kcT = k.sb("at_kcT", [64, 4, 64], BF16, es)
    vcs = k.sb("at_vcs", [64, 4, 64], F32, es)
    vc = k.sb("at_vc", [64, 4, 65], BF16, es)
    k.load(mask[:], I["namask"], writes=["at_mask"])
    k.memset("dve", vt[:, :, 64:65], 1.0, ["at_v1"])
    k.memset("dve", vc[:, :, 64:65], 1.0, ["at_vc1"])
    rk = ["at_q", "at_k", "at_v", "at_kcT", "at_vc", "at_v1", "at_vc1"]
    slots = list(range(NS))

    def gen():
        for h in range(4):
            hs = slice(64 * h, 64 * h + 64)

            def load_qkv(s0, Tn):
                k.load(stg[:, 0:Tn], S["PT"][2560 + 64 * h:2560 + 64 * h + 64, s0:s0 + Tn], writes=["at_stg"])
                k.cp("act", qT[:, 0:Tn], stg[:, 0:Tn], ["at_stg"], ["at_q"])
                k.load(stg[:, 0:Tn], S["PT"][2816 + 64 * h:2816 + 64 * h + 64, s0:s0 + Tn], writes=["at_stg"])
                k.cp("dve", kT[:, 0:Tn], stg[:, 0:Tn], ["at_stg"], ["at_k"])
                nr = Tn // 64
                k.load(stg[:, 0:Tn].rearrange("p (r d) -> p r d", d=64), S["VTOK"][s0:s0 + Tn, hs].rearrange("(r c) d -> c r d", c=64),
                       writes=["at_stg"])
                k.cp("pool", vt[:, 0:nr, 0:64], stg[:, 0:Tn].rearrange("p (r d) -> p r d", d=64), ["at_stg"], ["at_v"])
            load_qkv(0, TS)
            k.load(Bt[:], I["rpbT"][l, :, h], writes=["at_B"])
            k.tt("dve", Bt[:], Bt[:], mask[:].unsqueeze(1).to_broadcast([64, 15, 64]), ALU.add, ["at_B", "at_mask"], ["at_B"])
            k.load(kctok[:], I["cache_k"][l][:, hs].rearrange("(ch c) d -> c ch d", c=64), writes=["at_kctok"])
            k.load(vcs[:], I["cache_v"][l][:, hs].rearrange("(ch c) d -> c ch d", c=64), writes=["at_vcs"])
            k.cp("dve", vc[:, :, 0:64], vcs[:], ["at_vcs"], ["at_vc"])
            p, pk = yield from acquire(c)
            for ch in range(4):
                k.tr(p[0:64, ch * 64:(ch + 1) * 64], kctok[:, ch, :], c.ident[0:64, 0:64], ["at_kctok", "ident"], [pk])
            k.cp("dve", kcT[:].rearrange("p a b -> p (a b)"), p[0:64, 0:256], [pk], ["at_kcT"])
            release(c, pk)
            yield
            jobs = []
            kr = min(8, R)
            for r in range(R):
                rs = min(max(r - kr // 2, 0), R - kr)
                dr0 = rs - r + 7
                A = [(kT[:, (rs + i) * 64:(rs + i + 1) * 64], vt[:, rs + i, :]) for i in range(kr)]
                B = [(kcT[:, j, :], vc[:, j, :]) for j in range(4)]
                jobs.append(lambda slot, r=r, A=A, B=B, dr0=dr0: attn_core_gen(
                    c, qT[:, r * 64:(r + 1) * 64], 64, A, Bt[:, dr0:dr0 + kr, :], B, ot[:, r * 64:(r + 1) * 64], rk, ("at_o", r), T, slot))
            yield from drive_gen(jobs, slots)
            k.store(S["OT"][768 + 64 * h:768 + 64 * h + 64, 0:TS], ot[:, 0:TS], reads=[("at_o", r) for r in range(R)])
            yield
            for (s0, Tn, kind, pi_) in cfg.seqs[1:]:
                load_qkv(s0, Tn)
                yield
                jobs = []
                B = [(kT[:, j * 64:(j + 1) * 64], vt[:, j, :]) for j in range(Tn // 64)]
                for qb in range(Tn // 128):
                    jobs.append(lambda slot, qb=qb, B=B: attn_core_gen(
                        c, qT[:, qb * 128:(qb + 1) * 128], 128, [], None, B, ot[:, qb * 128:(qb + 1) * 128], rk, ("at_o", qb), T, slot))
                yield from drive_gen(jobs, slots)
                k.store(S["OT"][768 + 64 * h:768 + 64 * h + 64, s0:s0 + Tn], ot[:, 0:Tn], reads=[("at_o", qb) for qb in range(Tn // 128)])
                yield

    g = gen()
    if part == "scan":
        return g
    run_concurrent([g])
    k.barrier()
    es.close()


def la_alloc(c, es, delta, CH, nch, tag):
    k = c.k
    TT = nch * CH
    t = {"TT": TT, "nch": nch, "C": CH, "tag": tag}

    def fm(name):
        return k.sb(f"la{tag}_{name}", [64, TT], F32, es)

    def tk(name):
        return k.sb(f"la{tag}_{name}", [64, nch, 64], F32, es)
    t["d"] = []
    for d in range(2):
        u = {}
        for nm in ["K", "LW", "cum", "cumc", "Eabs", "Erel", "Em", "Qabs", "Qrel", "Kd", "Ke", "ybuf", "R", "V"]:
            u[nm] = fm(f"{nm}{d}")
        u["Gm"] = k.sb(f"la{tag}_Gm{d}", [64, nch], F32, es)
        u["KeT"], u["RKT"], u["Vm"] = tk(f"KeT{d}"), tk(f"RKT{d}"), tk(f"Vm{d}")
        u["S"] = [k.sb(f"la{tag}_S{d}_{i}", [64, 64], F32, es) for i in range(2)]
        u["si"] = 0
        if delta:
            for nm in ["KK", "BK", "cp", "E0", "KKabs", "KKrel", "Bd", "Be"]:
                u[nm] = fm(f"{nm}{d}")
            for nm in ["BeT", "RBT", "AkT", "M", "P", "IP", "Q", "M2", "P2"]:
                u[nm] = tk(f"{nm}{d}")
            u["rhs0"] = k.sb(f"la{tag}_rhs0{d}", [64, 64], F32, es)
            u["U"] = k.sb(f"la{tag}_U{d}", [64, 64], F32, es)
        t["d"].append(u)
    t["stmp"] = k.sb(f"la{tag}_stmp", [64, 64], F32, es)
    return t


def la_consts(c, es, CH, nch):
    k = c.k
    TT = CH * nch
    cst = {}
    cst["rmask"] = k.sb("la_rmask", [64, TT], F32, es)
    cst["rmaskb"] = k.sb("la_rmaskb", [64, TT], F32, es)
    k.memset("dve", cst["rmask"][:], 1.0, ["rmask"])
    k.memset("dve", cst["rmask"][:].rearrange("p (a b) -> p a b", b=CH)[:, :, 0:1], 0.0, ["rmask"])
    k.memset("dve", cst["rmaskb"][:], 1.0, ["rmask"])
    k.memset("dve", cst["rmaskb"][:].rearrange("p (a b) -> p a b", b=CH)[:, :, CH - 1:CH], 0.0, ["rmask"])
    return cst


def la_lane(c, t, cst, delta, arrs, YS, seq, st_in, st_out, transpose_state):
    k = c.k
    s0, T = seq
    CH = t["C"]
    MID = CH // 2
    TT = t["TT"]
    nch = t["nch"]
    assert T % TT == 0
    ntile = T // TT
    tm = c.tmask
    I64 = c.ident[0:64, 0:64]
    tag = t["tag"]
    U_ = t["d"]
    DK = [(tag, 0), (tag, 1)]

    def v3(ap):
        return ap[:, 0:TT].rearrange("p (a b) -> p a b", b=CH)

    def bc(ap2):
        return ap2[:, 0:nch].unsqueeze(2).to_broadcast([64, nch, CH])

    def mbc(mi):
        return tm[0:CH, mi, 0:CH].unsqueeze(1).to_broadcast([CH, nch, CH])

    def rvf(d):
        return (lambda ap: ap[:, 0:TT]) if d == 0 else (lambda ap: ap[:, 0:TT][:, ::-1])
    ibc = c.ident[0:64, 0:64].unsqueeze(1).to_broadcast([64, nch, 64])
    stk = ("stmp", tag)
    for d in range(2):
        u, dk = U_[d], DK[d]
        u["si"] = 0
        S0 = u["S"][0]
        if st_in is None:
            k.memset("dve", S0[:], 0.0, [("S", dk, 0)])
        elif transpose_state:
            k.load(t["stmp"][:], st_in[d], writes=[stk])
            p, pk = yield from acquire(c)
            k.tr(p[0:64, 0:64], t["stmp"][:], I64, [stk, "ident"], [pk])
            k.cp("dve", S0[:], p[0:64, 0:64], [pk], [("S", dk, 0)])
            release(c, pk)
        else:
            k.load(S0[:], st_in[d], writes=[("S", dk, 0)])
    yield
    for it in range(ntile):
        tis = [it, ntile - 1 - it]
        for d in range(2):
            u, dk = U_[d], DK[d]
            a0 = s0 + tis[d] * TT
            sl = slice(a0, a0 + TT)
            k.load(u["R"][:, 0:TT], arrs["R"][:, sl], writes=[("R", dk)])
            k.load(u["V"][:, 0:TT], arrs["V"][:, sl], writes=[("V", dk)])
            k.load(u["K"][:, 0:TT], arrs[f"K{d}"][:, sl], writes=[("K", dk)])
            k.load(u["LW"][:, 0:TT], arrs[f"LW{d}"][:, sl], writes=[("LW", dk)])
            if delta:
                k.load(u["KK"][:, 0:TT], arrs["KK"][:, sl], writes=[("KK", dk)])
                k.load(u["BK"][:, 0:TT], arrs["BK"][:, sl], writes=[("BK", dk)])
        yield
        pv = []
        for d in range(2):
            u, dk = U_[d], DK[d]
            p, pk = yield from acquire(c)
            pv.append((p, pk))
            for ch in range(nch):
                k.tr(p[0:CH, ch * 64:(ch + 1) * 64], u["V"][:, ch * CH:(ch + 1) * CH], I64, [("V", dk), "ident"], [pk])
        yield
        for d in range(2):
            u, dk = U_[d], DK[d]
            rv = rvf(d)
            p, pk = pv[d]
            k.cp("act", u["Vm"][0:CH, 0:nch, :].rearrange("p a b -> p (a b)"), p[0:CH, 0:nch * 64], [pk], [("Vm", dk)])
            release(c, pk)
            k.scan(rv(u["cum"]), rv(cst["rmask"] if d == 0 else cst["rmaskb"]), rv(u["LW"]), 0.0, [("LW", dk), "rmask"], [("cum", dk)])
            k.actf(u["Eabs"][:, 0:TT], u["cum"][:, 0:TT], AF.Exp, [("cum", dk)], [("Eabs", dk)])
            k.tt("dve", v3(u["cumc"]), v3(u["cum"]), v3(u["cum"])[:, :, MID:MID + 1].to_broadcast([64, nch, CH]), ALU.subtract,
                 [("cum", dk)], [("cumc", dk)])
            k.actf(u["Erel"][:, 0:TT], u["cumc"][:, 0:TT], AF.Exp, [("cumc", dk)], [("Erel", dk)])
            k.actf(u["Em"][:, 0:TT], u["cumc"][:, 0:TT], AF.Exp, [("cumc", dk)], [("Em", dk)], scale=-1.0)
            last = CH - 1 if d == 0 else 0
            k.actf(u["Gm"][:, 0:nch], v3(u["cumc"])[:, :, last], AF.Exp, [("cumc", dk)], [("Gm", dk)])
        yield
        for d in range(2):
            u, dk = U_[d], DK[d]
            k.tt("pool", u["Qabs"][:, 0:TT], u["R"][:, 0:TT], u["Eabs"][:, 0:TT], ALU.mult, [("R", dk), ("Eabs", dk)], [("Qabs", dk)])
            k.tt("pool", u["Qrel"][:, 0:TT], u["R"][:, 0:TT], u["Erel"][:, 0:TT], ALU.mult, [("R", dk), ("Erel", dk)], [("Qrel", dk)])
            k.tt("dve", u["Kd"][:, 0:TT], u["K"][:, 0:TT], u["Em"][:, 0:TT], ALU.mult, [("K", dk), ("Em", dk)], [("Kd", dk)])
            k.tt("dve", v3(u["Ke"]), v3(u["Kd"]), bc(u["Gm"]), ALU.mult, [("Kd", dk), ("Gm", dk)], [("Ke", dk)])
            if delta:
                k.tt("dve", u["cp"][:, 0:TT], u["cum"][:, 0:TT], u["LW"][:, 0:TT], ALU.subtract, [("cum", dk), ("LW", dk)], [("cp", dk)])
                k.actf(u["E0"][:, 0:TT], u["cp"][:, 0:TT], AF.Exp, [("cp", dk)], [("E0", dk)])
                k.tt("pool", u["KKabs"][:, 0:TT], u["KK"][:, 0:TT], u["E0"][:, 0:TT], ALU.mult, [("KK", dk), ("E0", dk)], [("KKabs", dk)])
                k.tt("dve", v3(u["cp"]), v3(u["cp"]), v3(u["cum"])[:, :, MID:MID + 1].to_broadcast([64, nch, CH]), ALU.subtract,
                     [("cp", dk), ("cum", dk)], [("cp", dk)])
                k.actf(u["E0"][:, 0:TT], u["cp"][:, 0:TT], AF.Exp, [("cp", dk)], [("E0", dk)])
                k.tt("pool", u["KKrel"][:, 0:TT], u["KK"][:, 0:TT], u["E0"][:, 0:TT], ALU.mult, [("KK", dk), ("E0", dk)], [("KKrel", dk)])
                k.tt("dve", u["Bd"][:, 0:TT], u["BK"][:, 0:TT], u["Em"][:, 0:TT], ALU.mult, [("BK", dk), ("Em", dk)], [("Bd", dk)])
                k.tt("dve", v3(u["Be"]), v3(u["Bd"]), bc(u["Gm"]), ALU.mult, [("Bd", dk), ("Gm", dk)], [("Be", dk)])
        yield
        pv = []
        for d in range(2):
            u, dk = U_[d], DK[d]
            p, pk = yield from acquire(c)
            pv.append((p, pk))
            for ch in range(nch):
                k.tr(p[0:CH, ch * 64:(ch + 1) * 64], u["Ke"][:, ch * CH:(ch + 1) * CH], I64, [("Ke", dk), "ident"], [pk])
            if delta:
                for ch in range(nch):
                    k.tr(p[0:CH, (nch + ch) * 64:(nch + ch + 1) * 64], u["Be"][:, ch * CH:(ch + 1) * CH], I64, [("Be", dk), "ident"], [pk])
        yield
        for d in range(2):
            u, dk = U_[d], DK[d]
            p, pk = pv[d]
            k.cp("act", u["KeT"][0:CH, 0:nch, :].rearrange("p a b -> p (a b)"), p[0:CH, 0:nch * 64], [pk], [("KeT", dk)])
            if delta:
                k.cp("dve", u["BeT"][0:CH, 0:nch, :].rearrange("p a b -> p (a b)"), p[0:CH, nch * 64:2 * nch * 64], [pk], [("BeT", dk)])
            release(c, pk)
        yield
        W4 = nch * CH
        specs = [("RKT", "Kd", "Qrel", "MI")]
        if delta:
            specs += [("RBT", "Bd", "Qrel", "MI"), ("AkT", "Kd", "KKrel", "MS"), ("M", "Bd", "KKrel", "MSneg"), ("P", "KKrel", "Bd", "MSntneg")]
        per = max(1, 512 // W4)
        for g0 in range(0, len(specs), per):
            grp = specs[g0:g0 + per]
            pv = []
            for d in range(2):
                u, dk = U_[d], DK[d]
                p, pk = yield from acquire(c)
                pv.append((p, pk))
                for si, (dst, lh, rh, mk_) in enumerate(grp):
                    for ch in range(nch):
                        cs_ = slice(ch * CH, (ch + 1) * CH)
                        k.mm(p[0:CH, si * W4 + ch * CH:si * W4 + (ch + 1) * CH], u[lh][:, cs_], u[rh][:, cs_], True, True,
                             [(lh, dk), (rh, dk)], [pk])
            yield
            for d in range(2):
                u, dk = U_[d], DK[d]
                p, pk = pv[d]
                mids = {"MI": 0, "MS": 1, "MSntneg": 5, "MSneg": 4} if d == 0 else {"MI": 2, "MS": 3, "MSntneg": 4, "MSneg": 5}
                for si, (dst, lh, rh, mk_) in enumerate(grp):
                    k.tt("dve", u[dst][0:CH, 0:nch, 0:CH],
                         p[0:CH, si * W4:(si + 1) * W4].rearrange("p (a b) -> p a b", b=CH) if False else
                         p[0:CH, si * W4:(si + 1) * W4].rearrange("p (a b) -> p a b", b=CH), mbc(mids[mk_]), ALU.mult,
                         [pk, "tmask"], [(dst, dk)])
                release(c, pk)
            yield
        if delta:
            cur = [["M", "P", "M2", "P2"], ["M", "P", "M2", "P2"]]
            for d in range(2):
                u, dk = U_[d], DK[d]
                k.tt("dve", u["Q"][:, 0:nch, :], u["M"][:, 0:nch, :], ibc, ALU.add, [("M", dk), "ident"], [("Q", dk)])
            W2 = nch * 64
            for lev in range(1, 6):
                pv = []
                for d in range(2):
                    u, dk = U_[d], DK[d]
                    Mc, Pc, Mn, Pn = cur[d]
                    p, pk = yield from acquire(c)
                    pv.append((p, pk))
                    for ch in range(nch):
                        k.mm(p[0:64, ch * 64:(ch + 1) * 64], u[Mc][:, ch, :], u[Pc][:, ch, :], True, True, [(Mc, dk), (Pc, dk)], [pk])
                    if lev < 5:
                        for ch in range(nch):
                            k.mm(p[0:64, W2 + ch * 64:W2 + (ch + 1) * 64], u[Pc][:, ch, :], u[Mc][:, ch, :], True, True, [(Mc, dk), (Pc, dk)], [pk])
                yield
                for d in range(2):
                    u, dk = U_[d], DK[d]
                    Mc, Pc, Mn, Pn = cur[d]
                    p, pk = pv[d]
                    k.cp("act", u[Pn][:, 0:nch, :].rearrange("p a b -> p (a b)"), p[0:64, 0:W2], [pk], [(Pn, dk)])
                    k.tt("dve", u["IP"][:, 0:nch, :], p[0:64, 0:W2].rearrange("p (a b) -> p a b", b=64), ibc, ALU.add, [pk, "ident"], [("IP", dk)])
                    if lev < 5:
                        k.cp("act", u[Mn][:, 0:nch, :].rearrange("p a b -> p (a b)"), p[0:64, W2:2 * W2], [pk], [(Mn, dk)])
                    release(c, pk)
                yield
                pv = []
                for d in range(2):
                    u, dk = U_[d], DK[d]
                    p, pk = yield from acquire(c)
                    pv.append((p, pk))
                    for ch in range(nch):
                        k.mm(p[0:64, ch * 64:(ch + 1) * 64], u["IP"][:, ch, :], u["Q"][:, ch, :], True, True, [("IP", dk), ("Q", dk)], [pk])
                yield
                for d in range(2):
                    u, dk = U_[d], DK[d]
                    p, pk = pv[d]
                    k.cp("dve", u["Q"][:, 0:nch, :].rearrange("p a b -> p (a b)"), p[0:64, 0:W2], [pk], [("Q", dk)])
                    release(c, pk)
                    Mc, Pc, Mn, Pn = cur[d]
                    cur[d] = [Mn, Pn, Mc, Pc]
                yield
        for ci in range(nch):
            chs = [ci, nch - 1 - ci]
            if delta:
                pv = []
                for d in range(2):
                    u, dk = U_[d], DK[d]
                    ch = chs[d]
                    cs_ = slice(ch * CH, (ch + 1) * CH)
                    Sc, Sck = u["S"][u["si"]], ("S", dk, u["si"])
                    p, pk = yield from acquire(c)
                    pv.append((p, pk))
                    k.mm(p[0:64, 0:64], u["KKabs"][:, cs_], Sc[:], True, False, [("KKabs", dk), Sck], [pk])
                    k.mm(p[0:64, 0:64], u["AkT"][:, ch, :], u["Vm"][:, ch, :], False, True, [("AkT", dk), ("Vm", dk)], [pk])
                yield
                for d in range(2):
                    u, dk = U_[d], DK[d]
                    p, pk = pv[d]
                    k.cp("dve" if d == 0 else "act", u["rhs0"][:], p[0:64, 0:64], [pk], [("rhs0", dk)])
                    release(c, pk)
                yield
                pv = []
                for d in range(2):
                    u, dk = U_[d], DK[d]
                    ch = chs[d]
                    p, pk = yield from acquire(c)
                    pv.append((p, pk))
                    k.mm(p[0:64, 0:64], u["Q"][:, ch, :], u["rhs0"][:], True, True, [("Q", dk), ("rhs0", dk)], [pk])
                yield
                for d in range(2):
                    u, dk = U_[d], DK[d]
                    p, pk = pv[d]
                    if d == 0:
                        k.ts("dve", u["U"][:], p[0:64, 0:64], -1.0, None, ALU.mult, None, [pk], [("U", dk)])
                    else:
                        k.actf(u["U"][:], p[0:64, 0:64], AF.Copy, [pk], [("U", dk)], scale=-1.0)
                    release(c, pk)
                yield
            pv = []
            for d in range(2):
                u, dk = U_[d], DK[d]
                ch = chs[d]
                cs_ = slice(ch * CH, (ch + 1) * CH)
                Sc, Sck = u["S"][u["si"]], ("S", dk, u["si"])
                p, pk = yield from acquire(c)
                pv.append((p, pk))
                k.mm(p[0:64, 0:CH], Sc[:], u["Qabs"][:, cs_], True, False, [Sck, ("Qabs", dk)], [pk])
                k.mm(p[0:64, 0:CH], u["Vm"][0:CH, ch, :], u["RKT"][0:CH, ch, 0:CH], False, not delta, [("Vm", dk), ("RKT", dk)], [pk])
                if delta:
                    k.mm(p[0:64, 0:CH], u["U"][0:CH, :], u["RBT"][0:CH, ch, 0:CH], False, True, [("U", dk), ("RBT", dk)], [pk])
                k.mm(p[0:64, 64:128], u["KeT"][0:CH, ch, :], u["Vm"][0:CH, ch, :], True, not delta, [("KeT", dk), ("Vm", dk)], [pk])
                if delta:
                    k.mm(p[0:64, 64:128], u["BeT"][0:CH, ch, :], u["U"][:], False, True, [("BeT", dk), ("U", dk)], [pk])
            yield
            for d in range(2):
                u, dk = U_[d], DK[d]
                ch = chs[d]
                cs_ = slice(ch * CH, (ch + 1) * CH)
                p, pk = pv[d]
                Sc, Sck = u["S"][u["si"]], ("S", dk, u["si"])
                Sn, Snk = u["S"][1 - u["si"]], ("S", dk, 1 - u["si"])
                k.cp("act", u["ybuf"][:, cs_], p[0:64, 0:CH], [pk], [("ybuf", dk)])
                gi = ch * CH + (CH - 1 if d == 0 else 0)
                k.stt(Sn[:], Sc[:], u["Eabs"][:, gi:gi + 1], p[0:64, 64:128], ALU.mult, ALU.add, [Sck, ("Eabs", dk), pk], [Snk])
                release(c, pk)
                u["si"] = 1 - u["si"]
            yield
        for d in range(2):
            u, dk = U_[d], DK[d]
            a0 = s0 + tis[d] * TT
            k.store(YS[d][:, a0:a0 + TT], u["ybuf"][:, 0:TT], reads=[("ybuf", dk)])
        yield
    if st_out is not None:
        for d in range(2):
            u, dk = U_[d], DK[d]
            Sc, Sck = u["S"][u["si"]], ("S", dk, u["si"])
            if transpose_state:
                p, pk = yield from acquire(c)
                k.tr(p[0:64, 0:64], Sc[:], I64, [Sck, "ident"], [pk])
                k.cp("dve", t["stmp"][:], p[0:64, 0:64], [pk], [stk])
                release(c, pk)
                k.store(st_out[d], t["stmp"][:], reads=[stk])
            else:
                k.store(st_out[d], Sc[:], reads=[Sck])
    yield


def drive_gen(jobs, tilesets):
    pending = list(jobs)
    free = list(tilesets)
    active = []
    while pending or active:
        while pending and free:
            ts = free.pop(0)
            active.append((pending.pop(0)(ts), ts))
        for item in list(active):
            g, ts = item
            try:
                next(g)
            except StopIteration:
                active.remove(item)
                free.append(ts)
        yield


def run_concurrent(gens):
    gens = list(gens)
    while gens:
        for g in list(gens):
            try:
                next(g)
            except StopIteration:
                gens.remove(g)


def drive(jobs, tilesets):
    run_concurrent([drive_gen(jobs, tilesets)])


def phase_hg(c, l, part=None, es_ext=None, NL=4):
    k, nc, cfg, S, I, O = c.k, c.nc, c.cfg, c.S, c.I, c.O
    NTOK = cfg.NTOK
    PTv = S["PT"].rearrange("(c p) t -> p c t", p=128)
    LAv = S["LAh"].rearrange("a (c p) t -> a p c t", p=128)
    LXv = S["LXh"].rearrange("a (c p) t -> a p c t", p=128)
    if part in (None, "prep"):
        es = ExitStack()
        col = k.sb("hg_col", [128, 2, 3], F32, es)
        ng = k.sb("hg_ng", [128, 2], F32, es)
        lb = k.sb("hg_lb", [128, 2], F32, es)
        oml = k.sb("hg_oml", [128, 2], F32, es)
        noml = k.sb("hg_noml", [128, 2], F32, es)
        k.load(col[:], I["hgcol"], writes=["hgcol"])
        k.load(ng[:], I["hgng"][l], writes=["hgng"])
        if l == 0:
            k.memset("dve", lb[:], 0.0, ["hglb"])
        else:
            k.tt("dve", lb[:], col[:, :, 1], col[:, :, 0], ALU.subtract, ["hgcol"], ["hglb"])
            k.actf(lb[:], lb[:], AF.Sigmoid, ["hglb"], ["hglb"])
        k.ts("dve", oml[:], lb[:], -1.0, 1.0, ALU.mult, ALU.add, ["hglb"], ["hgoml"])
        k.ts("dve", noml[:], oml[:], -1.0, None, ALU.mult, None, ["hgoml"], ["hgnoml"])
        pin = [k.sb(f"hg_pin{i}", [128, 10, 512], F32, es) for i in range(1)]
        wk = {nm: k.sb("hg_" + nm, [128, 2, 512], F32, es) for nm in ["R", "sig", "f", "K", "go"]}
        LAv = S["LAh"].rearrange("a (c p) t -> a p c t", p=128)
        LXv = S["LXh"].rearrange("a (c p) t -> a p c t", p=128)
        for ti, (t0, n, var) in enumerate(cfg.tiles):
            pi_ = pin[0]
            k.load(pi_[:, :, 0:n], PTv[:, 10:20, t0:t0 + n], reads=["PT"], writes=["hgpin"])
            k.actf(wk["R"][:, :, 0:n], pi_[:, 0:2, 0:n], AF.Silu, ["hgpin"], ["hgR"])
            k.store(LAv[0][:, :, t0:t0 + n], wk["R"][:, :, 0:n], reads=["hgR"], writes=["LA"])
            k.actf(wk["go"][:, :, 0:n], pi_[:, 8:10, 0:n], AF.Silu, ["hgpin"], ["hggo"])
            for hc in range(2):
                k.ts("dve", wk["go"][:, hc, 0:n], wk["go"][:, hc, 0:n], ng[:, hc:hc + 1], None, ALU.mult, None, ["hggo", "hgng"], ["hggo"])
            k.store(LXv[0][:, :, t0:t0 + n], wk["go"][:, :, 0:n], reads=["hggo"], writes=["LX"])
            for d in range(2):
                k.actf(wk["sig"][:, :, 0:n], pi_[:, 2 + 2 * d:4 + 2 * d, 0:n], AF.Sigmoid, ["hgpin"], ["hgsig"])
                for hc in range(2):
                    k.ts("dve", wk["f"][:, hc, 0:n], wk["sig"][:, hc, 0:n], oml[:, hc:hc + 1], lb[:, hc:hc + 1], ALU.mult, ALU.add,
                         ["hgsig", "hgoml", "hglb"], ["hgf"])
                    k.ts("pool", wk["K"][:, hc, 0:n], wk["sig"][:, hc, 0:n], noml[:, hc:hc + 1], oml[:, hc:hc + 1], ALU.mult, ALU.add,
                         ["hgsig", "hgoml", "hgnoml"], ["hgK"])
                k.ts("dve", wk["f"][:, :, 0:n], wk["f"][:, :, 0:n], 1e-30, None, ALU.max, None, ["hgf"], ["hgf"])
                k.actf(wk["f"][:, :, 0:n], wk["f"][:, :, 0:n], AF.Ln, ["hgf"], ["hgf"])
                k.store(LAv[4 + d][:, :, t0:t0 + n], wk["f"][:, :, 0:n], reads=["hgf"], writes=["LA"])
                k.store(LAv[2 + d][:, :, t0:t0 + n], wk["K"][:, :, 0:n], reads=["hgK"], writes=["LA"])
        k.barrier()
        es.close()

    if part in (None, "scan"):
      es = ExitStack() if es_ext is None else es_ext
      tsets = [la_alloc(c, es, False, 32, 4, f"h{i}") for i in range(NL)]
      cst = la_consts(c, es, 32, 4)
      jobs = []
      for (s0, T, kind, pi_) in cfg.seqs:
          for h in range(4):
              rows = slice(64 * h, 64 * h + 64)
              arrs = {"R": S["LAh"][0][rows], "V": S["PT"][2048 + 64 * h:2048 + 64 * h + 64], "K0": S["LAh"][2][rows], "K1": S["LAh"][3][rows],
                      "LW0": S["LAh"][4][rows], "LW1": S["LAh"][5][rows]}
              YS = [S["YSh"][0][rows], S["YSh"][1][rows]]
              st_in = [I["st_hg"][l, d, h] for d in range(2)] if kind == 1 else None
              st_out = [O["nhg"][pi_, l, d, h] for d in range(2)] if kind == 0 else None
              jobs.append(lambda ts, arrs=arrs, YS=YS, s0=s0, T=T, st_in=st_in, st_out=st_out:
                          la_lane(c, ts, cst, False, arrs, YS, (s0, T), st_in, st_out, False))
      g = drive_gen(jobs, tsets)
      if part == "scan":
          return g
      run_concurrent([g])
      k.barrier()
      es.close()
    if (cfg.stages is not None and "hg_noout" in cfg.stages) or part not in (None, "out"):
        return
    es = ExitStack()
    YSv = S["YSh"].rearrange("a (c p) t -> a p c t", p=128)
    OTv = S["OT"].rearrange("(c p) t -> p c t", p=128)
    y0 = k.sb("hgo_y0", [128, 2, 512], F32, es)
    y1 = k.sb("hgo_y1", [128, 2, 512], F32, es)
    go = k.sb("hgo_go", [128, 2, 512], F32, es)
    sq = k.sb("hgo_sq", [128, 2, 512], F32, es)
    epsc = k.sb("hgo_eps", [128, 1], F32, es)
    k.memset("dve", epsc[:], LN_EPS, ["hgeps"])
    for ti, (t0, n, var) in enumerate(cfg.tiles):
        k.load(y0[:, :, 0:n], YSv[0][:, :, t0:t0 + n], reads=["YS"], writes=["y0"])
        k.load(y1[:, :, 0:n], YSv[1][:, :, t0:t0 + n], reads=["YS"], writes=["y1"])
        k.load(go[:, :, 0:n], LXv[0][:, :, t0:t0 + n], reads=["LX"], writes=["go"])
        k.tt("dve", y0[:, :, 0:n], y0[:, :, 0:n], y1[:, :, 0:n], ALU.add, ["y0", "y1"], ["y0"])
        k.actf(sq[:, :, 0:n], y0[:, :, 0:n], AF.Square, ["y0"], ["sq"])
        for hc in range(2):
            p, pk = nextbank(c)
            k.mm(p[:, 0:n], c.bones64[:], sq[:, hc, 0:n], True, True, ["bones64", "sq"], [pk])
            k.actf(y1[:, hc, 0:n], p[:, 0:n], AF.Sqrt, [pk, "hgeps", "y1"], ["y1"], bias=epsc[:, 0:1])
        k.recip(y1[:, :, 0:n], y1[:, :, 0:n], ["y1"], ["y1"])
        k.tt("dve", y0[:, :, 0:n], y0[:, :, 0:n], y1[:, :, 0:n], ALU.mult, ["y0", "y1"], ["y0"])
        k.tt("dve", y0[:, :, 0:n], y0[:, :, 0:n], go[:, :, 0:n], ALU.mult, ["y0", "go"], ["y0"])
        k.store(OTv[:, 4:6, t0:t0 + n], y0[:, :, 0:n], reads=["y0"], writes=["OT"])
    k.barrier()
    es.close()


def phase_rw(c, l, part=None, es_ext=None, NL=4):
    k, nc, cfg, S, I, O = c.k, c.nc, c.cfg, c.S, c.I, c.O
    PTv = S["PT"].rearrange("(c p) t -> p c t", p=128)
    LAv = S["LAr"].rearrange("a (c p) t -> a p c t", p=128)
    LXv = S["LXr"].rearrange("a (c p) t -> a p c t", p=128)
    if part in (None, "prep"):
        es = ExitStack()
        mu = k.sb("rw_mu", [128, 2, 8], F32, es)
        cm = k.sb("rw_cm", [128, 8], F32, es)
        muad = k.sb("rw_muad", [64, 2], F32, es)
        cmad = k.sb("rw_cmad", [64, 1], F32, es)
        w0 = k.sb("rw_w0", [128, 2, 2], F32, es)
        col = k.sb("rw_col", [128, 2, 6], F32, es)
        omka = k.sb("rw_omka", [128, 2], F32, es)
        wup = k.sb("rw_wup", [64, 2, 256], F32, es)
        aup = k.sb("rw_aup", [64, 256], F32, es)
        gup = k.sb("rw_gup", [128, 256], F32, es)
        k.load(mu[:], I["rwmu"][l].rearrange("i p c -> p i c"), writes=["rwmu"])
        k.load(muad[:], I["rwmu_ad"][l], writes=["rwmuad"])
        k.load(w0[:], I["rww0"][l].rearrange("i p c -> p i c"), writes=["rww0"])
        k.load(col[:], I["rwcol"][l], writes=["rwcol"])
        k.load(wup[:], I["rw_w_up"][l].rearrange("i p c -> p i c"), writes=["rwwup"])
        k.load(aup[:], I["rw_a_up"][l], writes=["rwaup"])
        k.load(gup[:], I["rw_g_up"][l], writes=["rwgup"])
        k.tt("dve", cm[:], mu[:, 0, :], mu[:, 1, :], ALU.add, ["rwmu"], ["rwcm"])
        k.ts("dve", cm[:], cm[:], -1.0, 1.0, ALU.mult, ALU.add, ["rwcm"], ["rwcm"])
        k.tt("dve", cmad[:], muad[:, 0:1], muad[:, 1:2], ALU.add, ["rwmuad"], ["rwcmad"])
        k.ts("dve", cmad[:], cmad[:], -1.0, 1.0, ALU.mult, ALU.add, ["rwcmad"], ["rwcmad"])
        k.ts("dve", omka[:], col[:, :, 2], -1.0, 1.0, ALU.mult, ALU.add, ["rwcol"], ["rwomka"])
        pa = k.sb("rw_pa", [128, 8, 514], F32, es)
        pad = k.sb("rw_pad", [64, 514], F32, es)
        sh = k.sb("rw_sh", [128, 8, 512], F32, es)
        t2 = k.sb("rw_t2", [128, 8, 512], F32, es)
        adsh = k.sb("rw_adsh", [64, 512], F32, es)
        tw = k.sb("rw_tw", [64, 512], F32, es)
        sgd = k.sb("rw_sgd", [128, 512], F32, es)
        W = {nm: k.sb("rw_" + nm, [128, 2, 512], F32, es) for nm in ["a", "g", "lw0", "lw1", "kk", "kka", "keff", "tmp", "bon"]}
        EC = math.exp(-0.5)
        for (s0, T, kind, pi_) in cfg.seqs:
            for t0 in range(0, T, 512):
                n = min(512, T - t0)
                a0 = s0 + t0
                lo = max(s0, a0 - 1)
                hi = min(s0 + T, a0 + n + 1)
                if t0 == 0:
                    k.memset("dve", pa[:, :, 0:1], 0.0, ["rwpa"])
                    k.memset("dve", pad[:, 0:1], 0.0, ["rwpad"])
                if t0 + n == T:
                    k.memset("dve", pa[:, :, n + 1:n + 2], 0.0, ["rwpa"])
                    k.memset("dve", pad[:, n + 1:n + 2], 0.0, ["rwpad"])
                o0 = lo - (a0 - 1)
                k.load(pa[:, :, o0:o0 + hi - lo], PTv[:, 0:8, lo:hi], reads=["PT"], writes=["rwpa"])
                k.load(pad[:, o0:o0 + hi - lo], S["PT"][832:896, lo:hi], reads=["PT"], writes=["rwpad"])
                k.tt("dve", sh[:, :, 0:n], pa[:, :, 1:n + 1], cm[:].unsqueeze(2).to_broadcast([128, 8, n]), ALU.mult, ["rwpa", "rwcm"], ["rwsh"])
                k.tt("pool", t2[:, :, 0:n], pa[:, :, 0:n], mu[:, 0, :].unsqueeze(2).to_broadcast([128, 8, n]), ALU.mult, ["rwpa", "rwmu"], ["rwt2"])
                k.tt("dve", sh[:, :, 0:n], sh[:, :, 0:n], t2[:, :, 0:n], ALU.add, ["rwsh", "rwt2"], ["rwsh"])
                k.tt("pool", t2[:, :, 0:n], pa[:, :, 2:n + 2], mu[:, 1, :].unsqueeze(2).to_broadcast([128, 8, n]), ALU.mult, ["rwpa", "rwmu"], ["rwt2"])
                k.tt("dve", sh[:, :, 0:n], sh[:, :, 0:n], t2[:, :, 0:n], ALU.add, ["rwsh", "rwt2"], ["rwsh"])
                k.ts("dve", adsh[:, 0:n], pad[:, 1:n + 1], cmad[:, 0:1], None, ALU.mult, None, ["rwpad", "rwcmad"], ["rwadsh"])
                k.stt(adsh[:, 0:n], pad[:, 0:n], muad[:, 0:1], adsh[:, 0:n], ALU.mult, ALU.add, ["rwpad", "rwmuad", "rwadsh"], ["rwadsh"])
                k.stt(adsh[:, 0:n], pad[:, 2:n + 2], muad[:, 1:2], adsh[:, 0:n], ALU.mult, ALU.add, ["rwpad", "rwmuad", "rwadsh"], ["rwadsh"])
                k.actf(tw[:, 0:n], sh[0:64, 6, 0:n], AF.Tanh, ["rwsh"], ["rwtw"])
                k.actf(sgd[:, 0:n], sh[:, 7, 0:n], AF.Sigmoid, ["rwsh"], ["rwsgd"])
                for hc in range(2):
                    hsl = slice(hc * 128, (hc + 1) * 128)
                    r_, k_, v_ = sh[:, hc, 0:n], sh[:, 2 + hc, 0:n], sh[:, 4 + hc, 0:n]
                    p, pk = nextbank(c)
                    k.mm(p[:, 0:n], aup[:, hsl], adsh[:, 0:n], True, True, ["rwaup", "rwadsh"], [pk])
                    k.actf(W["a"][:, hc, 0:n], p[:, 0:n], AF.Sigmoid, [pk, "rwcol"], [("rwa", hc)], bias=col[:, hc, 0:1])
                    p, pk = nextbank(c)
                    k.mm(p[:, 0:n], gup[:, hsl], sgd[:, 0:n], True, True, ["rwgup", "rwsgd"], [pk])
                    k.cp("act", W["g"][:, hc, 0:n], p[:, 0:n], [pk], [("rwg", hc)])
                    for d in range(2):
                        p, pk = nextbank(c)
                        k.mm(p[:, 0:n], wup[:, d, hsl], tw[:, 0:n], True, True, ["rwwup", "rwtw"], [pk])
                        lw = W[f"lw{d}"]
                        k.actf(lw[:, hc, 0:n], p[:, 0:n], AF.Sigmoid, [pk, "rww0"], [("rwlw", d, hc)], bias=w0[:, d, hc:hc + 1])
                        k.ts("dve", lw[:, hc, 0:n], lw[:, hc, 0:n], -EC, None, ALU.mult, None, [("rwlw", d, hc)], [("rwlw", d, hc)])
                    kk = W["kk"]
                    k.ts("dve", kk[:, hc, 0:n], k_, col[:, hc, 1:2], None, ALU.mult, None, ["rwsh", "rwcol"], [("rwkk", hc)])
                    k.actf(W["tmp"][:, hc, 0:n], kk[:, hc, 0:n], AF.Square, [("rwkk", hc)], [("rwtmp", hc)])
                    p, pk = nextbank(c)
                    k.mm(p[:, 0:n], c.bones[:], W["tmp"][:, hc, 0:n], True, True, ["bones", ("rwtmp", hc)], [pk])
                    k.actf(W["tmp"][:, hc, 0:n], p[:, 0:n], AF.Sqrt, [pk], [("rwtmp", hc)])
                    k.ts("dve", W["tmp"][:, hc, 0:n], W["tmp"][:, hc, 0:n], 1e-12, None, ALU.max, None, [("rwtmp", hc)], [("rwtmp", hc)])
                    k.recip(W["tmp"][:, hc, 0:n], W["tmp"][:, hc, 0:n], [("rwtmp", hc)], [("rwtmp", hc)])
                    k.tt("dve", kk[:, hc, 0:n], kk[:, hc, 0:n], W["tmp"][:, hc, 0:n], ALU.mult, [("rwkk", hc), ("rwtmp", hc)], [("rwkk", hc)])
                    k.tt("pool", W["kka"][:, hc, 0:n], kk[:, hc, 0:n], W["a"][:, hc, 0:n], ALU.mult, [("rwkk", hc), ("rwa", hc)], [("rwkka", hc)])
                    k.ts("dve", W["tmp"][:, hc, 0:n], W["a"][:, hc, 0:n], col[:, hc, 2:3], omka[:, hc:hc + 1], ALU.mult, ALU.add,
                         [("rwa", hc), "rwcol", "rwomka", ("rwtmp", hc)], [("rwtmp", hc)])
                    k.tt("dve", W["keff"][:, hc, 0:n], k_, W["tmp"][:, hc, 0:n], ALU.mult, ["rwsh", ("rwtmp", hc)], [("rwkeff", hc)])
                    k.tt("dve", W["tmp"][:, hc, 0:n], r_, W["keff"][:, hc, 0:n], ALU.mult, ["rwsh", ("rwkeff", hc), ("rwtmp", hc)], [("rwtmp", hc)])
                    k.ts("dve", W["tmp"][:, hc, 0:n], W["tmp"][:, hc, 0:n], col[:, hc, 3:4], None, ALU.mult, None, [("rwtmp", hc), "rwcol"], [("rwtmp", hc)])
                    p, pk = nextbank(c)
                    k.mm(p[:, 0:n], c.bones[:], W["tmp"][:, hc, 0:n], True, True, ["bones", ("rwtmp", hc)], [pk])
                    k.tt("dve", W["bon"][:, hc, 0:n], p[:, 0:n], v_, ALU.mult, [pk, "rwsh"], [("rwbon", hc)])
                sl = slice(a0, a0 + n)
                k.store(LAv[0][:, :, sl], sh[:, 0:2, 0:n], reads=["rwsh"], writes=["LA"])
                k.store(LAv[1][:, :, sl], sh[:, 4:6, 0:n], reads=["rwsh"], writes=["LA"])
                k.store(LAv[2][:, :, sl], W["keff"][:, :, 0:n], reads=[("rwkeff", 0), ("rwkeff", 1)], writes=["LA"])
                k.store(LAv[4][:, :, sl], W["lw0"][:, :, 0:n], reads=[("rwlw", 0, 0), ("rwlw", 0, 1)], writes=["LA"])
                k.store(LAv[5][:, :, sl], W["lw1"][:, :, 0:n], reads=[("rwlw", 1, 0), ("rwlw", 1, 1)], writes=["LA"])
                k.store(LAv[6][:, :, sl], W["kk"][:, :, 0:n], reads=[("rwkk", 0), ("rwkk", 1)], writes=["LA"])
                k.store(LAv[7][:, :, sl], W["kka"][:, :, 0:n], reads=[("rwkka", 0), ("rwkka", 1)], writes=["LA"])
                k.store(LXv[0][:, :, sl], W["g"][:, :, 0:n], reads=[("rwg", 0), ("rwg", 1)], writes=["LX"])
                k.store(LXv[1][:, :, sl], W["bon"][:, :, 0:n], reads=[("rwbon", 0), ("rwbon", 1)], writes=["LX"])
        k.barrier()
        es.close()

    if (cfg.stages is not None and "rw_prep_only" in cfg.stages) or part == "prep":
        return
    if part in (None, "scan"):
        es = ExitStack() if es_ext is None else es_ext
        tsets = [la_alloc(c, es, True, 64, 2, f"r{i}") for i in range(NL)]
        cst = la_consts(c, es, 64, 2)
        jobs = []
        for (s0, T, kind, pi_) in cfg.seqs:
            for h in range(4):
                rows = slice(64 * h, 64 * h + 64)
                arrs = {"R": S["LAr"][0][rows], "V": S["LAr"][1][rows], "K0": S["LAr"][2][rows], "K1": S["LAr"][2][rows],
                        "LW0": S["LAr"][4][rows], "LW1": S["LAr"][5][rows], "KK": S["LAr"][6][rows], "BK": S["LAr"][7][rows]}
                YS = [S["YSr"][0][rows], S["YSr"][1][rows]]
                st_in = [I["st_rw"][l, d, h] for d in range(2)] if kind == 1 else None
                st_out = [O["nrw"][pi_, l, d, h] for d in range(2)] if kind == 0 else None
                jobs.append(lambda ts, arrs=arrs, YS=YS, s0=s0, T=T, st_in=st_in, st_out=st_out:
                            la_lane(c, ts, cst, True, arrs, YS, (s0, T), st_in, st_out, True))
        g = drive_gen(jobs, tsets)
        if part == "scan":
            return g
        run_concurrent([g])
        k.barrier()
        es.close()

    if (cfg.stages is not None and "rw_noout" in cfg.stages) or part not in (None, "out"):
        return
    es = ExitStack()
    YSv = S["YSr"].rearrange("a (c p) t -> a p c t", p=128)
    OTv = S["OT"].rearrange("(c p) t -> p c t", p=128)
    col = k.sb("rwo_col", [128, 2, 6], F32, es)
    k.load(col[:], I["rwcol"][l], writes=["rwcol"])
    y0 = k.sb("rwo_y0", [128, 2, 512], F32, es)
    y1 = k.sb("rwo_y1", [128, 2, 512], F32, es)
    g = k.sb("rwo_g", [128, 2, 512], F32, es)
    bon = k.sb("rwo_bon", [128, 2, 512], F32, es)
    sq = k.sb("rwo_sq", [128, 2, 512], F32, es)
    epsc = k.sb("rwo_eps", [128, 1], F32, es)
    k.memset("dve", epsc[:], RW_EPS, ["rweps"])
    for ti, (t0, n, var) in enumerate(cfg.tiles):
        k.load(y0[:, :, 0:n], YSv[0][:, :, t0:t0 + n], reads=["YS"], writes=["y0"])
        k.load(y1[:, :, 0:n], YSv[1][:, :, t0:t0 + n], reads=["YS"], writes=["y1"])
        k.load(g[:, :, 0:n], LXv[0][:, :, t0:t0 + n], reads=["LX"], writes=["g"])
        k.load(bon[:, :, 0:n], LXv[1][:, :, t0:t0 + n], reads=["LX"], writes=["bon"])
        k.tt("dve", y0[:, :, 0:n], y0[:, :, 0:n], y1[:, :, 0:n], ALU.add, ["y0", "y1"], ["y0"])
        for hc in range(2):
            p, pk = nextbank(c)
            k.mm(p[:, 0:n], c.bones64[:], y0[:, hc, 0:n], True, True, ["bones64", "y0"], [pk])
            k.tt("dve", y1[:, hc, 0:n], y0[:, hc, 0:n], p[:, 0:n], ALU.subtract, ["y0", pk, "y1"], [("yc", hc)])
            k.actf(sq[:, hc, 0:n], y1[:, hc, 0:n], AF.Square, [("yc", hc)], [("sq", hc)])
            p, pk = nextbank(c)
            k.mm(p[:, 0:n], c.bones64[:], sq[:, hc, 0:n], True, True, ["bones64", ("sq", hc)], [pk])
            k.actf(sq[:, hc, 0:n], p[:, 0:n], AF.Sqrt, [pk, "rweps"], [("sq", hc)], bias=epsc[:, 0:1])
            k.recip(sq[:, hc, 0:n], sq[:, hc, 0:n], [("sq", hc)], [("sq", hc)])
            k.tt("dve", y1[:, hc, 0:n], y1[:, hc, 0:n], sq[:, hc, 0:n], ALU.mult, [("yc", hc), ("sq", hc)], [("yc", hc)])
            k.ts("dve", y1[:, hc, 0:n], y1[:, hc, 0:n], col[:, hc, 4:5], col[:, hc, 5:6], ALU.mult, ALU.add, [("yc", hc), "rwcol"], [("yc", hc)])
            k.tt("dve", y1[:, hc, 0:n], y1[:, hc, 0:n], bon[:, hc, 0:n], ALU.add, [("yc", hc), "bon"], [("yc", hc)])
            k.tt("dve", y1[:, hc, 0:n], y1[:, hc, 0:n], g[:, hc, 0:n], ALU.mult, [("yc", hc), "g"], [("yc", hc)])
        k.store(OTv[:, 0:2, t0:t0 + n], y1[:, :, 0:n], reads=[("yc", 0), ("yc", 1)], writes=["OT", "y1"])
    k.barrier()
    es.close()


def phase_s5(c, l, part=None, es_ext=None, NW=2):
    k, nc, cfg, S, I, O = c.k, c.nc, c.cfg, c.S, c.I, c.O
    PI = math.pi
    TT = 128
    PTv = S["PT"].rearrange("(c p) t -> p c t", p=128)
    YSv = S["YS5"].rearrange("a (c p) t -> a p c t", p=128)
    if part in (None, "scan"):
        es = ExitStack() if es_ext is None else es_ext
        lam = k.sb("s5_lam", [128, 2, 8, 3], F32, es)
        k.load(lam[:], I["s5lam"][l].rearrange("d p j r -> p d j r"), writes=["s5lam"])
        BT = k.sb("s5_BT", [128, 2, 8, 128], F32, es)
        CT = k.sb("s5_CT", [128, 2, 8, 128], F32, es)
        for r in range(2):
            k.load(BT[:, r], I["s5BT"][l, r].rearrange("j p n -> p j n"), writes=["s5BT"])
            k.load(CT[:, r], I["s5CT"][l, r].rearrange("j p n -> p j n"), writes=["s5CT"])
        sm = {nm: k.sb("s5_" + nm, [128, 2, 8], F32, es) for nm in
              ["dt", "mag", "th", "th2", "msk", "cos", "sin", "abre", "abim", "den", "zre", "zim", "t1", "t2", "cw", "sw", "cw2"]}
        RT = {nm: k.sb("s5_" + nm, [128, 2, 8, TT], F32, es) for nm in ["RTre", "RTim", "DZre", "DZim"]}
        tA = k.sb("s5_tA", [128, 8, TT], F32, es)
        tB = k.sb("s5_tB", [128, 8, TT], F32, es)
        magz = k.sb("s5_magz", [128, 2, 8, TT], F32, es)
        A_ = lambda nm: sm[nm][:]
        are, aim, ldt = lam[:, :, :, 0], lam[:, :, :, 1], lam[:, :, :, 2]

        def tts(e, o, a, b, op, r, w):
            k.tt(e, sm[o][:], a, b, op, r, w)

        k.actf(A_("dt"), ldt, AF.Exp, ["s5lam"], ["dt"])
        tts("dve", "t1", are, A_("dt"), ALU.mult, ["s5lam", "dt"], ["t1"])
        k.actf(A_("mag"), A_("t1"), AF.Exp, ["t1"], ["mag"])
        tts("dve", "th", aim, A_("dt"), ALU.mult, ["s5lam", "dt"], ["th"])

        def reduce_pi(nm, iters):
            for _ in range(iters):
                k.ts("dve", A_("msk"), A_(nm), PI, -2.0 * PI, ALU.is_gt, ALU.mult, [nm], ["msk"])
                tts("dve", nm, A_(nm), A_("msk"), ALU.add, [nm, "msk"], [nm])
                k.ts("dve", A_("msk"), A_(nm), -PI, 2.0 * PI, ALU.is_lt, ALU.mult, [nm], ["msk"])
                tts("dve", nm, A_(nm), A_("msk"), ALU.add, [nm, "msk"], [nm])
        reduce_pi("th", 5)
        k.ts("dve", A_("th2"), A_("th"), PI / 2, None, ALU.add, None, ["th"], ["th2"])
        reduce_pi("th2", 1)
        k.actf(A_("sin"), A_("th"), AF.Sin, ["th"], ["sin"])
        k.actf(A_("cos"), A_("th2"), AF.Sin, ["th2"], ["cos"])
        tts("dve", "abre", A_("mag"), A_("cos"), ALU.mult, ["mag", "cos"], ["abre"])
        tts("dve", "abim", A_("mag"), A_("sin"), ALU.mult, ["mag", "sin"], ["abim"])
        tts("dve", "t1", are, are, ALU.mult, ["s5lam"], ["t1"])
        tts("dve", "t2", aim, aim, ALU.mult, ["s5lam"], ["t2"])
        tts("dve", "den", A_("t1"), A_("t2"), ALU.add, ["t1", "t2"], ["den"])
        k.recip(A_("den"), A_("den"), ["den"], ["den"])
        k.ts("dve", A_("t1"), A_("abre"), -1.0, None, ALU.add, None, ["abre"], ["t1"])
        tts("dve", "zre", A_("t1"), are, ALU.mult, ["t1", "s5lam"], ["zre"])
        tts("dve", "t2", A_("abim"), aim, ALU.mult, ["abim", "s5lam"], ["t2"])
        tts("dve", "zre", A_("zre"), A_("t2"), ALU.add, ["zre", "t2"], ["zre"])
        tts("dve", "zre", A_("zre"), A_("den"), ALU.mult, ["zre", "den"], ["zre"])
        tts("dve", "zim", A_("abim"), are, ALU.mult, ["abim", "s5lam"], ["zim"])
        tts("dve", "t2", A_("t1"), aim, ALU.mult, ["t1", "s5lam"], ["t2"])
        tts("dve", "zim", A_("zim"), A_("t2"), ALU.subtract, ["zim", "t2"], ["zim"])
        tts("dve", "zim", A_("zim"), A_("den"), ALU.mult, ["zim", "den"], ["zim"])
        for d in range(2):
            Rre, Rim = RT["RTre"][:, d], RT["RTim"][:, d]
            k.cp("dve", Rre[:, :, 0:1], sm["cos"][:, d, :].unsqueeze(2), ["cos"], [("RT", d)])
            k.cp("dve", Rim[:, :, 0:1], sm["sin"][:, d, :].unsqueeze(2), ["sin"], [("RT", d)])
            k.cp("dve", sm["cw"][:, d, :], sm["cos"][:, d, :], ["cos"], [("cw", d)])
            k.cp("dve", sm["sw"][:, d, :], sm["sin"][:, d, :], ["sin"], [("sw", d)])
            w = 1
            while w < TT:
                cwb = sm["cw"][:, d, :].unsqueeze(2).to_broadcast([128, 8, w])
                swb = sm["sw"][:, d, :].unsqueeze(2).to_broadcast([128, 8, w])
                k.tt("dve", tA[:, :, 0:w], Rre[:, :, 0:w], cwb, ALU.mult, [("RT", d), ("cw", d)], ["tA"])
                k.tt("pool", tB[:, :, 0:w], Rim[:, :, 0:w], swb, ALU.mult, [("RT", d), ("sw", d)], ["tB"])
                k.tt("dve", Rre[:, :, w:2 * w], tA[:, :, 0:w], tB[:, :, 0:w], ALU.subtract, ["tA", "tB", ("RT", d)], [("RT", d)])
                k.tt("dve", tA[:, :, 0:w], Rre[:, :, 0:w], swb, ALU.mult, [("RT", d), ("sw", d)], ["tA"])
                k.tt("pool", tB[:, :, 0:w], Rim[:, :, 0:w], cwb, ALU.mult, [("RT", d), ("cw", d)], ["tB"])
                k.tt("dve", Rim[:, :, w:2 * w], tA[:, :, 0:w], tB[:, :, 0:w], ALU.add, ["tA", "tB", ("RT", d)], [("RT", d)])
                k.tt("dve", sm["t1"][:, d, :], sm["cw"][:, d, :], sm["cw"][:, d, :], ALU.mult, [("cw", d), "t1"], ["t1"])
                k.tt("dve", sm["t2"][:, d, :], sm["sw"][:, d, :], sm["sw"][:, d, :], ALU.mult, [("sw", d), "t2"], ["t2"])
                k.tt("dve", sm["cw2"][:, d, :], sm["cw"][:, d, :], sm["sw"][:, d, :], ALU.mult, [("cw", d), ("sw", d)], ["cw2"])
                k.tt("dve", sm["cw"][:, d, :], sm["t1"][:, d, :], sm["t2"][:, d, :], ALU.subtract, ["t1", "t2"], [("cw", d)])
                k.ts("dve", sm["sw"][:, d, :], sm["cw2"][:, d, :], 2.0, None, ALU.mult, None, ["cw2"], [("sw", d)])
                w *= 2
            zre_b = sm["zre"][:, d, :].unsqueeze(2).to_broadcast([128, 8, TT])
            zim_b = sm["zim"][:, d, :].unsqueeze(2).to_broadcast([128, 8, TT])
            k.tt("dve", tA[:], Rre, zre_b, ALU.mult, [("RT", d), "zre"], ["tA"])
            k.tt("pool", tB[:], Rim, zim_b, ALU.mult, [("RT", d), "zim"], ["tB"])
            k.tt("dve", RT["DZre"][:, d], tA[:], tB[:], ALU.add, ["tA", "tB"], [("DZ", d)])
            k.tt("dve", tA[:], Rre, zim_b, ALU.mult, [("RT", d), "zim"], ["tA"])
            k.tt("pool", tB[:], Rim, zre_b, ALU.mult, [("RT", d), "zre"], ["tB"])
            k.tt("dve", RT["DZim"][:, d], tA[:], tB[:], ALU.subtract, ["tA", "tB", ("DZ", d)], [("DZ", d)])
        for d in range(2):
            k.cp("act", magz[:, d], sm["mag"][:, d, :].unsqueeze(2).to_broadcast([128, 8, TT]), ["mag"], [("magz", d)])
            fi = 0 if d == 0 else TT - 1
            k.memset("dve", magz[:, d, :, fi:fi + 1], 0.0, [("magz", d)])
        wsets = []
        for i in range(NW):
            w = {nm: k.sb(f"s5w{i}_{nm}", [128, 8, TT], F32, es) for nm in ["A", "B", "C", "D"]}
            w["Braw"] = k.sb(f"s5w{i}_Braw", [128, 8, 2, TT], F32, es)
            w["u"] = k.sb(f"s5w{i}_u", [128, 2, TT], F32, es)
            w["y"] = k.sb(f"s5w{i}_y", [128, 2, TT], F32, es)
            w["init"] = k.sb(f"s5w{i}_init", [128, 8, 2], F32, es)
            w["cin"] = k.sb(f"s5w{i}_cin", [128, 8, 2], F32, es)
            w["tag"] = i
            wsets.append(w)

        def sweep(w, s0, T, kind, pi_, d):
            tg = w["tag"]
            K_ = lambda nm: ("s5", tg, nm)
            nt = T // TT
            init, cin = w["init"], w["cin"]
            if kind == 1:
                k.load(init[:], I["st_s5"][l, d], writes=[K_("init")])
            else:
                k.memset("dve", init[:], 0.0, [K_("init")])
            first = 0 if d == 0 else TT - 1
            last = TT - 1 if d == 0 else 0
            rvt = (lambda ap: ap) if d == 0 else (lambda ap: ap[:, :, ::-1])
            flat = lambda ap: ap.rearrange("p a b -> p (a b)")
            rvf = (lambda ap: flat(ap)) if d == 0 else (lambda ap: flat(ap)[:, ::-1])
            DZr, DZi = rvt(RT["DZre"][:, d]), rvt(RT["DZim"][:, d])
            Rr, Ri = rvt(RT["RTre"][:, d]), rvt(RT["RTim"][:, d])
            A, B, C, D_ = w["A"], w["B"], w["C"], w["D"]
            yield
            for it in range(nt):
                ti = it if d == 0 else nt - 1 - it
                a0 = s0 + ti * TT
                k.load(w["u"][:], PTv[:, 8:10, a0:a0 + TT], writes=[K_("u")])
                yield
                for j2 in range(4):
                    p, pk = yield from acquire(c)
                    for jj in range(2):
                        j = 2 * j2 + jj
                        for r in range(2):
                            k.mm(p[:, (2 * jj + r) * TT:(2 * jj + r + 1) * TT], BT[:, r, j, :], w["u"][:, j // 4, :], True, True, ["s5BT", K_("u")], [pk])
                    k.cp("act", w["Braw"][:, 2 * j2:2 * j2 + 2].rearrange("p a r t -> p (a r t)"), p[:, 0:4 * TT], [pk], [K_("Braw")])
                    release(c, pk)
                yield
                Br, Bi = w["Braw"][:, :, 0, :], w["Braw"][:, :, 1, :]
                k.tt("dve", A[:], Br, DZr, ALU.mult, [K_("Braw"), ("DZ", d)], [K_("A")])
                k.tt("pool", B[:], Bi, DZi, ALU.mult, [K_("Braw"), ("DZ", d)], [K_("B")])
                yield
                k.tt("dve", A[:], A[:], B[:], ALU.subtract, [K_("A"), K_("B")], [K_("A")])
                k.tt("pool", C[:], Br, DZi, ALU.mult, [K_("Braw"), ("DZ", d)], [K_("C")])
                yield
                k.tt("pool", B[:], Bi, DZr, ALU.mult, [K_("Braw"), ("DZ", d), K_("A")], [K_("B")])
                k.tt("dve", cin[:], init[:], sm["mag"][:, d, :].unsqueeze(2).to_broadcast([128, 8, 2]), ALU.mult, [K_("init"), "mag"], [K_("cin")])
                k.tt("dve", A[:, :, first:first + 1], A[:, :, first:first + 1], cin[:, :, 0:1], ALU.add, [K_("A"), K_("cin")], [K_("A")])
                yield
                k.tt("pool", B[:], B[:], C[:], ALU.add, [K_("B"), K_("C")], [K_("B")])
                k.scan(rvf(C[:]), rvf(magz[:, d]), rvf(A[:]), 0.0, [K_("A"), ("magz", d), K_("B")], [K_("C")])
                yield
                k.tt("dve", B[:, :, first:first + 1], B[:, :, first:first + 1], cin[:, :, 1:2], ALU.add, [K_("B"), K_("cin")], [K_("B")])
                k.scan(rvf(D_[:]), rvf(magz[:, d]), rvf(B[:]), 0.0, [K_("B"), ("magz", d)], [K_("D")])
                yield
                k.tt("dve", A[:], C[:], Rr, ALU.mult, [K_("C"), ("RT", d)], [K_("A")])
                k.tt("pool", B[:], D_[:], Ri, ALU.mult, [K_("D"), ("RT", d)], [K_("B")])
                yield
                k.tt("dve", A[:], A[:], B[:], ALU.subtract, [K_("A"), K_("B")], [K_("A")])
                k.tt("pool", B[:], C[:], Ri, ALU.mult, [K_("C"), ("RT", d), K_("A")], [K_("B")])
                yield
                k.tt("dve", C[:], D_[:], Rr, ALU.mult, [K_("D"), ("RT", d), K_("B")], [K_("C")])
                yield
                k.stt(B[:], B[:], -1.0, C[:], ALU.mult, ALU.subtract, [K_("B"), K_("C")], [K_("B")])
                yield
                k.cp("act", init[:, :, 0:1], A[:, :, last:last + 1], [K_("A")], [K_("init")])
                k.actf(init[:, :, 1:2], B[:, :, last:last + 1], AF.Copy, [K_("B")], [K_("init")], scale=-1.0)
                for kc in range(2):
                    p, pk = yield from acquire(c)
                    for jj in range(4):
                        j = 4 * kc + jj
                        k.mm(p[:, 0:TT], CT[:, 0, j, :], A[:, j, :], jj == 0, False, ["s5CT", K_("A")], [pk])
                        k.mm(p[:, 0:TT], CT[:, 1, j, :], B[:, j, :], False, jj == 3, ["s5CT", K_("B")], [pk])
                    k.cp("act", w["y"][:, kc, :], p[:, 0:TT], [pk], [K_("y")])
                    release(c, pk)
                yield
                k.store(YSv[d][:, :, a0:a0 + TT], w["y"][:], reads=[K_("y")])
                yield
            if kind == 0:
                k.store(O["ns5"][pi_, l, d].rearrange("g n r -> (g n) r").rearrange("(j p) r -> p j r", p=128), init[:], reads=[K_("init")])
            yield

        jobs = []
        for (s0, T, kind, pi_) in cfg.seqs:
            for d in range(2):
                jobs.append(lambda w, s0=s0, T=T, kind=kind, pi_=pi_, d=d: sweep(w, s0, T, kind, pi_, d))
        g = drive_gen(jobs, wsets)
        if part == "scan":
            return g
        run_concurrent([g])
        k.barrier()
        es.close()

    if (cfg.stages is not None and "s5_noout" in cfg.stages) or part not in (None, "out"):
        return
    es = ExitStack()
    OTv = S["OT"].rearrange("(c p) t -> p c t", p=128)
    col = k.sb("s5o_col", [128, 2, 2], F32, es)
    glu = k.sb("s5o_glu", [128, 2, 256], F32, es)
    k.load(col[:], I["s5col"][l], writes=["s5col"])
    k.load(glu[:], I["s5glu"][l].rearrange("(kc p) n -> p kc n", p=128), writes=["s5glu"])
    y0 = k.sb("s5o_y0", [128, 2, 512], F32, es)
    y1 = k.sb("s5o_y1", [128, 2, 512], F32, es)
    u = k.sb("s5o_u", [128, 2, 512], F32, es)
    t_ = k.sb("s5o_t", [128, 2, 512], F32, es)
    for ti, (t0, n, var) in enumerate(cfg.tiles):
        k.load(y0[:, :, 0:n], YSv[0][:, :, t0:t0 + n], reads=["YS"], writes=["y0"])
        k.load(y1[:, :, 0:n], YSv[1][:, :, t0:t0 + n], reads=["YS"], writes=["y1"])
        k.load(u[:, :, 0:n], PTv[:, 8:10, t0:t0 + n], reads=["PT"], writes=["u"])
        k.tt("dve", y0[:, :, 0:n], y0[:, :, 0:n], y1[:, :, 0:n], ALU.add, ["y0", "y1"], ["y0"])
        for hc in range(2):
            k.stt(y0[:, hc, 0:n], u[:, hc, 0:n], col[:, hc, 0:1], y0[:, hc, 0:n], ALU.mult, ALU.add, ["u", "s5col", "y0"], ["y0"])
        k.actf(t_[:, :, 0:n], y0[:, :, 0:n], AF.Square, ["y0"], ["t"])
        k.ts("dve", t_[:, :, 0:n], t_[:, :, 0:n], 0.044715, 1.0, ALU.mult, ALU.add, ["t"], ["t"])
        k.tt("dve", t_[:, :, 0:n], t_[:, :, 0:n], y0[:, :, 0:n], ALU.mult, ["t", "y0"], ["t"])
        k.actf(t_[:, :, 0:n], t_[:, :, 0:n], AF.Sigmoid, ["t"], ["t"], scale=1.5957691216057308)
        k.tt("dve", y0[:, :, 0:n], y0[:, :, 0:n], t_[:, :, 0:n], ALU.mult, ["t", "y0"], ["y0"])
        for hc in range(2):
            p, pk = nextbank(c)
            for kc in range(2):
                k.mm(p[:, 0:n], glu[:, kc, hc * 128:(hc + 1) * 128], y0[:, kc, 0:n], kc == 0, kc == 1, ["s5glu", "y0"], [pk])
            k.actf(y1[:, hc, 0:n], p[:, 0:n], AF.Sigmoid, [pk, "s5col", "y1"], [("s5sg", hc)], bias=col[:, hc, 1:2])
            k.tt("dve", y1[:, hc, 0:n], y1[:, hc, 0:n], y0[:, hc, 0:n], ALU.mult, [("s5sg", hc), "y0"], [("s5sg", hc)])
        k.store(OTv[:, 2:4, t0:t0 + n], y1[:, :, 0:n], reads=[("s5sg", 0), ("s5sg", 1)], writes=["OT", "y1"])
    k.barrier()
    es.close()

def cols(v):
    v = np.asarray(v)
    n = v.shape[-1] // 128
    return np.ascontiguousarray(np.swapaxes(v.reshape(v.shape[:-1] + (n, 128)), -1, -2))


def prep_core(inp, b, cfg):
    NP, TP = cfg.NP, cfg.TP
    m = {}
    xs = inp["x_sample"][b]
    xp = inp["x_prompt"][b * NP:(b + 1) * NP].reshape(NP * TP, D)
    m["xin"] = np.ascontiguousarray(np.concatenate([xs, xp], axis=0))
    m["ident"] = np.eye(128, dtype=np.float32)
    cc = np.stack([cols(inp["c_ctx"]), cols(inp["c"][b])], axis=-1)
    m["ccol"] = np.ascontiguousarray(cc.astype(np.float32))
    m["w_mod"] = inp["w_mod"]
    m["bmod"] = cols(inp["b_mod"])
    m["lng"] = cols(inp["ln_g"])
    m["lnb"] = cols(inp["ln_b"])
    for nm in ("ffn_w_in", "ffn_w_out", "w_in", "w_out"):
        m[nm] = inp[nm]
    m["cache_k"] = np.ascontiguousarray(inp["cache_na_k"][b].reshape(L, 256, 256))
    m["cache_v"] = np.ascontiguousarray(inp["cache_na_v"][b].reshape(L, 256, 256))
    cc_, ww_ = np.meshgrid(np.arange(64), np.arange(64), indexing="ij")
    idx = np.clip(cc_ - ww_ + 15, 0, 30)
    rp = inp["na_rpb"][:, :, :, idx]
    m["rpbT"] = np.ascontiguousarray(np.transpose(rp, (0, 3, 1, 2, 4)))
    cs = np.clip(ww_ - 8, 0, 48)
    m["namask"] = np.where((cc_ >= cs) & (cc_ < cs + 16), 0.0, NEG).astype(np.float32)
    a_, b_ = np.meshgrid(np.arange(64), np.arange(64), indexing="ij")
    UI, US, LI, LS = (a_ <= b_), (a_ < b_), (a_ >= b_), (a_ > b_)
    m["tmask"] = np.stack([UI, US, LI, LS, -1.0 * US, -1.0 * LS]).astype(np.float32)
    bo = np.zeros((128, 128), np.float32)
    bo[0:64, 0:64] = 1.0
    bo[64:128, 64:128] = 1.0
    m["bones"] = bo
    hl = cols(inp["hg_lb"])
    m["hgcol"] = np.ascontiguousarray(np.stack([hl[0], hl[1], hl[1]], axis=-1).astype(np.float32))
    m["hgng"] = cols(inp["hg_norm_g"])
    m["st_hg"] = np.ascontiguousarray(inp["state_hgrn"][b])
    m["rwmu"] = cols(inp["rw_mu"])
    m["rwmu_ad"] = np.ascontiguousarray(np.transpose(inp["rw_mu"][:, :, 832:896], (0, 2, 1)))
    m["rww0"] = cols(inp["rw_w0"])
    m["rw_w_up"] = inp["rw_w_up"]
    m["rw_a_up"] = inp["rw_a_up"]
    m["rw_g_up"] = inp["rw_g_up"]
    rk = inp["rw_r_k"].reshape(L, 256)
    m["rwcol"] = np.ascontiguousarray(np.stack([cols(inp["rw_a0"]), cols(inp["rw_k_k"]), cols(inp["rw_k_a"]), cols(rk),
                                                 cols(inp["rw_lnx_g"]), cols(inp["rw_lnx_b"])], axis=-1).astype(np.float32))
    m["st_rw"] = np.ascontiguousarray(inp["state_rwkv"][b])
    def scol(v):
        return cols(v.reshape(v.shape[:-2] + (1024,)))
    ldt = np.repeat(inp["s5_log_dt"][..., None], 64, axis=-1)
    m["s5lam"] = np.ascontiguousarray(np.stack([scol(inp["s5_a_re"]), scol(inp["s5_a_im"]), scol(ldt)], axis=-1).astype(np.float32))
    BT = np.zeros((L, 2, 8, 128, 128), np.float32)
    CT = np.zeros((L, 2, 8, 128, 128), np.float32)
    for r, (bsrc, csrc) in enumerate(((inp["s5_b_re"], inp["s5_c_re"]), (inp["s5_b_im"], inp["s5_c_im"]))):
        for j in range(8):
            for gl in range(2):
                g = 2 * j + gl
                r0 = 32 * (j % 4) + 16 * gl
                BT[:, r, j, r0:r0 + 16, 64 * gl:64 * gl + 64] = np.transpose(bsrc[:, g], (0, 2, 1))
                CT[:, r, j, 64 * gl:64 * gl + 64, r0:r0 + 16] = np.transpose(csrc[:, g], (0, 2, 1))
    m["s5BT"], m["s5CT"] = BT, CT
    m["s5col"] = np.ascontiguousarray(np.stack([cols(inp["s5_d"]), cols(inp["s5_glu_b"])], axis=-1).astype(np.float32))
    m["s5glu"] = inp["s5_glu_w"]
    st = inp["state_s5"][b].reshape(L, 2, 8, 128, 2)
    m["st_s5"] = np.ascontiguousarray(np.transpose(st, (0, 1, 3, 2, 4)))
    return m


_CACHE = {}


def kernel(**inputs):
    cfg = Cfg()
    inp = {k_: np.asarray(v) for k_, v in inputs.items()}
    if "nc" not in _CACHE:
        _CACHE["nc"] = build(cfg)
    nc, c = _CACHE["nc"]
    in_maps = [prep_core(inp, b, cfg) for b in range(8)]
    res = run_bass_kernel_spmd(nc, in_maps, core_ids=list(range(8)))
    R = res.results
    NP, TP = cfg.NP, cfg.TP
    y_p = np.concatenate([r["y_p"].reshape(NP, TP, D) for r in R], axis=0)
    y_s = np.stack([r["y_s"] for r in R], axis=0)
    nk = np.concatenate([r["nk"].reshape(NP, L, TP, 4, 64) for r in R], axis=0)
    nv = np.concatenate([r["nv"].reshape(NP, L, TP, 4, 64) for r in R], axis=0)
    nrw = np.concatenate([r["nrw"] for r in R], axis=0)
    ns5 = np.concatenate([r["ns5"] for r in R], axis=0)
    nhg = np.concatenate([r["nhg"] for r in R], axis=0)
    return (y_p.astype(np.float32), y_s.astype(np.float32), nk.astype(np.float32), nv.astype(np.float32),
            nrw.astype(np.float32), ns5.astype(np.float32), nhg.astype(np.float32))
```

```python
import bisect
import math
from contextlib import ExitStack
import numpy as np
import concourse.bass as bass
import concourse.mybir as mybir
from concourse.bass_utils import run_bass_kernel_spmd

F32 = mybir.dt.float32
BF16 = mybir.dt.bfloat16
AF = mybir.ActivationFunctionType
ALU = mybir.AluOpType
AX = mybir.AxisListType
EPOCH = 20000
NDMASEM = 12

D = 1024
L = 2
DFF = 2816
NF = DFF // 128
INC = 3328
ALPHA = (2 * L) ** 0.25
LN_EPS = 1e-5
RW_EPS = 64e-5
NEG = -30000.0


class Eng:
    def __init__(self, K, name, eng, is_pe=False):
        self.K = K
        self.name = name
        self.eng = eng
        self.is_pe = is_pe
        self.insts = []
        self.marks = []
        self.sems = []
        self.seen = {}
        self.dseen = {}

    def sem_for(self, m):
        e = (m - 1) // EPOCH
        while len(self.sems) <= e:
            self.sems.append(self.K.new_sem(f"{self.name}_e{len(self.sems)}"))
        return self.sems[e], (m - 1) % EPOCH + 1


class DmaQ:
    def __init__(self, K, name, issuer):
        self.K = K
        self.name = name
        self.issuer = issuer
        self.n = 0
        self.sems = [K.new_sem(f"{name}_d{i}") for i in range(NDMASEM)]

    def semval(self, i):
        return self.sems[i % NDMASEM], 16 * (i // NDMASEM + 1)


class Res:
    __slots__ = ("w", "r")

    def __init__(self):
        self.w = None
        self.r = {}


class K:
    def __init__(self, nc):
        self.nc = nc
        self.es = ExitStack()
        self.nsem = 0
        self.pe = Eng(self, "pe", nc.tensor, is_pe=True)
        self.dve = Eng(self, "dve", nc.vector)
        self.act = Eng(self, "act", nc.scalar)
        self.pool = Eng(self, "pool", nc.gpsimd)
        self.sp = Eng(self, "sp", nc.sync)
        self.engs = {e.name: e for e in (self.pe, self.dve, self.act, self.pool, self.sp)}
        self.ld = DmaQ(self, "ld", self.sp)
        self.st = DmaQ(self, "st", self.pool)
        self.dq = {"ld": self.ld, "st": self.st}
        self.res = {}
        self.ninst = 0
        self.nwait = 0
        self.rr = 0

    def new_sem(self, name):
        self.nsem += 1
        return self.es.enter_context(self.nc.semaphore(name))

    def sb(self, name, shape, dt=F32, stack=None):
        self.uid = getattr(self, "uid", 0) + 1
        return (stack or self.es).enter_context(self.nc.sbuf_tensor(f"{name}_u{self.uid}", list(shape), dt))

    def ps(self, name, shape, dt=F32, stack=None):
        return (stack or self.es).enter_context(self.nc.psum_tensor(name, list(shape), dt))

    def _wait_inst(self, E, xname, idx):
        X = self.engs[xname]
        if E is X and E.is_pe:
            return
        p = bisect.bisect_left(X.marks, idx)
        if p < len(X.marks):
            m = p + 1
        else:
            m = len(X.marks) + 1
            sem, val = X.sem_for(m)
            X.insts[idx].then_inc(sem, 1)
            X.marks.append(idx)
        if E.seen.get(xname, 0) >= m:
            return
        sem, val = X.sem_for(m)
        E.eng.wait_ge(sem, val)
        self.nwait += 1
        E.seen[xname] = m

    def _wait_dma(self, E, qname, i):
        Q = self.dq[qname]
        key = (qname, i % NDMASEM)
        if E.dseen.get(key, -1) >= i:
            return
        sem, val = Q.semval(i)
        E.eng.wait_ge(sem, val)
        self.nwait += 1
        E.dseen[key] = i

    def _wait(self, E, dep):
        if dep[0] == "dma":
            self._wait_dma(E, dep[1], dep[2])
        else:
            self._wait_inst(E, dep[1], dep[2])

    @staticmethod
    def _rkey(me):
        if me[0] == "i":
            return ("i", me[1])
        return ("dma", me[1], me[2] % NDMASEM)

    def _deps(self, reads, writes):
        deps = {}

        def add(d):
            k = self._rkey(d)
            if k not in deps or deps[k][2] < d[2]:
                deps[k] = d
        for r in reads:
            rs = self.res.get(r)
            if rs is not None and rs.w is not None:
                add(rs.w)
        for w in writes:
            rs = self.res.get(w)
            if rs is not None:
                if rs.w is not None:
                    add(rs.w)
                for d in rs.r.values():
                    add(d)
        return list(deps.values())

    def _update(self, me, reads, writes):
        k = self._rkey(me)
        for r in reads:
            rs = self.res.get(r)
            if rs is None:
                rs = self.res[r] = Res()
            rs.r[k] = me
        for w in writes:
            rs = self.res.get(w)
            if rs is None:
                rs = self.res[w] = Res()
            rs.w = me
            rs.r = {}

    def op(self, E, fn, reads=(), writes=()):
        bk = [r for r in reads if isinstance(r, tuple) and r and r[0] == "bank"]
        if bk:
            reads = [r for r in reads if r not in bk]
            writes = list(writes) + bk
        for d in self._deps(reads, writes):
            self._wait(E, d)
        h = fn()
        idx = len(E.insts)
        E.insts.append(h)
        self._update(("i", E.name, idx), reads, writes)
        self.ninst += 1
        return h

    def dma(self, Q, out, in_, reads=(), writes=(), **kw):
        E = Q.issuer
        for d in self._deps(reads, writes):
            self._wait(E, d)
        i = Q.n
        if i >= NDMASEM:
            self._wait_dma(E, Q.name, i - NDMASEM)
        sem, val = Q.semval(i)
        h = E.eng.dma_start(out=out, in_=in_, **kw)
        h.then_inc(sem, 16)
        Q.n += 1
        self._update(("dma", Q.name, i), reads, writes)
        self.ninst += 1
        return h

    def load(self, out, in_, reads=(), writes=(), **kw):
        return self.dma(self.ld, out, in_, reads, writes, **kw)

    def store(self, out, in_, reads=(), writes=(), **kw):
        return self.dma(self.st, out, in_, reads, writes, **kw)

    def barrier(self):
        for E in self.engs.values():
            for X in self.engs.values():
                if X.insts:
                    self._wait_inst(E, X.name, len(X.insts) - 1)
            for Q in self.dq.values():
                for i in range(max(0, Q.n - NDMASEM), Q.n):
                    self._wait_dma(E, Q.name, i)
        self.res = {}

    def E(self, e):
        return self.engs[e]

    def any2(self):
        self.rr += 1
        return "dve" if self.rr % 2 else "act"

    def mm(self, out, lhsT, rhs, start, stop, r, w):
        nc = self.nc
        return self.op(self.pe, lambda: nc.tensor.matmul(out, lhsT=lhsT, rhs=rhs, start=start, stop=stop), r, w)

    def tr(self, out, in_, ident, r, w):
        nc = self.nc
        return self.op(self.pe, lambda: nc.tensor.transpose(out, in_, ident), r, w)

    def actf(self, out, in_, func, r, w, scale=1.0, bias=None):
        nc = self.nc
        if bias is None:
            return self.op(self.act, lambda: nc.scalar.activation(out=out, in_=in_, func=func, scale=scale), r, w)
        return self.op(self.act, lambda: nc.scalar.activation(out=out, in_=in_, func=func, scale=scale, bias=bias), r, w)

    def cp(self, e, out, in_, r, w):
        nc = self.nc
        if e == "act":
            return self.op(self.act, lambda: nc.scalar.copy(out, in_), r, w)
        eng = nc.vector if e == "dve" else nc.gpsimd
        return self.op(self.E(e), lambda: eng.tensor_copy(out, in_), r, w)

    def tt(self, e, out, in0, in1, op, r, w):
        eng = self.nc.vector if e == "dve" else self.nc.gpsimd
        return self.op(self.E(e), lambda: eng.tensor_tensor(out=out, in0=in0, in1=in1, op=op), r, w)

    def ts(self, e, out, in0, s1, s2, op0, op1, r, w):
        eng = self.nc.vector if e == "dve" else self.nc.gpsimd
        if s2 is None:
            return self.op(self.E(e), lambda: eng.tensor_scalar(out=out, in0=in0, scalar1=s1, scalar2=None, op0=op0), r, w)
        return self.op(self.E(e), lambda: eng.tensor_scalar(out=out, in0=in0, scalar1=s1, scalar2=s2, op0=op0, op1=op1), r, w)

    def stt(self, out, in0, scalar, in1, op0, op1, r, w):
        nc = self.nc
        return self.op(self.dve, lambda: nc.vector.scalar_tensor_tensor(out=out, in0=in0, scalar=scalar, in1=in1, op0=op0, op1=op1), r, w)

    def memset(self, e, ap, val, w):
        eng = self.nc.vector if e == "dve" else self.nc.gpsimd
        return self.op(self.E(e), lambda: eng.memset(ap, val), (), w)

    def recip(self, out, in_, r, w):
        nc = self.nc
        return self.op(self.dve, lambda: nc.vector.reciprocal(out=out, in_=in_), r, w)

    def scan(self, out, d0, d1, init, r, w):
        nc = self.nc
        return self.op(self.dve, lambda: nc.vector.tensor_tensor_scan(out=out, data0=d0, data1=d1, initial=init, op0=ALU.mult, op1=ALU.add), r, w)


class Cfg:
    def __init__(self, TS=4096, NP=4, TP=256, debug=False, stages=None):
        self.TS, self.NP, self.TP = TS, NP, TP
        self.NTOK = TS + NP * TP
        self.debug = debug
        self.stages = stages
        tiles = []
        t = 0
        while t < TS:
            n = min(512, TS - t)
            tiles.append((t, n, 1))
            t += n
        while t < self.NTOK:
            n = min(512, self.NTOK - t)
            tiles.append((t, n, 0))
            t += n
        self.tiles = tiles
        self.seqs = [(0, TS, 1, 0)] + [(TS + p * TP, TP, 0, p) for p in range(NP)]


class Ctx:
    pass


def build(cfg):
    nc = bass.Bass("TRN2", target_bir_lowering=False)
    k = K(nc)
    c = Ctx()
    c.nc, c.k, c.cfg = nc, k, cfg
    NTOK, TS, NP, TP = cfg.NTOK, cfg.TS, cfg.NP, cfg.TP
    skind = "ExternalOutput" if cfg.debug else "Internal"

    def din(name, shape, dt=F32):
        return nc.dram_tensor(name, list(shape), dt, kind="ExternalInput").ap()

    def dout(name, shape):
        return nc.dram_tensor(name, list(shape), F32, kind="ExternalOutput").ap()

    def dscr(name, shape, dt=F32):
        return nc.dram_tensor(name, list(shape), dt, kind=skind).ap()

    I = c.I = {}
    I["xin"] = din("xin", [NTOK, D])
    I["ident"] = din("ident", [128, 128])
    I["ccol"] = din("ccol", [128, 8, 2])
    I["w_mod"] = din("w_mod", [L, D, 9 * D])
    I["bmod"] = din("bmod", [L, 128, 72])
    I["lng"] = din("lng", [L, 3, 128, 8])
    I["lnb"] = din("lnb", [L, 3, 128, 8])
    I["ffn_w_in"] = din("ffn_w_in", [L, 2, D, 2 * DFF])
    I["ffn_w_out"] = din("ffn_w_out", [L, 2, DFF, D])
    I["w_in"] = din("w_in", [L, D, INC])
    I["w_out"] = din("w_out", [L, D, D])
    I["cache_k"] = din("cache_k", [L, 256, 256])
    I["cache_v"] = din("cache_v", [L, 256, 256])
    I["rpbT"] = din("rpbT", [L, 64, 4, 15, 64])
    I["namask"] = din("namask", [64, 64])
    I["tmask"] = din("tmask", [6, 64, 64])
    I["bones"] = din("bones", [128, 128])
    I["hgcol"] = din("hgcol", [128, 2, 3])
    I["hgng"] = din("hgng", [L, 128, 2])
    I["st_hg"] = din("st_hg", [L, 2, 4, 64, 64])
    I["rwmu"] = din("rwmu", [L, 2, 128, 8])
    I["rwmu_ad"] = din("rwmu_ad", [L, 64, 2])
    I["rww0"] = din("rww0", [L, 2, 128, 2])
    I["rw_w_up"] = din("rw_w_up", [L, 2, 64, 256])
    I["rw_a_up"] = din("rw_a_up", [L, 64, 256])
    I["rw_g_up"] = din("rw_g_up", [L, 128, 256])
    I["rwcol"] = din("rwcol", [L, 128, 2, 6])
    I["st_rw"] = din("st_rw", [L, 2, 4, 64, 64])
    I["s5lam"] = din("s5lam", [L, 2, 128, 8, 3])
    I["s5BT"] = din("s5BT", [L, 2, 8, 128, 128])
    I["s5CT"] = din("s5CT", [L, 2, 8, 128, 128])
    I["s5col"] = din("s5col", [L, 128, 2, 2])
    I["s5glu"] = din("s5glu", [L, 256, 256])
    I["st_s5"] = din("st_s5", [L, 2, 128, 8, 2])
    O = c.O = {}
    O["y_s"] = dout("y_s", [TS, D])
    O["y_p"] = dout("y_p", [NP * TP, D])
    O["nk"] = dout("nk", [NP, L, TP, 256])
    O["nv"] = dout("nv", [NP, L, TP, 256])
    O["nhg"] = dout("nhg", [NP, L, 2, 4, 64, 64])
    O["nrw"] = dout("nrw", [NP, L, 2, 4, 64, 64])
    O["ns5"] = dout("ns5", [NP, L, 2, 16, 64, 2])
    S = c.S = {}
    S["XT"] = dscr("XT", [D, NTOK])
    S["X1T"] = dscr("X1T", [D, NTOK])
    S["PT"] = dscr("PT", [INC, NTOK])
    S["OT"] = dscr("OT", [D, NTOK])
    S["VTOK"] = dscr("VTOK", [NTOK, 256])
    for sfx in ("h", "r"):
        S["LA" + sfx] = dscr("LA" + sfx, [8, 256, NTOK])
        S["LX" + sfx] = dscr("LX" + sfx, [2, 256, NTOK])
    for sfx in ("h", "r", "5"):
        S["YS" + sfx] = dscr("YS" + sfx, [2, 256, NTOK])
    S["W1s"] = dscr("W1s", [L, 2, 44, 128, 1024], BF16)
    S["W2s"] = dscr("W2s", [L, 2, 8, 128, NF * 128], BF16)
    S["Wis"] = dscr("Wis", [L, 26, 128, 1024], BF16)
    S["Wkv"] = dscr("Wkv", [L, 128, 8, 512], BF16)
    S["Wos"] = dscr("Wos", [L, 8, 128, 1024], BF16)

    c.bank = [k.ps(f"bank{i}", [128, 512]) for i in range(8)]
    c.bi = 0
    c.freeb = list(range(8))
    c.ident = k.sb("ident_sb", [128, 128])
    k.load(c.ident[:], I["ident"], writes=["ident"])
    c.onesD = k.sb("onesD", [128, 128])
    k.memset("dve", c.onesD[:], 1.0 / D, ["onesD"])
    c.epsln = k.sb("epsln", [128, 1])
    k.memset("dve", c.epsln[:], LN_EPS / (ALPHA * ALPHA), ["epsln"])
    c.modc = k.sb("modc", [128, 72, 2])
    c.osc = k.sb("osc", [128, 3, 8, 2])
    c.gco = k.sb("gco", [128, 3, 8, 2])
    c.lng = k.sb("lng_sb", [128, L, 3, 8])
    c.lnb = k.sb("lnb_sb", [128, L, 3, 8])
    k.load(c.lng[:], I["lng"].rearrange("l i p c -> p l i c"), writes=["lng"])
    k.load(c.lnb[:], I["lnb"].rearrange("l i p c -> p l i c"), writes=["lnb"])
    c.tmask = k.sb("tmask_sb", [64, 6, 64])
    k.load(c.tmask[:], I["tmask"].rearrange("m a b -> a m b"), writes=["tmask"])
    c.bones = k.sb("bones_sb", [128, 128])
    k.load(c.bones[:], I["bones"], writes=["bones"])
    c.bones64 = k.sb("bones64_sb", [128, 128])
    k.ts("dve", c.bones64[:], c.bones[:], 1.0 / 64, None, ALU.mult, None, ["bones"], ["bones64"])

    st = cfg.stages
    if st is None or "cast" in st:
        phase_cast(c)
    if st is None or "t0" in st:
        phase_transpose_in(c)
    for l in range(L):
        if st is None or "mod" in st:
            phase_mod(c, l)
        if st is None or "A" in st:
            phase_A(c, l)
        if st is None or "mix" in st:
            esp = ExitStack()
            run_concurrent([phase_rw(c, l, "prep_g", esp), phase_hg(c, l, "prep_g", esp)])
            k.barrier()
            esp.close()
            es1 = ExitStack()
            g1 = phase_rw(c, l, "scan", es1, NL=2)
            g2 = phase_s5(c, l, "scan", es1, NW=1)
            run_concurrent([g1, g2])
            k.barrier()
            es1.close()
            es2 = ExitStack()
            g3 = phase_hg(c, l, "scan", es2, NL=3)
            g4 = phase_attn(c, l, "scan", es2)
            g5 = phase_rw(c, l, "out_g", es2)
            g6 = phase_s5(c, l, "out_g", es2)
            run_concurrent([g3, g4, g5, g6])
            k.barrier()
            es2.close()
            phase_hg(c, l, "out")
        if st is not None and "attn" in st:
            phase_attn(c, l)
        if st is not None and "hg" in st:
            phase_hg(c, l)
        if st is not None and "rw" in st:
            phase_rw(c, l)
        if st is not None and "s5" in st:
            phase_s5(c, l)
        if st is not None and "A_only" in st:
            break
        if st is None or "C" in st:
            phase_C(c, l)
        if st is not None and "L0_only" in st:
            break
    assert sorted(c.freeb) == list(range(8)), c.freeb
    if st is None or "tout" in st:
        phase_transpose_out(c)
    k.barrier()
    c.k.es_keep = k.es
    return nc, c


def nextbank(c):
    i = c.freeb.pop(0)
    c.freeb.append(i)
    return c.bank[i], ("bank", i)


def acquire(c):
    while len(c.freeb) <= 2:
        yield
    i = c.freeb.pop(0)
    return c.bank[i], ("bank", i)


def release(c, pk):
    c.freeb.append(pk[1])


def phase_cast(c):
    k, nc, I, S = c.k, c.nc, c.I, c.S
    es = ExitStack()
    NB = 2
    f32t = [k.sb(f"cast_f{i}", [128, 4 * 1024], F32, es) for i in range(NB)]
    b16t = [k.sb(f"cast_b{i}", [128, 4 * 1024], BF16, es) for i in range(NB)]
    cnt = [0]
    engs = ["dve", "act", "pool"]

    def job(pairs, per):
        i = cnt[0] % NB
        e = engs[cnt[0] % 3]
        cnt[0] += 1
        n = per * len(pairs)
        for j, (src_ap, dst_ap) in enumerate(pairs):
            a_, b_ = src_ap.shape[1], src_ap.shape[2]
            k.load(f32t[i][:, j * per:(j + 1) * per].rearrange("p (a b) -> p a b", a=a_), src_ap, writes=[("cf", i, j)])
        k.cp(e, b16t[i][:, 0:n], f32t[i][:, 0:n], [("cf", i, j) for j in range(len(pairs))], [("cb", i)])
        for j, (src_ap, dst_ap) in enumerate(pairs):
            k.store(dst_ap, b16t[i][:, j * per:(j + 1) * per], reads=[("cb", i)])

    for l in range(L):
        for f in range(2):
            src = I["ffn_w_in"][l, f].rearrange("(kc p) n -> p kc n", p=128)
            for g0 in range(0, 44, 4):
                job([(src[:, :, g * 128:(g + 1) * 128], S["W1s"][l, f, g]) for g in range(g0, g0 + 4)], 1024)
            src = I["ffn_w_out"][l, f].rearrange("(fc p) n -> p fc n", p=128)
            for g in range(8):
                job([(src[:, :, g * 128:(g + 1) * 128], S["W2s"][l, f, g])], NF * 128)
        src = I["w_in"][l].rearrange("(kc p) n -> p kc n", p=128)
        for g0 in range(0, 26, 2):
            job([(src[:, :, g * 128:(g + 1) * 128], S["Wis"][l, g]) for g in range(g0, g0 + 2)], 1024)
        job([(src[:, :, 2816:3328], S["Wkv"][l].rearrange("p kc n -> p (kc n)"))], 4096)
        src = I["w_out"][l].rearrange("(kc p) n -> p kc n", p=128)
        for g0 in range(0, 8, 4):
            job([(src[:, :, g * 128:(g + 1) * 128], S["Wos"][l, g]) for g in range(g0, g0 + 4)], 1024)
    k.barrier()
    es.close()


def phase_transpose_in(c):
    k, nc, cfg = c.k, c.nc, c.cfg
    es = ExitStack()
    XTv = c.S["XT"].rearrange("(c p) t -> p c t", p=128)
    NB = 2
    xin_t = [k.sb(f"ti_x{i}", [128, 4, D], F32, es) for i in range(NB)]
    xT_t = [k.sb(f"ti_xT{i}", [128, 8, 512], F32, es) for i in range(NB)]
    for ti, (t0, n, var) in enumerate(cfg.tiles):
        b = ti % NB
        ns = n // 128
        k.load(xin_t[b][:, 0:ns, :], c.I["xin"][t0:t0 + n, :].rearrange("(s p) d -> p s d", p=128), writes=[("tix", b)])
        for ch in range(8):
            p, pk = nextbank(c)
            for s in range(ns):
                k.tr(p[:, s * 128:(s + 1) * 128], xin_t[b][:, s, ch * 128:(ch + 1) * 128], c.ident[:], [("tix", b), "ident"], [pk])
            k.cp(k.any2(), xT_t[b][:, ch, 0:n], p[:, 0:n], [pk], [("tixT", b, ch)])
        k.store(XTv[:, :, t0:t0 + n], xT_t[b][:, :, 0:n], reads=[("tixT", b, ch) for ch in range(8)], writes=[("XT", ti)])
    k.barrier()
    es.close()


def phase_transpose_out(c):
    k, nc, cfg = c.k, c.nc, c.cfg
    es = ExitStack()
    XTv = c.S["XT"].rearrange("(c p) t -> p c t", p=128)
    NB = 2
    xT_t = [k.sb(f"to_xT{i}", [128, 8, 512], F32, es) for i in range(NB)]
    yt = [k.sb(f"to_y{i}", [128, 4, D], F32, es) for i in range(NB)]
    for ti, (t0, n, var) in enumerate(cfg.tiles):
        b = ti % NB
        ns = n // 128
        k.load(xT_t[b][:, :, 0:n], XTv[:, :, t0:t0 + n], reads=[("XT", ti)], writes=[("toxT", b)])
        for s in range(ns):
            for h in range(2):
                p, pk = nextbank(c)
                for cc in range(4):
                    ch = h * 4 + cc
                    k.tr(p[:, cc * 128:(cc + 1) * 128], xT_t[b][:, ch, s * 128:(s + 1) * 128], c.ident[:], [("toxT", b), "ident"], [pk])
                k.cp(k.any2(), yt[b][:, s, h * 512:(h + 1) * 512], p[:], [pk], [("toy", b, s, h)])
        if var == 1:
            dst = c.O["y_s"][t0:t0 + n, :]
        else:
            dst = c.O["y_p"][t0 - cfg.TS:t0 - cfg.TS + n, :]
        k.store(dst.rearrange("(s p) d -> p s d", p=128), yt[b][:, 0:ns, :],
                reads=[("toy", b, s, h) for s in range(ns) for h in range(2)])
    k.barrier()
    es.close()


def phase_mod(c, l):
    k, nc, I = c.k, c.nc, c.I
    es = ExitStack()
    ccol = k.sb("mod_c", [128, 8, 2], F32, es)
    csil = k.sb("mod_cs", [128, 8, 2], F32, es)
    bm = k.sb("mod_bm", [128, 72], F32, es)
    k.load(ccol[:], I["ccol"], writes=["ccol"])
    k.load(bm[:], I["bmod"][l], writes=["bm"])
    k.actf(csil[:], ccol[:], AF.Silu, ["ccol"], ["csil"])
    FB = 1152
    NB = 2
    wt = [k.sb(f"mod_w{i}", [128, 8, FB], F32, es) for i in range(NB)]
    p, pk = nextbank(c)
    wv = I["w_mod"][l].rearrange("(kc p) f -> p kc f", p=128)
    for bi in range(9 * D // FB):
        b = bi % NB
        k.load(wt[b][:], wv[:, :, bi * FB:(bi + 1) * FB], writes=[("modw", b)])
        for fj in range(FB // 128):
            f = bi * (FB // 128) + fj
            for kc in range(8):
                k.mm(p[:, 2 * f:2 * f + 2], wt[b][:, kc, fj * 128:(fj + 1) * 128], csil[:, kc, :], kc == 0, kc == 7,
                     [("modw", b), "csil"], [pk])
    k.tt("dve", c.modc[:], p[:, 0:144].rearrange("p (f n) -> p f n", n=2), bm[:].unsqueeze(2).to_broadcast([128, 72, 2]), ALU.add,
         [pk, "bm"], ["modc"])
    for i in range(3):
        k.ts("dve", c.osc[:, i], c.modc[:, (3 * i + 1) * 8:(3 * i + 2) * 8, :], 1.0, None, ALU.add, None, ["modc"], ["osc"])
        coef = (0.5 if i != 1 else 1.0) / ALPHA
        k.ts("dve", c.gco[:, i], c.modc[:, (3 * i + 2) * 8:(3 * i + 3) * 8, :], coef, None, ALU.mult, None, ["modc"], ["gco"])
    k.barrier()
    es.close()


def layernorm(c, z, zk, n, gcol, bcol, xout, xoutk, tmp, li):
    k, nc = c.k, c.nc
    sq, msq, rstd = tmp["sq"], tmp["msq"], tmp["rstd"]
    pm, pmk = nextbank(c)
    pe2, pe2k = nextbank(c)
    for ch in range(8):
        k.actf(sq[:, ch, 0:n], z[:, ch, 0:n], AF.Square, [(zk, ch)], [("lnsq", ch)])
    for ch in range(8):
        k.mm(pm[:, 0:n], c.onesD[:], z[:, ch, 0:n], ch == 0, ch == 7, [(zk, ch), "onesD"], [pmk])
    for ch in range(8):
        k.mm(pe2[:, 0:n], c.onesD[:], sq[:, ch, 0:n], ch == 0, ch == 7, [("lnsq", ch), "onesD"], [pe2k])
    k.actf(msq[:, 0:n], pm[:, 0:n], AF.Square, [pmk], ["lnmsq"])
    k.tt("dve", msq[:, 0:n], pe2[:, 0:n], msq[:, 0:n], ALU.subtract, [pe2k, "lnmsq"], ["lnmsq"])
    k.actf(rstd[:, 0:n], msq[:, 0:n], AF.Sqrt, ["lnmsq", "epsln"], ["lnrstd"], bias=c.epsln[:, 0:1])
    k.recip(rstd[:, 0:n], rstd[:, 0:n], ["lnrstd"], ["lnrstd"])
    for ch in range(8):
        k.tt("dve", sq[:, ch, 0:n], z[:, ch, 0:n], pm[:, 0:n], ALU.subtract, [(zk, ch), pmk, ("lnsq", ch)], [("lnsq", ch)])
        e = "pool" if ch % 2 else "dve"
        k.tt(e, sq[:, ch, 0:n], sq[:, ch, 0:n], rstd[:, 0:n], ALU.mult, [("lnsq", ch), "lnrstd"], [("lnsq", ch)])
        k.actf(xout[:, ch, 0:n], sq[:, ch, 0:n], AF.Identity, [("lnsq", ch), "lng", "lnb"], [(xoutk, ch)],
               scale=gcol[:, ch:ch + 1], bias=bcol[:, ch:ch + 1])


def modulate(c, x, xk, n, i, var, xm, xmk):
    k = c.k
    for ch in range(8):
        k.actf(xm[:, ch, 0:n], x[:, ch, 0:n], AF.Identity, [(xk, ch), "osc", "modc"], [(xmk, ch)],
               scale=c.osc[:, i, ch, var:var + 1], bias=c.modc[:, (3 * i) * 8 + ch, var:var + 1])


def ffn(c, l, f, xm, xmk, n, h, zres, zresk, i, var, wb):
    k, nc, S = c.k, c.nc, c.S
    w1, w2, sg = wb["w1"], wb["w2"], wb["sg"]
    NW1 = len(w1)
    order = []
    for fc in range(NF):
        order.append(fc)
        order.append(NF + fc)

    def ldw1(j):
        b = j % NW1
        k.load(w1[b][:], S["W1s"][l, f, order[j]], writes=[("w1", b)])
    PF = NW1 - 1
    for j in range(min(PF, len(order))):
        ldw1(j)
    for fc in range(NF):
        banks = []
        for half in range(2):
            j = 2 * fc + half
            if j + PF < len(order):
                ldw1(j + PF)
            b = j % NW1
            p, pk = nextbank(c)
            banks.append((p, pk))
            for kc in range(8):
                k.mm(p[:, 0:n], w1[b][:, kc * 128:(kc + 1) * 128], xm[:, kc, 0:n], kc == 0, kc == 7, [("w1", b), (xmk, kc)], [pk])
        (pg, pgk), (pu, puk) = banks
        sb_ = fc % 2
        k.actf(sg[sb_][:, 0:n], pg[:, 0:n], AF.Silu, [pgk], [("sg", sb_)])
        k.tt("dve", h[:, fc, 0:n], sg[sb_][:, 0:n], pu[:, 0:n], ALU.mult, [("sg", sb_), puk], [("h", fc)])
    NW2 = len(w2)
    for dc in range(min(NW2 - 1, 8)):
        k.load(w2[dc % NW2][:], S["W2s"][l, f, dc], writes=[("w2", dc % NW2)])
    for dc in range(8):
        if dc + NW2 - 1 < 8:
            d2 = dc + NW2 - 1
            k.load(w2[d2 % NW2][:], S["W2s"][l, f, d2], writes=[("w2", d2 % NW2)])
        b = dc % NW2
        p, pk = nextbank(c)
        for fc in range(NF):
            k.mm(p[:, 0:n], w2[b][:, fc * 128:(fc + 1) * 128], h[:, fc, 0:n], fc == 0, fc == NF - 1, [("w2", b), ("h", fc)], [pk])
        k.stt(zres[:, dc, 0:n], p[:, 0:n], c.gco[:, i, dc, var:var + 1], zres[:, dc, 0:n], ALU.mult, ALU.add,
              [pk, "gco", (zresk, dc)], [(zresk, dc)])


def alloc_AC(c, es):
    k = c.k
    t = {}
    t["x"] = [k.sb(f"ac_x{i}", [128, 8, 512], F32, es) for i in range(2)]
    t["x1"] = k.sb("ac_x1", [128, 8, 512], F32, es)
    t["xm"] = k.sb("ac_xm", [128, 8, 512], BF16, es)
    t["h"] = k.sb("ac_h", [128, NF, 512], BF16, es)
    t["ln"] = {"sq": k.sb("ac_sq", [128, 8, 512], F32, es), "msq": k.sb("ac_msq", [128, 512], F32, es),
               "rstd": k.sb("ac_rstd", [128, 512], F32, es)}
    t["wb"] = {"w1": [k.sb(f"ac_w1_{i}", [128, 1024], BF16, es) for i in range(4)],
               "w2": [k.sb(f"ac_w2_{i}", [128, NF * 128], BF16, es) for i in range(2)],
               "sg": [k.sb(f"ac_sg{i}", [128, 512], F32, es) for i in range(2)]}
    t["wi"] = [k.sb(f"ac_wi{i}", [128, 1024], BF16, es) for i in range(4)]
    t["wkv"] = k.sb("ac_wkv", [128, 8, 512], BF16, es)
    t["pb"] = [k.sb(f"ac_pb{i}", [128, 2, 512], F32, es) for i in range(2)]
    t["tok"] = [k.sb(f"ac_tok{i}", [128, 512], F32, es) for i in range(2)]
    return t


def phase_A(c, l):
    k, nc, cfg, S, O = c.k, c.nc, c.cfg, c.S, c.O
    es = ExitStack()
    t = alloc_AC(c, es)
    XTv = S["XT"].rearrange("(c p) t -> p c t", p=128)
    X1Tv = S["X1T"].rearrange("(c p) t -> p c t", p=128)
    PTv = S["PT"].rearrange("(c p) t -> p c t", p=128)
    k.load(t["wkv"][:], S["Wkv"][l], writes=["wkv"])
    tiles = cfg.tiles

    def ldx(ti):
        t0, n, var = tiles[ti]
        b = ti % 2
        k.load(t["x"][b][:, :, 0:n], XTv[:, :, t0:t0 + n], reads=[("XT", ti)], writes=[(("x", b), ch) for ch in range(8)])
    ldx(0)
    for ti, (t0, n, var) in enumerate(tiles):
        b = ti % 2
        x, xk = t["x"][b], ("x", b)
        if ti + 1 < len(tiles):
            ldx(ti + 1)
        modulate(c, x, xk, n, 0, var, t["xm"], "xm")
        ffn(c, l, 0, t["xm"], "xm", n, t["h"], x, xk, 0, var, t["wb"])
        layernorm(c, x, xk, n, c.lng[:, l, 0, :], c.lnb[:, l, 0, :], t["x1"], "x1", t["ln"], 0)
        k.store(X1Tv[:, :, t0:t0 + n], t["x1"][:, :, 0:n], reads=[("x1", ch) for ch in range(8)], writes=[("X1T", ti)])
        modulate(c, t["x1"], "x1", n, 1, var, t["xm"], "xm")
        wi = t["wi"]
        NWI = len(wi)
        for j in range(NWI - 1):
            k.load(wi[j][:], S["Wis"][l, j], writes=[("wi", j)])
        for cc in range(26):
            if cc + NWI - 1 < 26:
                j = cc + NWI - 1
                k.load(wi[j % NWI][:], S["Wis"][l, j], writes=[("wi", j % NWI)])
            b2 = cc % NWI
            p, pk = nextbank(c)
            for kc in range(8):
                k.mm(p[:, 0:n], wi[b2][:, kc * 128:(kc + 1) * 128], t["xm"][:, kc, 0:n], kc == 0, kc == 7, [("wi", b2), ("xm", kc)], [pk])
            pbi = (cc // 2) % 2
            k.cp(k.any2(), t["pb"][pbi][:, cc % 2, 0:n], p[:, 0:n], [pk], [("pb", pbi, cc % 2)])
            if cc % 2 == 1:
                k.store(PTv[:, cc - 1:cc + 1, t0:t0 + n], t["pb"][pbi][:, :, 0:n], reads=[("pb", pbi, 0), ("pb", pbi, 1)], writes=[("PT", ti)])
        for s in range(n // 128):
            p, pk = nextbank(c)
            for kc in range(8):
                k.mm(p[:, :], t["xm"][:, kc, s * 128:(s + 1) * 128], t["wkv"][:, kc, :], kc == 0, kc == 7, [("xm", kc), "wkv"], [pk])
            tb = s % 2
            k.cp(k.any2(), t["tok"][tb][:], p[:], [pk], [("tok", tb)])
            ta = t0 + s * 128
            k.store(S["VTOK"][ta:ta + 128, :], t["tok"][tb][:, 256:512], reads=[("tok", tb)], writes=[("VTOK", ti)])
            if var == 0:
                q = ta - cfg.TS
                pi_, tt_ = q // cfg.TP, q % cfg.TP
                k.store(O["nk"][pi_, l, tt_:tt_ + 128, :], t["tok"][tb][:, 0:256], reads=[("tok", tb)])
                k.store(O["nv"][pi_, l, tt_:tt_ + 128, :], t["tok"][tb][:, 256:512], reads=[("tok", tb)])
    k.barrier()
    es.close()


def phase_C(c, l):
    k, nc, cfg, S = c.k, c.nc, c.cfg, c.S
    es = ExitStack()
    t = alloc_AC(c, es)
    XTv = S["XT"].rearrange("(c p) t -> p c t", p=128)
    X1Tv = S["X1T"].rearrange("(c p) t -> p c t", p=128)
    OTv = S["OT"].rearrange("(c p) t -> p c t", p=128)
    tiles = cfg.tiles
    ot = t["ln"]["sq"]
    wo = t["wi"]
    for ti, (t0, n, var) in enumerate(tiles):
        x1 = t["x1"]
        k.load(x1[:, :, 0:n], X1Tv[:, :, t0:t0 + n], reads=[("X1T", ti)], writes=[("x1", ch) for ch in range(8)])
        k.load(ot[:, :, 0:n], OTv[:, :, t0:t0 + n], reads=[("OT", ti)], writes=[("lnsq", ch) for ch in range(8)])
        for ch in range(8):
            k.cp(k.any2(), t["xm"][:, ch, 0:n], ot[:, ch, 0:n], [("lnsq", ch)], [("xm", ch)])
        NWO = len(wo)
        for j in range(NWO - 1):
            k.load(wo[j][:], S["Wos"][l, j], writes=[("wi", j)])
        for dc in range(8):
            if dc + NWO - 1 < 8:
                j = dc + NWO - 1
                k.load(wo[j % NWO][:], S["Wos"][l, j], writes=[("wi", j % NWO)])
            b2 = dc % NWO
            p, pk = nextbank(c)
            for kc in range(8):
                k.mm(p[:, 0:n], wo[b2][:, kc * 128:(kc + 1) * 128], t["xm"][:, kc, 0:n], kc == 0, kc == 7, [("wi", b2), ("xm", kc)], [pk])
            k.stt(x1[:, dc, 0:n], p[:, 0:n], c.gco[:, 1, dc, var:var + 1], x1[:, dc, 0:n], ALU.mult, ALU.add,
                  [pk, "gco", ("x1", dc)], [("x1", dc)])
        x2 = t["x"][0]
        layernorm(c, x1, "x1", n, c.lng[:, l, 1, :], c.lnb[:, l, 1, :], x2, ("x", 0), t["ln"], 1)
        modulate(c, x2, ("x", 0), n, 2, var, t["xm"], "xm")
        ffn(c, l, 1, t["xm"], "xm", n, t["h"], x2, ("x", 0), 2, var, t["wb"])
        x3 = t["x"][1]
        layernorm(c, x2, ("x", 0), n, c.lng[:, l, 2, :], c.lnb[:, l, 2, :], x3, ("x", 1), t["ln"], 2)
        k.store(XTv[:, :, t0:t0 + n], x3[:, :, 0:n], reads=[(("x", 1), ch) for ch in range(8)], writes=[("XT", ti)])
    k.barrier()
    es.close()


def attn_core_gen(c, q_ap, nq, A, bias_ap, B, out_ap, rkeys, okey, T, slot):
    k = c.k
    ev = []
    na, nb = len(A), len(B)
    if A:
        pa, pak = yield from acquire(c)
        for i, (kt, v) in enumerate(A):
            k.mm(pa[0:64, i * nq:(i + 1) * nq], kt, q_ap, True, True, rkeys, [pak])
    if B:
        pb, pbk = yield from acquire(c)
        for i, (kt, v) in enumerate(B):
            k.mm(pb[0:64, i * nq:(i + 1) * nq], kt, q_ap, True, True, rkeys, [pbk])
    yield
    if A:
        ea, eak = T["EA"][slot], ("EA", slot)
        eab, eabk = T["EAb"][slot], ("EAb", slot)
        k.stt(ea[:, 0:na, 0:nq], pa[0:64, 0:na * nq].rearrange("p (a q) -> p a q", q=nq), 0.125, bias_ap, ALU.mult, ALU.add,
              [pak, "at_B"], [eak])
        release(c, pak)
        k.actf(eab[:, 0:na, 0:nq], ea[:, 0:na, 0:nq], AF.Exp, [eak], [eabk])
        for i, (kt, v) in enumerate(A):
            ev.append((eab[:, i, 0:nq], v, eabk))
    if B:
        eb, ebk = T["EB"][slot], ("EB", slot)
        k.actf(eb[:, 0:nb * nq], pb[0:64, 0:nb * nq], AF.Exp, [pbk], [ebk], scale=0.125)
        release(c, pbk)
        for i, (kt, v) in enumerate(B):
            ev.append((eb[:, i * nq:(i + 1) * nq], v, ebk))
    yield
    pn, pnk = yield from acquire(c)
    n = len(ev)
    for i, (e, v, ek) in enumerate(ev):
        k.mm(pn[0:nq, 0:65], e, v, i == 0, i == n - 1, rkeys + [ek], [pnk])
    yield
    rd, rdk = T["rden"][slot], ("rden", slot)
    otk, otkk = T["otok"][slot], ("otok", slot)
    k.recip(rd[0:nq, 0:1], pn[0:nq, 64:65], [pnk], [rdk])
    k.ts("dve", otk[0:nq, :], pn[0:nq, 0:64], rd[0:nq, 0:1], None, ALU.mult, None, [pnk, rdk], [otkk])
    yield
    k.tr(pn[0:64, 128:128 + nq], otk[0:nq, :], c.ident[0:nq, 0:nq], [otkk, "ident"], [pnk])
    yield
    k.cp("act", out_ap, pn[0:64, 128:128 + nq], [pnk], [okey])
    release(c, pnk)
    yield


def phase_attn(c, l, part=None, es_ext=None):
    k, nc, cfg, S, I = c.k, c.nc, c.cfg, c.S, c.I
    es = ExitStack() if es_ext is None else es_ext
    TS, TP = cfg.TS, cfg.TP
    R = TS // 64
    Tmax = max(TS, TP)
    NS = 2
    stg = k.sb("at_stg", [64, Tmax], F32, es)
    qT = k.sb("at_q", [64, Tmax], BF16, es)
    kT = k.sb("at_k", [64, Tmax], BF16, es)
    vt = k.sb("at_v", [64, Tmax // 64, 65], BF16, es)
    ot = k.sb("at_o", [64, Tmax], F32, es)
    T = {}
    T["EA"] = [k.sb(f"at_ea{i}", [64, 8, 64], F32, es) for i in range(NS)]
    T["EAb"] = [k.sb(f"at_eab{i}", [64, 8, 64], BF16, es) for i in range(NS)]
    T["EB"] = [k.sb(f"at_eb{i}", [64, 512], BF16, es) for i in range(NS)]
    T["rden"] = [k.sb(f"at_rd{i}", [128, 1], F32, es) for i in range(NS)]
    T["otok"] = [k.sb(f"at_otok{i}", [128, 64], F32, es) for i in range(NS)]
    Bt = k.sb("at_B", [64, 15, 64], F32, es)
    mask = k.sb("at_mask", [64, 64], F32, es)
    kctok = k.sb("at_kctok", [64, 4, 64], F32, es)
    kcT = k.sb("at_kcT", [64, 4, 64], BF16, es)
    vcs = k.sb("at_vcs", [64, 4, 64], F32, es)
    vc = k.sb("at_vc", [64, 4, 65], BF16, es)
    k.load(mask[:], I["namask"], writes=["at_mask"])
    k.memset("dve", vt[:, :, 64:65], 1.0, ["at_v1"])
    k.memset("dve", vc[:, :, 64:65], 1.0, ["at_vc1"])
    rk = ["at_q", "at_k", "at_v", "at_kcT", "at_vc", "at_v1", "at_vc1"]
    slots = list(range(NS))

    def gen():
        for h in range(4):
            hs = slice(64 * h, 64 * h + 64)

            def load_qkv(s0, Tn):
                k.load(stg[:, 0:Tn], S["PT"][2560 + 64 * h:2560 + 64 * h + 64, s0:s0 + Tn], writes=["at_stg"])
                k.cp("act", qT[:, 0:Tn], stg[:, 0:Tn], ["at_stg"], ["at_q"])
                k.load(stg[:, 0:Tn], S["PT"][2816 + 64 * h:2816 + 64 * h + 64, s0:s0 + Tn], writes=["at_stg"])
                k.cp("dve", kT[:, 0:Tn], stg[:, 0:Tn], ["at_stg"], ["at_k"])
                nr = Tn // 64
                k.load(stg[:, 0:Tn].rearrange("p (r d) -> p r d", d=64), S["VTOK"][s0:s0 + Tn, hs].rearrange("(r c) d -> c r d", c=64),
                       writes=["at_stg"])
                k.cp("pool", vt[:, 0:nr, 0:64], stg[:, 0:Tn].rearrange("p (r d) -> p r d", d=64), ["at_stg"], ["at_v"])
            load_qkv(0, TS)
            k.load(Bt[:], I["rpbT"][l, :, h], writes=["at_B"])
            k.tt("dve", Bt[:], Bt[:], mask[:].unsqueeze(1).to_broadcast([64, 15, 64]), ALU.add, ["at_B", "at_mask"], ["at_B"])
            k.load(kctok[:], I["cache_k"][l][:, hs].rearrange("(ch c) d -> c ch d", c=64), writes=["at_kctok"])
            k.load(vcs[:], I["cache_v"][l][:, hs].rearrange("(ch c) d -> c ch d", c=64), writes=["at_vcs"])
            k.cp("dve", vc[:, :, 0:64], vcs[:], ["at_vcs"], ["at_vc"])
            p, pk = yield from acquire(c)
            for ch in range(4):
                k.tr(p[0:64, ch * 64:(ch + 1) * 64], kctok[:, ch, :], c.ident[0:64, 0:64], ["at_kctok", "ident"], [pk])
            k.cp("dve", kcT[:].rearrange("p a b -> p (a b)"), p[0:64, 0:256], [pk], ["at_kcT"])
            release(c, pk)
            yield
            jobs = []
            kr = min(8, R)
            for r in range(R):
                rs = min(max(r - kr // 2, 0), R - kr)
                dr0 = rs - r + 7
                A = [(kT[:, (rs + i) * 64:(rs + i + 1) * 64], vt[:, rs + i, :]) for i in range(kr)]
                B = [(kcT[:, j, :], vc[:, j, :]) for j in range(4)]
                jobs.append(lambda slot, r=r, A=A, B=B, dr0=dr0: attn_core_gen(
                    c, qT[:, r * 64:(r + 1) * 64], 64, A, Bt[:, dr0:dr0 + kr, :], B, ot[:, r * 64:(r + 1) * 64], rk, ("at_o", r), T, slot))
            yield from drive_gen(jobs, slots)
            k.store(S["OT"][768 + 64 * h:768 + 64 * h + 64, 0:TS], ot[:, 0:TS], reads=[("at_o", r) for r in range(R)])
            yield
            for (s0, Tn, kind, pi_) in cfg.seqs[1:]:
                load_qkv(s0, Tn)
                yield
                jobs = []
                B = [(kT[:, j * 64:(j + 1) * 64], vt[:, j, :]) for j in range(Tn // 64)]
                for qb in range(Tn // 128):
                    jobs.append(lambda slot, qb=qb, B=B: attn_core_gen(
                        c, qT[:, qb * 128:(qb + 1) * 128], 128, [], None, B, ot[:, qb * 128:(qb + 1) * 128], rk, ("at_o", qb), T, slot))
                yield from drive_gen(jobs, slots)
                k.store(S["OT"][768 + 64 * h:768 + 64 * h + 64, s0:s0 + Tn], ot[:, 0:Tn], reads=[("at_o", qb) for qb in range(Tn // 128)])
                yield

    g = gen()
    if part == "scan":
        return g
    run_concurrent([g])
    k.barrier()
    es.close()


def la_alloc(c, es, delta, CH, nch, tag):
    k = c.k
    TT = nch * CH
    t = {"TT": TT, "nch": nch, "C": CH, "tag": tag}

    def fm(name):
        return k.sb(f"la{tag}_{name}", [64, TT], F32, es)

    def tk(name):
        return k.sb(f"la{tag}_{name}", [64, nch, 64], F32, es)
    t["d"] = []
    for d in range(2):
        u = {}
        for nm in ["K", "LW", "cum", "cumc", "Eabs", "Erel", "Em", "Qabs", "Qrel", "Kd", "Ke", "ybuf", "R", "V"]:
            u[nm] = fm(f"{nm}{d}")
        u["Gm"] = k.sb(f"la{tag}_Gm{d}", [64, nch], F32, es)
        u["KeT"], u["RKT"], u["Vm"] = tk(f"KeT{d}"), tk(f"RKT{d}"), tk(f"Vm{d}")
        u["S"] = [k.sb(f"la{tag}_S{d}_{i}", [64, 64], F32, es) for i in range(2)]
        u["si"] = 0
        if delta:
            for nm in ["KK", "BK", "cp", "E0", "KKabs", "KKrel", "Bd", "Be"]:
                u[nm] = fm(f"{nm}{d}")
            for nm in ["BeT", "RBT", "AkT", "M", "P", "IP", "Q", "M2", "P2"]:
                u[nm] = tk(f"{nm}{d}")
            u["rhs0"] = k.sb(f"la{tag}_rhs0{d}", [64, 64], F32, es)
            u["U"] = k.sb(f"la{tag}_U{d}", [64, 64], F32, es)
        t["d"].append(u)
    t["stmp"] = k.sb(f"la{tag}_stmp", [64, 64], F32, es)
    return t


def la_consts(c, es, CH, nch):
    k = c.k
    TT = CH * nch
    cst = {}
    cst["rmask"] = k.sb("la_rmask", [64, TT], F32, es)
    cst["rmaskb"] = k.sb("la_rmaskb", [64, TT], F32, es)
    k.memset("dve", cst["rmask"][:], 1.0, ["rmask"])
    k.memset("dve", cst["rmask"][:].rearrange("p (a b) -> p a b", b=CH)[:, :, 0:1], 0.0, ["rmask"])
    k.memset("dve", cst["rmaskb"][:], 1.0, ["rmask"])
    k.memset("dve", cst["rmaskb"][:].rearrange("p (a b) -> p a b", b=CH)[:, :, CH - 1:CH], 0.0, ["rmask"])
    return cst


def la_lane(c, t, cst, delta, arrs, YS, seq, st_in, st_out, transpose_state):
    k = c.k
    s0, T = seq
    CH = t["C"]
    MID = CH // 2
    TT = t["TT"]
    nch = t["nch"]
    assert T % TT == 0
    ntile = T // TT
    tm = c.tmask
    I64 = c.ident[0:64, 0:64]
    tag = t["tag"]
    U_ = t["d"]
    DK = [(tag, 0), (tag, 1)]

    def v3(ap):
        return ap[:, 0:TT].rearrange("p (a b) -> p a b", b=CH)

    def bc(ap2):
        return ap2[:, 0:nch].unsqueeze(2).to_broadcast([64, nch, CH])

    def mbc(mi):
        return tm[0:CH, mi, 0:CH].unsqueeze(1).to_broadcast([CH, nch, CH])

    def rvf(d):
        return (lambda ap: ap[:, 0:TT]) if d == 0 else (lambda ap: ap[:, 0:TT][:, ::-1])
    ibc = c.ident[0:64, 0:64].unsqueeze(1).to_broadcast([64, nch, 64])
    stk = ("stmp", tag)
    for d in range(2):
        u, dk = U_[d], DK[d]
        u["si"] = 0
        S0 = u["S"][0]
        if st_in is None:
            k.memset("dve", S0[:], 0.0, [("S", dk, 0)])
        elif transpose_state:
            k.load(t["stmp"][:], st_in[d], writes=[stk])
            p, pk = yield from acquire(c)
            k.tr(p[0:64, 0:64], t["stmp"][:], I64, [stk, "ident"], [pk])
            k.cp("dve", S0[:], p[0:64, 0:64], [pk], [("S", dk, 0)])
            release(c, pk)
        else:
            k.load(S0[:], st_in[d], writes=[("S", dk, 0)])
    yield
    for it in range(ntile):
        tis = [it, ntile - 1 - it]
        for d in range(2):
            u, dk = U_[d], DK[d]
            a0 = s0 + tis[d] * TT
            sl = slice(a0, a0 + TT)
            k.load(u["R"][:, 0:TT], arrs["R"][:, sl], writes=[("R", dk)])
            k.load(u["V"][:, 0:TT], arrs["V"][:, sl], writes=[("V", dk)])
            k.load(u["K"][:, 0:TT], arrs[f"K{d}"][:, sl], writes=[("K", dk)])
            k.load(u["LW"][:, 0:TT], arrs[f"LW{d}"][:, sl], writes=[("LW", dk)])
            if delta:
                k.load(u["KK"][:, 0:TT], arrs["KK"][:, sl], writes=[("KK", dk)])
                k.load(u["BK"][:, 0:TT], arrs["BK"][:, sl], writes=[("BK", dk)])
        yield
        pv = []
        for d in range(2):
            u, dk = U_[d], DK[d]
            p, pk = yield from acquire(c)
            pv.append((p, pk))
            for ch in range(nch):
                k.tr(p[0:CH, ch * 64:(ch + 1) * 64], u["V"][:, ch * CH:(ch + 1) * CH], I64, [("V", dk), "ident"], [pk])
        yield
        for d in range(2):
            u, dk = U_[d], DK[d]
            rv = rvf(d)
            p, pk = pv[d]
            k.cp("act", u["Vm"][0:CH, 0:nch, :].rearrange("p a b -> p (a b)"), p[0:CH, 0:nch * 64], [pk], [("Vm", dk)])
            release(c, pk)
            k.scan(rv(u["cum"]), rv(cst["rmask"] if d == 0 else cst["rmaskb"]), rv(u["LW"]), 0.0, [("LW", dk), "rmask"], [("cum", dk)])
            k.actf(u["Eabs"][:, 0:TT], u["cum"][:, 0:TT], AF.Exp, [("cum", dk)], [("Eabs", dk)])
            k.tt("dve", v3(u["cumc"]), v3(u["cum"]), v3(u["cum"])[:, :, MID:MID + 1].to_broadcast([64, nch, CH]), ALU.subtract,
                 [("cum", dk)], [("cumc", dk)])
            k.actf(u["Erel"][:, 0:TT], u["cumc"][:, 0:TT], AF.Exp, [("cumc", dk)], [("Erel", dk)])
            k.actf(u["Em"][:, 0:TT], u["cumc"][:, 0:TT], AF.Exp, [("cumc", dk)], [("Em", dk)], scale=-1.0)
            last = CH - 1 if d == 0 else 0
            k.actf(u["Gm"][:, 0:nch], v3(u["cumc"])[:, :, last], AF.Exp, [("cumc", dk)], [("Gm", dk)])
        yield
        for d in range(2):
            u, dk = U_[d], DK[d]
            k.tt("pool", u["Qabs"][:, 0:TT], u["R"][:, 0:TT], u["Eabs"][:, 0:TT], ALU.mult, [("R", dk), ("Eabs", dk)], [("Qabs", dk)])
            k.tt("pool", u["Qrel"][:, 0:TT], u["R"][:, 0:TT], u["Erel"][:, 0:TT], ALU.mult, [("R", dk), ("Erel", dk)], [("Qrel", dk)])
            k.tt("dve", u["Kd"][:, 0:TT], u["K"][:, 0:TT], u["Em"][:, 0:TT], ALU.mult, [("K", dk), ("Em", dk)], [("Kd", dk)])
            k.tt("dve", v3(u["Ke"]), v3(u["Kd"]), bc(u["Gm"]), ALU.mult, [("Kd", dk), ("Gm", dk)], [("Ke", dk)])
            if delta:
                k.tt("dve", u["cp"][:, 0:TT], u["cum"][:, 0:TT], u["LW"][:, 0:TT], ALU.subtract, [("cum", dk), ("LW", dk)], [("cp", dk)])
                k.actf(u["E0"][:, 0:TT], u["cp"][:, 0:TT], AF.Exp, [("cp", dk)], [("E0", dk)])
                k.tt("pool", u["KKabs"][:, 0:TT], u["KK"][:, 0:TT], u["E0"][:, 0:TT], ALU.mult, [("KK", dk), ("E0", dk)], [("KKabs", dk)])
                k.tt("dve", v3(u["cp"]), v3(u["cp"]), v3(u["cum"])[:, :, MID:MID + 1].to_broadcast([64, nch, CH]), ALU.subtract,
                     [("cp", dk), ("cum", dk)], [("cp", dk)])
                k.actf(u["E0"][:, 0:TT], u["cp"][:, 0:TT], AF.Exp, [("cp", dk)], [("E0", dk)])
                k.tt("pool", u["KKrel"][:, 0:TT], u["KK"][:, 0:TT], u["E0"][:, 0:TT], ALU.mult, [("KK", dk), ("E0", dk)], [("KKrel", dk)])
                k.tt("dve", u["Bd"][:, 0:TT], u["BK"][:, 0:TT], u["Em"][:, 0:TT], ALU.mult, [("BK", dk), ("Em", dk)], [("Bd", dk)])
                k.tt("dve", v3(u["Be"]), v3(u["Bd"]), bc(u["Gm"]), ALU.mult, [("Bd", dk), ("Gm", dk)], [("Be", dk)])
        yield
        pv = []
        for d in range(2):
            u, dk = U_[d], DK[d]
            p, pk = yield from acquire(c)
            pv.append((p, pk))
            for ch in range(nch):
                k.tr(p[0:CH, ch * 64:(ch + 1) * 64], u["Ke"][:, ch * CH:(ch + 1) * CH], I64, [("Ke", dk), "ident"], [pk])
            if delta:
                for ch in range(nch):
                    k.tr(p[0:CH, (nch + ch) * 64:(nch + ch + 1) * 64], u["Be"][:, ch * CH:(ch + 1) * CH], I64, [("Be", dk), "ident"], [pk])
        yield
        for d in range(2):
            u, dk = U_[d], DK[d]
            p, pk = pv[d]
            k.cp("act", u["KeT"][0:CH, 0:nch, :].rearrange("p a b -> p (a b)"), p[0:CH, 0:nch * 64], [pk], [("KeT", dk)])
            if delta:
                k.cp("dve", u["BeT"][0:CH, 0:nch, :].rearrange("p a b -> p (a b)"), p[0:CH, nch * 64:2 * nch * 64], [pk], [("BeT", dk)])
            release(c, pk)
        yield
        W4 = nch * CH
        specs = [("RKT", "Kd", "Qrel", "MI")]
        if delta:
            specs += [("RBT", "Bd", "Qrel", "MI"), ("AkT", "Kd", "KKrel", "MS"), ("M", "Bd", "KKrel", "MSneg"), ("P", "KKrel", "Bd", "MSntneg")]
        per = max(1, 512 // W4)
        for g0 in range(0, len(specs), per):
            grp = specs[g0:g0 + per]
            pv = []
            for d in range(2):
                u, dk = U_[d], DK[d]
                p, pk = yield from acquire(c)
                pv.append((p, pk))
                for si, (dst, lh, rh, mk_) in enumerate(grp):
                    for ch in range(nch):
                        cs_ = slice(ch * CH, (ch + 1) * CH)
                        k.mm(p[0:CH, si * W4 + ch * CH:si * W4 + (ch + 1) * CH], u[lh][:, cs_], u[rh][:, cs_], True, True,
                             [(lh, dk), (rh, dk)], [pk])
            yield
            for d in range(2):
                u, dk = U_[d], DK[d]
                p, pk = pv[d]
                mids = {"MI": 0, "MS": 1, "MSntneg": 5, "MSneg": 4} if d == 0 else {"MI": 2, "MS": 3, "MSntneg": 4, "MSneg": 5}
                for si, (dst, lh, rh, mk_) in enumerate(grp):
                    k.tt("dve", u[dst][0:CH, 0:nch, 0:CH],
                         p[0:CH, si * W4:(si + 1) * W4].rearrange("p (a b) -> p a b", b=CH) if False else
                         p[0:CH, si * W4:(si + 1) * W4].rearrange("p (a b) -> p a b", b=CH), mbc(mids[mk_]), ALU.mult,
                         [pk, "tmask"], [(dst, dk)])
                release(c, pk)
            yield
        if delta:
            cur = [["M", "P", "M2", "P2"], ["M", "P", "M2", "P2"]]
            for d in range(2):
                u, dk = U_[d], DK[d]
                k.tt("dve", u["Q"][:, 0:nch, :], u["M"][:, 0:nch, :], ibc, ALU.add, [("M", dk), "ident"], [("Q", dk)])
            W2 = nch * 64
            for lev in range(1, 6):
                pv = []
                for d in range(2):
                    u, dk = U_[d], DK[d]
                    Mc, Pc, Mn, Pn = cur[d]
                    p, pk = yield from acquire(c)
                    pv.append((p, pk))
                    for ch in range(nch):
                        k.mm(p[0:64, ch * 64:(ch + 1) * 64], u[Mc][:, ch, :], u[Pc][:, ch, :], True, True, [(Mc, dk), (Pc, dk)], [pk])
                    if lev < 5:
                        for ch in range(nch):
                            k.mm(p[0:64, W2 + ch * 64:W2 + (ch + 1) * 64], u[Pc][:, ch, :], u[Mc][:, ch, :], True, True, [(Mc, dk), (Pc, dk)], [pk])
                yield
                for d in range(2):
                    u, dk = U_[d], DK[d]
                    Mc, Pc, Mn, Pn = cur[d]
                    p, pk = pv[d]
                    k.cp("act", u[Pn][:, 0:nch, :].rearrange("p a b -> p (a b)"), p[0:64, 0:W2], [pk], [(Pn, dk)])
                    k.tt("dve", u["IP"][:, 0:nch, :], p[0:64, 0:W2].rearrange("p (a b) -> p a b", b=64), ibc, ALU.add, [pk, "ident"], [("IP", dk)])
                    if lev < 5:
                        k.cp("act", u[Mn][:, 0:nch, :].rearrange("p a b -> p (a b)"), p[0:64, W2:2 * W2], [pk], [(Mn, dk)])
                    release(c, pk)
                yield
                pv = []
                for d in range(2):
                    u, dk = U_[d], DK[d]
                    p, pk = yield from acquire(c)
                    pv.append((p, pk))
                    for ch in range(nch):
                        k.mm(p[0:64, ch * 64:(ch + 1) * 64], u["IP"][:, ch, :], u["Q"][:, ch, :], True, True, [("IP", dk), ("Q", dk)], [pk])
                yield
                for d in range(2):
                    u, dk = U_[d], DK[d]
                    p, pk = pv[d]
                    k.cp("dve", u["Q"][:, 0:nch, :].rearrange("p a b -> p (a b)"), p[0:64, 0:W2], [pk], [("Q", dk)])
                    release(c, pk)
                    Mc, Pc, Mn, Pn = cur[d]
                    cur[d] = [Mn, Pn, Mc, Pc]
                yield
        for ci in range(nch):
            chs = [ci, nch - 1 - ci]
            if delta:
                pv = []
                for d in range(2):
                    u, dk = U_[d], DK[d]
                    ch = chs[d]
                    cs_ = slice(ch * CH, (ch + 1) * CH)
                    Sc, Sck = u["S"][u["si"]], ("S", dk, u["si"])
                    p, pk = yield from acquire(c)
                    pv.append((p, pk))
                    k.mm(p[0:64, 0:64], u["KKabs"][:, cs_], Sc[:], True, False, [("KKabs", dk), Sck], [pk])
                    k.mm(p[0:64, 0:64], u["AkT"][:, ch, :], u["Vm"][:, ch, :], False, True, [("AkT", dk), ("Vm", dk)], [pk])
                yield
                for d in range(2):
                    u, dk = U_[d], DK[d]
                    p, pk = pv[d]
                    k.cp("dve" if d == 0 else "act", u["rhs0"][:], p[0:64, 0:64], [pk], [("rhs0", dk)])
                    release(c, pk)
                yield
                pv = []
                for d in range(2):
                    u, dk = U_[d], DK[d]
                    ch = chs[d]
                    p, pk = yield from acquire(c)
                    pv.append((p, pk))
                    k.mm(p[0:64, 0:64], u["Q"][:, ch, :], u["rhs0"][:], True, True, [("Q", dk), ("rhs0", dk)], [pk])
                yield
                for d in range(2):
                    u, dk = U_[d], DK[d]
                    p, pk = pv[d]
                    if d == 0:
                        k.ts("dve", u["U"][:], p[0:64, 0:64], -1.0, None, ALU.mult, None, [pk], [("U", dk)])
                    else:
                        k.actf(u["U"][:], p[0:64, 0:64], AF.Copy, [pk], [("U", dk)], scale=-1.0)
                    release(c, pk)
                yield
            pv = []
            for d in range(2):
                u, dk = U_[d], DK[d]
                ch = chs[d]
                cs_ = slice(ch * CH, (ch + 1) * CH)
                Sc, Sck = u["S"][u["si"]], ("S", dk, u["si"])
                p, pk = yield from acquire(c)
                pv.append((p, pk))
                k.mm(p[0:64, 0:CH], Sc[:], u["Qabs"][:, cs_], True, False, [Sck, ("Qabs", dk)], [pk])
                k.mm(p[0:64, 0:CH], u["Vm"][0:CH, ch, :], u["RKT"][0:CH, ch, 0:CH], False, not delta, [("Vm", dk), ("RKT", dk)], [pk])
                if delta:
                    k.mm(p[0:64, 0:CH], u["U"][0:CH, :], u["RBT"][0:CH, ch, 0:CH], False, True, [("U", dk), ("RBT", dk)], [pk])
                k.mm(p[0:64, 64:128], u["KeT"][0:CH, ch, :], u["Vm"][0:CH, ch, :], True, not delta, [("KeT", dk), ("Vm", dk)], [pk])
                if delta:
                    k.mm(p[0:64, 64:128], u["BeT"][0:CH, ch, :], u["U"][:], False, True, [("BeT", dk), ("U", dk)], [pk])
            yield
            for d in range(2):
                u, dk = U_[d], DK[d]
                ch = chs[d]
                cs_ = slice(ch * CH, (ch + 1) * CH)
                p, pk = pv[d]
                Sc, Sck = u["S"][u["si"]], ("S", dk, u["si"])
                Sn, Snk = u["S"][1 - u["si"]], ("S", dk, 1 - u["si"])
                k.cp("act", u["ybuf"][:, cs_], p[0:64, 0:CH], [pk], [("ybuf", dk)])
                gi = ch * CH + (CH - 1 if d == 0 else 0)
                k.stt(Sn[:], Sc[:], u["Eabs"][:, gi:gi + 1], p[0:64, 64:128], ALU.mult, ALU.add, [Sck, ("Eabs", dk), pk], [Snk])
                release(c, pk)
                u["si"] = 1 - u["si"]
            yield
        for d in range(2):
            u, dk = U_[d], DK[d]
            a0 = s0 + tis[d] * TT
            k.store(YS[d][:, a0:a0 + TT], u["ybuf"][:, 0:TT], reads=[("ybuf", dk)])
        yield
    if st_out is not None:
        for d in range(2):
            u, dk = U_[d], DK[d]
            Sc, Sck = u["S"][u["si"]], ("S", dk, u["si"])
            if transpose_state:
                p, pk = yield from acquire(c)
                k.tr(p[0:64, 0:64], Sc[:], I64, [Sck, "ident"], [pk])
                k.cp("dve", t["stmp"][:], p[0:64, 0:64], [pk], [stk])
                release(c, pk)
                k.store(st_out[d], t["stmp"][:], reads=[stk])
            else:
                k.store(st_out[d], Sc[:], reads=[Sck])
    yield


def drive_gen(jobs, tilesets):
    pending = list(jobs)
    free = list(tilesets)
    active = []
    while pending or active:
        while pending and free:
            ts = free.pop(0)
            active.append((pending.pop(0)(ts), ts))
        for item in list(active):
            g, ts = item
            try:
                next(g)
            except StopIteration:
                active.remove(item)
                free.append(ts)
        yield


def run_concurrent(gens):
    gens = list(gens)
    while gens:
        for g in list(gens):
            try:
                next(g)
            except StopIteration:
                gens.remove(g)


def drive(jobs, tilesets):
    run_concurrent([drive_gen(jobs, tilesets)])


def phase_hg(c, l, part=None, es_ext=None, NL=4):
    k, nc, cfg, S, I, O = c.k, c.nc, c.cfg, c.S, c.I, c.O
    NTOK = cfg.NTOK
    PTv = S["PT"].rearrange("(c p) t -> p c t", p=128)
    LAv = S["LAh"].rearrange("a (c p) t -> a p c t", p=128)
    LXv = S["LXh"].rearrange("a (c p) t -> a p c t", p=128)
    if part in (None, "prep", "prep_g"):
        es = ExitStack() if es_ext is None else es_ext
        col = k.sb("hg_col", [128, 2, 3], F32, es)
        ng = k.sb("hg_ng", [128, 2], F32, es)
        lb = k.sb("hg_lb", [128, 2], F32, es)
        oml = k.sb("hg_oml", [128, 2], F32, es)
        noml = k.sb("hg_noml", [128, 2], F32, es)
        k.load(col[:], I["hgcol"], writes=["hgcol"])
        k.load(ng[:], I["hgng"][l], writes=["hgng"])
        if l == 0:
            k.memset("dve", lb[:], 0.0, ["hglb"])
        else:
            k.tt("dve", lb[:], col[:, :, 1], col[:, :, 0], ALU.subtract, ["hgcol"], ["hglb"])
            k.actf(lb[:], lb[:], AF.Sigmoid, ["hglb"], ["hglb"])
        k.ts("dve", oml[:], lb[:], -1.0, 1.0, ALU.mult, ALU.add, ["hglb"], ["hgoml"])
        k.ts("dve", noml[:], oml[:], -1.0, None, ALU.mult, None, ["hgoml"], ["hgnoml"])
        pin = [k.sb(f"hg_pin{i}", [128, 10, 512], F32, es) for i in range(1)]
        wk = {nm: k.sb("hg_" + nm, [128, 2, 512], F32, es) for nm in ["R", "sig", "f", "K", "go"]}
        LAv = S["LAh"].rearrange("a (c p) t -> a p c t", p=128)
        LXv = S["LXh"].rearrange("a (c p) t -> a p c t", p=128)
        def _gen():
            for ti, (t0, n, var) in enumerate(cfg.tiles):
                yield
                pi_ = pin[0]
                k.load(pi_[:, :, 0:n], PTv[:, 10:20, t0:t0 + n], reads=["PT"], writes=["hgpin"])
                k.actf(wk["R"][:, :, 0:n], pi_[:, 0:2, 0:n], AF.Silu, ["hgpin"], ["hgR"])
                k.store(LAv[0][:, :, t0:t0 + n], wk["R"][:, :, 0:n], reads=["hgR"], writes=["LA"])
                k.actf(wk["go"][:, :, 0:n], pi_[:, 8:10, 0:n], AF.Silu, ["hgpin"], ["hggo"])
                for hc in range(2):
                    k.ts("dve", wk["go"][:, hc, 0:n], wk["go"][:, hc, 0:n], ng[:, hc:hc + 1], None, ALU.mult, None, ["hggo", "hgng"], ["hggo"])
                k.store(LXv[0][:, :, t0:t0 + n], wk["go"][:, :, 0:n], reads=["hggo"], writes=["LX"])
                for d in range(2):
                    k.actf(wk["sig"][:, :, 0:n], pi_[:, 2 + 2 * d:4 + 2 * d, 0:n], AF.Sigmoid, ["hgpin"], ["hgsig"])
                    for hc in range(2):
                        k.ts("dve", wk["f"][:, hc, 0:n], wk["sig"][:, hc, 0:n], oml[:, hc:hc + 1], lb[:, hc:hc + 1], ALU.mult, ALU.add,
                             ["hgsig", "hgoml", "hglb"], ["hgf"])
                        k.ts("pool", wk["K"][:, hc, 0:n], wk["sig"][:, hc, 0:n], noml[:, hc:hc + 1], oml[:, hc:hc + 1], ALU.mult, ALU.add,
                             ["hgsig", "hgoml", "hgnoml"], ["hgK"])
                    k.ts("dve", wk["f"][:, :, 0:n], wk["f"][:, :, 0:n], 1e-30, None, ALU.max, None, ["hgf"], ["hgf"])
                    k.actf(wk["f"][:, :, 0:n], wk["f"][:, :, 0:n], AF.Ln, ["hgf"], ["hgf"])
                    k.store(LAv[4 + d][:, :, t0:t0 + n], wk["f"][:, :, 0:n], reads=["hgf"], writes=["LA"])
                    k.store(LAv[2 + d][:, :, t0:t0 + n], wk["K"][:, :, 0:n], reads=["hgK"], writes=["LA"])
            yield
        _g = _gen()
        if part == "prep_g":
            return _g
        run_concurrent([_g])
        k.barrier()
        es.close()
    if part in (None, "scan"):
      es = ExitStack() if es_ext is None else es_ext
      tsets = [la_alloc(c, es, False, 32, 4, f"h{i}") for i in range(NL)]
      cst = la_consts(c, es, 32, 4)
      jobs = []
      for (s0, T, kind, pi_) in cfg.seqs:
          for h in range(4):
              rows = slice(64 * h, 64 * h + 64)
              arrs = {"R": S["LAh"][0][rows], "V": S["PT"][2048 + 64 * h:2048 + 64 * h + 64], "K0": S["LAh"][2][rows], "K1": S["LAh"][3][rows],
                      "LW0": S["LAh"][4][rows], "LW1": S["LAh"][5][rows]}
              YS = [S["YSh"][0][rows], S["YSh"][1][rows]]
              st_in = [I["st_hg"][l, d, h] for d in range(2)] if kind == 1 else None
              st_out = [O["nhg"][pi_, l, d, h] for d in range(2)] if kind == 0 else None
              jobs.append(lambda ts, arrs=arrs, YS=YS, s0=s0, T=T, st_in=st_in, st_out=st_out:
                          la_lane(c, ts, cst, False, arrs, YS, (s0, T), st_in, st_out, False))
      g = drive_gen(jobs, tsets)
      if part == "scan":
          return g
      run_concurrent([g])
      k.barrier()
      es.close()
    if (cfg.stages is not None and "hg_noout" in cfg.stages) or part not in (None, "out", "out_g"):
        return
    es = ExitStack() if es_ext is None else es_ext
    YSv = S["YSh"].rearrange("a (c p) t -> a p c t", p=128)
    OTv = S["OT"].rearrange("(c p) t -> p c t", p=128)
    y0 = k.sb("hgo_y0", [128, 2, 512], F32, es)
    y1 = k.sb("hgo_y1", [128, 2, 512], F32, es)
    go = k.sb("hgo_go", [128, 2, 512], F32, es)
    sq = k.sb("hgo_sq", [128, 2, 512], F32, es)
    epsc = k.sb("hgo_eps", [128, 1], F32, es)
    k.memset("dve", epsc[:], LN_EPS, ["hgeps"])
    def _gen():
        for ti, (t0, n, var) in enumerate(cfg.tiles):
            yield
            k.load(y0[:, :, 0:n], YSv[0][:, :, t0:t0 + n], reads=["YS"], writes=["hy0"])
            k.load(y1[:, :, 0:n], YSv[1][:, :, t0:t0 + n], reads=["YS"], writes=["hy1"])
            k.load(go[:, :, 0:n], LXv[0][:, :, t0:t0 + n], reads=["LX"], writes=["hgo"])
            k.tt("dve", y0[:, :, 0:n], y0[:, :, 0:n], y1[:, :, 0:n], ALU.add, ["hy0", "hy1"], ["hy0"])
            k.actf(sq[:, :, 0:n], y0[:, :, 0:n], AF.Square, ["hy0"], ["hsq"])
            for hc in range(2):
                p, pk = nextbank(c)
                k.mm(p[:, 0:n], c.bones64[:], sq[:, hc, 0:n], True, True, ["bones64", "hsq"], [pk])
                k.actf(y1[:, hc, 0:n], p[:, 0:n], AF.Sqrt, [pk, "hgeps", "hy1"], ["hy1"], bias=epsc[:, 0:1])
            k.recip(y1[:, :, 0:n], y1[:, :, 0:n], ["hy1"], ["hy1"])
            k.tt("dve", y0[:, :, 0:n], y0[:, :, 0:n], y1[:, :, 0:n], ALU.mult, ["hy0", "hy1"], ["hy0"])
            k.tt("dve", y0[:, :, 0:n], y0[:, :, 0:n], go[:, :, 0:n], ALU.mult, ["hy0", "hgo"], ["hy0"])
            k.store(OTv[:, 4:6, t0:t0 + n], y0[:, :, 0:n], reads=["hy0"], writes=["OTh"])


        yield
    _g = _gen()
    if part == "out_g":
        return _g
    run_concurrent([_g])
    k.barrier()
    es.close()
def phase_rw(c, l, part=None, es_ext=None, NL=4):
    k, nc, cfg, S, I, O = c.k, c.nc, c.cfg, c.S, c.I, c.O
    PTv = S["PT"].rearrange("(c p) t -> p c t", p=128)
    LAv = S["LAr"].rearrange("a (c p) t -> a p c t", p=128)
    LXv = S["LXr"].rearrange("a (c p) t -> a p c t", p=128)
    if part in (None, "prep", "prep_g"):
        es = ExitStack() if es_ext is None else es_ext
        mu = k.sb("rw_mu", [128, 2, 8], F32, es)
        cm = k.sb("rw_cm", [128, 8], F32, es)
        muad = k.sb("rw_muad", [64, 2], F32, es)
        cmad = k.sb("rw_cmad", [64, 1], F32, es)
        w0 = k.sb("rw_w0", [128, 2, 2], F32, es)
        col = k.sb("rw_col", [128, 2, 6], F32, es)
        omka = k.sb("rw_omka", [128, 2], F32, es)
        wup = k.sb("rw_wup", [64, 2, 256], F32, es)
        aup = k.sb("rw_aup", [64, 256], F32, es)
        gup = k.sb("rw_gup", [128, 256], F32, es)
        k.load(mu[:], I["rwmu"][l].rearrange("i p c -> p i c"), writes=["rwmu"])
        k.load(muad[:], I["rwmu_ad"][l], writes=["rwmuad"])
        k.load(w0[:], I["rww0"][l].rearrange("i p c -> p i c"), writes=["rww0"])
        k.load(col[:], I["rwcol"][l], writes=["rwcol"])
        k.load(wup[:], I["rw_w_up"][l].rearrange("i p c -> p i c"), writes=["rwwup"])
        k.load(aup[:], I["rw_a_up"][l], writes=["rwaup"])
        k.load(gup[:], I["rw_g_up"][l], writes=["rwgup"])
        k.tt("dve", cm[:], mu[:, 0, :], mu[:, 1, :], ALU.add, ["rwmu"], ["rwcm"])
        k.ts("dve", cm[:], cm[:], -1.0, 1.0, ALU.mult, ALU.add, ["rwcm"], ["rwcm"])
        k.tt("dve", cmad[:], muad[:, 0:1], muad[:, 1:2], ALU.add, ["rwmuad"], ["rwcmad"])
        k.ts("dve", cmad[:], cmad[:], -1.0, 1.0, ALU.mult, ALU.add, ["rwcmad"], ["rwcmad"])
        k.ts("dve", omka[:], col[:, :, 2], -1.0, 1.0, ALU.mult, ALU.add, ["rwcol"], ["rwomka"])
        pa = k.sb("rw_pa", [128, 8, 514], F32, es)
        pad = k.sb("rw_pad", [64, 514], F32, es)
        sh = k.sb("rw_sh", [128, 8, 512], F32, es)
        t2 = k.sb("rw_t2", [128, 8, 512], F32, es)
        adsh = k.sb("rw_adsh", [64, 512], F32, es)
        tw = k.sb("rw_tw", [64, 512], F32, es)
        sgd = k.sb("rw_sgd", [128, 512], F32, es)
        W = {nm: k.sb("rw_" + nm, [128, 2, 512], F32, es) for nm in ["a", "g", "lw0", "lw1", "kk", "kka", "keff", "tmp", "bon"]}
        EC = math.exp(-0.5)
        def _gen():
            for (s0, T, kind, pi_) in cfg.seqs:
                for t0 in range(0, T, 512):
                    yield
                    n = min(512, T - t0)
                    a0 = s0 + t0
                    lo = max(s0, a0 - 1)
                    hi = min(s0 + T, a0 + n + 1)
                    if t0 == 0:
                        k.memset("dve", pa[:, :, 0:1], 0.0, ["rwpa"])
                        k.memset("dve", pad[:, 0:1], 0.0, ["rwpad"])
                    if t0 + n == T:
                        k.memset("dve", pa[:, :, n + 1:n + 2], 0.0, ["rwpa"])
                        k.memset("dve", pad[:, n + 1:n + 2], 0.0, ["rwpad"])
                    o0 = lo - (a0 - 1)
                    k.load(pa[:, :, o0:o0 + hi - lo], PTv[:, 0:8, lo:hi], reads=["PT"], writes=["rwpa"])
                    k.load(pad[:, o0:o0 + hi - lo], S["PT"][832:896, lo:hi], reads=["PT"], writes=["rwpad"])
                    k.tt("dve", sh[:, :, 0:n], pa[:, :, 1:n + 1], cm[:].unsqueeze(2).to_broadcast([128, 8, n]), ALU.mult, ["rwpa", "rwcm"], ["rwsh"])
                    k.tt("pool", t2[:, :, 0:n], pa[:, :, 0:n], mu[:, 0, :].unsqueeze(2).to_broadcast([128, 8, n]), ALU.mult, ["rwpa", "rwmu"], ["rwt2"])
                    k.tt("dve", sh[:, :, 0:n], sh[:, :, 0:n], t2[:, :, 0:n], ALU.add, ["rwsh", "rwt2"], ["rwsh"])
                    k.tt("pool", t2[:, :, 0:n], pa[:, :, 2:n + 2], mu[:, 1, :].unsqueeze(2).to_broadcast([128, 8, n]), ALU.mult, ["rwpa", "rwmu"], ["rwt2"])
                    k.tt("dve", sh[:, :, 0:n], sh[:, :, 0:n], t2[:, :, 0:n], ALU.add, ["rwsh", "rwt2"], ["rwsh"])
                    k.ts("dve", adsh[:, 0:n], pad[:, 1:n + 1], cmad[:, 0:1], None, ALU.mult, None, ["rwpad", "rwcmad"], ["rwadsh"])
                    k.stt(adsh[:, 0:n], pad[:, 0:n], muad[:, 0:1], adsh[:, 0:n], ALU.mult, ALU.add, ["rwpad", "rwmuad", "rwadsh"], ["rwadsh"])
                    k.stt(adsh[:, 0:n], pad[:, 2:n + 2], muad[:, 1:2], adsh[:, 0:n], ALU.mult, ALU.add, ["rwpad", "rwmuad", "rwadsh"], ["rwadsh"])
                    k.actf(tw[:, 0:n], sh[0:64, 6, 0:n], AF.Tanh, ["rwsh"], ["rwtw"])
                    k.actf(sgd[:, 0:n], sh[:, 7, 0:n], AF.Sigmoid, ["rwsh"], ["rwsgd"])
                    for hc in range(2):
                        hsl = slice(hc * 128, (hc + 1) * 128)
                        r_, k_, v_ = sh[:, hc, 0:n], sh[:, 2 + hc, 0:n], sh[:, 4 + hc, 0:n]
                        p, pk = nextbank(c)
                        k.mm(p[:, 0:n], aup[:, hsl], adsh[:, 0:n], True, True, ["rwaup", "rwadsh"], [pk])
                        k.actf(W["a"][:, hc, 0:n], p[:, 0:n], AF.Sigmoid, [pk, "rwcol"], [("rwa", hc)], bias=col[:, hc, 0:1])
                        p, pk = nextbank(c)
                        k.mm(p[:, 0:n], gup[:, hsl], sgd[:, 0:n], True, True, ["rwgup", "rwsgd"], [pk])
                        k.cp("act", W["g"][:, hc, 0:n], p[:, 0:n], [pk], [("rwg", hc)])
                        for d in range(2):
                            p, pk = nextbank(c)
                            k.mm(p[:, 0:n], wup[:, d, hsl], tw[:, 0:n], True, True, ["rwwup", "rwtw"], [pk])
                            lw = W[f"lw{d}"]
                            k.actf(lw[:, hc, 0:n], p[:, 0:n], AF.Sigmoid, [pk, "rww0"], [("rwlw", d, hc)], bias=w0[:, d, hc:hc + 1])
                            k.ts("dve", lw[:, hc, 0:n], lw[:, hc, 0:n], -EC, None, ALU.mult, None, [("rwlw", d, hc)], [("rwlw", d, hc)])
                        kk = W["kk"]
                        k.ts("dve", kk[:, hc, 0:n], k_, col[:, hc, 1:2], None, ALU.mult, None, ["rwsh", "rwcol"], [("rwkk", hc)])
                        k.actf(W["tmp"][:, hc, 0:n], kk[:, hc, 0:n], AF.Square, [("rwkk", hc)], [("rwtmp", hc)])
                        p, pk = nextbank(c)
                        k.mm(p[:, 0:n], c.bones[:], W["tmp"][:, hc, 0:n], True, True, ["bones", ("rwtmp", hc)], [pk])
                        k.actf(W["tmp"][:, hc, 0:n], p[:, 0:n], AF.Sqrt, [pk], [("rwtmp", hc)])
                        k.ts("dve", W["tmp"][:, hc, 0:n], W["tmp"][:, hc, 0:n], 1e-12, None, ALU.max, None, [("rwtmp", hc)], [("rwtmp", hc)])
                        k.recip(W["tmp"][:, hc, 0:n], W["tmp"][:, hc, 0:n], [("rwtmp", hc)], [("rwtmp", hc)])
                        k.tt("dve", kk[:, hc, 0:n], kk[:, hc, 0:n], W["tmp"][:, hc, 0:n], ALU.mult, [("rwkk", hc), ("rwtmp", hc)], [("rwkk", hc)])
                        k.tt("pool", W["kka"][:, hc, 0:n], kk[:, hc, 0:n], W["a"][:, hc, 0:n], ALU.mult, [("rwkk", hc), ("rwa", hc)], [("rwkka", hc)])
                        k.ts("dve", W["tmp"][:, hc, 0:n], W["a"][:, hc, 0:n], col[:, hc, 2:3], omka[:, hc:hc + 1], ALU.mult, ALU.add,
                             [("rwa", hc), "rwcol", "rwomka", ("rwtmp", hc)], [("rwtmp", hc)])
                        k.tt("dve", W["keff"][:, hc, 0:n], k_, W["tmp"][:, hc, 0:n], ALU.mult, ["rwsh", ("rwtmp", hc)], [("rwkeff", hc)])
                        k.tt("dve", W["tmp"][:, hc, 0:n], r_, W["keff"][:, hc, 0:n], ALU.mult, ["rwsh", ("rwkeff", hc), ("rwtmp", hc)], [("rwtmp", hc)])
                        k.ts("dve", W["tmp"][:, hc, 0:n], W["tmp"][:, hc, 0:n], col[:, hc, 3:4], None, ALU.mult, None, [("rwtmp", hc), "rwcol"], [("rwtmp", hc)])
                        p, pk = nextbank(c)
                        k.mm(p[:, 0:n], c.bones[:], W["tmp"][:, hc, 0:n], True, True, ["bones", ("rwtmp", hc)], [pk])
                        k.tt("dve", W["bon"][:, hc, 0:n], p[:, 0:n], v_, ALU.mult, [pk, "rwsh"], [("rwbon", hc)])
                    sl = slice(a0, a0 + n)
                    k.store(LAv[0][:, :, sl], sh[:, 0:2, 0:n], reads=["rwsh"], writes=["LA"])
                    k.store(LAv[1][:, :, sl], sh[:, 4:6, 0:n], reads=["rwsh"], writes=["LA"])
                    k.store(LAv[2][:, :, sl], W["keff"][:, :, 0:n], reads=[("rwkeff", 0), ("rwkeff", 1)], writes=["LA"])
                    k.store(LAv[4][:, :, sl], W["lw0"][:, :, 0:n], reads=[("rwlw", 0, 0), ("rwlw", 0, 1)], writes=["LA"])
                    k.store(LAv[5][:, :, sl], W["lw1"][:, :, 0:n], reads=[("rwlw", 1, 0), ("rwlw", 1, 1)], writes=["LA"])
                    k.store(LAv[6][:, :, sl], W["kk"][:, :, 0:n], reads=[("rwkk", 0), ("rwkk", 1)], writes=["LA"])
                    k.store(LAv[7][:, :, sl], W["kka"][:, :, 0:n], reads=[("rwkka", 0), ("rwkka", 1)], writes=["LA"])
                    k.store(LXv[0][:, :, sl], W["g"][:, :, 0:n], reads=[("rwg", 0), ("rwg", 1)], writes=["LX"])
                    k.store(LXv[1][:, :, sl], W["bon"][:, :, 0:n], reads=[("rwbon", 0), ("rwbon", 1)], writes=["LX"])
            yield
        _g = _gen()
        if part == "prep_g":
            return _g
        run_concurrent([_g])
        k.barrier()
        es.close()
    if (cfg.stages is not None and "rw_prep_only" in cfg.stages) or part == "prep":
        return
    if part in (None, "scan"):
        es = ExitStack() if es_ext is None else es_ext
        tsets = [la_alloc(c, es, True, 64, 2, f"r{i}") for i in range(NL)]
        cst = la_consts(c, es, 64, 2)
        jobs = []
        for (s0, T, kind, pi_) in cfg.seqs:
            for h in range(4):
                rows = slice(64 * h, 64 * h + 64)
                arrs = {"R": S["LAr"][0][rows], "V": S["LAr"][1][rows], "K0": S["LAr"][2][rows], "K1": S["LAr"][2][rows],
                        "LW0": S["LAr"][4][rows], "LW1": S["LAr"][5][rows], "KK": S["LAr"][6][rows], "BK": S["LAr"][7][rows]}
                YS = [S["YSr"][0][rows], S["YSr"][1][rows]]
                st_in = [I["st_rw"][l, d, h] for d in range(2)] if kind == 1 else None
                st_out = [O["nrw"][pi_, l, d, h] for d in range(2)] if kind == 0 else None
                jobs.append(lambda ts, arrs=arrs, YS=YS, s0=s0, T=T, st_in=st_in, st_out=st_out:
                            la_lane(c, ts, cst, True, arrs, YS, (s0, T), st_in, st_out, True))
        g = drive_gen(jobs, tsets)
        if part == "scan":
            return g
        run_concurrent([g])
        k.barrier()
        es.close()

    if (cfg.stages is not None and "rw_noout" in cfg.stages) or part not in (None, "out", "out_g"):
        return
    es = ExitStack() if es_ext is None else es_ext
    YSv = S["YSr"].rearrange("a (c p) t -> a p c t", p=128)
    OTv = S["OT"].rearrange("(c p) t -> p c t", p=128)
    col = k.sb("rwo_col", [128, 2, 6], F32, es)
    k.load(col[:], I["rwcol"][l], writes=["rwcol"])
    y0 = k.sb("rwo_y0", [128, 2, 512], F32, es)
    y1 = k.sb("rwo_y1", [128, 2, 512], F32, es)
    g = k.sb("rwo_g", [128, 2, 512], F32, es)
    bon = k.sb("rwo_bon", [128, 2, 512], F32, es)
    sq = k.sb("rwo_sq", [128, 2, 512], F32, es)
    epsc = k.sb("rwo_eps", [128, 1], F32, es)
    k.memset("dve", epsc[:], RW_EPS, ["rweps"])
    def _gen():
        for ti, (t0, n, var) in enumerate(cfg.tiles):
            yield
            k.load(y0[:, :, 0:n], YSv[0][:, :, t0:t0 + n], reads=["YS"], writes=["ry0"])
            k.load(y1[:, :, 0:n], YSv[1][:, :, t0:t0 + n], reads=["YS"], writes=["ry1"])
            k.load(g[:, :, 0:n], LXv[0][:, :, t0:t0 + n], reads=["LX"], writes=["rg"])
            k.load(bon[:, :, 0:n], LXv[1][:, :, t0:t0 + n], reads=["LX"], writes=["rbon"])
            k.tt("dve", y0[:, :, 0:n], y0[:, :, 0:n], y1[:, :, 0:n], ALU.add, ["ry0", "ry1"], ["ry0"])
            for hc in range(2):
                p, pk = nextbank(c)
                k.mm(p[:, 0:n], c.bones64[:], y0[:, hc, 0:n], True, True, ["bones64", "ry0"], [pk])
                k.tt("dve", y1[:, hc, 0:n], y0[:, hc, 0:n], p[:, 0:n], ALU.subtract, ["ry0", pk, "ry1"], [("ryc", hc)])
                k.actf(sq[:, hc, 0:n], y1[:, hc, 0:n], AF.Square, [("ryc", hc)], [("rsq", hc)])
                p, pk = nextbank(c)
                k.mm(p[:, 0:n], c.bones64[:], sq[:, hc, 0:n], True, True, ["bones64", ("rsq", hc)], [pk])
                k.actf(sq[:, hc, 0:n], p[:, 0:n], AF.Sqrt, [pk, "rweps"], [("rsq", hc)], bias=epsc[:, 0:1])
                k.recip(sq[:, hc, 0:n], sq[:, hc, 0:n], [("rsq", hc)], [("rsq", hc)])
                k.tt("dve", y1[:, hc, 0:n], y1[:, hc, 0:n], sq[:, hc, 0:n], ALU.mult, [("ryc", hc), ("rsq", hc)], [("ryc", hc)])
                k.ts("dve", y1[:, hc, 0:n], y1[:, hc, 0:n], col[:, hc, 4:5], col[:, hc, 5:6], ALU.mult, ALU.add, [("ryc", hc), "rwcol"], [("ryc", hc)])
                k.tt("dve", y1[:, hc, 0:n], y1[:, hc, 0:n], bon[:, hc, 0:n], ALU.add, [("ryc", hc), "rbon"], [("ryc", hc)])
                k.tt("dve", y1[:, hc, 0:n], y1[:, hc, 0:n], g[:, hc, 0:n], ALU.mult, [("ryc", hc), "rg"], [("ryc", hc)])
            k.store(OTv[:, 0:2, t0:t0 + n], y1[:, :, 0:n], reads=[("ryc", 0), ("ryc", 1)], writes=["OTr", "ry1"])


        yield
    _g = _gen()
    if part == "out_g":
        return _g
    run_concurrent([_g])
    k.barrier()
    es.close()
def phase_s5(c, l, part=None, es_ext=None, NW=2):
    k, nc, cfg, S, I, O = c.k, c.nc, c.cfg, c.S, c.I, c.O
    PI = math.pi
    TT = 128
    PTv = S["PT"].rearrange("(c p) t -> p c t", p=128)
    YSv = S["YS5"].rearrange("a (c p) t -> a p c t", p=128)
    if part in (None, "scan"):
        es = ExitStack() if es_ext is None else es_ext
        lam = k.sb("s5_lam", [128, 2, 8, 3], F32, es)
        k.load(lam[:], I["s5lam"][l].rearrange("d p j r -> p d j r"), writes=["s5lam"])
        BT = k.sb("s5_BT", [128, 2, 8, 128], F32, es)
        CT = k.sb("s5_CT", [128, 2, 8, 128], F32, es)
        for r in range(2):
            k.load(BT[:, r], I["s5BT"][l, r].rearrange("j p n -> p j n"), writes=["s5BT"])
            k.load(CT[:, r], I["s5CT"][l, r].rearrange("j p n -> p j n"), writes=["s5CT"])
        sm = {nm: k.sb("s5_" + nm, [128, 2, 8], F32, es) for nm in
              ["dt", "mag", "th", "th2", "msk", "cos", "sin", "abre", "abim", "den", "zre", "zim", "t1", "t2", "cw", "sw", "cw2"]}
        RT = {nm: k.sb("s5_" + nm, [128, 2, 8, TT], F32, es) for nm in ["RTre", "RTim", "DZre", "DZim"]}
        tA = k.sb("s5_tA", [128, 8, TT], F32, es)
        tB = k.sb("s5_tB", [128, 8, TT], F32, es)
        magz = k.sb("s5_magz", [128, 2, 8, TT], F32, es)
        A_ = lambda nm: sm[nm][:]
        are, aim, ldt = lam[:, :, :, 0], lam[:, :, :, 1], lam[:, :, :, 2]

        def tts(e, o, a, b, op, r, w):
            k.tt(e, sm[o][:], a, b, op, r, w)

        k.actf(A_("dt"), ldt, AF.Exp, ["s5lam"], ["dt"])
        tts("dve", "t1", are, A_("dt"), ALU.mult, ["s5lam", "dt"], ["t1"])
        k.actf(A_("mag"), A_("t1"), AF.Exp, ["t1"], ["mag"])
        tts("dve", "th", aim, A_("dt"), ALU.mult, ["s5lam", "dt"], ["th"])

        def reduce_pi(nm, iters):
            for _ in range(iters):
                k.ts("dve", A_("msk"), A_(nm), PI, -2.0 * PI, ALU.is_gt, ALU.mult, [nm], ["msk"])
                tts("dve", nm, A_(nm), A_("msk"), ALU.add, [nm, "msk"], [nm])
                k.ts("dve", A_("msk"), A_(nm), -PI, 2.0 * PI, ALU.is_lt, ALU.mult, [nm], ["msk"])
                tts("dve", nm, A_(nm), A_("msk"), ALU.add, [nm, "msk"], [nm])
        reduce_pi("th", 5)
        k.ts("dve", A_("th2"), A_("th"), PI / 2, None, ALU.add, None, ["th"], ["th2"])
        reduce_pi("th2", 1)
        k.actf(A_("sin"), A_("th"), AF.Sin, ["th"], ["sin"])
        k.actf(A_("cos"), A_("th2"), AF.Sin, ["th2"], ["cos"])
        tts("dve", "abre", A_("mag"), A_("cos"), ALU.mult, ["mag", "cos"], ["abre"])
        tts("dve", "abim", A_("mag"), A_("sin"), ALU.mult, ["mag", "sin"], ["abim"])
        tts("dve", "t1", are, are, ALU.mult, ["s5lam"], ["t1"])
        tts("dve", "t2", aim, aim, ALU.mult, ["s5lam"], ["t2"])
        tts("dve", "den", A_("t1"), A_("t2"), ALU.add, ["t1", "t2"], ["den"])
        k.recip(A_("den"), A_("den"), ["den"], ["den"])
        k.ts("dve", A_("t1"), A_("abre"), -1.0, None, ALU.add, None, ["abre"], ["t1"])
        tts("dve", "zre", A_("t1"), are, ALU.mult, ["t1", "s5lam"], ["zre"])
        tts("dve", "t2", A_("abim"), aim, ALU.mult, ["abim", "s5lam"], ["t2"])
        tts("dve", "zre", A_("zre"), A_("t2"), ALU.add, ["zre", "t2"], ["zre"])
        tts("dve", "zre", A_("zre"), A_("den"), ALU.mult, ["zre", "den"], ["zre"])
        tts("dve", "zim", A_("abim"), are, ALU.mult, ["abim", "s5lam"], ["zim"])
        tts("dve", "t2", A_("t1"), aim, ALU.mult, ["t1", "s5lam"], ["t2"])
        tts("dve", "zim", A_("zim"), A_("t2"), ALU.subtract, ["zim", "t2"], ["zim"])
        tts("dve", "zim", A_("zim"), A_("den"), ALU.mult, ["zim", "den"], ["zim"])
        for d in range(2):
            Rre, Rim = RT["RTre"][:, d], RT["RTim"][:, d]
            k.cp("dve", Rre[:, :, 0:1], sm["cos"][:, d, :].unsqueeze(2), ["cos"], [("RT", d)])
            k.cp("dve", Rim[:, :, 0:1], sm["sin"][:, d, :].unsqueeze(2), ["sin"], [("RT", d)])
            k.cp("dve", sm["cw"][:, d, :], sm["cos"][:, d, :], ["cos"], [("cw", d)])
            k.cp("dve", sm["sw"][:, d, :], sm["sin"][:, d, :], ["sin"], [("sw", d)])
            w = 1
            while w < TT:
                cwb = sm["cw"][:, d, :].unsqueeze(2).to_broadcast([128, 8, w])
                swb = sm["sw"][:, d, :].unsqueeze(2).to_broadcast([128, 8, w])
                k.tt("dve", tA[:, :, 0:w], Rre[:, :, 0:w], cwb, ALU.mult, [("RT", d), ("cw", d)], ["tA"])
                k.tt("pool", tB[:, :, 0:w], Rim[:, :, 0:w], swb, ALU.mult, [("RT", d), ("sw", d)], ["tB"])
                k.tt("dve", Rre[:, :, w:2 * w], tA[:, :, 0:w], tB[:, :, 0:w], ALU.subtract, ["tA", "tB", ("RT", d)], [("RT", d)])
                k.tt("dve", tA[:, :, 0:w], Rre[:, :, 0:w], swb, ALU.mult, [("RT", d), ("sw", d)], ["tA"])
                k.tt("pool", tB[:, :, 0:w], Rim[:, :, 0:w], cwb, ALU.mult, [("RT", d), ("cw", d)], ["tB"])
                k.tt("dve", Rim[:, :, w:2 * w], tA[:, :, 0:w], tB[:, :, 0:w], ALU.add, ["tA", "tB", ("RT", d)], [("RT", d)])
                k.tt("dve", sm["t1"][:, d, :], sm["cw"][:, d, :], sm["cw"][:, d, :], ALU.mult, [("cw", d), "t1"], ["t1"])
                k.tt("dve", sm["t2"][:, d, :], sm["sw"][:, d, :], sm["sw"][:, d, :], ALU.mult, [("sw", d), "t2"], ["t2"])
                k.tt("dve", sm["cw2"][:, d, :], sm["cw"][:, d, :], sm["sw"][:, d, :], ALU.mult, [("cw", d), ("sw", d)], ["cw2"])
                k.tt("dve", sm["cw"][:, d, :], sm["t1"][:, d, :], sm["t2"][:, d, :], ALU.subtract, ["t1", "t2"], [("cw", d)])
                k.ts("dve", sm["sw"][:, d, :], sm["cw2"][:, d, :], 2.0, None, ALU.mult, None, ["cw2"], [("sw", d)])
                w *= 2
            zre_b = sm["zre"][:, d, :].unsqueeze(2).to_broadcast([128, 8, TT])
            zim_b = sm["zim"][:, d, :].unsqueeze(2).to_broadcast([128, 8, TT])
            k.tt("dve", tA[:], Rre, zre_b, ALU.mult, [("RT", d), "zre"], ["tA"])
            k.tt("pool", tB[:], Rim, zim_b, ALU.mult, [("RT", d), "zim"], ["tB"])
            k.tt("dve", RT["DZre"][:, d], tA[:], tB[:], ALU.add, ["tA", "tB"], [("DZ", d)])
            k.tt("dve", tA[:], Rre, zim_b, ALU.mult, [("RT", d), "zim"], ["tA"])
            k.tt("pool", tB[:], Rim, zre_b, ALU.mult, [("RT", d), "zre"], ["tB"])
            k.tt("dve", RT["DZim"][:, d], tA[:], tB[:], ALU.subtract, ["tA", "tB", ("DZ", d)], [("DZ", d)])
        for d in range(2):
            k.cp("act", magz[:, d], sm["mag"][:, d, :].unsqueeze(2).to_broadcast([128, 8, TT]), ["mag"], [("magz", d)])
            fi = 0 if d == 0 else TT - 1
            k.memset("dve", magz[:, d, :, fi:fi + 1], 0.0, [("magz", d)])
        wsets = []
        for i in range(NW):
            w = {nm: k.sb(f"s5w{i}_{nm}", [128, 8, TT], F32, es) for nm in ["A", "B", "C", "D"]}
            w["Braw"] = k.sb(f"s5w{i}_Braw", [128, 8, 2, TT], F32, es)
            w["u"] = k.sb(f"s5w{i}_u", [128, 2, TT], F32, es)
            w["y"] = k.sb(f"s5w{i}_y", [128, 2, TT], F32, es)
            w["init"] = k.sb(f"s5w{i}_init", [128, 8, 2], F32, es)
            w["cin"] = k.sb(f"s5w{i}_cin", [128, 8, 2], F32, es)
            w["tag"] = i
            wsets.append(w)

        def sweep(w, s0, T, kind, pi_, d):
            tg = w["tag"]
            K_ = lambda nm: ("s5", tg, nm)
            nt = T // TT
            init, cin = w["init"], w["cin"]
            if kind == 1:
                k.load(init[:], I["st_s5"][l, d], writes=[K_("init")])
            else:
                k.memset("dve", init[:], 0.0, [K_("init")])
            first = 0 if d == 0 else TT - 1
            last = TT - 1 if d == 0 else 0
            rvt = (lambda ap: ap) if d == 0 else (lambda ap: ap[:, :, ::-1])
            flat = lambda ap: ap.rearrange("p a b -> p (a b)")
            rvf = (lambda ap: flat(ap)) if d == 0 else (lambda ap: flat(ap)[:, ::-1])
            DZr, DZi = rvt(RT["DZre"][:, d]), rvt(RT["DZim"][:, d])
            Rr, Ri = rvt(RT["RTre"][:, d]), rvt(RT["RTim"][:, d])
            A, B, C, D_ = w["A"], w["B"], w["C"], w["D"]
            yield
            for it in range(nt):
                ti = it if d == 0 else nt - 1 - it
                a0 = s0 + ti * TT
                k.load(w["u"][:], PTv[:, 8:10, a0:a0 + TT], writes=[K_("u")])
                yield
                for j2 in range(4):
                    p, pk = yield from acquire(c)
                    for jj in range(2):
                        j = 2 * j2 + jj
                        for r in range(2):
                            k.mm(p[:, (2 * jj + r) * TT:(2 * jj + r + 1) * TT], BT[:, r, j, :], w["u"][:, j // 4, :], True, True, ["s5BT", K_("u")], [pk])
                    k.cp("act", w["Braw"][:, 2 * j2:2 * j2 + 2].rearrange("p a r t -> p (a r t)"), p[:, 0:4 * TT], [pk], [K_("Braw")])
                    release(c, pk)
                yield
                Br, Bi = w["Braw"][:, :, 0, :], w["Braw"][:, :, 1, :]
                k.tt("dve", A[:], Br, DZr, ALU.mult, [K_("Braw"), ("DZ", d)], [K_("A")])
                k.tt("pool", B[:], Bi, DZi, ALU.mult, [K_("Braw"), ("DZ", d)], [K_("B")])
                yield
                k.tt("dve", A[:], A[:], B[:], ALU.subtract, [K_("A"), K_("B")], [K_("A")])
                k.tt("pool", C[:], Br, DZi, ALU.mult, [K_("Braw"), ("DZ", d)], [K_("C")])
                yield
                k.tt("pool", B[:], Bi, DZr, ALU.mult, [K_("Braw"), ("DZ", d), K_("A")], [K_("B")])
                k.tt("dve", cin[:], init[:], sm["mag"][:, d, :].unsqueeze(2).to_broadcast([128, 8, 2]), ALU.mult, [K_("init"), "mag"], [K_("cin")])
                k.tt("dve", A[:, :, first:first + 1], A[:, :, first:first + 1], cin[:, :, 0:1], ALU.add, [K_("A"), K_("cin")], [K_("A")])
                yield
                k.tt("pool", B[:], B[:], C[:], ALU.add, [K_("B"), K_("C")], [K_("B")])
                k.scan(rvf(C[:]), rvf(magz[:, d]), rvf(A[:]), 0.0, [K_("A"), ("magz", d), K_("B")], [K_("C")])
                yield
                k.tt("dve", B[:, :, first:first + 1], B[:, :, first:first + 1], cin[:, :, 1:2], ALU.add, [K_("B"), K_("cin")], [K_("B")])
                k.scan(rvf(D_[:]), rvf(magz[:, d]), rvf(B[:]), 0.0, [K_("B"), ("magz", d)], [K_("D")])
                yield
                k.tt("dve", A[:], C[:], Rr, ALU.mult, [K_("C"), ("RT", d)], [K_("A")])
                k.tt("pool", B[:], D_[:], Ri, ALU.mult, [K_("D"), ("RT", d)], [K_("B")])
                yield
                k.tt("dve", A[:], A[:], B[:], ALU.subtract, [K_("A"), K_("B")], [K_("A")])
                k.tt("pool", B[:], C[:], Ri, ALU.mult, [K_("C"), ("RT", d), K_("A")], [K_("B")])
                yield
                k.tt("dve", C[:], D_[:], Rr, ALU.mult, [K_("D"), ("RT", d), K_("B")], [K_("C")])
                yield
                k.stt(B[:], B[:], -1.0, C[:], ALU.mult, ALU.subtract, [K_("B"), K_("C")], [K_("B")])
                yield
                k.cp("act", init[:, :, 0:1], A[:, :, last:last + 1], [K_("A")], [K_("init")])
                k.actf(init[:, :, 1:2], B[:, :, last:last + 1], AF.Copy, [K_("B")], [K_("init")], scale=-1.0)
                for kc in range(2):
                    p, pk = yield from acquire(c)
                    for jj in range(4):
                        j = 4 * kc + jj
                        k.mm(p[:, 0:TT], CT[:, 0, j, :], A[:, j, :], jj == 0, False, ["s5CT", K_("A")], [pk])
                        k.mm(p[:, 0:TT], CT[:, 1, j, :], B[:, j, :], False, jj == 3, ["s5CT", K_("B")], [pk])
                    k.cp("act", w["y"][:, kc, :], p[:, 0:TT], [pk], [K_("y")])
                    release(c, pk)
                yield
                k.store(YSv[d][:, :, a0:a0 + TT], w["y"][:], reads=[K_("y")])
                yield
            if kind == 0:
                k.store(O["ns5"][pi_, l, d].rearrange("g n r -> (g n) r").rearrange("(j p) r -> p j r", p=128), init[:], reads=[K_("init")])
            yield

        jobs = []
        for (s0, T, kind, pi_) in cfg.seqs:
            for d in range(2):
                jobs.append(lambda w, s0=s0, T=T, kind=kind, pi_=pi_, d=d: sweep(w, s0, T, kind, pi_, d))
        g = drive_gen(jobs, wsets)
        if part == "scan":
            return g
        run_concurrent([g])
        k.barrier()
        es.close()

    if (cfg.stages is not None and "s5_noout" in cfg.stages) or part not in (None, "out", "out_g"):
        return
    es = ExitStack() if es_ext is None else es_ext
    OTv = S["OT"].rearrange("(c p) t -> p c t", p=128)
    col = k.sb("s5o_col", [128, 2, 2], F32, es)
    glu = k.sb("s5o_glu", [128, 2, 256], F32, es)
    k.load(col[:], I["s5col"][l], writes=["s5col"])
    k.load(glu[:], I["s5glu"][l].rearrange("(kc p) n -> p kc n", p=128), writes=["s5glu"])
    y0 = k.sb("s5o_y0", [128, 2, 512], F32, es)
    y1 = k.sb("s5o_y1", [128, 2, 512], F32, es)
    u = k.sb("s5o_u", [128, 2, 512], F32, es)
    t_ = k.sb("s5o_t", [128, 2, 512], F32, es)
    def _gen():
        for ti, (t0, n, var) in enumerate(cfg.tiles):
            yield
            k.load(y0[:, :, 0:n], YSv[0][:, :, t0:t0 + n], reads=["YS"], writes=["sy0"])
            k.load(y1[:, :, 0:n], YSv[1][:, :, t0:t0 + n], reads=["YS"], writes=["sy1"])
            k.load(u[:, :, 0:n], PTv[:, 8:10, t0:t0 + n], reads=["PT"], writes=["su"])
            k.tt("dve", y0[:, :, 0:n], y0[:, :, 0:n], y1[:, :, 0:n], ALU.add, ["sy0", "sy1"], ["sy0"])
            for hc in range(2):
                k.stt(y0[:, hc, 0:n], u[:, hc, 0:n], col[:, hc, 0:1], y0[:, hc, 0:n], ALU.mult, ALU.add, ["su", "s5col", "sy0"], ["sy0"])
            k.actf(t_[:, :, 0:n], y0[:, :, 0:n], AF.Square, ["sy0"], ["st"])
            k.ts("dve", t_[:, :, 0:n], t_[:, :, 0:n], 0.044715, 1.0, ALU.mult, ALU.add, ["st"], ["st"])
            k.tt("dve", t_[:, :, 0:n], t_[:, :, 0:n], y0[:, :, 0:n], ALU.mult, ["st", "sy0"], ["st"])
            k.actf(t_[:, :, 0:n], t_[:, :, 0:n], AF.Sigmoid, ["st"], ["st"], scale=1.5957691216057308)
            k.tt("dve", y0[:, :, 0:n], y0[:, :, 0:n], t_[:, :, 0:n], ALU.mult, ["st", "sy0"], ["sy0"])
            for hc in range(2):
                p, pk = nextbank(c)
                for kc in range(2):
                    k.mm(p[:, 0:n], glu[:, kc, hc * 128:(hc + 1) * 128], y0[:, kc, 0:n], kc == 0, kc == 1, ["s5glu", "sy0"], [pk])
                k.actf(y1[:, hc, 0:n], p[:, 0:n], AF.Sigmoid, [pk, "s5col", "sy1"], [("s5sg", hc)], bias=col[:, hc, 1:2])
                k.tt("dve", y1[:, hc, 0:n], y1[:, hc, 0:n], y0[:, hc, 0:n], ALU.mult, [("s5sg", hc), "sy0"], [("s5sg", hc)])
            k.store(OTv[:, 2:4, t0:t0 + n], y1[:, :, 0:n], reads=[("s5sg", 0), ("s5sg", 1)], writes=["OT5", "sy1"])
        yield
    _g = _gen()
    if part == "out_g":
        return _g
    run_concurrent([_g])
    k.barrier()
    es.close()
def cols(v):
    v = np.asarray(v)
    n = v.shape[-1] // 128
    return np.ascontiguousarray(np.swapaxes(v.reshape(v.shape[:-1] + (n, 128)), -1, -2))


def prep_core(inp, b, cfg):
    NP, TP = cfg.NP, cfg.TP
    m = {}
    xs = inp["x_sample"][b]
    xp = inp["x_prompt"][b * NP:(b + 1) * NP].reshape(NP * TP, D)
    m["xin"] = np.ascontiguousarray(np.concatenate([xs, xp], axis=0))
    m["ident"] = np.eye(128, dtype=np.float32)
    cc = np.stack([cols(inp["c_ctx"]), cols(inp["c"][b])], axis=-1)
    m["ccol"] = np.ascontiguousarray(cc.astype(np.float32))
    m["w_mod"] = inp["w_mod"]
    m["bmod"] = cols(inp["b_mod"])
    m["lng"] = cols(inp["ln_g"])
    m["lnb"] = cols(inp["ln_b"])
    for nm in ("ffn_w_in", "ffn_w_out", "w_in", "w_out"):
        m[nm] = inp[nm]
    m["cache_k"] = np.ascontiguousarray(inp["cache_na_k"][b].reshape(L, 256, 256))
    m["cache_v"] = np.ascontiguousarray(inp["cache_na_v"][b].reshape(L, 256, 256))
    cc_, ww_ = np.meshgrid(np.arange(64), np.arange(64), indexing="ij")
    idx = np.clip(cc_ - ww_ + 15, 0, 30)
    rp = inp["na_rpb"][:, :, :, idx]
    m["rpbT"] = np.ascontiguousarray(np.transpose(rp, (0, 3, 1, 2, 4)))
    cs = np.clip(ww_ - 8, 0, 48)
    m["namask"] = np.where((cc_ >= cs) & (cc_ < cs + 16), 0.0, NEG).astype(np.float32)
    a_, b_ = np.meshgrid(np.arange(64), np.arange(64), indexing="ij")
    UI, US, LI, LS = (a_ <= b_), (a_ < b_), (a_ >= b_), (a_ > b_)
    m["tmask"] = np.stack([UI, US, LI, LS, -1.0 * US, -1.0 * LS]).astype(np.float32)
    bo = np.zeros((128, 128), np.float32)
    bo[0:64, 0:64] = 1.0
    bo[64:128, 64:128] = 1.0
    m["bones"] = bo
    hl = cols(inp["hg_lb"])
    m["hgcol"] = np.ascontiguousarray(np.stack([hl[0], hl[1], hl[1]], axis=-1).astype(np.float32))
    m["hgng"] = cols(inp["hg_norm_g"])
    m["st_hg"] = np.ascontiguousarray(inp["state_hgrn"][b])
    m["rwmu"] = cols(inp["rw_mu"])
    m["rwmu_ad"] = np.ascontiguousarray(np.transpose(inp["rw_mu"][:, :, 832:896], (0, 2, 1)))
    m["rww0"] = cols(inp["rw_w0"])
    m["rw_w_up"] = inp["rw_w_up"]
    m["rw_a_up"] = inp["rw_a_up"]
    m["rw_g_up"] = inp["rw_g_up"]
    rk = inp["rw_r_k"].reshape(L, 256)
    m["rwcol"] = np.ascontiguousarray(np.stack([cols(inp["rw_a0"]), cols(inp["rw_k_k"]), cols(inp["rw_k_a"]), cols(rk),
                                                 cols(inp["rw_lnx_g"]), cols(inp["rw_lnx_b"])], axis=-1).astype(np.float32))
    m["st_rw"] = np.ascontiguousarray(inp["state_rwkv"][b])
    def scol(v):
        return cols(v.reshape(v.shape[:-2] + (1024,)))
    ldt = np.repeat(inp["s5_log_dt"][..., None], 64, axis=-1)
    m["s5lam"] = np.ascontiguousarray(np.stack([scol(inp["s5_a_re"]), scol(inp["s5_a_im"]), scol(ldt)], axis=-1).astype(np.float32))
    BT = np.zeros((L, 2, 8, 128, 128), np.float32)
    CT = np.zeros((L, 2, 8, 128, 128), np.float32)
    for r, (bsrc, csrc) in enumerate(((inp["s5_b_re"], inp["s5_c_re"]), (inp["s5_b_im"], inp["s5_c_im"]))):
        for j in range(8):
            for gl in range(2):
                g = 2 * j + gl
                r0 = 32 * (j % 4) + 16 * gl
                BT[:, r, j, r0:r0 + 16, 64 * gl:64 * gl + 64] = np.transpose(bsrc[:, g], (0, 2, 1))
                CT[:, r, j, 64 * gl:64 * gl + 64, r0:r0 + 16] = np.transpose(csrc[:, g], (0, 2, 1))
    m["s5BT"], m["s5CT"] = BT, CT
    m["s5col"] = np.ascontiguousarray(np.stack([cols(inp["s5_d"]), cols(inp["s5_glu_b"])], axis=-1).astype(np.float32))
    m["s5glu"] = inp["s5_glu_w"]
    st = inp["state_s5"][b].reshape(L, 2, 8, 128, 2)
    m["st_s5"] = np.ascontiguousarray(np.transpose(st, (0, 1, 3, 2, 4)))
    return m


_CACHE = {}


def kernel(**inputs):
    cfg = Cfg()
    inp = {k_: np.asarray(v) for k_, v in inputs.items()}
    if "nc" not in _CACHE:
        _CACHE["nc"] = build(cfg)
    nc, c = _CACHE["nc"]
    in_maps = [prep_core(inp, b, cfg) for b in range(8)]
    res = run_bass_kernel_spmd(nc, in_maps, core_ids=list(range(8)))
    R = res.results
    NP, TP = cfg.NP, cfg.TP
    y_p = np.concatenate([r["y_p"].reshape(NP, TP, D) for r in R], axis=0)
    y_s = np.stack([r["y_s"] for r in R], axis=0)
    nk = np.concatenate([r["nk"].reshape(NP, L, TP, 4, 64) for r in R], axis=0)
    nv = np.concatenate([r["nv"].reshape(NP, L, TP, 4, 64) for r in R], axis=0)
    nrw = np.concatenate([r["nrw"] for r in R], axis=0)
    ns5 = np.concatenate([r["ns5"] for r in R], axis=0)
    nhg = np.concatenate([r["nhg"] for r in R], axis=0)
    return (y_p.astype(np.float32), y_s.astype(np.float32), nk.astype(np.float32), nv.astype(np.float32),
            nrw.astype(np.float32), ns5.astype(np.float32), nhg.astype(np.float32))
```

```python
import bisect
import math
from contextlib import ExitStack
import numpy as np
import concourse.bass as bass
import concourse.mybir as mybir
from concourse.bass_utils import run_bass_kernel_spmd

F32 = mybir.dt.float32
BF16 = mybir.dt.bfloat16
F32R = mybir.dt.float32r
AF = mybir.ActivationFunctionType
ALU = mybir.AluOpType
AX = mybir.AxisListType
EPOCH = 20000
NDMASEM = 12

D = 1024
L = 2
DFF = 2816
NF = DFF // 128
INC = 3328
ALPHA = (2 * L) ** 0.25
LN_EPS = 1e-5
RW_EPS = 64e-5
NEG = -30000.0


class Eng:
    def __init__(self, K, name, eng, is_pe=False):
        self.K = K
        self.name = name
        self.eng = eng
        self.is_pe = is_pe
        self.insts = []
        self.marks = []
        self.sems = []
        self.seen = {}
        self.dseen = {}

    def sem_for(self, m):
        e = (m - 1) // EPOCH
        while len(self.sems) <= e:
            self.sems.append(self.K.new_sem(f"{self.name}_e{len(self.sems)}"))
        return self.sems[e], (m - 1) % EPOCH + 1


class DmaQ:
    def __init__(self, K, name, issuer):
        self.K = K
        self.name = name
        self.issuer = issuer
        self.n = 0
        self.sems = [K.new_sem(f"{name}_d{i}") for i in range(NDMASEM)]

    def semval(self, i):
        return self.sems[i % NDMASEM], 16 * (i // NDMASEM + 1)


class Res:
    __slots__ = ("w", "r")

    def __init__(self):
        self.w = None
        self.r = {}


class K:
    def __init__(self, nc):
        self.nc = nc
        self.es = ExitStack()
        self.nsem = 0
        self.pe = Eng(self, "pe", nc.tensor, is_pe=True)
        self.dve = Eng(self, "dve", nc.vector)
        self.act = Eng(self, "act", nc.scalar)
        self.pool = Eng(self, "pool", nc.gpsimd)
        self.sp = Eng(self, "sp", nc.sync)
        self.engs = {e.name: e for e in (self.pe, self.dve, self.act, self.pool, self.sp)}
        self.ld = DmaQ(self, "ld", self.sp)
        self.st = DmaQ(self, "st", self.pool)
        self.dq = {"ld": self.ld, "st": self.st}
        self.res = {}
        self.ninst = 0
        self.nwait = 0
        self.rr = 0

    def new_sem(self, name):
        self.nsem += 1
        return self.es.enter_context(self.nc.semaphore(name))

    def sb(self, name, shape, dt=F32, stack=None):
        self.uid = getattr(self, "uid", 0) + 1
        return (stack or self.es).enter_context(self.nc.sbuf_tensor(f"{name}_u{self.uid}", list(shape), dt))

    def ps(self, name, shape, dt=F32, stack=None):
        return (stack or self.es).enter_context(self.nc.psum_tensor(name, list(shape), dt))

    def _wait_inst(self, E, xname, idx):
        X = self.engs[xname]
        if E is X and E.is_pe:
            return
        p = bisect.bisect_left(X.marks, idx)
        if p < len(X.marks):
            m = p + 1
        else:
            m = len(X.marks) + 1
            sem, val = X.sem_for(m)
            X.insts[idx].then_inc(sem, 1)
            X.marks.append(idx)
        if E.seen.get(xname, 0) >= m:
            return
        sem, val = X.sem_for(m)
        E.eng.wait_ge(sem, val)
        self.nwait += 1
        E.seen[xname] = m

    def _wait_dma(self, E, qname, i):
        Q = self.dq[qname]
        key = (qname, i % NDMASEM)
        if E.dseen.get(key, -1) >= i:
            return
        sem, val = Q.semval(i)
        E.eng.wait_ge(sem, val)
        self.nwait += 1
        E.dseen[key] = i

    def _wait(self, E, dep):
        if dep[0] == "dma":
            self._wait_dma(E, dep[1], dep[2])
        else:
            self._wait_inst(E, dep[1], dep[2])

    @staticmethod
    def _rkey(me):
        if me[0] == "i":
            return ("i", me[1])
        return ("dma", me[1], me[2] % NDMASEM)

    def _deps(self, reads, writes):
        deps = {}

        def add(d):
            k = self._rkey(d)
            if k not in deps or deps[k][2] < d[2]:
                deps[k] = d
        for r in reads:
            rs = self.res.get(r)
            if rs is not None and rs.w is not None:
                add(rs.w)
        for w in writes:
            rs = self.res.get(w)
            if rs is not None:
                if rs.w is not None:
                    add(rs.w)
                for d in rs.r.values():
                    add(d)
        return list(deps.values())

    def _update(self, me, reads, writes):
        k = self._rkey(me)
        for r in reads:
            rs = self.res.get(r)
            if rs is None:
                rs = self.res[r] = Res()
            rs.r[k] = me
        for w in writes:
            rs = self.res.get(w)
            if rs is None:
                rs = self.res[w] = Res()
            rs.w = me
            rs.r = {}

    def op(self, E, fn, reads=(), writes=()):
        bk = [r for r in reads if isinstance(r, tuple) and r and r[0] == "bank"]
        if bk:
            reads = [r for r in reads if r not in bk]
            writes = list(writes) + bk
        for d in self._deps(reads, writes):
            self._wait(E, d)
        h = fn()
        idx = len(E.insts)
        E.insts.append(h)
        self._update(("i", E.name, idx), reads, writes)
        self.ninst += 1
        return h

    def dma(self, Q, out, in_, reads=(), writes=(), **kw):
        E = Q.issuer
        for d in self._deps(reads, writes):
            self._wait(E, d)
        i = Q.n
        if i >= NDMASEM:
            self._wait_dma(E, Q.name, i - NDMASEM)
        sem, val = Q.semval(i)
        h = E.eng.dma_start(out=out, in_=in_, **kw)
        h.then_inc(sem, 16)
        Q.n += 1
        self._update(("dma", Q.name, i), reads, writes)
        self.ninst += 1
        return h

    def load(self, out, in_, reads=(), writes=(), **kw):
        return self.dma(self.ld, out, in_, reads, writes, **kw)

    def store(self, out, in_, reads=(), writes=(), **kw):
        return self.dma(self.st, out, in_, reads, writes, **kw)

    def barrier(self):
        for E in self.engs.values():
            for X in self.engs.values():
                if X.insts:
                    self._wait_inst(E, X.name, len(X.insts) - 1)
            for Q in self.dq.values():
                for i in range(max(0, Q.n - NDMASEM), Q.n):
                    self._wait_dma(E, Q.name, i)
        self.res = {}

    def E(self, e):
        return self.engs[e]

    def any2(self):
        self.rr += 1
        return "dve" if self.rr % 2 else "act"

    def mm(self, out, lhsT, rhs, start, stop, r, w):
        nc = self.nc
        return self.op(self.pe, lambda: nc.tensor.matmul(out, lhsT=lhsT, rhs=rhs, start=start, stop=stop), r, w)

    def tr(self, out, in_, ident, r, w):
        nc = self.nc
        return self.op(self.pe, lambda: nc.tensor.transpose(out, in_, ident), r, w)

    def actf(self, out, in_, func, r, w, scale=1.0, bias=None):
        nc = self.nc
        if bias is None:
            return self.op(self.act, lambda: nc.scalar.activation(out=out, in_=in_, func=func, scale=scale), r, w)
        return self.op(self.act, lambda: nc.scalar.activation(out=out, in_=in_, func=func, scale=scale, bias=bias), r, w)

    def cp(self, e, out, in_, r, w):
        nc = self.nc
        if e == "act":
            return self.op(self.act, lambda: nc.scalar.copy(out, in_), r, w)
        eng = nc.vector if e == "dve" else nc.gpsimd
        return self.op(self.E(e), lambda: eng.tensor_copy(out, in_), r, w)

    def tt(self, e, out, in0, in1, op, r, w):
        eng = self.nc.vector if e == "dve" else self.nc.gpsimd
        return self.op(self.E(e), lambda: eng.tensor_tensor(out=out, in0=in0, in1=in1, op=op), r, w)

    def ts(self, e, out, in0, s1, s2, op0, op1, r, w):
        eng = self.nc.vector if e == "dve" else self.nc.gpsimd
        if s2 is None:
            return self.op(self.E(e), lambda: eng.tensor_scalar(out=out, in0=in0, scalar1=s1, scalar2=None, op0=op0), r, w)
        return self.op(self.E(e), lambda: eng.tensor_scalar(out=out, in0=in0, scalar1=s1, scalar2=s2, op0=op0, op1=op1), r, w)

    def stt(self, out, in0, scalar, in1, op0, op1, r, w):
        nc = self.nc
        return self.op(self.dve, lambda: nc.vector.scalar_tensor_tensor(out=out, in0=in0, scalar=scalar, in1=in1, op0=op0, op1=op1), r, w)

    def memset(self, e, ap, val, w):
        eng = self.nc.vector if e == "dve" else self.nc.gpsimd
        return self.op(self.E(e), lambda: eng.memset(ap, val), (), w)

    def recip(self, out, in_, r, w):
        nc = self.nc
        return self.op(self.dve, lambda: nc.vector.reciprocal(out=out, in_=in_), r, w)

    def scan(self, out, d0, d1, init, r, w):
        nc = self.nc
        return self.op(self.dve, lambda: nc.vector.tensor_tensor_scan(out=out, data0=d0, data1=d1, initial=init, op0=ALU.mult, op1=ALU.add), r, w)


class Cfg:
    def __init__(self, TS=4096, NP=4, TP=256, debug=False, stages=None):
        self.TS, self.NP, self.TP = TS, NP, TP
        self.NTOK = TS + NP * TP
        self.debug = debug
        self.stages = stages
        tiles = []
        t = 0
        while t < TS:
            n = min(512, TS - t)
            tiles.append((t, n, 1))
            t += n
        while t < self.NTOK:
            n = min(512, self.NTOK - t)
            tiles.append((t, n, 0))
            t += n
        self.tiles = tiles
        self.seqs = [(0, TS, 1, 0)] + [(TS + p * TP, TP, 0, p) for p in range(NP)]


class Ctx:
    pass


def build(cfg):
    nc = bass.Bass("TRN2", target_bir_lowering=False)
    k = K(nc)
    c = Ctx()
    c.nc, c.k, c.cfg = nc, k, cfg
    NTOK, TS, NP, TP = cfg.NTOK, cfg.TS, cfg.NP, cfg.TP
    skind = "ExternalOutput" if cfg.debug else "Internal"

    def din(name, shape, dt=F32):
        return nc.dram_tensor(name, list(shape), dt, kind="ExternalInput").ap()

    def dout(name, shape):
        return nc.dram_tensor(name, list(shape), F32, kind="ExternalOutput").ap()

    def dscr(name, shape, dt=F32):
        return nc.dram_tensor(name, list(shape), dt, kind=skind).ap()

    I = c.I = {}
    I["xin"] = din("xin", [NTOK, D])
    I["ident"] = din("ident", [128, 128])
    I["ccol"] = din("ccol", [128, 8, 2])
    I["w_mod"] = din("w_mod", [L, D, 9 * D])
    I["bmod"] = din("bmod", [L, 128, 72])
    I["lng"] = din("lng", [L, 3, 128, 8])
    I["lnb"] = din("lnb", [L, 3, 128, 8])
    I["ffn_w_in"] = din("ffn_w_in", [L, 2, D, 2 * DFF])
    I["ffn_w_out"] = din("ffn_w_out", [L, 2, DFF, D])
    I["w_in"] = din("w_in", [L, D, INC])
    I["w_out"] = din("w_out", [L, D, D])
    I["cache_k"] = din("cache_k", [L, 256, 256])
    I["cache_v"] = din("cache_v", [L, 256, 256])
    I["rpbT"] = din("rpbT", [L, 64, 4, 15, 64])
    I["namask"] = din("namask", [64, 64])
    I["tmask"] = din("tmask", [6, 64, 64])
    I["bones"] = din("bones", [128, 128])
    I["hgcol"] = din("hgcol", [128, 2, 3])
    I["hgng"] = din("hgng", [L, 128, 2])
    I["st_hg"] = din("st_hg", [L, 2, 4, 64, 64])
    I["rwmu"] = din("rwmu", [L, 2, 128, 8])
    I["rwmu_ad"] = din("rwmu_ad", [L, 64, 2])
    I["rww0"] = din("rww0", [L, 2, 128, 2])
    I["rw_w_up"] = din("rw_w_up", [L, 2, 64, 256])
    I["rw_a_up"] = din("rw_a_up", [L, 64, 256])
    I["rw_g_up"] = din("rw_g_up", [L, 128, 256])
    I["rwcol"] = din("rwcol", [L, 128, 2, 6])
    I["st_rw"] = din("st_rw", [L, 2, 4, 64, 64])
    I["s5lam"] = din("s5lam", [L, 2, 128, 8, 3])
    I["s5BT"] = din("s5BT", [L, 2, 8, 128, 128])
    I["s5CT"] = din("s5CT", [L, 2, 8, 128, 128])
    I["s5col"] = din("s5col", [L, 128, 2, 2])
    I["s5glu"] = din("s5glu", [L, 256, 256])
    I["st_s5"] = din("st_s5", [L, 2, 128, 8, 2])
    O = c.O = {}
    O["y_s"] = dout("y_s", [TS, D])
    O["y_p"] = dout("y_p", [NP * TP, D])
    O["nk"] = dout("nk", [NP, L, TP, 256])
    O["nv"] = dout("nv", [NP, L, TP, 256])
    O["nhg"] = dout("nhg", [NP, L, 2, 4, 64, 64])
    O["nrw"] = dout("nrw", [NP, L, 2, 4, 64, 64])
    O["ns5"] = dout("ns5", [NP, L, 2, 16, 64, 2])
    S = c.S = {}
    S["XT"] = dscr("XT", [D, NTOK])
    S["X1T"] = dscr("X1T", [D, NTOK])
    S["PT"] = dscr("PT", [INC, NTOK])
    S["OT"] = dscr("OT", [D, NTOK])
    S["VTOK"] = dscr("VTOK", [NTOK, 256])
    for sfx in ("h", "r"):
        S["LA" + sfx] = dscr("LA" + sfx, [8, 256, NTOK])
        S["LX" + sfx] = dscr("LX" + sfx, [2, 256, NTOK])
    for sfx in ("h", "r", "5"):
        S["YS" + sfx] = dscr("YS" + sfx, [2, 256, NTOK])
    S["W1s"] = dscr("W1s", [L, 2, 44, 128, 1024], BF16)
    S["W2s"] = dscr("W2s", [L, 2, 8, 128, NF * 128], BF16)
    S["Wis"] = dscr("Wis", [L, 26, 128, 1024], BF16)
    S["Wkv"] = dscr("Wkv", [L, 128, 8, 512], BF16)
    S["Wos"] = dscr("Wos", [L, 8, 128, 1024], BF16)

    c.bank = [k.ps(f"bank{i}", [128, 512]) for i in range(8)]
    c.bi = 0
    c.freeb = list(range(8))
    c.ident = k.sb("ident_sb", [128, 128])
    k.load(c.ident[:], I["ident"], writes=["ident"])
    c.onesD = k.sb("onesD", [128, 128])
    k.memset("dve", c.onesD[:], 1.0 / D, ["onesD"])
    c.epsln = k.sb("epsln", [128, 1])
    k.memset("dve", c.epsln[:], LN_EPS / (ALPHA * ALPHA), ["epsln"])
    c.modc = k.sb("modc", [128, 72, 2])
    c.osc = k.sb("osc", [128, 3, 8, 2])
    c.gco = k.sb("gco", [128, 3, 8, 2])
    c.lng = k.sb("lng_sb", [128, L, 3, 8])
    c.lnb = k.sb("lnb_sb", [128, L, 3, 8])
    k.load(c.lng[:], I["lng"].rearrange("l i p c -> p l i c"), writes=["lng"])
    k.load(c.lnb[:], I["lnb"].rearrange("l i p c -> p l i c"), writes=["lnb"])
    c.tmask = k.sb("tmask_sb", [64, 6, 64])
    k.load(c.tmask[:], I["tmask"].rearrange("m a b -> a m b"), writes=["tmask"])
    c.bones = k.sb("bones_sb", [128, 128])
    k.load(c.bones[:], I["bones"], writes=["bones"])
    c.bones64 = k.sb("bones64_sb", [128, 128])
    k.ts("dve", c.bones64[:], c.bones[:], 1.0 / 64, None, ALU.mult, None, ["bones"], ["bones64"])

    st = cfg.stages
    if st is None or "cast" in st:
        phase_cast(c)
    if st is None or "t0" in st:
        phase_transpose_in(c)
    for l in range(L):
        if st is None or "mod" in st:
            phase_mod(c, l)
        if st is None or "A" in st:
            phase_A(c, l)
        if st is None or "mix" in st:
            esp = ExitStack()
            run_concurrent([phase_rw(c, l, "prep_g", esp), phase_hg(c, l, "prep_g", esp)])
            k.barrier()
            esp.close()
            es1 = ExitStack()
            g1 = phase_rw(c, l, "scan", es1, NL=2)
            g2 = phase_s5(c, l, "scan", es1, NW=1)
            run_concurrent([g1, g2])
            k.barrier()
            es1.close()
            es2 = ExitStack()
            g3 = phase_hg(c, l, "scan", es2, NL=3)
            g4 = phase_attn(c, l, "scan", es2)
            g5 = phase_rw(c, l, "out_g", es2)
            g6 = phase_s5(c, l, "out_g", es2)
            run_concurrent([g3, g4, g5, g6])
            k.barrier()
            es2.close()
            phase_hg(c, l, "out")
        if st is not None and "attn" in st:
            phase_attn(c, l)
        if st is not None and "hg" in st:
            phase_hg(c, l)
        if st is not None and "rw" in st:
            phase_rw(c, l)
        if st is not None and "s5" in st:
            phase_s5(c, l)
        if st is not None and "A_only" in st:
            break
        if st is None or "C" in st:
            phase_C(c, l)
        if st is not None and "L0_only" in st:
            break
    assert sorted(c.freeb) == list(range(8)), c.freeb
    if st is None or "tout" in st:
        phase_transpose_out(c)
    k.barrier()
    c.k.es_keep = k.es
    return nc, c


def nextbank(c):
    i = c.freeb.pop(0)
    c.freeb.append(i)
    return c.bank[i], ("bank", i)


def acquire(c):
    while len(c.freeb) <= 2:
        yield
    i = c.freeb.pop(0)
    return c.bank[i], ("bank", i)


def release(c, pk):
    c.freeb.append(pk[1])


def phase_cast(c):
    k, nc, I, S = c.k, c.nc, c.I, c.S
    es = ExitStack()
    NB = 2
    f32t = [k.sb(f"cast_f{i}", [128, 4 * 1024], F32, es) for i in range(NB)]
    b16t = [k.sb(f"cast_b{i}", [128, 4 * 1024], BF16, es) for i in range(NB)]
    cnt = [0]
    engs = ["dve", "act", "pool"]

    def job(pairs, per):
        i = cnt[0] % NB
        e = engs[cnt[0] % 3]
        cnt[0] += 1
        n = per * len(pairs)
        for j, (src_ap, dst_ap) in enumerate(pairs):
            a_, b_ = src_ap.shape[1], src_ap.shape[2]
            k.load(f32t[i][:, j * per:(j + 1) * per].rearrange("p (a b) -> p a b", a=a_), src_ap, writes=[("cf", i, j)])
        k.cp(e, b16t[i][:, 0:n], f32t[i][:, 0:n], [("cf", i, j) for j in range(len(pairs))], [("cb", i)])
        for j, (src_ap, dst_ap) in enumerate(pairs):
            k.store(dst_ap, b16t[i][:, j * per:(j + 1) * per], reads=[("cb", i)])

    for l in range(L):
        for f in range(2):
            src = I["ffn_w_in"][l, f].rearrange("(kc p) n -> p kc n", p=128)
            for g0 in range(0, 44, 4):
                job([(src[:, :, g * 128:(g + 1) * 128], S["W1s"][l, f, g]) for g in range(g0, g0 + 4)], 1024)
            src = I["ffn_w_out"][l, f].rearrange("(fc p) n -> p fc n", p=128)
            for g in range(8):
                job([(src[:, :, g * 128:(g + 1) * 128], S["W2s"][l, f, g])], NF * 128)
        src = I["w_in"][l].rearrange("(kc p) n -> p kc n", p=128)
        for g0 in range(0, 26, 2):
            job([(src[:, :, g * 128:(g + 1) * 128], S["Wis"][l, g]) for g in range(g0, g0 + 2)], 1024)
        job([(src[:, :, 2816:3328], S["Wkv"][l].rearrange("p kc n -> p (kc n)"))], 4096)
        src = I["w_out"][l].rearrange("(kc p) n -> p kc n", p=128)
        for g0 in range(0, 8, 4):
            job([(src[:, :, g * 128:(g + 1) * 128], S["Wos"][l, g]) for g in range(g0, g0 + 4)], 1024)
    k.barrier()
    es.close()


def phase_transpose_in(c):
    k, nc, cfg = c.k, c.nc, c.cfg
    es = ExitStack()
    XTv = c.S["XT"].rearrange("(c p) t -> p c t", p=128)
    NB = 2
    xin_t = [k.sb(f"ti_x{i}", [128, 4, D], F32, es) for i in range(NB)]
    xT_t = [k.sb(f"ti_xT{i}", [128, 8, 512], F32, es) for i in range(NB)]
    for ti, (t0, n, var) in enumerate(cfg.tiles):
        b = ti % NB
        ns = n // 128
        k.load(xin_t[b][:, 0:ns, :], c.I["xin"][t0:t0 + n, :].rearrange("(s p) d -> p s d", p=128), writes=[("tix", b)])
        for ch in range(8):
            p, pk = nextbank(c)
            for s in range(ns):
                k.tr(p[:, s * 128:(s + 1) * 128], xin_t[b][:, s, ch * 128:(ch + 1) * 128], c.ident[:], [("tix", b), "ident"], [pk])
            k.cp(k.any2(), xT_t[b][:, ch, 0:n], p[:, 0:n], [pk], [("tixT", b, ch)])
        k.store(XTv[:, :, t0:t0 + n], xT_t[b][:, :, 0:n], reads=[("tixT", b, ch) for ch in range(8)], writes=[("XT", ti)])
    k.barrier()
    es.close()


def phase_transpose_out(c):
    k, nc, cfg = c.k, c.nc, c.cfg
    es = ExitStack()
    XTv = c.S["XT"].rearrange("(c p) t -> p c t", p=128)
    NB = 2
    xT_t = [k.sb(f"to_xT{i}", [128, 8, 512], F32, es) for i in range(NB)]
    yt = [k.sb(f"to_y{i}", [128, 4, D], F32, es) for i in range(NB)]
    for ti, (t0, n, var) in enumerate(cfg.tiles):
        b = ti % NB
        ns = n // 128
        k.load(xT_t[b][:, :, 0:n], XTv[:, :, t0:t0 + n], reads=[("XT", ti)], writes=[("toxT", b)])
        for s in range(ns):
            for h in range(2):
                p, pk = nextbank(c)
                for cc in range(4):
                    ch = h * 4 + cc
                    k.tr(p[:, cc * 128:(cc + 1) * 128], xT_t[b][:, ch, s * 128:(s + 1) * 128], c.ident[:], [("toxT", b), "ident"], [pk])
                k.cp(k.any2(), yt[b][:, s, h * 512:(h + 1) * 512], p[:], [pk], [("toy", b, s, h)])
        if var == 1:
            dst = c.O["y_s"][t0:t0 + n, :]
        else:
            dst = c.O["y_p"][t0 - cfg.TS:t0 - cfg.TS + n, :]
        k.store(dst.rearrange("(s p) d -> p s d", p=128), yt[b][:, 0:ns, :],
                reads=[("toy", b, s, h) for s in range(ns) for h in range(2)])
    k.barrier()
    es.close()


def phase_mod(c, l):
    k, nc, I = c.k, c.nc, c.I
    es = ExitStack()
    ccol = k.sb("mod_c", [128, 8, 2], F32, es)
    csil = k.sb("mod_cs", [128, 8, 2], F32, es)
    bm = k.sb("mod_bm", [128, 72], F32, es)
    k.load(ccol[:], I["ccol"], writes=["ccol"])
    k.load(bm[:], I["bmod"][l], writes=["bm"])
    k.actf(csil[:], ccol[:], AF.Silu, ["ccol"], ["csil"])
    FB = 1152
    NB = 2
    wt = [k.sb(f"mod_w{i}", [128, 8, FB], F32, es) for i in range(NB)]
    p, pk = nextbank(c)
    wv = I["w_mod"][l].rearrange("(kc p) f -> p kc f", p=128)
    for bi in range(9 * D // FB):
        b = bi % NB
        k.load(wt[b][:], wv[:, :, bi * FB:(bi + 1) * FB], writes=[("modw", b)])
        for fj in range(FB // 128):
            f = bi * (FB // 128) + fj
            for kc in range(8):
                k.mm(p[:, 2 * f:2 * f + 2], wt[b][:, kc, fj * 128:(fj + 1) * 128], csil[:, kc, :], kc == 0, kc == 7,
                     [("modw", b), "csil"], [pk])
    k.tt("dve", c.modc[:], p[:, 0:144].rearrange("p (f n) -> p f n", n=2), bm[:].unsqueeze(2).to_broadcast([128, 72, 2]), ALU.add,
         [pk, "bm"], ["modc"])
    for i in range(3):
        k.ts("dve", c.osc[:, i], c.modc[:, (3 * i + 1) * 8:(3 * i + 2) * 8, :], 1.0, None, ALU.add, None, ["modc"], ["osc"])
        coef = (0.5 if i != 1 else 1.0) / ALPHA
        k.ts("dve", c.gco[:, i], c.modc[:, (3 * i + 2) * 8:(3 * i + 3) * 8, :], coef, None, ALU.mult, None, ["modc"], ["gco"])
    k.barrier()
    es.close()


def layernorm(c, z, zk, n, gcol, bcol, xout, xoutk, tmp, li):
    k, nc = c.k, c.nc
    sq, msq, rstd = tmp["sq"], tmp["msq"], tmp["rstd"]
    pm, pmk = nextbank(c)
    pe2, pe2k = nextbank(c)
    for ch in range(8):
        k.actf(sq[:, ch, 0:n], z[:, ch, 0:n], AF.Square, [(zk, ch)], [("lnsq", ch)])
    for ch in range(8):
        k.mm(pm[:, 0:n], c.onesD[:], z[:, ch, 0:n], ch == 0, ch == 7, [(zk, ch), "onesD"], [pmk])
    for ch in range(8):
        k.mm(pe2[:, 0:n], c.onesD[:], sq[:, ch, 0:n], ch == 0, ch == 7, [("lnsq", ch), "onesD"], [pe2k])
    k.actf(msq[:, 0:n], pm[:, 0:n], AF.Square, [pmk], ["lnmsq"])
    k.tt("dve", msq[:, 0:n], pe2[:, 0:n], msq[:, 0:n], ALU.subtract, [pe2k, "lnmsq"], ["lnmsq"])
    k.actf(rstd[:, 0:n], msq[:, 0:n], AF.Sqrt, ["lnmsq", "epsln"], ["lnrstd"], bias=c.epsln[:, 0:1])
    k.recip(rstd[:, 0:n], rstd[:, 0:n], ["lnrstd"], ["lnrstd"])
    for ch in range(8):
        k.tt("dve", sq[:, ch, 0:n], z[:, ch, 0:n], pm[:, 0:n], ALU.subtract, [(zk, ch), pmk, ("lnsq", ch)], [("lnsq", ch)])
        e = "pool" if ch % 2 else "dve"
        k.tt(e, sq[:, ch, 0:n], sq[:, ch, 0:n], rstd[:, 0:n], ALU.mult, [("lnsq", ch), "lnrstd"], [("lnsq", ch)])
        k.actf(xout[:, ch, 0:n], sq[:, ch, 0:n], AF.Identity, [("lnsq", ch), "lng", "lnb"], [(xoutk, ch)],
               scale=gcol[:, ch:ch + 1], bias=bcol[:, ch:ch + 1])


def modulate(c, x, xk, n, i, var, xm, xmk):
    k = c.k
    for ch in range(8):
        k.actf(xm[:, ch, 0:n], x[:, ch, 0:n], AF.Identity, [(xk, ch), "osc", "modc"], [(xmk, ch)],
               scale=c.osc[:, i, ch, var:var + 1], bias=c.modc[:, (3 * i) * 8 + ch, var:var + 1])


def ffn(c, l, f, xm, xmk, n, h, zres, zresk, i, var, wb):
    k, nc, S = c.k, c.nc, c.S
    w1, w2, sg = wb["w1"], wb["w2"], wb["sg"]
    NW1 = len(w1)
    order = []
    for fc in range(NF):
        order.append(fc)
        order.append(NF + fc)

    def ldw1(j):
        b = j % NW1
        k.load(w1[b][:], S["W1s"][l, f, order[j]], writes=[("w1", b)])
    PF = NW1 - 1
    for j in range(min(PF, len(order))):
        ldw1(j)
    for fc in range(NF):
        banks = []
        for half in range(2):
            j = 2 * fc + half
            if j + PF < len(order):
                ldw1(j + PF)
            b = j % NW1
            p, pk = nextbank(c)
            banks.append((p, pk))
            for kc in range(8):
                k.mm(p[:, 0:n], w1[b][:, kc * 128:(kc + 1) * 128], xm[:, kc, 0:n], kc == 0, kc == 7, [("w1", b), (xmk, kc)], [pk])
        (pg, pgk), (pu, puk) = banks
        sb_ = fc % 2
        k.actf(sg[sb_][:, 0:n], pg[:, 0:n], AF.Silu, [pgk], [("sg", sb_)])
        k.tt("dve", h[:, fc, 0:n], sg[sb_][:, 0:n], pu[:, 0:n], ALU.mult, [("sg", sb_), puk], [("h", fc)])
    NW2 = len(w2)
    for dc in range(min(NW2 - 1, 8)):
        k.load(w2[dc % NW2][:], S["W2s"][l, f, dc], writes=[("w2", dc % NW2)])
    for dc in range(8):
        if dc + NW2 - 1 < 8:
            d2 = dc + NW2 - 1
            k.load(w2[d2 % NW2][:], S["W2s"][l, f, d2], writes=[("w2", d2 % NW2)])
        b = dc % NW2
        p, pk = nextbank(c)
        for fc in range(NF):
            k.mm(p[:, 0:n], w2[b][:, fc * 128:(fc + 1) * 128], h[:, fc, 0:n], fc == 0, fc == NF - 1, [("w2", b), ("h", fc)], [pk])
        k.stt(zres[:, dc, 0:n], p[:, 0:n], c.gco[:, i, dc, var:var + 1], zres[:, dc, 0:n], ALU.mult, ALU.add,
              [pk, "gco", (zresk, dc)], [(zresk, dc)])


def alloc_AC(c, es):
    k = c.k
    t = {}
    t["x"] = [k.sb(f"ac_x{i}", [128, 8, 512], F32, es) for i in range(2)]
    t["x1"] = k.sb("ac_x1", [128, 8, 512], F32, es)
    t["xm"] = k.sb("ac_xm", [128, 8, 512], BF16, es)
    t["h"] = k.sb("ac_h", [128, NF, 512], BF16, es)
    t["ln"] = {"sq": k.sb("ac_sq", [128, 8, 512], F32, es), "msq": k.sb("ac_msq", [128, 512], F32, es),
               "rstd": k.sb("ac_rstd", [128, 512], F32, es)}
    t["wb"] = {"w1": [k.sb(f"ac_w1_{i}", [128, 1024], BF16, es) for i in range(4)],
               "w2": [k.sb(f"ac_w2_{i}", [128, NF * 128], BF16, es) for i in range(2)],
               "sg": [k.sb(f"ac_sg{i}", [128, 512], F32, es) for i in range(2)]}
    t["wi"] = [k.sb(f"ac_wi{i}", [128, 1024], BF16, es) for i in range(4)]
    t["wkv"] = k.sb("ac_wkv", [128, 8, 512], BF16, es)
    t["pb"] = [k.sb(f"ac_pb{i}", [128, 2, 512], F32, es) for i in range(2)]
    t["tok"] = [k.sb(f"ac_tok{i}", [128, 512], F32, es) for i in range(2)]
    return t


def phase_A(c, l):
    k, nc, cfg, S, O = c.k, c.nc, c.cfg, c.S, c.O
    es = ExitStack()
    t = alloc_AC(c, es)
    XTv = S["XT"].rearrange("(c p) t -> p c t", p=128)
    X1Tv = S["X1T"].rearrange("(c p) t -> p c t", p=128)
    PTv = S["PT"].rearrange("(c p) t -> p c t", p=128)
    k.load(t["wkv"][:], S["Wkv"][l], writes=["wkv"])
    tiles = cfg.tiles

    def ldx(ti):
        t0, n, var = tiles[ti]
        b = ti % 2
        k.load(t["x"][b][:, :, 0:n], XTv[:, :, t0:t0 + n], reads=[("XT", ti)], writes=[(("x", b), ch) for ch in range(8)])
    ldx(0)
    for ti, (t0, n, var) in enumerate(tiles):
        b = ti % 2
        x, xk = t["x"][b], ("x", b)
        if ti + 1 < len(tiles):
            ldx(ti + 1)
        modulate(c, x, xk, n, 0, var, t["xm"], "xm")
        ffn(c, l, 0, t["xm"], "xm", n, t["h"], x, xk, 0, var, t["wb"])
        layernorm(c, x, xk, n, c.lng[:, l, 0, :], c.lnb[:, l, 0, :], t["x1"], "x1", t["ln"], 0)
        k.store(X1Tv[:, :, t0:t0 + n], t["x1"][:, :, 0:n], reads=[("x1", ch) for ch in range(8)], writes=[("X1T", ti)])
        modulate(c, t["x1"], "x1", n, 1, var, t["xm"], "xm")
        wi = t["wi"]
        NWI = len(wi)
        for j in range(NWI - 1):
            k.load(wi[j][:], S["Wis"][l, j], writes=[("wi", j)])
        for cc in range(26):
            if cc + NWI - 1 < 26:
                j = cc + NWI - 1
                k.load(wi[j % NWI][:], S["Wis"][l, j], writes=[("wi", j % NWI)])
            b2 = cc % NWI
            p, pk = nextbank(c)
            for kc in range(8):
                k.mm(p[:, 0:n], wi[b2][:, kc * 128:(kc + 1) * 128], t["xm"][:, kc, 0:n], kc == 0, kc == 7, [("wi", b2), ("xm", kc)], [pk])
            pbi = (cc // 2) % 2
            k.cp(k.any2(), t["pb"][pbi][:, cc % 2, 0:n], p[:, 0:n], [pk], [("pb", pbi, cc % 2)])
            if cc % 2 == 1:
                k.store(PTv[:, cc - 1:cc + 1, t0:t0 + n], t["pb"][pbi][:, :, 0:n], reads=[("pb", pbi, 0), ("pb", pbi, 1)], writes=[("PT", ti)])
        for s in range(n // 128):
            p, pk = nextbank(c)
            for kc in range(8):
                k.mm(p[:, :], t["xm"][:, kc, s * 128:(s + 1) * 128], t["wkv"][:, kc, :], kc == 0, kc == 7, [("xm", kc), "wkv"], [pk])
            tb = s % 2
            k.cp(k.any2(), t["tok"][tb][:], p[:], [pk], [("tok", tb)])
            ta = t0 + s * 128
            k.store(S["VTOK"][ta:ta + 128, :], t["tok"][tb][:, 256:512], reads=[("tok", tb)], writes=[("VTOK", ti)])
            if var == 0:
                q = ta - cfg.TS
                pi_, tt_ = q // cfg.TP, q % cfg.TP
                k.store(O["nk"][pi_, l, tt_:tt_ + 128, :], t["tok"][tb][:, 0:256], reads=[("tok", tb)])
                k.store(O["nv"][pi_, l, tt_:tt_ + 128, :], t["tok"][tb][:, 256:512], reads=[("tok", tb)])
    k.barrier()
    es.close()


def phase_C(c, l):
    k, nc, cfg, S = c.k, c.nc, c.cfg, c.S
    es = ExitStack()
    t = alloc_AC(c, es)
    XTv = S["XT"].rearrange("(c p) t -> p c t", p=128)
    X1Tv = S["X1T"].rearrange("(c p) t -> p c t", p=128)
    OTv = S["OT"].rearrange("(c p) t -> p c t", p=128)
    tiles = cfg.tiles
    ot = t["ln"]["sq"]
    wo = t["wi"]
    for ti, (t0, n, var) in enumerate(tiles):
        x1 = t["x1"]
        k.load(x1[:, :, 0:n], X1Tv[:, :, t0:t0 + n], reads=[("X1T", ti)], writes=[("x1", ch) for ch in range(8)])
        k.load(ot[:, :, 0:n], OTv[:, :, t0:t0 + n], reads=[("OT", ti)], writes=[("lnsq", ch) for ch in range(8)])
        for ch in range(8):
            k.cp(k.any2(), t["xm"][:, ch, 0:n], ot[:, ch, 0:n], [("lnsq", ch)], [("xm", ch)])
        NWO = len(wo)
        for j in range(NWO - 1):
            k.load(wo[j][:], S["Wos"][l, j], writes=[("wi", j)])
        for dc in range(8):
            if dc + NWO - 1 < 8:
                j = dc + NWO - 1
                k.load(wo[j % NWO][:], S["Wos"][l, j], writes=[("wi", j % NWO)])
            b2 = dc % NWO
            p, pk = nextbank(c)
            for kc in range(8):
                k.mm(p[:, 0:n], wo[b2][:, kc * 128:(kc + 1) * 128], t["xm"][:, kc, 0:n], kc == 0, kc == 7, [("wi", b2), ("xm", kc)], [pk])
            k.stt(x1[:, dc, 0:n], p[:, 0:n], c.gco[:, 1, dc, var:var + 1], x1[:, dc, 0:n], ALU.mult, ALU.add,
                  [pk, "gco", ("x1", dc)], [("x1", dc)])
        x2 = t["x"][0]
        layernorm(c, x1, "x1", n, c.lng[:, l, 1, :], c.lnb[:, l, 1, :], x2, ("x", 0), t["ln"], 1)
        modulate(c, x2, ("x", 0), n, 2, var, t["xm"], "xm")
        ffn(c, l, 1, t["xm"], "xm", n, t["h"], x2, ("x", 0), 2, var, t["wb"])
        x3 = t["x"][1]
        layernorm(c, x2, ("x", 0), n, c.lng[:, l, 2, :], c.lnb[:, l, 2, :], x3, ("x", 1), t["ln"], 2)
        k.store(XTv[:, :, t0:t0 + n], x3[:, :, 0:n], reads=[(("x", 1), ch) for ch in range(8)], writes=[("XT", ti)])
    k.barrier()
    es.close()


def attn_core_gen(c, q_ap, nq, A, bias_ap, B, out_ap, rkeys, okey, T, slot):
    k = c.k
    ev = []
    na, nb = len(A), len(B)
    if A:
        pa, pak = yield from acquire(c)
        for i, (kt, v) in enumerate(A):
            k.mm(pa[0:64, i * nq:(i + 1) * nq], kt, q_ap, True, True, rkeys, [pak])
    if B:
        pb, pbk = yield from acquire(c)
        for i, (kt, v) in enumerate(B):
            k.mm(pb[0:64, i * nq:(i + 1) * nq], kt, q_ap, True, True, rkeys, [pbk])
    yield
    if A:
        ea, eak = T["EA"][slot], ("EA", slot)
        eab, eabk = T["EAb"][slot], ("EAb", slot)
        k.stt(ea[:, 0:na, 0:nq], pa[0:64, 0:na * nq].rearrange("p (a q) -> p a q", q=nq), 0.125, bias_ap, ALU.mult, ALU.add,
              [pak, "at_B"], [eak])
        release(c, pak)
        k.actf(eab[:, 0:na, 0:nq], ea[:, 0:na, 0:nq], AF.Exp, [eak], [eabk])
        for i, (kt, v) in enumerate(A):
            ev.append((eab[:, i, 0:nq], v, eabk))
    if B:
        eb, ebk = T["EB"][slot], ("EB", slot)
        k.actf(eb[:, 0:nb * nq], pb[0:64, 0:nb * nq], AF.Exp, [pbk], [ebk], scale=0.125)
        release(c, pbk)
        for i, (kt, v) in enumerate(B):
            ev.append((eb[:, i * nq:(i + 1) * nq], v, ebk))
    yield
    pn, pnk = yield from acquire(c)
    n = len(ev)
    for i, (e, v, ek) in enumerate(ev):
        k.mm(pn[0:nq, 0:65], e, v, i == 0, i == n - 1, rkeys + [ek], [pnk])
    yield
    rd, rdk = T["rden"][slot], ("rden", slot)
    otk, otkk = T["otok"][slot], ("otok", slot)
    k.recip(rd[0:nq, 0:1], pn[0:nq, 64:65], [pnk], [rdk])
    k.ts("dve", otk[0:nq, :], pn[0:nq, 0:64], rd[0:nq, 0:1], None, ALU.mult, None, [pnk, rdk], [otkk])
    yield
    k.tr(pn[0:64, 128:128 + nq], otk[0:nq, :], c.ident[0:nq, 0:nq], [otkk, "ident"], [pnk])
    yield
    k.cp("act", out_ap, pn[0:64, 128:128 + nq], [pnk], [okey])
    release(c, pnk)
    yield


def phase_attn(c, l, part=None, es_ext=None):
    k, nc, cfg, S, I = c.k, c.nc, c.cfg, c.S, c.I
    es = ExitStack() if es_ext is None else es_ext
    TS, TP = cfg.TS, cfg.TP
    R = TS // 64
    Tmax = max(TS, TP)
    NS = 2
    stg = k.sb("at_stg", [64, Tmax], F32, es)
    qT = k.sb("at_q", [64, Tmax], BF16, es)
    kT = k.sb("at_k", [64, Tmax], BF16, es)
    vt = k.sb("at_v", [64, Tmax // 64, 65], BF16, es)
    ot = k.sb("at_o", [64, Tmax], F32, es)
    T = {}
    T["EA"] = [k.sb(f"at_ea{i}", [64, 8, 64], F32, es) for i in range(NS)]
    T["EAb"] = [k.sb(f"at_eab{i}", [64, 8, 64], BF16, es) for i in range(NS)]
    T["EB"] = [k.sb(f"at_eb{i}", [64, 512], BF16, es) for i in range(NS)]
    T["rden"] = [k.sb(f"at_rd{i}", [128, 1], F32, es) for i in range(NS)]
    T["otok"] = [k.sb(f"at_otok{i}", [128, 64], F32, es) for i in range(NS)]
    Bt = k.sb("at_B", [64, 15, 64], F32, es)
    mask = k.sb("at_mask", [64, 64], F32, es)
    kctok = k.sb("at_kctok", [64, 4, 64], F32, es)
    kcT = k.sb("at_kcT", [64, 4, 64], BF16, es)
    vcs = k.sb("at_vcs", [64, 4, 64], F32, es)
    vc = k.sb("at_vc", [64, 4, 65], BF16, es)
    k.load(mask[:], I["namask"], writes=["at_mask"])
    k.memset("dve", vt[:, :, 64:65], 1.0, ["at_v1"])
    k.memset("dve", vc[:, :, 64:65], 1.0, ["at_vc1"])
    rk = ["at_q", "at_k", "at_v", "at_kcT", "at_vc", "at_v1", "at_vc1"]
    slots = list(range(NS))

    def gen():
        for h in range(4):
            hs = slice(64 * h, 64 * h + 64)

            def load_qkv(s0, Tn):
                k.load(stg[:, 0:Tn], S["PT"][2560 + 64 * h:2560 + 64 * h + 64, s0:s0 + Tn], writes=["at_stg"])
                k.cp("act", qT[:, 0:Tn], stg[:, 0:Tn], ["at_stg"], ["at_q"])
                k.load(stg[:, 0:Tn], S["PT"][2816 + 64 * h:2816 + 64 * h + 64, s0:s0 + Tn], writes=["at_stg"])
                k.cp("dve", kT[:, 0:Tn], stg[:, 0:Tn], ["at_stg"], ["at_k"])
                nr = Tn // 64
                k.load(stg[:, 0:Tn].rearrange("p (r d) -> p r d", d=64), S["VTOK"][s0:s0 + Tn, hs].rearrange("(r c) d -> c r d", c=64),
                       writes=["at_stg"])
                k.cp("pool", vt[:, 0:nr, 0:64], stg[:, 0:Tn].rearrange("p (r d) -> p r d", d=64), ["at_stg"], ["at_v"])
            load_qkv(0, TS)
            k.load(Bt[:], I["rpbT"][l, :, h], writes=["at_B"])
            k.tt("dve", Bt[:], Bt[:], mask[:].unsqueeze(1).to_broadcast([64, 15, 64]), ALU.add, ["at_B", "at_mask"], ["at_B"])
            k.load(kctok[:], I["cache_k"][l][:, hs].rearrange("(ch c) d -> c ch d", c=64), writes=["at_kctok"])
            k.load(vcs[:], I["cache_v"][l][:, hs].rearrange("(ch c) d -> c ch d", c=64), writes=["at_vcs"])
            k.cp("dve", vc[:, :, 0:64], vcs[:], ["at_vcs"], ["at_vc"])
            p, pk = yield from acquire(c)
            for ch in range(4):
                k.tr(p[0:64, ch * 64:(ch + 1) * 64], kctok[:, ch, :], c.ident[0:64, 0:64], ["at_kctok", "ident"], [pk])
            k.cp("dve", kcT[:].rearrange("p a b -> p (a b)"), p[0:64, 0:256], [pk], ["at_kcT"])
            release(c, pk)
            yield
            jobs = []
            kr = min(8, R)
            for r in range(R):
                rs = min(max(r - kr // 2, 0), R - kr)
                dr0 = rs - r + 7
                A = [(kT[:, (rs + i) * 64:(rs + i + 1) * 64], vt[:, rs + i, :]) for i in range(kr)]
                B = [(kcT[:, j, :], vc[:, j, :]) for j in range(4)]
                jobs.append(lambda slot, r=r, A=A, B=B, dr0=dr0: attn_core_gen(
                    c, qT[:, r * 64:(r + 1) * 64], 64, A, Bt[:, dr0:dr0 + kr, :], B, ot[:, r * 64:(r + 1) * 64], rk, ("at_o", r), T, slot))
            yield from drive_gen(jobs, slots)
            k.store(S["OT"][768 + 64 * h:768 + 64 * h + 64, 0:TS], ot[:, 0:TS], reads=[("at_o", r) for r in range(R)])
            yield
            for (s0, Tn, kind, pi_) in cfg.seqs[1:]:
                load_qkv(s0, Tn)
                yield
                jobs = []
                B = [(kT[:, j * 64:(j + 1) * 64], vt[:, j, :]) for j in range(Tn // 64)]
                for qb in range(Tn // 128):
                    jobs.append(lambda slot, qb=qb, B=B: attn_core_gen(
                        c, qT[:, qb * 128:(qb + 1) * 128], 128, [], None, B, ot[:, qb * 128:(qb + 1) * 128], rk, ("at_o", qb), T, slot))
                yield from drive_gen(jobs, slots)
                k.store(S["OT"][768 + 64 * h:768 + 64 * h + 64, s0:s0 + Tn], ot[:, 0:Tn], reads=[("at_o", qb) for qb in range(Tn // 128)])
                yield

    g = gen()
    if part == "scan":
        return g
    run_concurrent([g])
    k.barrier()
    es.close()


def la_alloc(c, es, delta, CH, nch, tag):
    k = c.k
    TT = nch * CH
    t = {"TT": TT, "nch": nch, "C": CH, "tag": tag}

    def fm(name):
        return k.sb(f"la{tag}_{name}", [64, TT], F32, es)

    def tk(name):
        return k.sb(f"la{tag}_{name}", [64, nch, 64], F32, es)
    t["d"] = []
    for d in range(2):
        u = {}
        for nm in ["K", "LW", "cum", "cumc", "Eabs", "Erel", "Em", "Qabs", "Qrel", "Kd", "Ke", "ybuf", "R", "V"]:
            u[nm] = fm(f"{nm}{d}")
        u["Gm"] = k.sb(f"la{tag}_Gm{d}", [64, nch], F32, es)
        u["KeT"], u["RKT"], u["Vm"] = tk(f"KeT{d}"), tk(f"RKT{d}"), tk(f"Vm{d}")
        u["S"] = [k.sb(f"la{tag}_S{d}_{i}", [64, 64], F32, es) for i in range(2)]
        u["Sr"] = [k.sb(f"la{tag}_Sr{d}_{i}", [64, 64], F32, es) for i in range(2)]
        u["si"] = 0
        if delta:
            for nm in ["KK", "BK", "cp", "E0", "KKabs", "KKrel", "Bd", "Be"]:
                u[nm] = fm(f"{nm}{d}")
            for nm in ["BeT", "RBT", "AkT", "M", "P", "IP", "Q", "M2", "P2"]:
                u[nm] = tk(f"{nm}{d}")
            u["rhs0"] = k.sb(f"la{tag}_rhs0{d}", [64, 64], F32, es)
            u["U"] = k.sb(f"la{tag}_U{d}", [64, 64], F32, es)
        t["d"].append(u)
    t["stmp"] = k.sb(f"la{tag}_stmp", [64, 64], F32, es)
    return t


def la_consts(c, es, CH, nch):
    k = c.k
    TT = CH * nch
    cst = {}
    cst["rmask"] = k.sb("la_rmask", [64, TT], F32, es)
    cst["rmaskb"] = k.sb("la_rmaskb", [64, TT], F32, es)
    k.memset("dve", cst["rmask"][:], 1.0, ["rmask"])
    k.memset("dve", cst["rmask"][:].rearrange("p (a b) -> p a b", b=CH)[:, :, 0:1], 0.0, ["rmask"])
    k.memset("dve", cst["rmaskb"][:], 1.0, ["rmask"])
    k.memset("dve", cst["rmaskb"][:].rearrange("p (a b) -> p a b", b=CH)[:, :, CH - 1:CH], 0.0, ["rmask"])
    return cst


def la_lane(c, t, cst, delta, arrs, YS, seq, st_in, st_out, transpose_state):
    k = c.k
    s0, T = seq
    CH = t["C"]
    MID = CH // 2
    TT = t["TT"]
    nch = t["nch"]
    assert T % TT == 0
    ntile = T // TT
    tm = c.tmask
    I64 = c.ident[0:64, 0:64]
    tag = t["tag"]
    U_ = t["d"]
    DK = [(tag, 0), (tag, 1)]

    def v3(ap):
        return ap[:, 0:TT].rearrange("p (a b) -> p a b", b=CH)

    def bc(ap2):
        return ap2[:, 0:nch].unsqueeze(2).to_broadcast([64, nch, CH])

    def mbc(mi):
        return tm[0:CH, mi, 0:CH].unsqueeze(1).to_broadcast([CH, nch, CH])

    def rvf(d):
        return (lambda ap: ap[:, 0:TT]) if d == 0 else (lambda ap: ap[:, 0:TT][:, ::-1])
    ibc = c.ident[0:64, 0:64].unsqueeze(1).to_broadcast([64, nch, 64])
    fr = lambda ap: ap.bitcast(F32R)
    stk = ("stmp", tag)
    for d in range(2):
        u, dk = U_[d], DK[d]
        u["si"] = 0
        S0 = u["S"][0]
        if st_in is None:
            k.memset("dve", S0[:], 0.0, [("S", dk, 0)])
        elif transpose_state:
            k.load(t["stmp"][:], st_in[d], writes=[stk])
            p, pk = yield from acquire(c)
            k.tr(p[0:64, 0:64], t["stmp"][:], I64, [stk, "ident"], [pk])
            k.cp("dve", S0[:], p[0:64, 0:64], [pk], [("S", dk, 0)])
            release(c, pk)
        else:
            k.load(S0[:], st_in[d], writes=[("S", dk, 0)])
        k.cp("act", fr(u["Sr"][0][:]), S0[:], [("S", dk, 0)], [("Sr", dk, 0)])
    yield
    for it in range(ntile):
        tis = [it, ntile - 1 - it]
        for d in range(2):
            u, dk = U_[d], DK[d]
            a0 = s0 + tis[d] * TT
            sl = slice(a0, a0 + TT)
            k.load(u["R"][:, 0:TT], arrs["R"][:, sl], writes=[("R", dk)])
            k.load(u["V"][:, 0:TT], arrs["V"][:, sl], writes=[("V", dk)])
            k.load(u["K"][:, 0:TT], arrs[f"K{d}"][:, sl], writes=[("K", dk)])
            k.load(u["LW"][:, 0:TT], arrs[f"LW{d}"][:, sl], writes=[("LW", dk)])
            if delta:
                k.load(u["KK"][:, 0:TT], arrs["KK"][:, sl], writes=[("KK", dk)])
                k.load(u["BK"][:, 0:TT], arrs["BK"][:, sl], writes=[("BK", dk)])
        yield
        pv = []
        for d in range(2):
            u, dk = U_[d], DK[d]
            p, pk = yield from acquire(c)
            pv.append((p, pk))
            for ch in range(nch):
                k.tr(p[0:CH, ch * 64:(ch + 1) * 64], u["V"][:, ch * CH:(ch + 1) * CH], I64, [("V", dk), "ident"], [pk])
        yield
        for d in range(2):
            u, dk = U_[d], DK[d]
            rv = rvf(d)
            p, pk = pv[d]
            k.cp("act", fr(u["Vm"][0:CH, 0:nch, :].rearrange("p a b -> p (a b)")), p[0:CH, 0:nch * 64], [pk], [("Vm", dk)])
            release(c, pk)
            k.scan(rv(u["cum"]), rv(cst["rmask"] if d == 0 else cst["rmaskb"]), rv(u["LW"]), 0.0, [("LW", dk), "rmask"], [("cum", dk)])
            k.actf(u["Eabs"][:, 0:TT], u["cum"][:, 0:TT], AF.Exp, [("cum", dk)], [("Eabs", dk)])
            k.tt("dve", v3(u["cumc"]), v3(u["cum"]), v3(u["cum"])[:, :, MID:MID + 1].to_broadcast([64, nch, CH]), ALU.subtract,
                 [("cum", dk)], [("cumc", dk)])
            k.actf(u["Erel"][:, 0:TT], u["cumc"][:, 0:TT], AF.Exp, [("cumc", dk)], [("Erel", dk)])
            k.actf(u["Em"][:, 0:TT], u["cumc"][:, 0:TT], AF.Exp, [("cumc", dk)], [("Em", dk)], scale=-1.0)
            last = CH - 1 if d == 0 else 0
            k.actf(u["Gm"][:, 0:nch], v3(u["cumc"])[:, :, last], AF.Exp, [("cumc", dk)], [("Gm", dk)])
        yield
        for d in range(2):
            u, dk = U_[d], DK[d]
            k.tt("pool", fr(u["Qabs"][:, 0:TT]), u["R"][:, 0:TT], u["Eabs"][:, 0:TT], ALU.mult, [("R", dk), ("Eabs", dk)], [("Qabs", dk)])
            k.tt("pool", fr(u["Qrel"][:, 0:TT]), u["R"][:, 0:TT], u["Erel"][:, 0:TT], ALU.mult, [("R", dk), ("Erel", dk)], [("Qrel", dk)])
            k.tt("dve", fr(u["Kd"][:, 0:TT]), u["K"][:, 0:TT], u["Em"][:, 0:TT], ALU.mult, [("K", dk), ("Em", dk)], [("Kd", dk)])
            k.tt("dve", v3(u["Ke"]), v3(u["Kd"]), bc(u["Gm"]), ALU.mult, [("Kd", dk), ("Gm", dk)], [("Ke", dk)])
            if delta:
                k.tt("dve", u["cp"][:, 0:TT], u["cum"][:, 0:TT], u["LW"][:, 0:TT], ALU.subtract, [("cum", dk), ("LW", dk)], [("cp", dk)])
                k.actf(u["E0"][:, 0:TT], u["cp"][:, 0:TT], AF.Exp, [("cp", dk)], [("E0", dk)])
                k.tt("pool", fr(u["KKabs"][:, 0:TT]), u["KK"][:, 0:TT], u["E0"][:, 0:TT], ALU.mult, [("KK", dk), ("E0", dk)], [("KKabs", dk)])
                k.tt("dve", v3(u["cp"]), v3(u["cp"]), v3(u["cum"])[:, :, MID:MID + 1].to_broadcast([64, nch, CH]), ALU.subtract,
                     [("cp", dk), ("cum", dk)], [("cp", dk)])
                k.actf(u["E0"][:, 0:TT], u["cp"][:, 0:TT], AF.Exp, [("cp", dk)], [("E0", dk)])
                k.tt("pool", u["KKrel"][:, 0:TT], u["KK"][:, 0:TT], u["E0"][:, 0:TT], ALU.mult, [("KK", dk), ("E0", dk)], [("KKrel", dk)])
                k.tt("dve", u["Bd"][:, 0:TT], u["BK"][:, 0:TT], u["Em"][:, 0:TT], ALU.mult, [("BK", dk), ("Em", dk)], [("Bd", dk)])
                k.tt("dve", v3(u["Be"]), v3(u["Bd"]), bc(u["Gm"]), ALU.mult, [("Bd", dk), ("Gm", dk)], [("Be", dk)])
        yield
        pv = []
        for d in range(2):
            u, dk = U_[d], DK[d]
            p, pk = yield from acquire(c)
            pv.append((p, pk))
            for ch in range(nch):
                k.tr(p[0:CH, ch * 64:(ch + 1) * 64], u["Ke"][:, ch * CH:(ch + 1) * CH], I64, [("Ke", dk), "ident"], [pk])
            if delta:
                for ch in range(nch):
                    k.tr(p[0:CH, (nch + ch) * 64:(nch + ch + 1) * 64], u["Be"][:, ch * CH:(ch + 1) * CH], I64, [("Be", dk), "ident"], [pk])
        yield
        for d in range(2):
            u, dk = U_[d], DK[d]
            p, pk = pv[d]
            k.cp("act", fr(u["KeT"][0:CH, 0:nch, :].rearrange("p a b -> p (a b)")), p[0:CH, 0:nch * 64], [pk], [("KeT", dk)])
            if delta:
                k.cp("dve", fr(u["BeT"][0:CH, 0:nch, :].rearrange("p a b -> p (a b)")), p[0:CH, nch * 64:2 * nch * 64], [pk], [("BeT", dk)])
            release(c, pk)
        yield
        W4 = nch * CH
        specs = [("RKT", "Kd", "Qrel", "MI")]
        if delta:
            specs += [("RBT", "Bd", "Qrel", "MI"), ("AkT", "Kd", "KKrel", "MS"), ("M", "Bd", "KKrel", "MSneg"), ("P", "KKrel", "Bd", "MSntneg")]
        per = max(1, 512 // W4)
        for g0 in range(0, len(specs), per):
            grp = specs[g0:g0 + per]
            pv = []
            for d in range(2):
                u, dk = U_[d], DK[d]
                p, pk = yield from acquire(c)
                pv.append((p, pk))
                for si, (dst, lh, rh, mk_) in enumerate(grp):
                    for ch in range(nch):
                        cs_ = slice(ch * CH, (ch + 1) * CH)
                        opf = fr if dst == "RKT" else (lambda x: x)
                        k.mm(p[0:CH, si * W4 + ch * CH:si * W4 + (ch + 1) * CH], opf(u[lh][:, cs_]), opf(u[rh][:, cs_]), True, True,
                             [(lh, dk), (rh, dk)], [pk])
            yield
            for d in range(2):
                u, dk = U_[d], DK[d]
                p, pk = pv[d]
                mids = {"MI": 0, "MS": 1, "MSntneg": 5, "MSneg": 4} if d == 0 else {"MI": 2, "MS": 3, "MSntneg": 4, "MSneg": 5}
                for si, (dst, lh, rh, mk_) in enumerate(grp):
                    k.tt("dve", (fr if dst in ("RKT", "RBT", "AkT") else (lambda x: x))(u[dst][0:CH, 0:nch, 0:CH]),
                         p[0:CH, si * W4:(si + 1) * W4].rearrange("p (a b) -> p a b", b=CH) if False else
                         p[0:CH, si * W4:(si + 1) * W4].rearrange("p (a b) -> p a b", b=CH), mbc(mids[mk_]), ALU.mult,
                         [pk, "tmask"], [(dst, dk)])
                release(c, pk)
            yield
        if delta:
            cur = [["M", "P", "M2", "P2"], ["M", "P", "M2", "P2"]]
            for d in range(2):
                u, dk = U_[d], DK[d]
                k.tt("dve", u["Q"][:, 0:nch, :], u["M"][:, 0:nch, :], ibc, ALU.add, [("M", dk), "ident"], [("Q", dk)])
            W2 = nch * 64
            for lev in range(1, 6):
                pv = []
                for d in range(2):
                    u, dk = U_[d], DK[d]
                    Mc, Pc, Mn, Pn = cur[d]
                    p, pk = yield from acquire(c)
                    pv.append((p, pk))
                    for ch in range(nch):
                        k.mm(p[0:64, ch * 64:(ch + 1) * 64], u[Mc][:, ch, :], u[Pc][:, ch, :], True, True, [(Mc, dk), (Pc, dk)], [pk])
                    if lev < 5:
                        for ch in range(nch):
                            k.mm(p[0:64, W2 + ch * 64:W2 + (ch + 1) * 64], u[Pc][:, ch, :], u[Mc][:, ch, :], True, True, [(Mc, dk), (Pc, dk)], [pk])
                yield
                for d in range(2):
                    u, dk = U_[d], DK[d]
                    Mc, Pc, Mn, Pn = cur[d]
                    p, pk = pv[d]
                    k.cp("act", u[Pn][:, 0:nch, :].rearrange("p a b -> p (a b)"), p[0:64, 0:W2], [pk], [(Pn, dk)])
                    k.tt("dve", u["IP"][:, 0:nch, :], p[0:64, 0:W2].rearrange("p (a b) -> p a b", b=64), ibc, ALU.add, [pk, "ident"], [("IP", dk)])
                    if lev < 5:
                        k.cp("act", u[Mn][:, 0:nch, :].rearrange("p a b -> p (a b)"), p[0:64, W2:2 * W2], [pk], [(Mn, dk)])
                    release(c, pk)
                yield
                pv = []
                for d in range(2):
                    u, dk = U_[d], DK[d]
                    p, pk = yield from acquire(c)
                    pv.append((p, pk))
                    for ch in range(nch):
                        k.mm(p[0:64, ch * 64:(ch + 1) * 64], u["IP"][:, ch, :], u["Q"][:, ch, :], True, True, [("IP", dk), ("Q", dk)], [pk])
                yield
                for d in range(2):
                    u, dk = U_[d], DK[d]
                    p, pk = pv[d]
                    k.cp("dve", u["Q"][:, 0:nch, :].rearrange("p a b -> p (a b)"), p[0:64, 0:W2], [pk], [("Q", dk)])
                    release(c, pk)
                    Mc, Pc, Mn, Pn = cur[d]
                    cur[d] = [Mn, Pn, Mc, Pc]
                yield
        for ci in range(nch):
            chs = [ci, nch - 1 - ci]
            if delta:
                pv = []
                for d in range(2):
                    u, dk = U_[d], DK[d]
                    ch = chs[d]
                    cs_ = slice(ch * CH, (ch + 1) * CH)
                    Sc, Sck = u["S"][u["si"]], ("S", dk, u["si"])
                    p, pk = yield from acquire(c)
                    pv.append((p, pk))
                    Sr, Srk = u["Sr"][u["si"]], ("Sr", dk, u["si"])
                    k.mm(p[0:64, 0:64], fr(u["KKabs"][:, cs_]), fr(Sr[:]), True, False, [("KKabs", dk), Srk], [pk])
                    k.mm(p[0:64, 0:64], fr(u["AkT"][:, ch, :]), fr(u["Vm"][:, ch, :]), False, True, [("AkT", dk), ("Vm", dk)], [pk])
                yield
                for d in range(2):
                    u, dk = U_[d], DK[d]
                    p, pk = pv[d]
                    k.cp("dve" if d == 0 else "act", u["rhs0"][:], p[0:64, 0:64], [pk], [("rhs0", dk)])
                    release(c, pk)
                yield
                pv = []
                for d in range(2):
                    u, dk = U_[d], DK[d]
                    ch = chs[d]
                    p, pk = yield from acquire(c)
                    pv.append((p, pk))
                    k.mm(p[0:64, 0:64], u["Q"][:, ch, :], u["rhs0"][:], True, True, [("Q", dk), ("rhs0", dk)], [pk])
                yield
                for d in range(2):
                    u, dk = U_[d], DK[d]
                    p, pk = pv[d]
                    if d == 0:
                        k.ts("dve", fr(u["U"][:]), p[0:64, 0:64], -1.0, None, ALU.mult, None, [pk], [("U", dk)])
                    else:
                        k.actf(fr(u["U"][:]), p[0:64, 0:64], AF.Copy, [pk], [("U", dk)], scale=-1.0)
                    release(c, pk)
                yield
            pv = []
            for d in range(2):
                u, dk = U_[d], DK[d]
                ch = chs[d]
                cs_ = slice(ch * CH, (ch + 1) * CH)
                Sc, Sck = u["S"][u["si"]], ("S", dk, u["si"])
                p, pk = yield from acquire(c)
                pv.append((p, pk))
                Sr, Srk = u["Sr"][u["si"]], ("Sr", dk, u["si"])
                k.mm(p[0:64, 0:CH], fr(Sr[:]), fr(u["Qabs"][:, cs_]), True, False, [Srk, ("Qabs", dk)], [pk])
                k.mm(p[0:64, 0:CH], fr(u["Vm"][0:CH, ch, :]), fr(u["RKT"][0:CH, ch, 0:CH]), False, not delta, [("Vm", dk), ("RKT", dk)], [pk])
                if delta:
                    k.mm(p[0:64, 0:CH], fr(u["U"][0:CH, :]), fr(u["RBT"][0:CH, ch, 0:CH]), False, True, [("U", dk), ("RBT", dk)], [pk])
                k.mm(p[0:64, 64:128], fr(u["KeT"][0:CH, ch, :]), fr(u["Vm"][0:CH, ch, :]), True, not delta, [("KeT", dk), ("Vm", dk)], [pk])
                if delta:
                    k.mm(p[0:64, 64:128], fr(u["BeT"][0:CH, ch, :]), fr(u["U"][:]), False, True, [("BeT", dk), ("U", dk)], [pk])
            yield
            for d in range(2):
                u, dk = U_[d], DK[d]
                ch = chs[d]
                cs_ = slice(ch * CH, (ch + 1) * CH)
                p, pk = pv[d]
                Sc, Sck = u["S"][u["si"]], ("S", dk, u["si"])
                Sn, Snk = u["S"][1 - u["si"]], ("S", dk, 1 - u["si"])
                k.cp("act", u["ybuf"][:, cs_], p[0:64, 0:CH], [pk], [("ybuf", dk)])
                gi = ch * CH + (CH - 1 if d == 0 else 0)
                k.stt(Sn[:], Sc[:], u["Eabs"][:, gi:gi + 1], p[0:64, 64:128], ALU.mult, ALU.add, [Sck, ("Eabs", dk), pk], [Snk])
                k.cp("act", fr(u["Sr"][1 - u["si"]][:]), Sn[:], [Snk], [("Sr", dk, 1 - u["si"])])
                release(c, pk)
                u["si"] = 1 - u["si"]
            yield
        for d in range(2):
            u, dk = U_[d], DK[d]
            a0 = s0 + tis[d] * TT
            k.store(YS[d][:, a0:a0 + TT], u["ybuf"][:, 0:TT], reads=[("ybuf", dk)])
        yield
    if st_out is not None:
        for d in range(2):
            u, dk = U_[d], DK[d]
            Sc, Sck = u["S"][u["si"]], ("S", dk, u["si"])
            if transpose_state:
                p, pk = yield from acquire(c)
                k.tr(p[0:64, 0:64], Sc[:], I64, [Sck, "ident"], [pk])
                k.cp("dve", t["stmp"][:], p[0:64, 0:64], [pk], [stk])
                release(c, pk)
                k.store(st_out[d], t["stmp"][:], reads=[stk])
            else:
                k.store(st_out[d], Sc[:], reads=[Sck])
    yield


def drive_gen(jobs, tilesets):
    pending = list(jobs)
    free = list(tilesets)
    active = []
    while pending or active:
        while pending and free:
            ts = free.pop(0)
            active.append((pending.pop(0)(ts), ts))
        for item in list(active):
            g, ts = item
            try:
                next(g)
            except StopIteration:
                active.remove(item)
                free.append(ts)
        yield


def run_concurrent(gens):
    gens = list(gens)
    while gens:
        for g in list(gens):
            try:
                next(g)
            except StopIteration:
                gens.remove(g)


def drive(jobs, tilesets):
    run_concurrent([drive_gen(jobs, tilesets)])


def phase_hg(c, l, part=None, es_ext=None, NL=4):
    k, nc, cfg, S, I, O = c.k, c.nc, c.cfg, c.S, c.I, c.O
    NTOK = cfg.NTOK
    PTv = S["PT"].rearrange("(c p) t -> p c t", p=128)
    LAv = S["LAh"].rearrange("a (c p) t -> a p c t", p=128)
    LXv = S["LXh"].rearrange("a (c p) t -> a p c t", p=128)
    if part in (None, "prep", "prep_g"):
        es = ExitStack() if es_ext is None else es_ext
        col = k.sb("hg_col", [128, 2, 3], F32, es)
        ng = k.sb("hg_ng", [128, 2], F32, es)
        lb = k.sb("hg_lb", [128, 2], F32, es)
        oml = k.sb("hg_oml", [128, 2], F32, es)
        noml = k.sb("hg_noml", [128, 2], F32, es)
        k.load(col[:], I["hgcol"], writes=["hgcol"])
        k.load(ng[:], I["hgng"][l], writes=["hgng"])
        if l == 0:
            k.memset("dve", lb[:], 0.0, ["hglb"])
        else:
            k.tt("dve", lb[:], col[:, :, 1], col[:, :, 0], ALU.subtract, ["hgcol"], ["hglb"])
            k.actf(lb[:], lb[:], AF.Sigmoid, ["hglb"], ["hglb"])
        k.ts("dve", oml[:], lb[:], -1.0, 1.0, ALU.mult, ALU.add, ["hglb"], ["hgoml"])
        k.ts("dve", noml[:], oml[:], -1.0, None, ALU.mult, None, ["hgoml"], ["hgnoml"])
        pin = [k.sb(f"hg_pin{i}", [128, 10, 512], F32, es) for i in range(1)]
        wk = {nm: k.sb("hg_" + nm, [128, 2, 512], F32, es) for nm in ["R", "sig", "f", "K", "go"]}
        LAv = S["LAh"].rearrange("a (c p) t -> a p c t", p=128)
        LXv = S["LXh"].rearrange("a (c p) t -> a p c t", p=128)
        def _gen():
            for ti, (t0, n, var) in enumerate(cfg.tiles):
                yield
                pi_ = pin[0]
                k.load(pi_[:, :, 0:n], PTv[:, 10:20, t0:t0 + n], reads=["PT"], writes=["hgpin"])
                k.actf(wk["R"][:, :, 0:n], pi_[:, 0:2, 0:n], AF.Silu, ["hgpin"], ["hgR"])
                k.store(LAv[0][:, :, t0:t0 + n], wk["R"][:, :, 0:n], reads=["hgR"], writes=["LA"])
                k.actf(wk["go"][:, :, 0:n], pi_[:, 8:10, 0:n], AF.Silu, ["hgpin"], ["hggo"])
                for hc in range(2):
                    k.ts("dve", wk["go"][:, hc, 0:n], wk["go"][:, hc, 0:n], ng[:, hc:hc + 1], None, ALU.mult, None, ["hggo", "hgng"], ["hggo"])
                k.store(LXv[0][:, :, t0:t0 + n], wk["go"][:, :, 0:n], reads=["hggo"], writes=["LX"])
                for d in range(2):
                    k.actf(wk["sig"][:, :, 0:n], pi_[:, 2 + 2 * d:4 + 2 * d, 0:n], AF.Sigmoid, ["hgpin"], ["hgsig"])
                    for hc in range(2):
                        k.ts("dve", wk["f"][:, hc, 0:n], wk["sig"][:, hc, 0:n], oml[:, hc:hc + 1], lb[:, hc:hc + 1], ALU.mult, ALU.add,
                             ["hgsig", "hgoml", "hglb"], ["hgf"])
                        k.ts("pool", wk["K"][:, hc, 0:n], wk["sig"][:, hc, 0:n], noml[:, hc:hc + 1], oml[:, hc:hc + 1], ALU.mult, ALU.add,
                             ["hgsig", "hgoml", "hgnoml"], ["hgK"])
                    k.ts("dve", wk["f"][:, :, 0:n], wk["f"][:, :, 0:n], 1e-30, None, ALU.max, None, ["hgf"], ["hgf"])
                    k.actf(wk["f"][:, :, 0:n], wk["f"][:, :, 0:n], AF.Ln, ["hgf"], ["hgf"])
                    k.store(LAv[4 + d][:, :, t0:t0 + n], wk["f"][:, :, 0:n], reads=["hgf"], writes=["LA"])
                    k.store(LAv[2 + d][:, :, t0:t0 + n], wk["K"][:, :, 0:n], reads=["hgK"], writes=["LA"])
            yield
        _g = _gen()
        if part == "prep_g":
            return _g
        run_concurrent([_g])
        k.barrier()
        es.close()
    if part in (None, "scan"):
      es = ExitStack() if es_ext is None else es_ext
      tsets = [la_alloc(c, es, False, 32, 4, f"h{i}") for i in range(NL)]
      cst = la_consts(c, es, 32, 4)
      jobs = []
      for (s0, T, kind, pi_) in cfg.seqs:
          for h in range(4):
              rows = slice(64 * h, 64 * h + 64)
              arrs = {"R": S["LAh"][0][rows], "V": S["PT"][2048 + 64 * h:2048 + 64 * h + 64], "K0": S["LAh"][2][rows], "K1": S["LAh"][3][rows],
                      "LW0": S["LAh"][4][rows], "LW1": S["LAh"][5][rows]}
              YS = [S["YSh"][0][rows], S["YSh"][1][rows]]
              st_in = [I["st_hg"][l, d, h] for d in range(2)] if kind == 1 else None
              st_out = [O["nhg"][pi_, l, d, h] for d in range(2)] if kind == 0 else None
              jobs.append(lambda ts, arrs=arrs, YS=YS, s0=s0, T=T, st_in=st_in, st_out=st_out:
                          la_lane(c, ts, cst, False, arrs, YS, (s0, T), st_in, st_out, False))
      g = drive_gen(jobs, tsets)
      if part == "scan":
          return g
      run_concurrent([g])
      k.barrier()
      es.close()
    if (cfg.stages is not None and "hg_noout" in cfg.stages) or part not in (None, "out", "out_g"):
        return
    es = ExitStack() if es_ext is None else es_ext
    YSv = S["YSh"].rearrange("a (c p) t -> a p c t", p=128)
    OTv = S["OT"].rearrange("(c p) t -> p c t", p=128)
    y0 = k.sb("hgo_y0", [128, 2, 512], F32, es)
    y1 = k.sb("hgo_y1", [128, 2, 512], F32, es)
    go = k.sb("hgo_go", [128, 2, 512], F32, es)
    sq = k.sb("hgo_sq", [128, 2, 512], F32, es)
    epsc = k.sb("hgo_eps", [128, 1], F32, es)
    k.memset("dve", epsc[:], LN_EPS, ["hgeps"])
    def _gen():
        for ti, (t0, n, var) in enumerate(cfg.tiles):
            yield
            k.load(y0[:, :, 0:n], YSv[0][:, :, t0:t0 + n], reads=["YS"], writes=["hy0"])
            k.load(y1[:, :, 0:n], YSv[1][:, :, t0:t0 + n], reads=["YS"], writes=["hy1"])
            k.load(go[:, :, 0:n], LXv[0][:, :, t0:t0 + n], reads=["LX"], writes=["hgo"])
            k.tt("dve", y0[:, :, 0:n], y0[:, :, 0:n], y1[:, :, 0:n], ALU.add, ["hy0", "hy1"], ["hy0"])
            k.actf(sq[:, :, 0:n], y0[:, :, 0:n], AF.Square, ["hy0"], ["hsq"])
            for hc in range(2):
                p, pk = nextbank(c)
                k.mm(p[:, 0:n], c.bones64[:], sq[:, hc, 0:n], True, True, ["bones64", "hsq"], [pk])
                k.actf(y1[:, hc, 0:n], p[:, 0:n], AF.Sqrt, [pk, "hgeps", "hy1"], ["hy1"], bias=epsc[:, 0:1])
            k.recip(y1[:, :, 0:n], y1[:, :, 0:n], ["hy1"], ["hy1"])
            k.tt("dve", y0[:, :, 0:n], y0[:, :, 0:n], y1[:, :, 0:n], ALU.mult, ["hy0", "hy1"], ["hy0"])
            k.tt("dve", y0[:, :, 0:n], y0[:, :, 0:n], go[:, :, 0:n], ALU.mult, ["hy0", "hgo"], ["hy0"])
            k.store(OTv[:, 4:6, t0:t0 + n], y0[:, :, 0:n], reads=["hy0"], writes=["OTh"])


        yield
    _g = _gen()
    if part == "out_g":
        return _g
    run_concurrent([_g])
    k.barrier()
    es.close()
def phase_rw(c, l, part=None, es_ext=None, NL=4):
    k, nc, cfg, S, I, O = c.k, c.nc, c.cfg, c.S, c.I, c.O
    PTv = S["PT"].rearrange("(c p) t -> p c t", p=128)
    LAv = S["LAr"].rearrange("a (c p) t -> a p c t", p=128)
    LXv = S["LXr"].rearrange("a (c p) t -> a p c t", p=128)
    if part in (None, "prep", "prep_g"):
        es = ExitStack() if es_ext is None else es_ext
        mu = k.sb("rw_mu", [128, 2, 8], F32, es)
        cm = k.sb("rw_cm", [128, 8], F32, es)
        muad = k.sb("rw_muad", [64, 2], F32, es)
        cmad = k.sb("rw_cmad", [64, 1], F32, es)
        w0 = k.sb("rw_w0", [128, 2, 2], F32, es)
        col = k.sb("rw_col", [128, 2, 6], F32, es)
        omka = k.sb("rw_omka", [128, 2], F32, es)
        wup = k.sb("rw_wup", [64, 2, 256], F32, es)
        aup = k.sb("rw_aup", [64, 256], F32, es)
        gup = k.sb("rw_gup", [128, 256], F32, es)
        k.load(mu[:], I["rwmu"][l].rearrange("i p c -> p i c"), writes=["rwmu"])
        k.load(muad[:], I["rwmu_ad"][l], writes=["rwmuad"])
        k.load(w0[:], I["rww0"][l].rearrange("i p c -> p i c"), writes=["rww0"])
        k.load(col[:], I["rwcol"][l], writes=["rwcol"])
        k.load(wup[:], I["rw_w_up"][l].rearrange("i p c -> p i c"), writes=["rwwup"])
        k.load(aup[:], I["rw_a_up"][l], writes=["rwaup"])
        k.load(gup[:], I["rw_g_up"][l], writes=["rwgup"])
        k.tt("dve", cm[:], mu[:, 0, :], mu[:, 1, :], ALU.add, ["rwmu"], ["rwcm"])
        k.ts("dve", cm[:], cm[:], -1.0, 1.0, ALU.mult, ALU.add, ["rwcm"], ["rwcm"])
        k.tt("dve", cmad[:], muad[:, 0:1], muad[:, 1:2], ALU.add, ["rwmuad"], ["rwcmad"])
        k.ts("dve", cmad[:], cmad[:], -1.0, 1.0, ALU.mult, ALU.add, ["rwcmad"], ["rwcmad"])
        k.ts("dve", omka[:], col[:, :, 2], -1.0, 1.0, ALU.mult, ALU.add, ["rwcol"], ["rwomka"])
        pa = k.sb("rw_pa", [128, 8, 514], F32, es)
        pad = k.sb("rw_pad", [64, 514], F32, es)
        sh = k.sb("rw_sh", [128, 8, 512], F32, es)
        t2 = k.sb("rw_t2", [128, 8, 512], F32, es)
        adsh = k.sb("rw_adsh", [64, 512], F32, es)
        tw = k.sb("rw_tw", [64, 512], F32, es)
        sgd = k.sb("rw_sgd", [128, 512], F32, es)
        W = {nm: k.sb("rw_" + nm, [128, 2, 512], F32, es) for nm in ["a", "g", "lw0", "lw1", "kk", "kka", "keff", "tmp", "bon"]}
        EC = math.exp(-0.5)
        def _gen():
            for (s0, T, kind, pi_) in cfg.seqs:
                for t0 in range(0, T, 512):
                    yield
                    n = min(512, T - t0)
                    a0 = s0 + t0
                    lo = max(s0, a0 - 1)
                    hi = min(s0 + T, a0 + n + 1)
                    if t0 == 0:
                        k.memset("dve", pa[:, :, 0:1], 0.0, ["rwpa"])
                        k.memset("dve", pad[:, 0:1], 0.0, ["rwpad"])
                    if t0 + n == T:
                        k.memset("dve", pa[:, :, n + 1:n + 2], 0.0, ["rwpa"])
                        k.memset("dve", pad[:, n + 1:n + 2], 0.0, ["rwpad"])
                    o0 = lo - (a0 - 1)
                    k.load(pa[:, :, o0:o0 + hi - lo], PTv[:, 0:8, lo:hi], reads=["PT"], writes=["rwpa"])
                    k.load(pad[:, o0:o0 + hi - lo], S["PT"][832:896, lo:hi], reads=["PT"], writes=["rwpad"])
                    k.tt("dve", sh[:, :, 0:n], pa[:, :, 1:n + 1], cm[:].unsqueeze(2).to_broadcast([128, 8, n]), ALU.mult, ["rwpa", "rwcm"], ["rwsh"])
                    k.tt("pool", t2[:, :, 0:n], pa[:, :, 0:n], mu[:, 0, :].unsqueeze(2).to_broadcast([128, 8, n]), ALU.mult, ["rwpa", "rwmu"], ["rwt2"])
                    k.tt("dve", sh[:, :, 0:n], sh[:, :, 0:n], t2[:, :, 0:n], ALU.add, ["rwsh", "rwt2"], ["rwsh"])
                    k.tt("pool", t2[:, :, 0:n], pa[:, :, 2:n + 2], mu[:, 1, :].unsqueeze(2).to_broadcast([128, 8, n]), ALU.mult, ["rwpa", "rwmu"], ["rwt2"])
                    k.tt("dve", sh[:, :, 0:n], sh[:, :, 0:n], t2[:, :, 0:n], ALU.add, ["rwsh", "rwt2"], ["rwsh"])
                    k.ts("dve", adsh[:, 0:n], pad[:, 1:n + 1], cmad[:, 0:1], None, ALU.mult, None, ["rwpad", "rwcmad"], ["rwadsh"])
                    k.stt(adsh[:, 0:n], pad[:, 0:n], muad[:, 0:1], adsh[:, 0:n], ALU.mult, ALU.add, ["rwpad", "rwmuad", "rwadsh"], ["rwadsh"])
                    k.stt(adsh[:, 0:n], pad[:, 2:n + 2], muad[:, 1:2], adsh[:, 0:n], ALU.mult, ALU.add, ["rwpad", "rwmuad", "rwadsh"], ["rwadsh"])
                    k.actf(tw[:, 0:n], sh[0:64, 6, 0:n], AF.Tanh, ["rwsh"], ["rwtw"])
                    k.actf(sgd[:, 0:n], sh[:, 7, 0:n], AF.Sigmoid, ["rwsh"], ["rwsgd"])
                    for hc in range(2):
                        hsl = slice(hc * 128, (hc + 1) * 128)
                        r_, k_, v_ = sh[:, hc, 0:n], sh[:, 2 + hc, 0:n], sh[:, 4 + hc, 0:n]
                        p, pk = nextbank(c)
                        k.mm(p[:, 0:n], aup[:, hsl], adsh[:, 0:n], True, True, ["rwaup", "rwadsh"], [pk])
                        k.actf(W["a"][:, hc, 0:n], p[:, 0:n], AF.Sigmoid, [pk, "rwcol"], [("rwa", hc)], bias=col[:, hc, 0:1])
                        p, pk = nextbank(c)
                        k.mm(p[:, 0:n], gup[:, hsl], sgd[:, 0:n], True, True, ["rwgup", "rwsgd"], [pk])
                        k.cp("act", W["g"][:, hc, 0:n], p[:, 0:n], [pk], [("rwg", hc)])
                        for d in range(2):
                            p, pk = nextbank(c)
                            k.mm(p[:, 0:n], wup[:, d, hsl], tw[:, 0:n], True, True, ["rwwup", "rwtw"], [pk])
                            lw = W[f"lw{d}"]
                            k.actf(lw[:, hc, 0:n], p[:, 0:n], AF.Sigmoid, [pk, "rww0"], [("rwlw", d, hc)], bias=w0[:, d, hc:hc + 1])
                            k.ts("dve", lw[:, hc, 0:n], lw[:, hc, 0:n], -EC, None, ALU.mult, None, [("rwlw", d, hc)], [("rwlw", d, hc)])
                        kk = W["kk"]
                        k.ts("dve", kk[:, hc, 0:n], k_, col[:, hc, 1:2], None, ALU.mult, None, ["rwsh", "rwcol"], [("rwkk", hc)])
                        k.actf(W["tmp"][:, hc, 0:n], kk[:, hc, 0:n], AF.Square, [("rwkk", hc)], [("rwtmp", hc)])
                        p, pk = nextbank(c)
                        k.mm(p[:, 0:n], c.bones[:], W["tmp"][:, hc, 0:n], True, True, ["bones", ("rwtmp", hc)], [pk])
                        k.actf(W["tmp"][:, hc, 0:n], p[:, 0:n], AF.Sqrt, [pk], [("rwtmp", hc)])
                        k.ts("dve", W["tmp"][:, hc, 0:n], W["tmp"][:, hc, 0:n], 1e-12, None, ALU.max, None, [("rwtmp", hc)], [("rwtmp", hc)])
                        k.recip(W["tmp"][:, hc, 0:n], W["tmp"][:, hc, 0:n], [("rwtmp", hc)], [("rwtmp", hc)])
                        k.tt("dve", kk[:, hc, 0:n], kk[:, hc, 0:n], W["tmp"][:, hc, 0:n], ALU.mult, [("rwkk", hc), ("rwtmp", hc)], [("rwkk", hc)])
                        k.tt("pool", W["kka"][:, hc, 0:n], kk[:, hc, 0:n], W["a"][:, hc, 0:n], ALU.mult, [("rwkk", hc), ("rwa", hc)], [("rwkka", hc)])
                        k.ts("dve", W["tmp"][:, hc, 0:n], W["a"][:, hc, 0:n], col[:, hc, 2:3], omka[:, hc:hc + 1], ALU.mult, ALU.add,
                             [("rwa", hc), "rwcol", "rwomka", ("rwtmp", hc)], [("rwtmp", hc)])
                        k.tt("dve", W["keff"][:, hc, 0:n], k_, W["tmp"][:, hc, 0:n], ALU.mult, ["rwsh", ("rwtmp", hc)], [("rwkeff", hc)])
                        k.tt("dve", W["tmp"][:, hc, 0:n], r_, W["keff"][:, hc, 0:n], ALU.mult, ["rwsh", ("rwkeff", hc), ("rwtmp", hc)], [("rwtmp", hc)])
                        k.ts("dve", W["tmp"][:, hc, 0:n], W["tmp"][:, hc, 0:n], col[:, hc, 3:4], None, ALU.mult, None, [("rwtmp", hc), "rwcol"], [("rwtmp", hc)])
                        p, pk = nextbank(c)
                        k.mm(p[:, 0:n], c.bones[:], W["tmp"][:, hc, 0:n], True, True, ["bones", ("rwtmp", hc)], [pk])
                        k.tt("dve", W["bon"][:, hc, 0:n], p[:, 0:n], v_, ALU.mult, [pk, "rwsh"], [("rwbon", hc)])
                    sl = slice(a0, a0 + n)
                    k.store(LAv[0][:, :, sl], sh[:, 0:2, 0:n], reads=["rwsh"], writes=["LA"])
                    k.store(LAv[1][:, :, sl], sh[:, 4:6, 0:n], reads=["rwsh"], writes=["LA"])
                    k.store(LAv[2][:, :, sl], W["keff"][:, :, 0:n], reads=[("rwkeff", 0), ("rwkeff", 1)], writes=["LA"])
                    k.store(LAv[4][:, :, sl], W["lw0"][:, :, 0:n], reads=[("rwlw", 0, 0), ("rwlw", 0, 1)], writes=["LA"])
                    k.store(LAv[5][:, :, sl], W["lw1"][:, :, 0:n], reads=[("rwlw", 1, 0), ("rwlw", 1, 1)], writes=["LA"])
                    k.store(LAv[6][:, :, sl], W["kk"][:, :, 0:n], reads=[("rwkk", 0), ("rwkk", 1)], writes=["LA"])
                    k.store(LAv[7][:, :, sl], W["kka"][:, :, 0:n], reads=[("rwkka", 0), ("rwkka", 1)], writes=["LA"])
                    k.store(LXv[0][:, :, sl], W["g"][:, :, 0:n], reads=[("rwg", 0), ("rwg", 1)], writes=["LX"])
                    k.store(LXv[1][:, :, sl], W["bon"][:, :, 0:n], reads=[("rwbon", 0), ("rwbon", 1)], writes=["LX"])
            yield
        _g = _gen()
        if part == "prep_g":
            return _g
        run_concurrent([_g])
        k.barrier()
        es.close()
    if (cfg.stages is not None and "rw_prep_only" in cfg.stages) or part == "prep":
        return
    if part in (None, "scan"):
        es = ExitStack() if es_ext is None else es_ext
        tsets = [la_alloc(c, es, True, 64, 2, f"r{i}") for i in range(NL)]
        cst = la_consts(c, es, 64, 2)
        jobs = []
        for (s0, T, kind, pi_) in cfg.seqs:
            for h in range(4):
                rows = slice(64 * h, 64 * h + 64)
                arrs = {"R": S["LAr"][0][rows], "V": S["LAr"][1][rows], "K0": S["LAr"][2][rows], "K1": S["LAr"][2][rows],
                        "LW0": S["LAr"][4][rows], "LW1": S["LAr"][5][rows], "KK": S["LAr"][6][rows], "BK": S["LAr"][7][rows]}
                YS = [S["YSr"][0][rows], S["YSr"][1][rows]]
                st_in = [I["st_rw"][l, d, h] for d in range(2)] if kind == 1 else None
                st_out = [O["nrw"][pi_, l, d, h] for d in range(2)] if kind == 0 else None
                jobs.append(lambda ts, arrs=arrs, YS=YS, s0=s0, T=T, st_in=st_in, st_out=st_out:
                            la_lane(c, ts, cst, True, arrs, YS, (s0, T), st_in, st_out, True))
        g = drive_gen(jobs, tsets)
        if part == "scan":
            return g
        run_concurrent([g])
        k.barrier()
        es.close()

    if (cfg.stages is not None and "rw_noout" in cfg.stages) or part not in (None, "out", "out_g"):
        return
    es = ExitStack() if es_ext is None else es_ext
    YSv = S["YSr"].rearrange("a (c p) t -> a p c t", p=128)
    OTv = S["OT"].rearrange("(c p) t -> p c t", p=128)
    col = k.sb("rwo_col", [128, 2, 6], F32, es)
    k.load(col[:], I["rwcol"][l], writes=["rwcol"])
    y0 = k.sb("rwo_y0", [128, 2, 512], F32, es)
    y1 = k.sb("rwo_y1", [128, 2, 512], F32, es)
    g = k.sb("rwo_g", [128, 2, 512], F32, es)
    bon = k.sb("rwo_bon", [128, 2, 512], F32, es)
    sq = k.sb("rwo_sq", [128, 2, 512], F32, es)
    epsc = k.sb("rwo_eps", [128, 1], F32, es)
    k.memset("dve", epsc[:], RW_EPS, ["rweps"])
    def _gen():
        for ti, (t0, n, var) in enumerate(cfg.tiles):
            yield
            k.load(y0[:, :, 0:n], YSv[0][:, :, t0:t0 + n], reads=["YS"], writes=["ry0"])
            k.load(y1[:, :, 0:n], YSv[1][:, :, t0:t0 + n], reads=["YS"], writes=["ry1"])
            k.load(g[:, :, 0:n], LXv[0][:, :, t0:t0 + n], reads=["LX"], writes=["rg"])
            k.load(bon[:, :, 0:n], LXv[1][:, :, t0:t0 + n], reads=["LX"], writes=["rbon"])
            k.tt("dve", y0[:, :, 0:n], y0[:, :, 0:n], y1[:, :, 0:n], ALU.add, ["ry0", "ry1"], ["ry0"])
            for hc in range(2):
                p, pk = nextbank(c)
                k.mm(p[:, 0:n], c.bones64[:], y0[:, hc, 0:n], True, True, ["bones64", "ry0"], [pk])
                k.tt("dve", y1[:, hc, 0:n], y0[:, hc, 0:n], p[:, 0:n], ALU.subtract, ["ry0", pk, "ry1"], [("ryc", hc)])
                k.actf(sq[:, hc, 0:n], y1[:, hc, 0:n], AF.Square, [("ryc", hc)], [("rsq", hc)])
                p, pk = nextbank(c)
                k.mm(p[:, 0:n], c.bones64[:], sq[:, hc, 0:n], True, True, ["bones64", ("rsq", hc)], [pk])
                k.actf(sq[:, hc, 0:n], p[:, 0:n], AF.Sqrt, [pk, "rweps"], [("rsq", hc)], bias=epsc[:, 0:1])
                k.recip(sq[:, hc, 0:n], sq[:, hc, 0:n], [("rsq", hc)], [("rsq", hc)])
                k.tt("dve", y1[:, hc, 0:n], y1[:, hc, 0:n], sq[:, hc, 0:n], ALU.mult, [("ryc", hc), ("rsq", hc)], [("ryc", hc)])
                k.ts("dve", y1[:, hc, 0:n], y1[:, hc, 0:n], col[:, hc, 4:5], col[:, hc, 5:6], ALU.mult, ALU.add, [("ryc", hc), "rwcol"], [("ryc", hc)])
                k.tt("dve", y1[:, hc, 0:n], y1[:, hc, 0:n], bon[:, hc, 0:n], ALU.add, [("ryc", hc), "rbon"], [("ryc", hc)])
                k.tt("dve", y1[:, hc, 0:n], y1[:, hc, 0:n], g[:, hc, 0:n], ALU.mult, [("ryc", hc), "rg"], [("ryc", hc)])
            k.store(OTv[:, 0:2, t0:t0 + n], y1[:, :, 0:n], reads=[("ryc", 0), ("ryc", 1)], writes=["OTr", "ry1"])


        yield
    _g = _gen()
    if part == "out_g":
        return _g
    run_concurrent([_g])
    k.barrier()
    es.close()
def phase_s5(c, l, part=None, es_ext=None, NW=2):
    k, nc, cfg, S, I, O = c.k, c.nc, c.cfg, c.S, c.I, c.O
    PI = math.pi
    TT = 128
    PTv = S["PT"].rearrange("(c p) t -> p c t", p=128)
    YSv = S["YS5"].rearrange("a (c p) t -> a p c t", p=128)
    if part in (None, "scan"):
        es = ExitStack() if es_ext is None else es_ext
        lam = k.sb("s5_lam", [128, 2, 8, 3], F32, es)
        k.load(lam[:], I["s5lam"][l].rearrange("d p j r -> p d j r"), writes=["s5lam"])
        BT = k.sb("s5_BT", [128, 2, 8, 128], F32, es)
        CT = k.sb("s5_CT", [128, 2, 8, 128], F32, es)
        for r in range(2):
            k.load(BT[:, r], I["s5BT"][l, r].rearrange("j p n -> p j n"), writes=["s5BT"])
            k.load(CT[:, r], I["s5CT"][l, r].rearrange("j p n -> p j n"), writes=["s5CT"])
        sm = {nm: k.sb("s5_" + nm, [128, 2, 8], F32, es) for nm in
              ["dt", "mag", "th", "th2", "msk", "cos", "sin", "abre", "abim", "den", "zre", "zim", "t1", "t2", "cw", "sw", "cw2"]}
        RT = {nm: k.sb("s5_" + nm, [128, 2, 8, TT], F32, es) for nm in ["RTre", "RTim", "DZre", "DZim"]}
        tA = k.sb("s5_tA", [128, 8, TT], F32, es)
        tB = k.sb("s5_tB", [128, 8, TT], F32, es)
        magz = k.sb("s5_magz", [128, 2, 8, TT], F32, es)
        A_ = lambda nm: sm[nm][:]
        are, aim, ldt = lam[:, :, :, 0], lam[:, :, :, 1], lam[:, :, :, 2]

        def tts(e, o, a, b, op, r, w):
            k.tt(e, sm[o][:], a, b, op, r, w)

        k.actf(A_("dt"), ldt, AF.Exp, ["s5lam"], ["dt"])
        tts("dve", "t1", are, A_("dt"), ALU.mult, ["s5lam", "dt"], ["t1"])
        k.actf(A_("mag"), A_("t1"), AF.Exp, ["t1"], ["mag"])
        tts("dve", "th", aim, A_("dt"), ALU.mult, ["s5lam", "dt"], ["th"])

        def reduce_pi(nm, iters):
            for _ in range(iters):
                k.ts("dve", A_("msk"), A_(nm), PI, -2.0 * PI, ALU.is_gt, ALU.mult, [nm], ["msk"])
                tts("dve", nm, A_(nm), A_("msk"), ALU.add, [nm, "msk"], [nm])
                k.ts("dve", A_("msk"), A_(nm), -PI, 2.0 * PI, ALU.is_lt, ALU.mult, [nm], ["msk"])
                tts("dve", nm, A_(nm), A_("msk"), ALU.add, [nm, "msk"], [nm])
        reduce_pi("th", 5)
        k.ts("dve", A_("th2"), A_("th"), PI / 2, None, ALU.add, None, ["th"], ["th2"])
        reduce_pi("th2", 1)
        k.actf(A_("sin"), A_("th"), AF.Sin, ["th"], ["sin"])
        k.actf(A_("cos"), A_("th2"), AF.Sin, ["th2"], ["cos"])
        tts("dve", "abre", A_("mag"), A_("cos"), ALU.mult, ["mag", "cos"], ["abre"])
        tts("dve", "abim", A_("mag"), A_("sin"), ALU.mult, ["mag", "sin"], ["abim"])
        tts("dve", "t1", are, are, ALU.mult, ["s5lam"], ["t1"])
        tts("dve", "t2", aim, aim, ALU.mult, ["s5lam"], ["t2"])
        tts("dve", "den", A_("t1"), A_("t2"), ALU.add, ["t1", "t2"], ["den"])
        k.recip(A_("den"), A_("den"), ["den"], ["den"])
        k.ts("dve", A_("t1"), A_("abre"), -1.0, None, ALU.add, None, ["abre"], ["t1"])
        tts("dve", "zre", A_("t1"), are, ALU.mult, ["t1", "s5lam"], ["zre"])
        tts("dve", "t2", A_("abim"), aim, ALU.mult, ["abim", "s5lam"], ["t2"])
        tts("dve", "zre", A_("zre"), A_("t2"), ALU.add, ["zre", "t2"], ["zre"])
        tts("dve", "zre", A_("zre"), A_("den"), ALU.mult, ["zre", "den"], ["zre"])
        tts("dve", "zim", A_("abim"), are, ALU.mult, ["abim", "s5lam"], ["zim"])
        tts("dve", "t2", A_("t1"), aim, ALU.mult, ["t1", "s5lam"], ["t2"])
        tts("dve", "zim", A_("zim"), A_("t2"), ALU.subtract, ["zim", "t2"], ["zim"])
        tts("dve", "zim", A_("zim"), A_("den"), ALU.mult, ["zim", "den"], ["zim"])
        for d in range(2):
            Rre, Rim = RT["RTre"][:, d], RT["RTim"][:, d]
            k.cp("dve", Rre[:, :, 0:1], sm["cos"][:, d, :].unsqueeze(2), ["cos"], [("RT", d)])
            k.cp("dve", Rim[:, :, 0:1], sm["sin"][:, d, :].unsqueeze(2), ["sin"], [("RT", d)])
            k.cp("dve", sm["cw"][:, d, :], sm["cos"][:, d, :], ["cos"], [("cw", d)])
            k.cp("dve", sm["sw"][:, d, :], sm["sin"][:, d, :], ["sin"], [("sw", d)])
            w = 1
            while w < TT:
                cwb = sm["cw"][:, d, :].unsqueeze(2).to_broadcast([128, 8, w])
                swb = sm["sw"][:, d, :].unsqueeze(2).to_broadcast([128, 8, w])
                k.tt("dve", tA[:, :, 0:w], Rre[:, :, 0:w], cwb, ALU.mult, [("RT", d), ("cw", d)], ["tA"])
                k.tt("pool", tB[:, :, 0:w], Rim[:, :, 0:w], swb, ALU.mult, [("RT", d), ("sw", d)], ["tB"])
                k.tt("dve", Rre[:, :, w:2 * w], tA[:, :, 0:w], tB[:, :, 0:w], ALU.subtract, ["tA", "tB", ("RT", d)], [("RT", d)])
                k.tt("dve", tA[:, :, 0:w], Rre[:, :, 0:w], swb, ALU.mult, [("RT", d), ("sw", d)], ["tA"])
                k.tt("pool", tB[:, :, 0:w], Rim[:, :, 0:w], cwb, ALU.mult, [("RT", d), ("cw", d)], ["tB"])
                k.tt("dve", Rim[:, :, w:2 * w], tA[:, :, 0:w], tB[:, :, 0:w], ALU.add, ["tA", "tB", ("RT", d)], [("RT", d)])
                k.tt("dve", sm["t1"][:, d, :], sm["cw"][:, d, :], sm["cw"][:, d, :], ALU.mult, [("cw", d), "t1"], ["t1"])
                k.tt("dve", sm["t2"][:, d, :], sm["sw"][:, d, :], sm["sw"][:, d, :], ALU.mult, [("sw", d), "t2"], ["t2"])
                k.tt("dve", sm["cw2"][:, d, :], sm["cw"][:, d, :], sm["sw"][:, d, :], ALU.mult, [("cw", d), ("sw", d)], ["cw2"])
                k.tt("dve", sm["cw"][:, d, :], sm["t1"][:, d, :], sm["t2"][:, d, :], ALU.subtract, ["t1", "t2"], [("cw", d)])
                k.ts("dve", sm["sw"][:, d, :], sm["cw2"][:, d, :], 2.0, None, ALU.mult, None, ["cw2"], [("sw", d)])
                w *= 2
            zre_b = sm["zre"][:, d, :].unsqueeze(2).to_broadcast([128, 8, TT])
            zim_b = sm["zim"][:, d, :].unsqueeze(2).to_broadcast([128, 8, TT])
            k.tt("dve", tA[:], Rre, zre_b, ALU.mult, [("RT", d), "zre"], ["tA"])
            k.tt("pool", tB[:], Rim, zim_b, ALU.mult, [("RT", d), "zim"], ["tB"])
            k.tt("dve", RT["DZre"][:, d], tA[:], tB[:], ALU.add, ["tA", "tB"], [("DZ", d)])
            k.tt("dve", tA[:], Rre, zim_b, ALU.mult, [("RT", d), "zim"], ["tA"])
            k.tt("pool", tB[:], Rim, zre_b, ALU.mult, [("RT", d), "zre"], ["tB"])
            k.tt("dve", RT["DZim"][:, d], tA[:], tB[:], ALU.subtract, ["tA", "tB", ("DZ", d)], [("DZ", d)])
        for d in range(2):
            k.cp("act", magz[:, d], sm["mag"][:, d, :].unsqueeze(2).to_broadcast([128, 8, TT]), ["mag"], [("magz", d)])
            fi = 0 if d == 0 else TT - 1
            k.memset("dve", magz[:, d, :, fi:fi + 1], 0.0, [("magz", d)])
        wsets = []
        for i in range(NW):
            w = {nm: k.sb(f"s5w{i}_{nm}", [128, 8, TT], F32, es) for nm in ["A", "B", "C", "D"]}
            w["Braw"] = k.sb(f"s5w{i}_Braw", [128, 8, 2, TT], F32, es)
            w["u"] = k.sb(f"s5w{i}_u", [128, 2, TT], F32, es)
            w["y"] = k.sb(f"s5w{i}_y", [128, 2, TT], F32, es)
            w["init"] = k.sb(f"s5w{i}_init", [128, 8, 2], F32, es)
            w["cin"] = k.sb(f"s5w{i}_cin", [128, 8, 2], F32, es)
            w["tag"] = i
            wsets.append(w)

        def sweep(w, s0, T, kind, pi_, d):
            tg = w["tag"]
            K_ = lambda nm: ("s5", tg, nm)
            nt = T // TT
            init, cin = w["init"], w["cin"]
            if kind == 1:
                k.load(init[:], I["st_s5"][l, d], writes=[K_("init")])
            else:
                k.memset("dve", init[:], 0.0, [K_("init")])
            first = 0 if d == 0 else TT - 1
            last = TT - 1 if d == 0 else 0
            rvt = (lambda ap: ap) if d == 0 else (lambda ap: ap[:, :, ::-1])
            flat = lambda ap: ap.rearrange("p a b -> p (a b)")
            rvf = (lambda ap: flat(ap)) if d == 0 else (lambda ap: flat(ap)[:, ::-1])
            DZr, DZi = rvt(RT["DZre"][:, d]), rvt(RT["DZim"][:, d])
            Rr, Ri = rvt(RT["RTre"][:, d]), rvt(RT["RTim"][:, d])
            A, B, C, D_ = w["A"], w["B"], w["C"], w["D"]
            yield
            for it in range(nt):
                ti = it if d == 0 else nt - 1 - it
                a0 = s0 + ti * TT
                k.load(w["u"][:], PTv[:, 8:10, a0:a0 + TT], writes=[K_("u")])
                yield
                for j2 in range(4):
                    p, pk = yield from acquire(c)
                    for jj in range(2):
                        j = 2 * j2 + jj
                        for r in range(2):
                            k.mm(p[:, (2 * jj + r) * TT:(2 * jj + r + 1) * TT], BT[:, r, j, :], w["u"][:, j // 4, :], True, True, ["s5BT", K_("u")], [pk])
                    k.cp("act", w["Braw"][:, 2 * j2:2 * j2 + 2].rearrange("p a r t -> p (a r t)"), p[:, 0:4 * TT], [pk], [K_("Braw")])
                    release(c, pk)
                yield
                Br, Bi = w["Braw"][:, :, 0, :], w["Braw"][:, :, 1, :]
                k.tt("dve", A[:], Br, DZr, ALU.mult, [K_("Braw"), ("DZ", d)], [K_("A")])
                k.tt("pool", B[:], Bi, DZi, ALU.mult, [K_("Braw"), ("DZ", d)], [K_("B")])
                yield
                k.tt("dve", A[:], A[:], B[:], ALU.subtract, [K_("A"), K_("B")], [K_("A")])
                k.tt("pool", C[:], Br, DZi, ALU.mult, [K_("Braw"), ("DZ", d)], [K_("C")])
                yield
                k.tt("pool", B[:], Bi, DZr, ALU.mult, [K_("Braw"), ("DZ", d), K_("A")], [K_("B")])
                k.tt("dve", cin[:], init[:], sm["mag"][:, d, :].unsqueeze(2).to_broadcast([128, 8, 2]), ALU.mult, [K_("init"), "mag"], [K_("cin")])
                k.tt("dve", A[:, :, first:first + 1], A[:, :, first:first + 1], cin[:, :, 0:1], ALU.add, [K_("A"), K_("cin")], [K_("A")])
                yield
                k.tt("pool", B[:], B[:], C[:], ALU.add, [K_("B"), K_("C")], [K_("B")])
                k.scan(rvf(C[:]), rvf(magz[:, d]), rvf(A[:]), 0.0, [K_("A"), ("magz", d), K_("B")], [K_("C")])
                yield
                k.tt("dve", B[:, :, first:first + 1], B[:, :, first:first + 1], cin[:, :, 1:2], ALU.add, [K_("B"), K_("cin")], [K_("B")])
                k.scan(rvf(D_[:]), rvf(magz[:, d]), rvf(B[:]), 0.0, [K_("B"), ("magz", d)], [K_("D")])
                yield
                k.tt("dve", A[:], C[:], Rr, ALU.mult, [K_("C"), ("RT", d)], [K_("A")])
                k.tt("pool", B[:], D_[:], Ri, ALU.mult, [K_("D"), ("RT", d)], [K_("B")])
                yield
                k.tt("dve", A[:], A[:], B[:], ALU.subtract, [K_("A"), K_("B")], [K_("A")])
                k.tt("pool", B[:], C[:], Ri, ALU.mult, [K_("C"), ("RT", d), K_("A")], [K_("B")])
                yield
                k.tt("dve", C[:], D_[:], Rr, ALU.mult, [K_("D"), ("RT", d), K_("B")], [K_("C")])
                yield
                k.stt(B[:], B[:], -1.0, C[:], ALU.mult, ALU.subtract, [K_("B"), K_("C")], [K_("B")])
                yield
                k.cp("act", init[:, :, 0:1], A[:, :, last:last + 1], [K_("A")], [K_("init")])
                k.actf(init[:, :, 1:2], B[:, :, last:last + 1], AF.Copy, [K_("B")], [K_("init")], scale=-1.0)
                for kc in range(2):
                    p, pk = yield from acquire(c)
                    for jj in range(4):
                        j = 4 * kc + jj
                        k.mm(p[:, 0:TT], CT[:, 0, j, :], A[:, j, :], jj == 0, False, ["s5CT", K_("A")], [pk])
                        k.mm(p[:, 0:TT], CT[:, 1, j, :], B[:, j, :], False, jj == 3, ["s5CT", K_("B")], [pk])
                    k.cp("act", w["y"][:, kc, :], p[:, 0:TT], [pk], [K_("y")])
                    release(c, pk)
                yield
                k.store(YSv[d][:, :, a0:a0 + TT], w["y"][:], reads=[K_("y")])
                yield
            if kind == 0:
                k.store(O["ns5"][pi_, l, d].rearrange("g n r -> (g n) r").rearrange("(j p) r -> p j r", p=128), init[:], reads=[K_("init")])
            yield

        jobs = []
        for (s0, T, kind, pi_) in cfg.seqs:
            for d in range(2):
                jobs.append(lambda w, s0=s0, T=T, kind=kind, pi_=pi_, d=d: sweep(w, s0, T, kind, pi_, d))
        g = drive_gen(jobs, wsets)
        if part == "scan":
            return g
        run_concurrent([g])
        k.barrier()
        es.close()

    if (cfg.stages is not None and "s5_noout" in cfg.stages) or part not in (None, "out", "out_g"):
        return
    es = ExitStack() if es_ext is None else es_ext
    OTv = S["OT"].rearrange("(c p) t -> p c t", p=128)
    col = k.sb("s5o_col", [128, 2, 2], F32, es)
    glu = k.sb("s5o_glu", [128, 2, 256], F32, es)
    k.load(col[:], I["s5col"][l], writes=["s5col"])
    k.load(glu[:], I["s5glu"][l].rearrange("(kc p) n -> p kc n", p=128), writes=["s5glu"])
    y0 = k.sb("s5o_y0", [128, 2, 512], F32, es)
    y1 = k.sb("s5o_y1", [128, 2, 512], F32, es)
    u = k.sb("s5o_u", [128, 2, 512], F32, es)
    t_ = k.sb("s5o_t", [128, 2, 512], F32, es)
    def _gen():
        for ti, (t0, n, var) in enumerate(cfg.tiles):
            yield
            k.load(y0[:, :, 0:n], YSv[0][:, :, t0:t0 + n], reads=["YS"], writes=["sy0"])
            k.load(y1[:, :, 0:n], YSv[1][:, :, t0:t0 + n], reads=["YS"], writes=["sy1"])
            k.load(u[:, :, 0:n], PTv[:, 8:10, t0:t0 + n], reads=["PT"], writes=["su"])
            k.tt("dve", y0[:, :, 0:n], y0[:, :, 0:n], y1[:, :, 0:n], ALU.add, ["sy0", "sy1"], ["sy0"])
            for hc in range(2):
                k.stt(y0[:, hc, 0:n], u[:, hc, 0:n], col[:, hc, 0:1], y0[:, hc, 0:n], ALU.mult, ALU.add, ["su", "s5col", "sy0"], ["sy0"])
            k.actf(t_[:, :, 0:n], y0[:, :, 0:n], AF.Square, ["sy0"], ["st"])
            k.ts("dve", t_[:, :, 0:n], t_[:, :, 0:n], 0.044715, 1.0, ALU.mult, ALU.add, ["st"], ["st"])
            k.tt("dve", t_[:, :, 0:n], t_[:, :, 0:n], y0[:, :, 0:n], ALU.mult, ["st", "sy0"], ["st"])
            k.actf(t_[:, :, 0:n], t_[:, :, 0:n], AF.Sigmoid, ["st"], ["st"], scale=1.5957691216057308)
            k.tt("dve", y0[:, :, 0:n], y0[:, :, 0:n], t_[:, :, 0:n], ALU.mult, ["st", "sy0"], ["sy0"])
            for hc in range(2):
                p, pk = nextbank(c)
                for kc in range(2):
                    k.mm(p[:, 0:n], glu[:, kc, hc * 128:(hc + 1) * 128], y0[:, kc, 0:n], kc == 0, kc == 1, ["s5glu", "sy0"], [pk])
                k.actf(y1[:, hc, 0:n], p[:, 0:n], AF.Sigmoid, [pk, "s5col", "sy1"], [("s5sg", hc)], bias=col[:, hc, 1:2])
                k.tt("dve", y1[:, hc, 0:n], y1[:, hc, 0:n], y0[:, hc, 0:n], ALU.mult, [("s5sg", hc), "sy0"], [("s5sg", hc)])
            k.store(OTv[:, 2:4, t0:t0 + n], y1[:, :, 0:n], reads=[("s5sg", 0), ("s5sg", 1)], writes=["OT5", "sy1"])
        yield
    _g = _gen()
    if part == "out_g":
        return _g
    run_concurrent([_g])
    k.barrier()
    es.close()
def cols(v):
    v = np.asarray(v)
    n = v.shape[-1] // 128
    return np.ascontiguousarray(np.swapaxes(v.reshape(v.shape[:-1] + (n, 128)), -1, -2))


def prep_core(inp, b, cfg):
    NP, TP = cfg.NP, cfg.TP
    m = {}
    xs = inp["x_sample"][b]
    xp = inp["x_prompt"][b * NP:(b + 1) * NP].reshape(NP * TP, D)
    m["xin"] = np.ascontiguousarray(np.concatenate([xs, xp], axis=0))
    m["ident"] = np.eye(128, dtype=np.float32)
    cc = np.stack([cols(inp["c_ctx"]), cols(inp["c"][b])], axis=-1)
    m["ccol"] = np.ascontiguousarray(cc.astype(np.float32))
    m["w_mod"] = inp["w_mod"]
    m["bmod"] = cols(inp["b_mod"])
    m["lng"] = cols(inp["ln_g"])
    m["lnb"] = cols(inp["ln_b"])
    for nm in ("ffn_w_in", "ffn_w_out", "w_in", "w_out"):
        m[nm] = inp[nm]
    m["cache_k"] = np.ascontiguousarray(inp["cache_na_k"][b].reshape(L, 256, 256))
    m["cache_v"] = np.ascontiguousarray(inp["cache_na_v"][b].reshape(L, 256, 256))
    cc_, ww_ = np.meshgrid(np.arange(64), np.arange(64), indexing="ij")
    idx = np.clip(cc_ - ww_ + 15, 0, 30)
    rp = inp["na_rpb"][:, :, :, idx]
    m["rpbT"] = np.ascontiguousarray(np.transpose(rp, (0, 3, 1, 2, 4)))
    cs = np.clip(ww_ - 8, 0, 48)
    m["namask"] = np.where((cc_ >= cs) & (cc_ < cs + 16), 0.0, NEG).astype(np.float32)
    a_, b_ = np.meshgrid(np.arange(64), np.arange(64), indexing="ij")
    UI, US, LI, LS = (a_ <= b_), (a_ < b_), (a_ >= b_), (a_ > b_)
    m["tmask"] = np.stack([UI, US, LI, LS, -1.0 * US, -1.0 * LS]).astype(np.float32)
    bo = np.zeros((128, 128), np.float32)
    bo[0:64, 0:64] = 1.0
    bo[64:128, 64:128] = 1.0
    m["bones"] = bo
    hl = cols(inp["hg_lb"])
    m["hgcol"] = np.ascontiguousarray(np.stack([hl[0], hl[1], hl[1]], axis=-1).astype(np.float32))
    m["hgng"] = cols(inp["hg_norm_g"])
    m["st_hg"] = np.ascontiguousarray(inp["state_hgrn"][b])
    m["rwmu"] = cols(inp["rw_mu"])
    m["rwmu_ad"] = np.ascontiguousarray(np.transpose(inp["rw_mu"][:, :, 832:896], (0, 2, 1)))
    m["rww0"] = cols(inp["rw_w0"])
    m["rw_w_up"] = inp["rw_w_up"]
    m["rw_a_up"] = inp["rw_a_up"]
    m["rw_g_up"] = inp["rw_g_up"]
    rk = inp["rw_r_k"].reshape(L, 256)
    m["rwcol"] = np.ascontiguousarray(np.stack([cols(inp["rw_a0"]), cols(inp["rw_k_k"]), cols(inp["rw_k_a"]), cols(rk),
                                                 cols(inp["rw_lnx_g"]), cols(inp["rw_lnx_b"])], axis=-1).astype(np.float32))
    m["st_rw"] = np.ascontiguousarray(inp["state_rwkv"][b])
    def scol(v):
        return cols(v.reshape(v.shape[:-2] + (1024,)))
    ldt = np.repeat(inp["s5_log_dt"][..., None], 64, axis=-1)
    m["s5lam"] = np.ascontiguousarray(np.stack([scol(inp["s5_a_re"]), scol(inp["s5_a_im"]), scol(ldt)], axis=-1).astype(np.float32))
    BT = np.zeros((L, 2, 8, 128, 128), np.float32)
    CT = np.zeros((L, 2, 8, 128, 128), np.float32)
    for r, (bsrc, csrc) in enumerate(((inp["s5_b_re"], inp["s5_c_re"]), (inp["s5_b_im"], inp["s5_c_im"]))):
        for j in range(8):
            for gl in range(2):
                g = 2 * j + gl
                r0 = 32 * (j % 4) + 16 * gl
                BT[:, r, j, r0:r0 + 16, 64 * gl:64 * gl + 64] = np.transpose(bsrc[:, g], (0, 2, 1))
                CT[:, r, j, 64 * gl:64 * gl + 64, r0:r0 + 16] = np.transpose(csrc[:, g], (0, 2, 1))
    m["s5BT"], m["s5CT"] = BT, CT
    m["s5col"] = np.ascontiguousarray(np.stack([cols(inp["s5_d"]), cols(inp["s5_glu_b"])], axis=-1).astype(np.float32))
    m["s5glu"] = inp["s5_glu_w"]
    st = inp["state_s5"][b].reshape(L, 2, 8, 128, 2)
    m["st_s5"] = np.ascontiguousarray(np.transpose(st, (0, 1, 3, 2, 4)))
    return m


_CACHE = {}


def kernel(**inputs):
    cfg = Cfg()
    inp = {k_: np.asarray(v) for k_, v in inputs.items()}
    if "nc" not in _CACHE:
        _CACHE["nc"] = build(cfg)
    nc, c = _CACHE["nc"]
    in_maps = [prep_core(inp, b, cfg) for b in range(8)]
    res = run_bass_kernel_spmd(nc, in_maps, core_ids=list(range(8)))
    R = res.results
    NP, TP = cfg.NP, cfg.TP
    y_p = np.concatenate([r["y_p"].reshape(NP, TP, D) for r in R], axis=0)
    y_s = np.stack([r["y_s"] for r in R], axis=0)
    nk = np.concatenate([r["nk"].reshape(NP, L, TP, 4, 64) for r in R], axis=0)
    nv = np.concatenate([r["nv"].reshape(NP, L, TP, 4, 64) for r in R], axis=0)
    nrw = np.concatenate([r["nrw"] for r in R], axis=0)
    ns5 = np.concatenate([r["ns5"] for r in R], axis=0)
    nhg = np.concatenate([r["nhg"] for r in R], axis=0)
    return (y_p.astype(np.float32), y_s.astype(np.float32), nk.astype(np.float32), nv.astype(np.float32),
            nrw.astype(np.float32), ns5.astype(np.float32), nhg.astype(np.float32))
```
